# Optimizing a Trainium2 kernel written in Bass

```python
import math
import jax, jax.numpy as jnp
from jax import lax
import numpy as np

D_MODEL = 2048
BATCH = 8
SEQ = 2048
DEPTH = 1
DEC_BATCH = 128
DEC_SEQ = 4
PAST_LEN = 2048
PAGE_SIZE = 128

M_HEADS = 4
M_DK = D_MODEL // 8
M_DV = D_MODEL // 8
M_WIDTH = M_HEADS * M_DV
CONV_W = 4
M_CHUNK = 64
A_HEADS = 8
A_HEAD_DIM = D_MODEL // 16
A_WIDTH = A_HEADS * A_HEAD_DIM
A_KV_HEADS = 2
IDX_HEADS = 8
IDX_DIM = 64
TOPK_MAX = 256
Q_BLOCK = 128
N_BUCKETS = 32
MAX_DISTANCE = 128
D_FF = 5632
N_MOD = 9
EPS = 1e-6
NEG = -1e30
PROJ_WIDTH = 4 * M_WIDTH + 2 * M_HEADS + A_WIDTH + 2 * A_KV_HEADS * A_HEAD_DIM + IDX_HEADS * IDX_DIM + IDX_DIM + IDX_HEADS

kernel_name = 'hymba_mlstm_dsa_macaron_step'


def proj_offsets():
    sizes = [M_WIDTH, M_WIDTH, M_WIDTH, M_WIDTH, M_HEADS, M_HEADS,
             A_WIDTH, A_KV_HEADS * A_HEAD_DIM, A_KV_HEADS * A_HEAD_DIM,
             IDX_HEADS * IDX_DIM, IDX_DIM, IDX_HEADS]
    return np.cumsum(sizes)[:-1].tolist()


def rms_norm(x, g):
    x32 = x.astype(jnp.float32)
    y = x32 * lax.rsqrt(jnp.mean(x32 * x32, axis=-1, keepdims=True) + EPS)
    return (y * g.astype(jnp.float32)).astype(x.dtype)


def swiglu(h, w_gate, w_up, w_down):
    return (jax.nn.silu(h @ w_gate) * (h @ w_up)) @ w_down


def t5_bucket(dist):
    max_exact = N_BUCKETS // 2
    d = jnp.maximum(dist, 0)
    ratio = jnp.log(jnp.maximum(d, 1).astype(jnp.float32) / max_exact) / math.log(MAX_DISTANCE / max_exact)
    large = max_exact + (ratio * (N_BUCKETS - max_exact)).astype(jnp.int32)
    large = jnp.minimum(large, N_BUCKETS - 1)
    return jnp.where(d < max_exact, d, large)


def causal_conv(x_new, buf, w):
    t = x_new.shape[1]
    xp = jnp.concatenate([buf.astype(x_new.dtype), x_new], axis=1)
    y = xp[:, 0:t] * w[0]
    for j in range(1, CONV_W):
        y = y + xp[:, j:j + t] * w[j]
    return y, xp[:, xp.shape[1] - (CONV_W - 1):]


def mlstm_inputs(q_raw, k_raw, v_raw, i_raw, f_raw, conv_buf, conv_w, gate_b):
    b, t, _ = q_raw.shape
    qk, new_buf = causal_conv(jnp.concatenate([q_raw, k_raw], axis=-1), conv_buf, conv_w)
    qk = jax.nn.silu(qk.astype(jnp.float32))

    def heads(a):
        return a.reshape(b, t, M_HEADS, -1).transpose(0, 2, 1, 3)

    q = heads(qk[..., :M_WIDTH])
    k = heads(qk[..., M_WIDTH:]) * (M_DK ** -0.5)
    v = heads(v_raw.astype(jnp.float32))
    gates = (jnp.concatenate([i_raw, f_raw], axis=-1) + gate_b).astype(jnp.float32).transpose(0, 2, 1)
    ig = gates[:, :M_HEADS]
    lf = jax.nn.log_sigmoid(gates[:, M_HEADS:])
    return q, k, v, ig, lf, new_buf


def mlstm_chunk(carry, inp):
    c0, n0, m0 = carry
    q, k, v, ig, lf = inp
    l = q.shape[2]
    bcum = jnp.cumsum(lf, axis=-1)
    causal = jnp.tril(jnp.ones((l, l), bool))
    d_log = jnp.where(causal, bcum[..., :, None] - bcum[..., None, :] + ig[..., None, :], -jnp.inf)
    s_log = bcum + m0[..., None]
    m = jnp.maximum(s_log, jnp.max(d_log, axis=-1))
    dw = jnp.exp(d_log - m[..., None])
    sw = jnp.exp(s_log - m)
    scores = jnp.einsum('bhld,bhsd->bhls', q, k) * dw
    num = jnp.einsum('bhls,bhse->bhle', scores, v) + sw[..., None] * jnp.einsum('bhld,bhde->bhle', q, c0)
    den = jnp.sum(scores, axis=-1) + sw * jnp.einsum('bhld,bhd->bhl', q, n0)
    h = num / jnp.maximum(jnp.abs(den), jnp.exp(-m))[..., None]
    m_new = m[..., -1]
    wl = jnp.exp(bcum[..., -1:] - bcum + ig - m_new[..., None])
    decay = jnp.exp(bcum[..., -1] + m0 - m_new)
    c_new = decay[..., None, None] * c0 + jnp.einsum('bhl,bhld,bhle->bhde', wl, k, v)
    n_new = decay[..., None] * n0 + jnp.einsum('bhl,bhld->bhd', wl, k)
    return (c_new, n_new, m_new), h


def mlstm_scan(state, q, k, v, ig, lf):
    b, hh, t, _ = q.shape
    l = M_CHUNK if t % M_CHUNK == 0 else t
    nc = t // l

    def chunks(a):
        return jnp.moveaxis(a.reshape(a.shape[:2] + (nc, l) + a.shape[3:]), 2, 0)

    state, h = lax.scan(mlstm_chunk, state, (chunks(q), chunks(k), chunks(v), chunks(ig), chunks(lf)))
    h = jnp.moveaxis(h, 0, 2).reshape(b, hh, t, -1)
    return state, h


def mlstm_out(h, o_raw, g):
    b, hh, t, dv = h.shape
    hn = h * lax.rsqrt(jnp.mean(h * h, axis=-1, keepdims=True) + EPS)
    hn = jnp.swapaxes(hn, 1, 2) * g.reshape(hh, dv).astype(jnp.float32)
    return (hn.reshape(b, t, hh * dv) * jax.nn.sigmoid(o_raw.astype(jnp.float32))).astype(o_raw.dtype)


def dsa_heads(q_raw, k_raw, v_raw, qi_raw, ki_raw, w_raw, qn_g, kn_g):
    b, t, _ = q_raw.shape
    q = rms_norm(q_raw.reshape(b, t, A_HEADS, A_HEAD_DIM), qn_g)
    k = rms_norm(k_raw.reshape(b, t, A_KV_HEADS, A_HEAD_DIM), kn_g)
    v = v_raw.reshape(b, t, A_KV_HEADS, A_HEAD_DIM)
    qi = qi_raw.reshape(b, t, IDX_HEADS, IDX_DIM)
    wi = w_raw * (IDX_HEADS ** -0.5)
    return q, k, v, qi, ki_raw, wi


def indexer_scores(qi, wi, ki, q_pos, k_pos):
    s = jnp.einsum('btjd,bsd->btjs', qi.astype(jnp.float32), ki.astype(jnp.float32)) * (IDX_DIM ** -0.5)
    score = jnp.einsum('btj,btjs->bts', wi.astype(jnp.float32), jax.nn.relu(s))
    return jnp.where(k_pos[None, None, :] <= q_pos[None, :, None], score, NEG)


def sparse_attend(q, kg, vg, sel_pos, q_pos, t5_bias):
    b, t, _, hd = q.shape
    g, r = A_KV_HEADS, A_HEADS // A_KV_HEADS
    n = sel_pos.shape[-1]
    qg = q.reshape(b, t, g, r, hd).astype(jnp.float32)
    logits = jnp.einsum('btgrd,btngd->btgrn', qg, kg.astype(jnp.float32)) * (hd ** -0.5)
    dist = q_pos[None, :, None] - sel_pos
    bias = t5_bias[t5_bucket(dist)].astype(jnp.float32)
    bias = bias.reshape(b, t, n, g, r).transpose(0, 1, 3, 4, 2)
    logits = jnp.where((dist >= 0)[:, :, None, None, :], logits + bias, NEG)
    p = jax.nn.softmax(logits, axis=-1)
    o = jnp.einsum('btgrn,btngd->btgrd', p, vg.astype(jnp.float32))
    return o.reshape(b, t, A_HEADS * hd).astype(q.dtype)


def dsa_prompt(q, k, v, qi, ki, wi, t5_bias):
    b, t = q.shape[:2]
    k_sel = min(TOPK_MAX, t // 4)
    nb = t // Q_BLOCK
    pos = jnp.arange(t, dtype=jnp.int32)

    def to_blocks(a):
        return jnp.swapaxes(a.reshape((a.shape[0], nb, Q_BLOCK) + a.shape[2:]), 0, 1)

    def gather_rows(rows, idx):
        return jax.vmap(lambda rb, ib: rb[ib])(rows, idx)

    def block(args):
        qb, qib, wb, pb = args
        score = indexer_scores(qib, wb, ki, pb, pos)
        _, idx = lax.top_k(score, k_sel)
        return sparse_attend(qb, gather_rows(k, idx), gather_rows(v, idx), idx, pb, t5_bias)

    out = lax.map(block, (to_blocks(q), to_blocks(qi), to_blocks(wi), pos.reshape(nb, Q_BLOCK)))
    return jnp.swapaxes(out, 0, 1).reshape(b, t, A_WIDTH)


def dsa_sample(q, k, v, qi, ki, wi, cache_k, cache_v, cache_idx_k, page_table, t5_bias):
    b, t = q.shape[:2]
    n_pages = page_table.shape[1]
    past = n_pages * PAGE_SIZE
    total = past + t
    k_sel = min(TOPK_MAX, total // 4)
    q_pos = past + jnp.arange(t, dtype=jnp.int32)
    k_pos = jnp.arange(total, dtype=jnp.int32)
    ki_past = cache_idx_k[page_table].reshape(b, past, IDX_DIM)
    ki_all = jnp.concatenate([ki_past.astype(ki.dtype), ki], axis=1)
    score = indexer_scores(qi, wi, ki_all, q_pos, k_pos)
    _, idx = lax.top_k(score, k_sel)
    in_past = idx < past
    pidx = jnp.minimum(idx, past - 1)
    phys = page_table[jnp.arange(b)[:, None, None], pidx // PAGE_SIZE] * PAGE_SIZE + pidx % PAGE_SIZE
    nidx = jnp.clip(idx - past, 0, t - 1)

    def gather(pool, rows):
        from_past = pool.reshape(-1, A_KV_HEADS, A_HEAD_DIM)[phys]
        from_new = jax.vmap(lambda rb, ib: rb[ib])(rows, nidx)
        return jnp.where(in_past[..., None, None], from_past.astype(rows.dtype), from_new)

    return sparse_attend(q, gather(cache_k, k), gather(cache_v, v), idx, q_pos, t5_bias)


def decoder_layer(x, c, rec_state, past, t5_bias, ffn1_norm_g, ffn1_w_gate, ffn1_w_up, ffn1_w_down,
                  mix_norm_g, w_in, mlstm_conv_w, mlstm_gate_b, mlstm_out_g, q_norm_g, k_norm_g, w_out,
                  ffn2_norm_g, ffn2_w_gate, ffn2_w_up, ffn2_w_down, w_ada, b_ada):
    mods = (c @ w_ada + b_ada)[:, None, :]
    sh1, sc1, g1, sh2, sc2, g2, sh3, sc3, g3 = jnp.split(mods, N_MOD, axis=-1)
    h = rms_norm(x, ffn1_norm_g) * (1 + sc1) + sh1
    x = x + 0.5 * g1 * swiglu(h, ffn1_w_gate, ffn1_w_up, ffn1_w_down)
    h = rms_norm(x, mix_norm_g) * (1 + sc2) + sh2
    (q_m, k_m, v_m, o_m, i_m, f_m, q_a, k_a, v_a, qi_a, ki_a, wi_a) = jnp.split(h @ w_in, proj_offsets(), axis=-1)
    c0, n0, m0, conv_buf = rec_state
    q, k, v, ig, lf, conv_new = mlstm_inputs(q_m, k_m, v_m, i_m, f_m, conv_buf, mlstm_conv_w, mlstm_gate_b)
    (c_new, n_new, m_new), hm = mlstm_scan((c0, n0, m0), q, k, v, ig, lf)
    mo = mlstm_out(hm, o_m, mlstm_out_g)
    qa, ka, va, qi, ki, wi = dsa_heads(q_a, k_a, v_a, qi_a, ki_a, wi_a, q_norm_g, k_norm_g)
    if past is None:
        ao = dsa_prompt(qa, ka, va, qi, ki, wi, t5_bias)
    else:
        ao = dsa_sample(qa, ka, va, qi, ki, wi, past[0], past[1], past[2], past[3], t5_bias)
    x = x + g2 * (jnp.concatenate([mo, ao], axis=-1) @ w_out)
    h = rms_norm(x, ffn2_norm_g) * (1 + sc3) + sh3
    x = x + 0.5 * g3 * swiglu(h, ffn2_w_gate, ffn2_w_up, ffn2_w_down)
    new_state = (ka, va, ki, c_new.astype(x.dtype), n_new.astype(x.dtype), m_new.astype(x.dtype), conv_new)
    return x, new_state


def setup_inputs(seed: int = 0) -> dict:
    key = jax.random.key(seed)
    ks = iter(jax.random.split(key, 40))

    def nrm(shape, scale=1.0):
        return scale * jax.random.normal(next(ks), shape, jnp.float32)

    def gain(shape):
        return 1.0 + nrm(shape, 0.05)

    d = D_MODEL
    n_pages = PAST_LEN // PAGE_SIZE
    n_used = DEC_BATCH * n_pages
    n_pool = n_used + max(1, n_used // 4)
    page_table = jax.random.permutation(next(ks), n_pool)[:n_used].reshape(DEC_BATCH, n_pages).astype(jnp.int32)
    return {
        'x_prompt': nrm((BATCH, SEQ, d)),
        'x_sample': nrm((DEC_BATCH, DEC_SEQ, d)),
        'c_prompt': nrm((BATCH, d)),
        'c_sample': nrm((DEC_BATCH, d)),
        'cache_k': nrm((DEPTH, n_pool, PAGE_SIZE, A_KV_HEADS, A_HEAD_DIM)),
        'cache_v': nrm((DEPTH, n_pool, PAGE_SIZE, A_KV_HEADS, A_HEAD_DIM)),
        'cache_idx_k': nrm((DEPTH, n_pool, PAGE_SIZE, IDX_DIM)),
        'page_table': page_table,
        'state_C': nrm((DEPTH, DEC_BATCH, M_HEADS, M_DK, M_DV), 0.1),
        'state_n': nrm((DEPTH, DEC_BATCH, M_HEADS, M_DK), 0.1),
        'state_m': nrm((DEPTH, DEC_BATCH, M_HEADS)),
        'state_conv': nrm((DEPTH, DEC_BATCH, CONV_W - 1, 2 * M_WIDTH)),
        'ffn1_norm_g': gain((DEPTH, d)),
        'ffn1_w_gate': nrm((DEPTH, d, D_FF), d ** -0.5),
        'ffn1_w_up': nrm((DEPTH, d, D_FF), d ** -0.5),
        'ffn1_w_down': nrm((DEPTH, D_FF, d), D_FF ** -0.5),
        'mix_norm_g': gain((DEPTH, d)),
        'w_in': nrm((DEPTH, d, PROJ_WIDTH), d ** -0.5),
        'mlstm_conv_w': nrm((DEPTH, CONV_W, 2 * M_WIDTH), CONV_W ** -0.5),
        'mlstm_gate_b': jnp.concatenate([nrm((DEPTH, M_HEADS), 0.1), 3.0 + nrm((DEPTH, M_HEADS), 0.5)], axis=-1),
        'mlstm_out_g': gain((DEPTH, M_WIDTH)),
        'q_norm_g': gain((DEPTH, A_HEAD_DIM)),
        'k_norm_g': gain((DEPTH, A_HEAD_DIM)),
        't5_bias': nrm((N_BUCKETS, A_HEADS), 0.5),
        'w_out': nrm((DEPTH, M_WIDTH + A_WIDTH, d), (M_WIDTH + A_WIDTH) ** -0.5),
        'ffn2_norm_g': gain((DEPTH, d)),
        'ffn2_w_gate': nrm((DEPTH, d, D_FF), d ** -0.5),
        'ffn2_w_up': nrm((DEPTH, d, D_FF), d ** -0.5),
        'ffn2_w_down': nrm((DEPTH, D_FF, d), D_FF ** -0.5),
        'w_ada': nrm((DEPTH, d, N_MOD * d), 0.5 * d ** -0.5),
        'b_ada': nrm((DEPTH, N_MOD * d), 0.02),
    }


def reference(x_prompt, x_sample, c_prompt, c_sample, cache_k, cache_v, cache_idx_k, page_table,
              state_C, state_n, state_m, state_conv, ffn1_norm_g, ffn1_w_gate, ffn1_w_up, ffn1_w_down,
              mix_norm_g, w_in, mlstm_conv_w, mlstm_gate_b, mlstm_out_g, q_norm_g, k_norm_g, t5_bias,
              w_out, ffn2_norm_g, ffn2_w_gate, ffn2_w_up, ffn2_w_down, w_ada, b_ada):
    xp, xs = x_prompt, x_sample
    bp = xp.shape[0]
    st_p, st_s = [], []
    for l in range(DEPTH):
        lw = (ffn1_norm_g[l], ffn1_w_gate[l], ffn1_w_up[l], ffn1_w_down[l], mix_norm_g[l], w_in[l],
              mlstm_conv_w[l], mlstm_gate_b[l], mlstm_out_g[l], q_norm_g[l], k_norm_g[l], w_out[l],
              ffn2_norm_g[l], ffn2_w_gate[l], ffn2_w_up[l], ffn2_w_down[l], w_ada[l], b_ada[l])
        init_p = (jnp.zeros((bp, M_HEADS, M_DK, M_DV), jnp.float32),
                  jnp.zeros((bp, M_HEADS, M_DK), jnp.float32),
                  jnp.zeros((bp, M_HEADS), jnp.float32),
                  jnp.zeros((bp, CONV_W - 1, 2 * M_WIDTH), xp.dtype))
        xp, sp = decoder_layer(xp, c_prompt, init_p, None, t5_bias, *lw)
        init_s = (state_C[l].astype(jnp.float32), state_n[l].astype(jnp.float32),
                  state_m[l].astype(jnp.float32), state_conv[l])
        past_s = (cache_k[l], cache_v[l], cache_idx_k[l], page_table)
        xs, ss = decoder_layer(xs, c_sample, init_s, past_s, t5_bias, *lw)
        st_p.append(sp)
        st_s.append(ss)
    return (xp, xs,
            jnp.stack([s[0] for s in st_p]), jnp.stack([s[1] for s in st_p]), jnp.stack([s[2] for s in st_p]),
            jnp.stack([s[3] for s in st_p]), jnp.stack([s[4] for s in st_p]), jnp.stack([s[5] for s in st_p]),
            jnp.stack([s[6] for s in st_p]),
            jnp.stack([s[0] for s in st_s]), jnp.stack([s[1] for s in st_s]), jnp.stack([s[2] for s in st_s]),
            jnp.stack([s[3] for s in st_s]), jnp.stack([s[4] for s in st_s]), jnp.stack([s[5] for s in st_s]),
            jnp.stack([s[6] for s in st_s]))
```

```python
import contextlib
import types
import numpy as np
import concourse.bass as bass
import concourse.mybir as mybir
from concourse.bass_utils import run_bass_kernel_spmd

F32 = mybir.dt.float32
BF16 = mybir.dt.bfloat16
I32 = mybir.dt.int32
U32 = mybir.dt.uint32
AF = mybir.ActivationFunctionType
ALU = mybir.AluOpType
AX = mybir.AxisListType


class Res:
    __slots__ = ("name", "w", "rs", "rd")

    def __init__(self, name=""):
        self.name = name
        self.w = None
        self.rs = {}
        self.rd = []


def _freeze(fn):
    cl = fn.__closure__
    if not cl:
        return fn
    cells = []
    for c in cl:
        try:
            cells.append(types.CellType(c.cell_contents))
        except ValueError:
            cells.append(c)
    g = types.FunctionType(fn.__code__, fn.__globals__, fn.__name__, fn.__defaults__, tuple(cells))
    g.__kwdefaults__ = fn.__kwdefaults__
    return g


class Op:
    __slots__ = ("eng", "fn", "dma", "n", "deps", "need_inc", "sem", "val", "prev_val")

    def __init__(self, eng, fn, dma, n):
        self.eng = eng
        self.fn = fn
        self.dma = dma
        self.n = n
        self.deps = []
        self.need_inc = False
        self.sem = None
        self.val = 0
        self.prev_val = 0


ENGS = ("sp", "act", "dve", "pool", "pe")
NDSEM = 20


class Prog:
    def __init__(self, nc):
        self.nc = nc
        self.ops = []
        self.last = {}
        self.pend_dma = []

    def add(self, eng, fn, reads=(), writes=(), dma=False, n=1):
        op = Op(eng, _freeze(fn), dma, n)
        deps = []
        for r in reads:
            if r.w is not None:
                deps.append((r.w, 0))
        for w in writes:
            if w.w is not None:
                deps.append((w.w, 1))
            for o in w.rs.values():
                deps.append((o, 2))
            for o in w.rd:
                deps.append((o, 2))
        for r in reads:
            if dma:
                r.rd.append(op)
            else:
                r.rs[eng] = op
        for w in writes:
            w.w = op
            w.rs = {}
            w.rd = []
        seen = set()
        for p, kind in deps:
            if p is op or id(p) in seen:
                continue
            if p.eng == eng and not p.dma and not dma:
                if eng == "pe" or kind != 0:
                    continue
            seen.add(id(p))
            op.deps.append(p)
            p.need_inc = True
        self.ops.append(op)
        if dma:
            self.pend_dma.append(op)
        else:
            self.last[eng] = op
        return op

    def barrier(self):
        lasts = dict(self.last)
        dmas = list(self.pend_dma)
        self.pend_dma = []
        for e in ENGS:
            op = Op(e, lambda eng: eng.nop(), False, 1)
            for x, p in lasts.items():
                if x != e:
                    op.deps.append(p)
                    p.need_inc = True
            for p in dmas:
                op.deps.append(p)
            self.ops.append(op)
            self.last[e] = op

    def emit(self):
        nc = self.nc
        with contextlib.ExitStack() as es:
            csem = {e: es.enter_context(nc.semaphore("c_" + e)) for e in ENGS}
            dsem = {e: [es.enter_context(nc.semaphore("d_%s%d" % (e, i))) for i in range(NDSEM)]
                    for e in ("sp", "act", "pool")}
            cnt = {e: 0 for e in ENGS}
            dcount = {e: 0 for e in dsem}
            dval = {e: [0] * NDSEM for e in dsem}
            for op in self.ops:
                if op.dma:
                    q = op.eng
                    slot = dcount[q] % NDSEM
                    dcount[q] += 1
                    op.sem = dsem[q][slot]
                    op.prev_val = dval[q][slot]
                    dval[q][slot] += 16 * op.n
                    op.val = dval[q][slot]
                elif op.need_inc:
                    cnt[op.eng] += 1
                    op.val = cnt[op.eng]
                    op.sem = csem[op.eng]
            per = {e: [o for o in self.ops if o.eng == e] for e in ENGS}

            def run(ename, e):
                waited = {}

                def wait(sem, val):
                    k = id(sem)
                    if waited.get(k, 0) < val:
                        e.wait_ge(sem, val)
                        waited[k] = val

                for op in per[ename]:
                    for p in op.deps:
                        wait(p.sem, p.val)
                    if op.dma:
                        if op.prev_val > 0:
                            wait(op.sem, op.prev_val)
                        ins = op.fn(e)
                        if not isinstance(ins, (list, tuple)):
                            ins = [ins]
                        assert len(ins) == op.n, (len(ins), op.n)
                        for i in ins:
                            i.then_inc(op.sem, 16)
                    else:
                        ins = op.fn(e)
                        if op.need_inc:
                            ins.then_inc(op.sem, 1)
                if ename in dsem:
                    for s, v in zip(dsem[ename], dval[ename]):
                        if v > 0:
                            wait(s, v)

            with nc.allow_non_contiguous_dma(reason="small strided state / index transfers"), nc.Block() as block:
                @block.sync
                def _(e):
                    run("sp", e)

                @block.scalar
                def _(e):
                    run("act", e)

                @block.vector
                def _(e):
                    run("dve", e)

                @block.gpsimd
                def _(e):
                    run("pool", e)

                @block.tensor
                def _(e):
                    run("pe", e)


D = 2048
NKC = 16
DFF = 5632
NFG = 22
NP_TOK = 2048
NS_TOK = 64
NT = NP_TOK + NS_TOK
NSB = 16
PW = 6224
EPS = 1e-6
TILES = [(0, 1024), (1024, 1088)]
TS = 1088


def subs_of(t0):
    if t0 == 0:
        return [(0, 512, False), (512, 512, False)]
    return [(0, 512, False), (512, 512, False), (1024, 64, True)]


def padw(width, dt):
    per = 64 // mybir.dt.size(dt)
    return ((width + per - 1) // per) * per


class SB:
    def __init__(self, t, width, dt):
        self.t = t
        self.W = width
        self.dt = dt

    def v(self, off, *dims, p0=0, np_=128):
        return bass.AP(self.t, p0 * self.W + off, [[self.W, np_]] + [list(d) for d in dims])


class SBV(SB):
    def __init__(self, base, off0):
        self.t = base.t
        self.W = base.W
        self.dt = base.dt
        self.off0 = off0

    def v(self, off, *dims, p0=0, np_=128):
        return bass.AP(self.t, p0 * self.W + self.off0 + off, [[self.W, np_]] + [list(d) for d in dims])


def dv(ap, off, *dims):
    return bass.AP(ap.tensor, off, [list(d) for d in dims])


class Ctx:
    def dump(self, name, sbt, reads):
        if ("dump_" + name) not in self.dbg:
            return
        o = self.nc.dram_tensor("dump_" + name, [128, sbt.W], sbt.dt, kind="ExternalOutput").ap()
        self.P.add("sp", lambda e: e.dma_start(out=o, in_=sbt.v(0, (1, sbt.W))), reads=list(reads), dma=True)


def build_program(dbg=None):
    dbg = dbg or set()
    nc = bass.Bass("TRN2", target_bir_lowering=False)
    P = Prog(nc)
    C = Ctx()
    C.nc = nc
    C.P = P
    C.dbg = dbg

    def din(name, shape, dt=F32):
        return nc.dram_tensor(name, list(shape), dt, kind="ExternalInput").ap()

    def dout(name, shape, dt=F32):
        return nc.dram_tensor(name, list(shape), dt, kind="ExternalOutput").ap()

    def dscr(name, shape, dt=F32):
        return nc.dram_tensor(name, list(shape), dt, kind="Internal").ap()

    I = {}
    I["xp"] = din("xp", [NP_TOK, D])
    I["xs"] = din("xs", [NS_TOK, D])
    I["c17"] = din("c17", [17, D])
    I["ident"] = din("ident", [128, 128])
    for nm in ("ffn1", "ffn2"):
        I[nm + "_g"] = din(nm + "_norm_g", [NKC, 128])
        I[nm + "_wg"] = din(nm + "_w_gate", [D, DFF])
        I[nm + "_wu"] = din(nm + "_w_up", [D, DFF])
        I[nm + "_wd"] = din(nm + "_w_down", [DFF, D])
    I["mix_g"] = din("mix_norm_g", [NKC, 128])
    I["w_ada"] = din("w_ada", [D, 9 * D])
    I["b_ada"] = din("b_ada", [144, 128])
    I["w_in"] = din("w_in", [D, PW])
    I["w_out"] = din("w_out", [D, D])
    I["gate_b"] = din("gate_b", [8, 1])
    I["qng"] = din("qng", [128, 1])
    I["kng"] = din("kng", [128, 1])
    I["conv_w"] = din("conv_w", [64, 128])
    I["out_g"] = din("out_g", [8, 128])
    I["tri"] = din("tri", [64, 64])
    I["sel8"] = din("sel8", [8, 1024])
    I["mk64"] = din("mk64", [1, 512])
    I["mks"] = din("mks", [1, 192])
    I["st_C"] = din("st_C", [NSB, 4, 256, 256])
    I["st_n"] = din("st_n", [NSB, 4, 256])
    I["st_m"] = din("st_m", [NSB, 4])
    I["st_conv"] = din("st_conv", [NSB * 3, D])
    I["bdp"] = din("bdp", [6, 128, 512])
    I["bsp"] = din("bsp", [2, 128, 256])
    I["bsn"] = din("bsn", [2, 4, 16])
    I["negtri"] = din("negtri", [128, 128])
    I["ptab"] = din("ptab", [1, 256], I32)
    I["pidx"] = din("pidx", [128, 1])
    NPOOL = 2560
    I["ck"] = din("ck", [NPOOL * 128, 256])
    I["cv"] = din("cv", [NPOOL * 128, 256])
    I["cidx"] = din("cidx", [NPOOL * 128, 64])
    C.I = I
    O = {}
    C.O = O
    S = {}
    C.S = S
    S["X1"] = din("X1", [128, NKC, NT]) if "noffn" in dbg else dscr("X1", [128, NKC, NT])
    S["QK"] = dscr("QK", [128, NKC, NT])
    S["OM"] = dscr("OM", [128, 8, NT], BF16)
    S["QI"] = dscr("QI", [128, 4, NT], BF16)
    S["QA"] = dscr("QA", [128, 8, NT], BF16)
    S["KA"] = dscr("KA", [128, 2, NT], BF16)
    S["KI2"] = dscr("KI2", [128, NT], BF16)
    S["GT"] = dscr("GT", [8, NT])
    S["VM"] = dscr("VM", [NT, 1028], BF16)
    S["VA"] = dscr("VA", [NT, 256], BF16)
    S["WI"] = dscr("WI", [NT, 8])
    O["k_p"] = dout("k_p", [NP_TOK, 256])
    O["v_p"] = dout("v_p", [NP_TOK, 256])
    O["idxk_p"] = dout("idxk_p", [NP_TOK, 64])
    O["k_s"] = dout("k_s", [NS_TOK, 256])
    O["v_s"] = dout("v_s", [NS_TOK, 256])
    O["idxk_s"] = dout("idxk_s", [NS_TOK, 64])
    O["conv_p"] = dout("conv_p", [3, D])
    O["conv_s"] = dout("conv_s", [NSB, 3, D])
    O["y_p"] = dout("y_p", [NP_TOK, D])
    O["y_s"] = dout("y_s", [NS_TOK, D])
    O["C_p"] = dout("C_p", [4, 256, 256])
    O["n_p"] = dout("n_p", [4, 256])
    O["m_p"] = dout("m_p", [1, 4])
    O["C_s"] = dout("C_s", [NSB, 4, 256, 256])
    O["n_s"] = dout("n_s", [NSB, 4, 256])
    O["m_s"] = dout("m_s", [NSB, 4])
    S["MOAO"] = dscr("MOAO", [128, NKC, NT], BF16)
    if "MOAO" in dbg:
        O["dbg_MOAO"] = dout("dbg_MOAO", [128, NKC, NT], BF16)
    DBG_SCR = {"QK": F32, "OM": BF16, "QI": BF16, "QA": BF16, "KA": BF16, "KI2": BF16, "GT": F32,
               "VM": BF16, "VA": BF16, "WI": F32}
    for k_, dt_ in DBG_SCR.items():
        if k_ in dbg:
            O["dbg_" + k_] = dout("dbg_" + k_, list(S[k_].shape), dt_)
    if "X1" in dbg:
        O["dbg_X1"] = dout("dbg_X1", [128, NKC, NT])
    if "mods" in dbg:
        O["dbg_mods"] = dout("dbg_mods", [128, 144 * 17])

    with contextlib.ExitStack() as es:
        def sb(name, width, dt):
            return SB(es.enter_context(nc.sbuf_tensor("s_" + name, [128, padw(width, dt)], dt)), padw(width, dt), dt)

        C.sb = sb
        C.ps = [SB(es.enter_context(nc.psum_tensor("ps%d" % i, [128, 512], F32)), 512, F32) for i in range(8)]
        C.rps = [Res("ps%d" % i) for i in range(8)]
        C.ident = sb("ident", 128, F32)
        C.identb = sb("identb", 128, BF16)
        C.ones_b = sb("ones_b", 128, BF16)
        C.rconst = Res("const")
        P.add("sp", lambda e: e.dma_start(out=C.ident.v(0, (1, 128)), in_=I["ident"]), writes=[C.rconst], dma=True)
        P.add("dve", lambda e: e.tensor_copy(C.identb.v(0, (1, 128)), C.ident.v(0, (1, 128))),
              reads=[C.rconst], writes=[C.rconst])
        P.add("dve", lambda e: e.memset(C.ones_b.v(0, (1, 128)), 1.0), writes=[C.rconst])
        C.oneb = sb("oneb", 1, F32)
        P.add("dve", lambda e: e.memset(C.oneb.v(0, (1, 1)), 1.0), writes=[C.rconst])
        C.epsb = sb("epsb", 1, F32)
        P.add("dve", lambda e: e.memset(C.epsb.v(0, (1, 1)), EPS), writes=[C.rconst])
        C.mods = sb("mods", 144 * 17, F32)
        C.rmods = Res("mods")
        C.gn = {}
        for k in ("ffn1_g", "mix_g", "ffn2_g"):
            C.gn[k] = sb("gn_" + k, NKC, F32)
        C.rgn = Res("gn")

        phase_mods(C)
        if "mods" in dbg:
            P.add("sp", lambda e: e.dma_start(out=O["dbg_mods"], in_=C.mods.v(0, (1, 144 * 17))),
                  reads=[C.rmods], dma=True)
        rX1 = [Res("X1_%d" % i) for i in range(len(TILES))]
        C.rX1 = rX1

        def load_x_in(ti, xt, rxs):
            load_x_tokenmajor(C, ti, xt, rxs)

        def store_x1(ti, xt, rxs):
            t0, T = TILES[ti]
            P.add("sp", lambda e: e.dma_start(out=dv(S["X1"], t0, (NKC * NT, 128), (NT, NKC), (1, T)),
                                              in_=xt.v(0, (TS, NKC), (1, T))),
                  reads=[r for row in rxs for r in row], writes=[rX1[ti]], dma=True)

        if "noffn" not in dbg:
            phase_ffn(C, "ffn1", 0, load_x_in, store_x1)
        if "X1" in dbg:
            for ti, (t0, T) in enumerate(TILES):
                pass
            P.add("sp", lambda e: e.dma_start(out=O["dbg_X1"], in_=S["X1"]), reads=rX1, dma=True)
        if "noproj" not in dbg:
            phase_proj(C)
        for k_ in DBG_SCR:
            if k_ in dbg:
                P.add("sp", lambda e, k_=k_: e.dma_start(out=O["dbg_" + k_], in_=S[k_]), dma=True)
        if "nomlstm" not in dbg:
            phase_mlstm(C)
        if "nodsa" not in dbg:
            phase_dsa(C)
            if "dsa_nosample" not in dbg:
                phase_dsa_sample(C)
        if "MOAO" in dbg:
            P.add("sp", lambda e: e.dma_start(out=O["dbg_MOAO"], in_=S["MOAO"]), dma=True)

        def load_x1(ti, xt, rxs):
            t0, T = TILES[ti]
            P.add("sp", lambda e: e.dma_start(out=xt.v(0, (TS, NKC), (1, T)),
                                              in_=dv(S["X1"], t0, (NKC * NT, 128), (NT, NKC), (1, T))),
                  reads=[rX1[ti]], writes=[r for row in rxs for r in row], dma=True)

        def store_y(ti, xt, rxs):
            t0, T = TILES[ti]
            stage, rstage = C.xstage, C.rxstage
            blocks = []
            for si, (o, n, samp) in enumerate(subs_of(t0)):
                if samp:
                    blocks.append((o, 64, O["y_s"], 0, si))
                else:
                    for b in range(n // 128):
                        blocks.append((o + b * 128, 128, O["y_p"], (t0 + o + b * 128) * D, si))
            for (o, n, dst, doff, si) in blocks:
                for half in range(2):
                    for q in range(2):
                        bank = q + 2 * half
                        for j in range(4):
                            c = half * 8 + q * 4 + j
                            P.add("pe", lambda e, n=n, o=o, c=c, j=j, bank=bank: e.transpose(
                                C.ps[bank].v(j * 128, (1, 128), np_=n), xt.v(c * TS + o, (1, n)),
                                C.ident.v(0, (1, 128))), reads=[rxs[c][si], C.rconst], writes=[C.rps[bank]])
                        P.add("act", lambda e, half=half, q=q, n=n, bank=bank: e.activation(
                            out=stage[half].v(q * 512, (1, 512), np_=n), in_=C.ps[bank].v(0, (1, 512), np_=n),
                            func=AF.Copy), reads=[C.rps[bank]], writes=[rstage[half]])
                    P.add("sp", lambda e, half=half, n=n, dst=dst, doff=doff: e.dma_start(
                        out=dv(dst, doff + half * 1024, (D, n), (1, 1024)), in_=stage[half].v(0, (1, 1024), np_=n)),
                        reads=[rstage[half]], dma=True)

        if "noffn2" not in dbg:
            phase_ffn(C, "ffn2", 2, load_x1, store_y)
        P.emit()
    return nc


def load_featmajor_small(C, dst, src_ap, nrows, rdst, tmpname):
    P = C.P
    tmp = C.sb(tmpname, 128, F32)
    done = 0
    blk = 0
    rt = Res()
    while done < nrows:
        n = min(128, nrows - done)
        P.add("sp", lambda e, done=done, n=n: e.dma_start(out=tmp.v(0, (1, 128), np_=n),
                                                          in_=dv(src_ap, done * 128, (128, n), (1, 128))),
              writes=[rt], dma=True)
        bank = 7
        P.add("pe", lambda e, n=n: e.transpose(C.ps[bank].v(0, (1, n)), tmp.v(0, (1, 128), np_=n),
                                               C.ident.v(0, (1, n), np_=n)),
              reads=[rt, C.rconst], writes=[C.rps[bank]])
        P.add("dve", lambda e, done=done, n=n: e.tensor_copy(dst.v(done, (1, n)), C.ps[bank].v(0, (1, n))),
              reads=[C.rps[bank]], writes=[rdst, rt])
        done += n
        blk += 1


def phase_mods(C):
    P, I, nc = C.P, C.I, C.nc
    with contextlib.ExitStack() as es:
        def sb(name, width, dt):
            return SB(es.enter_context(nc.sbuf_tensor("s_" + name, [128, padw(width, dt)], dt)), padw(width, dt), dt)
        sbo = C.sb
        C.sb = sb
        for k in ("ffn1_g", "mix_g", "ffn2_g"):
            load_featmajor_small(C, C.gn[k], I[k], NKC, C.rgn, "tmp_" + k)
        bada = sb("bada", 144, F32)
        rb = Res("bada")
        load_featmajor_small(C, bada, I["b_ada"], 144, rb, "tmp_bada")
        c_tm = sb("c_tm", D, F32)
        rc = Res()
        P.add("sp", lambda e: e.dma_start(out=c_tm.v(0, (1, D), np_=17), in_=I["c17"]), writes=[rc], dma=True)
        cT = sb("cT", NKC * 17, BF16)
        rcT = Res()
        for kc in range(NKC):
            P.add("pe", lambda e, kc=kc: e.transpose(C.ps[6].v(kc * 17, (1, 17)),
                                                     c_tm.v(kc * 128, (1, 128), np_=17),
                                                     C.ident.v(0, (1, 17), np_=17)),
                  reads=[rc, C.rconst], writes=[C.rps[6]])
        P.add("dve", lambda e: e.tensor_copy(cT.v(0, (1, NKC * 17)), C.ps[6].v(0, (1, NKC * 17))),
              reads=[C.rps[6]], writes=[rcT])
        wb = [sb("wada%d" % i, NKC * 512, BF16) for i in range(2)]
        rwb = [Res(), Res()]
        for blk in range(36):
            s = blk % 2
            P.add("pool", lambda e, blk=blk, s=s: e.dma_start(
                out=wb[s].v(0, (512, NKC), (1, 512)),
                in_=dv(I["w_ada"], blk * 512, (9 * D, 128), (128 * 9 * D, NKC), (1, 512))),
                writes=[rwb[s]], dma=True)
            bank = 4 + (blk % 2)
            for cc in range(4):
                j = blk * 4 + cc
                for kc in range(NKC):
                    P.add("pe", lambda e, s=s, cc=cc, kc=kc, bank=bank: e.matmul(
                        C.ps[bank].v(cc * 17, (1, 17)),
                        wb[s].v(kc * 512 + cc * 128, (1, 128)),
                        cT.v(kc * 17, (1, 17)), start=(kc == 0), stop=(kc == NKC - 1)),
                        reads=[rwb[s], rcT], writes=[C.rps[bank]])
            for cc in range(4):
                j = blk * 4 + cc
                P.add("act", lambda e, j=j, cc=cc, bank=bank: e.activation(
                    out=C.mods.v(j * 17, (1, 17)), in_=C.ps[bank].v(cc * 17, (1, 17)),
                    func=AF.Identity, bias=bada.v(j, (1, 1)), scale=1.0),
                    reads=[C.rps[bank], rb], writes=[C.rmods])
        P.barrier()
        C.sb = sbo


def load_x_tokenmajor(C, ti, xt, rxs):
    P, I = C.P, C.I
    t0, T = TILES[ti]
    stage, rstage = C.xstage, C.rxstage
    blocks = []
    for si, (o, n, samp) in enumerate(subs_of(t0)):
        if samp:
            blocks.append((o, 64, I["xs"], 0, si))
        else:
            for b in range(n // 128):
                blocks.append((o + b * 128, 128, I["xp"], (t0 + o + b * 128) * D, si))
    for bi, (o, n, src, soff, si) in enumerate(blocks):
        for half in range(2):
            s = (bi * 2 + half) % 2
            P.add("sp", lambda e, s=s, n=n, src=src, soff=soff, half=half: e.dma_start(
                out=stage[s].v(0, (1, 1024), np_=n),
                in_=dv(src, soff + half * 1024, (D, n), (1, 1024))), writes=[rstage[s]], dma=True)
            for q in range(2):
                bank = q + 2 * half
                for j in range(4):
                    P.add("pe", lambda e, s=s, n=n, q=q, j=j, bank=bank: e.transpose(
                        C.ps[bank].v(j * 128, (1, n)),
                        stage[s].v((q * 4 + j) * 128, (1, 128), np_=n),
                        C.ident.v(0, (1, n), np_=n)),
                        reads=[rstage[s], C.rconst], writes=[C.rps[bank]])
                c0 = half * 8 + q * 4
                eng = "act" if q == 0 else "dve"
                if eng == "act":
                    fn = lambda e, c0=c0, o=o, n=n, bank=bank: e.activation(
                        out=xt.v(c0 * TS + o, (TS, 4), (1, n)), in_=C.ps[bank].v(0, (128, 4), (1, n)),
                        func=AF.Copy)
                else:
                    fn = lambda e, c0=c0, o=o, n=n, bank=bank: e.tensor_copy(
                        xt.v(c0 * TS + o, (TS, 4), (1, n)), C.ps[bank].v(0, (128, 4), (1, n)))
                P.add(eng, fn, reads=[C.rps[bank]], writes=[rxs[c][si] for c in range(c0, c0 + 4)])


def phase_ffn(C, nm, sl, load_x, store_x):
    P, I, nc = C.P, C.I, C.nc
    wg, wu, wd = I[nm + "_wg"], I[nm + "_wu"], I[nm + "_wd"]
    gn = C.gn[nm + "_g"]
    shb, scb, gtb = (3 * sl) * 16, (3 * sl + 1) * 16, (3 * sl + 2) * 16
    TM = 1088
    with contextlib.ExitStack() as es:
        def sb(name, width, dt):
            return SB(es.enter_context(nc.sbuf_tensor(nm + "_" + name, [128, padw(width, dt)], dt)), padw(width, dt), dt)
        xt = sb("xt", NKC * TM, F32)
        ht = sb("ht", NKC * TM, BF16)
        wgu = [[sb("wg%d" % i, NKC * 256, BF16), sb("wu%d" % i, NKC * 256, BF16)] for i in range(2)]
        wdb = [sb("wd%d" % i, 2 * D, BF16) for i in range(2)]
        actg = [sb("actg%d" % i, 2 * TM, BF16) for i in range(2)]
        sg = [sb("sg%d" % i, 512, BF16) for i in range(2)]
        rstd = sb("rstd", TM, F32)
        xsq = [sb("xsq%d" % i, 512, BF16) for i in range(2)]
        tmp = [sb("tmp%d" % i, 512, F32) for i in range(2)]
        A17 = sb("A17", NKC * 17, F32)
        G17 = sb("G17", NKC * 17, F32)
        E1 = sb("E1", NKC * 64, F32)
        E2 = sb("E2", NKC * 64, F32)
        T1 = sb("T1", NKC * 64, F32)
        C.xstage = [sb("xstage%d" % i, 1024, F32) for i in range(2)]
        C.rxstage = [Res(), Res()]
        rwgu = [Res(), Res()]
        rwd = [Res(), Res()]
        rsg = [Res(), Res()]
        rxsq = [Res(), Res()]
        rtmp = [Res(), Res()]
        rA, rG, rE1, rE2, rT1, rrstd = Res(), Res(), Res(), Res(), Res(), Res()
        P.add("dve", lambda e: e.tensor_scalar(A17.v(0, (1, NKC * 17)), C.mods.v(scb * 17, (1, NKC * 17)),
                                               1.0, None, op0=ALU.add), reads=[C.rmods], writes=[rA])
        P.add("dve", lambda e: e.tensor_tensor(out=A17.v(0, (17, NKC), (1, 17)), in0=A17.v(0, (17, NKC), (1, 17)),
                                               in1=gn.v(0, (1, NKC), (0, 17)), op=ALU.mult),
              reads=[rA, C.rgn], writes=[rA])
        P.add("dve", lambda e: e.tensor_scalar(G17.v(0, (1, NKC * 17)), C.mods.v(gtb * 17, (1, NKC * 17)),
                                               0.5, None, op0=ALU.mult), reads=[C.rmods], writes=[rG])
        P.add("dve", lambda e: e.tensor_copy(E1.v(0, (64, NKC), (4, NSB), (1, 4)),
                                             A17.v(1, (17, NKC), (1, NSB), (0, 4))), reads=[rA], writes=[rE1])
        P.add("dve", lambda e: e.tensor_copy(E2.v(0, (64, NKC), (4, NSB), (1, 4)),
                                             C.mods.v(shb * 17 + 1, (17, NKC), (1, NSB), (0, 4))),
              reads=[C.rmods], writes=[rE2])
        fgc = 0
        dbank = 0
        if nm == "ffn2":
            Gs2 = sb("Gs2", NKC * 64, F32)
            rGs2 = Res()
            P.add("dve", lambda e: e.tensor_copy(Gs2.v(0, (64, NKC), (4, NSB), (1, 4)),
                                                 C.mods.v(80 * 17 + 1, (17, NKC), (1, NSB), (0, 4))),
                  reads=[C.rmods], writes=[rGs2])
        rxs = [[Res() for _ in range(3)] for _ in range(NKC)]
        rh = [Res() for _ in range(3)]
        ract_all = [[Res() for _ in range(3)] for _ in range(2)]
        for ti, (t0, T) in enumerate(TILES):
            subs = subs_of(t0)
            if ti > 0:
                P.barrier()
            load_x(ti, xt, rxs)
            if nm == "ffn2":
                for si, (o, n, samp) in enumerate(subs):
                    P.add("sp", lambda e, o=o, n=n, t0=t0: e.dma_start(
                        out=ht.v(o, (TS, NKC), (1, n)),
                        in_=dv(C.S["MOAO"], t0 + o, (NKC * NT, 128), (NT, NKC), (1, n))), writes=[rh[si]], dma=True)
                for dp in range(8):
                    ws = fgc % 2
                    fgc += 1
                    P.add("pool", lambda e, ws=ws, dp=dp: e.dma_start(
                        out=wgu[ws][0].v(0, (256, NKC), (1, 256)),
                        in_=dv(I["w_out"], dp * 256, (D, 128), (128 * D, NKC), (1, 256))), writes=[rwgu[ws]], dma=True)
                    for j in range(2):
                        dc = dp * 2 + j
                        for si, (o, n, samp) in enumerate(subs):
                            db = 4 + dbank % 3
                            dbank += 1
                            for kc in range(NKC):
                                P.add("pe", lambda e, ws=ws, j=j, kc=kc, o=o, n=n, db=db: e.matmul(
                                    C.ps[db].v(0, (1, n)), wgu[ws][0].v(kc * 256 + j * 128, (1, 128)),
                                    ht.v(kc * TS + o, (1, n)), start=(kc == 0), stop=(kc == NKC - 1)),
                                    reads=[rwgu[ws], rh[si]], writes=[C.rps[db]])
                            if not samp:
                                P.add("dve", lambda e, dc=dc, o=o, n=n, db=db: e.scalar_tensor_tensor(
                                    out=xt.v(dc * TS + o, (1, n)), in0=C.ps[db].v(0, (1, n)),
                                    scalar=C.mods.v((80 + dc) * 17, (1, 1)), in1=xt.v(dc * TS + o, (1, n)),
                                    op0=ALU.mult, op1=ALU.add),
                                    reads=[C.rps[db], C.rmods, rxs[dc][si]], writes=[rxs[dc][si]])
                            else:
                                P.add("dve", lambda e, dc=dc, db=db: e.tensor_tensor(
                                    out=T1.v(dc * 64, (1, 64)), in0=C.ps[db].v(0, (1, 64)),
                                    in1=Gs2.v(dc * 64, (1, 64)), op=ALU.mult),
                                    reads=[C.rps[db], rGs2], writes=[rT1])
                                P.add("dve", lambda e, dc=dc, o=o: e.tensor_tensor(
                                    out=xt.v(dc * TS + o, (1, 64)), in0=xt.v(dc * TS + o, (1, 64)),
                                    in1=T1.v(dc * 64, (1, 64)), op=ALU.add),
                                    reads=[rT1, rxs[dc][si]], writes=[rxs[dc][si]])
            for si, (o, n, samp) in enumerate(subs):
                for c in range(NKC):
                    s = c % 2
                    P.add("act", lambda e, s=s, c=c, o=o, n=n: e.activation(
                        out=xsq[s].v(0, (1, n)), in_=xt.v(c * TS + o, (1, n)), func=AF.Square),
                        reads=[rxs[c][si]], writes=[rxsq[s]])
                    P.add("pe", lambda e, s=s, c=c, n=n: e.matmul(
                        C.ps[7].v(0, (1, n)), C.ones_b.v(0, (1, 128)), xsq[s].v(0, (1, n)),
                        start=(c == 0), stop=(c == NKC - 1)), reads=[rxsq[s], C.rconst], writes=[C.rps[7]])
                P.add("act", lambda e, o=o, n=n: e.activation(
                    out=rstd.v(o, (1, n)), in_=C.ps[7].v(0, (1, n)), func=AF.Sqrt, bias=C.epsb.v(0, (1, 1)),
                    scale=1.0 / D), reads=[C.rps[7], C.rconst], writes=[rrstd])
                P.add("dve", lambda e, o=o, n=n: e.reciprocal(rstd.v(o, (1, n)), rstd.v(o, (1, n))),
                      reads=[rrstd], writes=[rrstd])
                if not samp:
                    for c in range(NKC):
                        s = c % 2
                        P.add("dve", lambda e, s=s, c=c, o=o, n=n: e.scalar_tensor_tensor(
                            out=tmp[s].v(0, (1, n)), in0=xt.v(c * TS + o, (1, n)), scalar=A17.v(c * 17, (1, 1)),
                            in1=rstd.v(o, (1, n)), op0=ALU.mult, op1=ALU.mult),
                            reads=[rxs[c][si], rA, rrstd], writes=[rtmp[s]])
                        P.add("act", lambda e, s=s, c=c, o=o, n=n: e.activation(
                            out=ht.v(c * TS + o, (1, n)), in_=tmp[s].v(0, (1, n)), func=AF.Identity,
                            bias=C.mods.v((shb + c) * 17, (1, 1)), scale=1.0),
                            reads=[rtmp[s], C.rmods], writes=[rh[si]])
                else:
                    P.add("dve", lambda e, o=o: e.tensor_tensor(
                        out=T1.v(0, (64, NKC), (1, 64)), in0=xt.v(o, (TS, NKC), (1, 64)),
                        in1=rstd.v(o, (0, NKC), (1, 64)), op=ALU.mult),
                        reads=[rxs[c][si] for c in range(NKC)] + [rrstd], writes=[rT1])
                    P.add("dve", lambda e: e.tensor_tensor(
                        out=T1.v(0, (1, NKC * 64)), in0=T1.v(0, (1, NKC * 64)), in1=E1.v(0, (1, NKC * 64)),
                        op=ALU.mult), reads=[rT1, rE1], writes=[rT1])
                    P.add("dve", lambda e, o=o: e.tensor_tensor(
                        out=ht.v(o, (TS, NKC), (1, 64)), in0=T1.v(0, (64, NKC), (1, 64)),
                        in1=E2.v(0, (64, NKC), (1, 64)), op=ALU.add), reads=[rT1, rE2], writes=[rh[si]])
                    P.add("dve", lambda e: e.tensor_copy(E1.v(0, (64, NKC), (4, NSB), (1, 4)),
                                                         G17.v(1, (17, NKC), (1, NSB), (0, 4))),
                          reads=[rG, rT1], writes=[rE1])
            for fg in range(NFG):
                ws = fgc % 2
                fgc += 1
                P.add("pool", lambda e, ws=ws, fg=fg: e.dma_start(
                    out=wgu[ws][0].v(0, (256, NKC), (1, 256)),
                    in_=dv(wg, fg * 256, (DFF, 128), (128 * DFF, NKC), (1, 256))), writes=[rwgu[ws]], dma=True)
                P.add("pool", lambda e, ws=ws, fg=fg: e.dma_start(
                    out=wgu[ws][1].v(0, (256, NKC), (1, 256)),
                    in_=dv(wu, fg * 256, (DFF, 128), (128 * DFF, NKC), (1, 256))), writes=[rwgu[ws]], dma=True)
                P.add("pool", lambda e, ws=ws, fg=fg: e.dma_start(
                    out=wdb[ws].v(0, (D, 2), (1, D)),
                    in_=dv(wd, fg * 256 * D, (D, 128), (128 * D, 2), (1, D))), writes=[rwd[ws]], dma=True)
                ract = ract_all[ws]
                for fc in range(2):
                    for si, (o, n, samp) in enumerate(subs):
                        gb = (fc * len(subs) + si) % 2
                        ub = 2 + gb
                        for kc in range(NKC):
                            P.add("pe", lambda e, ws=ws, fc=fc, kc=kc, o=o, n=n, gb=gb: e.matmul(
                                C.ps[gb].v(0, (1, n)), wgu[ws][0].v(kc * 256 + fc * 128, (1, 128)),
                                ht.v(kc * TS + o, (1, n)), start=(kc == 0), stop=(kc == NKC - 1)),
                                reads=[rwgu[ws], rh[si]], writes=[C.rps[gb]])
                        for kc in range(NKC):
                            P.add("pe", lambda e, ws=ws, fc=fc, kc=kc, o=o, n=n, ub=ub: e.matmul(
                                C.ps[ub].v(0, (1, n)), wgu[ws][1].v(kc * 256 + fc * 128, (1, 128)),
                                ht.v(kc * TS + o, (1, n)), start=(kc == 0), stop=(kc == NKC - 1)),
                                reads=[rwgu[ws], rh[si]], writes=[C.rps[ub]])
                        P.add("act", lambda e, gb=gb, n=n: e.activation(
                            out=sg[gb].v(0, (1, n)), in_=C.ps[gb].v(0, (1, n)), func=AF.Silu),
                            reads=[C.rps[gb]], writes=[rsg[gb]])
                        P.add("dve", lambda e, ws=ws, fc=fc, o=o, n=n, gb=gb, ub=ub: e.tensor_tensor(
                            out=actg[ws].v(fc * TS + o, (1, n)), in0=sg[gb].v(0, (1, n)),
                            in1=C.ps[ub].v(0, (1, n)), op=ALU.mult),
                            reads=[rsg[gb], C.rps[ub]], writes=[ract[si]])
                for si, (o, n, samp) in enumerate(subs):
                    for dc in range(NKC):
                        db = 4 + dbank % 3
                        dbank += 1
                        for fc in range(2):
                            P.add("pe", lambda e, ws=ws, fc=fc, dc=dc, o=o, n=n, db=db: e.matmul(
                                C.ps[db].v(0, (1, n)), wdb[ws].v(fc * D + dc * 128, (1, 128)),
                                actg[ws].v(fc * TS + o, (1, n)), start=(fc == 0), stop=(fc == 1)),
                                reads=[rwd[ws], ract[si]], writes=[C.rps[db]])
                        if not samp:
                            P.add("dve", lambda e, dc=dc, o=o, n=n, db=db: e.scalar_tensor_tensor(
                                out=xt.v(dc * TS + o, (1, n)), in0=C.ps[db].v(0, (1, n)),
                                scalar=G17.v(dc * 17, (1, 1)), in1=xt.v(dc * TS + o, (1, n)),
                                op0=ALU.mult, op1=ALU.add),
                                reads=[C.rps[db], rG, rxs[dc][si]], writes=[rxs[dc][si]])
                        else:
                            P.add("dve", lambda e, dc=dc, db=db: e.tensor_tensor(
                                out=T1.v(dc * 64, (1, 64)), in0=C.ps[db].v(0, (1, 64)),
                                in1=E1.v(dc * 64, (1, 64)), op=ALU.mult),
                                reads=[C.rps[db], rE1], writes=[rT1])
                            P.add("dve", lambda e, dc=dc, o=o: e.tensor_tensor(
                                out=xt.v(dc * TS + o, (1, 64)), in0=xt.v(dc * TS + o, (1, 64)),
                                in1=T1.v(dc * 64, (1, 64)), op=ALU.add),
                                reads=[rT1, rxs[dc][si]], writes=[rxs[dc][si]])
            store_x(ti, xt, rxs)
        P.barrier()


def prep_shared(inp):
    sh = {}
    sh["ident"] = np.eye(128, dtype=np.float32)
    for nm in ("ffn1", "ffn2"):
        sh[nm + "_norm_g"] = np.ascontiguousarray(inp[nm + "_norm_g"][0].reshape(NKC, 128))
        sh[nm + "_w_gate"] = inp[nm + "_w_gate"][0]
        sh[nm + "_w_up"] = inp[nm + "_w_up"][0]
        sh[nm + "_w_down"] = inp[nm + "_w_down"][0]
    sh["mix_norm_g"] = np.ascontiguousarray(inp["mix_norm_g"][0].reshape(NKC, 128))
    sh["w_ada"] = inp["w_ada"][0]
    sh["b_ada"] = np.ascontiguousarray(inp["b_ada"][0].reshape(144, 128))
    sh["w_in"] = inp["w_in"][0]
    sh["w_out"] = inp["w_out"][0]
    sh["gate_b"] = np.ascontiguousarray(inp["mlstm_gate_b"][0].reshape(8, 1))
    sh["qng"] = np.ascontiguousarray(inp["q_norm_g"][0].reshape(128, 1))
    sh["kng"] = np.ascontiguousarray(inp["k_norm_g"][0].reshape(128, 1))
    sh["conv_w"] = np.ascontiguousarray(inp["mlstm_conv_w"][0].reshape(4, NKC, 128).reshape(64, 128))
    sh["out_g"] = np.ascontiguousarray(inp["mlstm_out_g"][0].reshape(8, 128))
    sh["tri"] = np.triu(np.ones((64, 64), np.float32))
    sel = np.zeros((8, 8, 128), np.float32)
    for r_ in range(8):
        sel[r_, r_, :] = 1.0
    sh["sel8"] = sel.reshape(8, 1024)
    mk = np.ones((1, 512), np.float32)
    mk[0, ::64] = 0.0
    sh["mk64"] = mk
    first = (np.arange(64) % 4 == 0)
    mks = np.zeros((3, 64), np.float32)
    mks[0] = np.where(first, 0.0, 1.0)
    mks[1] = np.where(first, 0.0, -1e30)
    mks[2] = np.where(first, -1e30, 0.0)
    sh["mks"] = mks.reshape(1, 192)
    sh["bdp"], sh["bsp"], sh["bsn"] = host_bias_tables(np.asarray(inp["t5_bias"]))
    sh["pidx"] = np.arange(128, dtype=np.float32).reshape(128, 1)
    sh["ck"] = inp["cache_k"][0].reshape(-1, 256)
    sh["cv"] = inp["cache_v"][0].reshape(-1, 256)
    sh["cidx"] = inp["cache_idx_k"][0].reshape(-1, 64)
    sh["negtri"] = np.where(np.arange(128)[None, :] > np.arange(128)[:, None], np.float32(NEGB), np.float32(0.0)).astype(np.float32)
    return sh


def prep_core(inp, core, sh):
    m = dict(sh)
    m["xp"] = inp["x_prompt"][core]
    m["xs"] = np.ascontiguousarray(inp["x_sample"][NSB * core:NSB * (core + 1)].reshape(NS_TOK, D))
    sl = slice(NSB * core, NSB * (core + 1))
    m["ptab"] = np.ascontiguousarray(np.asarray(inp["page_table"][sl]).astype(np.int32).reshape(1, 256))
    m["st_C"] = np.ascontiguousarray(inp["state_C"][0, sl])
    m["st_n"] = np.ascontiguousarray(inp["state_n"][0, sl])
    m["st_m"] = np.ascontiguousarray(inp["state_m"][0, sl])
    m["st_conv"] = np.ascontiguousarray(inp["state_conv"][0, sl].reshape(NSB * 3, D))
    m["c17"] = np.ascontiguousarray(np.concatenate(
        [inp["c_prompt"][core:core + 1], inp["c_sample"][NSB * core:NSB * (core + 1)]], axis=0))
    return m


class NormBufs:
    pass


def norm_setup(C, sb, gn, sl, with_gate, gate_scale):
    P = C.P
    N = NormBufs()
    N.shb, N.scb, N.gtb = (3 * sl) * 16, (3 * sl + 1) * 16, (3 * sl + 2) * 16
    N.A17 = sb("A17", NKC * 17, F32)
    N.G17 = sb("G17", NKC * 17, F32)
    N.E1 = sb("E1", NKC * 64, F32)
    N.E2 = sb("E2", NKC * 64, F32)
    N.T1 = sb("T1", NKC * 64, F32)
    N.rstd = sb("rstd", TS, F32)
    N.xsq = [sb("xsq%d" % i, 512, BF16) for i in range(2)]
    N.tmp = [sb("tmp%d" % i, 512, F32) for i in range(2)]
    N.rxsq = [Res(), Res()]
    N.rtmp = [Res(), Res()]
    N.rA, N.rG, N.rE1, N.rE2, N.rT1, N.rrstd = Res(), Res(), Res(), Res(), Res(), Res()
    A17, G17, E1, E2 = N.A17, N.G17, N.E1, N.E2
    scb, gtb, shb = N.scb, N.gtb, N.shb
    P.add("dve", lambda e: e.tensor_scalar(A17.v(0, (1, NKC * 17)), C.mods.v(scb * 17, (1, NKC * 17)),
                                           1.0, None, op0=ALU.add), reads=[C.rmods], writes=[N.rA])
    P.add("dve", lambda e: e.tensor_tensor(out=A17.v(0, (17, NKC), (1, 17)), in0=A17.v(0, (17, NKC), (1, 17)),
                                           in1=gn.v(0, (1, NKC), (0, 17)), op=ALU.mult),
          reads=[N.rA, C.rgn], writes=[N.rA])
    P.add("dve", lambda e: e.tensor_scalar(G17.v(0, (1, NKC * 17)), C.mods.v(gtb * 17, (1, NKC * 17)),
                                           gate_scale, None, op0=ALU.mult), reads=[C.rmods], writes=[N.rG])
    P.add("dve", lambda e: e.tensor_copy(E1.v(0, (64, NKC), (4, NSB), (1, 4)),
                                         A17.v(1, (17, NKC), (1, NSB), (0, 4))), reads=[N.rA], writes=[N.rE1])
    P.add("dve", lambda e: e.tensor_copy(E2.v(0, (64, NKC), (4, NSB), (1, 4)),
                                         C.mods.v(shb * 17 + 1, (17, NKC), (1, NSB), (0, 4))),
          reads=[C.rmods], writes=[N.rE2])
    return N


def norm_sub(C, N, xt, xoff, xstride, rx_list, ht, hoff, rh, o, n, samp):
    P = C.P
    rstd, xsq, tmp, A17, E1, E2, T1 = N.rstd, N.xsq, N.tmp, N.A17, N.E1, N.E2, N.T1
    for c in range(NKC):
        s = c % 2
        P.add("act", lambda e, s=s, c=c: e.activation(
            out=xsq[s].v(0, (1, n)), in_=xt.v(c * xstride + xoff, (1, n)), func=AF.Square),
            reads=[rx_list[c]], writes=[N.rxsq[s]])
        P.add("pe", lambda e, s=s, c=c: e.matmul(
            C.ps[7].v(0, (1, n)), C.ones_b.v(0, (1, 128)), xsq[s].v(0, (1, n)),
            start=(c == 0), stop=(c == NKC - 1)), reads=[N.rxsq[s], C.rconst], writes=[C.rps[7]])
    P.add("act", lambda e: e.activation(
        out=rstd.v(o, (1, n)), in_=C.ps[7].v(0, (1, n)), func=AF.Sqrt, bias=C.epsb.v(0, (1, 1)),
        scale=1.0 / D), reads=[C.rps[7], C.rconst], writes=[N.rrstd])
    P.add("dve", lambda e: e.reciprocal(rstd.v(o, (1, n)), rstd.v(o, (1, n))),
          reads=[N.rrstd], writes=[N.rrstd])
    if not samp:
        for c in range(NKC):
            s = c % 2
            P.add("dve", lambda e, s=s, c=c: e.scalar_tensor_tensor(
                out=tmp[s].v(0, (1, n)), in0=xt.v(c * xstride + xoff, (1, n)), scalar=A17.v(c * 17, (1, 1)),
                in1=rstd.v(o, (1, n)), op0=ALU.mult, op1=ALU.mult),
                reads=[rx_list[c], N.rA, N.rrstd], writes=[N.rtmp[s]])
            P.add("act", lambda e, s=s, c=c: e.activation(
                out=ht.v(c * TS + hoff, (1, n)), in_=tmp[s].v(0, (1, n)), func=AF.Identity,
                bias=C.mods.v((N.shb + c) * 17, (1, 1)), scale=1.0),
                reads=[N.rtmp[s], C.rmods], writes=[rh])
    else:
        P.add("dve", lambda e: e.tensor_tensor(
            out=T1.v(0, (64, NKC), (1, 64)), in0=xt.v(xoff, (xstride, NKC), (1, 64)),
            in1=rstd.v(o, (0, NKC), (1, 64)), op=ALU.mult),
            reads=list(rx_list) + [N.rrstd], writes=[N.rT1])
        P.add("dve", lambda e: e.tensor_tensor(
            out=T1.v(0, (1, NKC * 64)), in0=T1.v(0, (1, NKC * 64)), in1=E1.v(0, (1, NKC * 64)),
            op=ALU.mult), reads=[N.rT1, N.rE1], writes=[N.rT1])
        P.add("dve", lambda e: e.tensor_tensor(
            out=ht.v(hoff, (TS, NKC), (1, 64)), in0=T1.v(0, (64, NKC), (1, 64)),
            in1=E2.v(0, (64, NKC), (1, 64)), op=ALU.add), reads=[N.rT1, N.rE2], writes=[rh])


WI_SCALE = (8 ** -0.5) * (64 ** -0.5)
TMB = 2
FM_GROUPS = [("qk", 0, 4, 0), ("qk", 512, 4, 4), ("qk", 1024, 4, 8), ("qk", 1536, 4, 12),
             ("om", 3072, 4, 0), ("om", 3584, 4, 4),
             ("qa", 4104, 4, 0), ("qa", 4616, 4, 4),
             ("ka", 5128, 2, 0), ("qi", 5640, 4, 0), ("vm", 2048, 4, 0), ("vm", 2560, 4, 1)]


def phase_proj(C):
    P, I, S, O, nc = C.P, C.I, C.S, C.O, C.nc
    w_in = I["w_in"]
    with contextlib.ExitStack() as es:
        def sb(name, width, dt):
            return SB(es.enter_context(nc.sbuf_tensor("pj_" + name, [128, padw(width, dt)], dt)), padw(width, dt), dt)
        N = norm_setup(C, sb, C.gn["mix_g"], 1, False, 1.0)
        xsb = sb("xsb", NKC * 512, F32)
        rxsb = [Res() for _ in range(NKC)]
        ht = sb("ht", NKC * TS, BF16)
        rh = [Res() for _ in range(3)]
        wbuf = [sb("wb%d" % i, NKC * 512, BF16) for i in range(2)]
        rwb = [Res(), Res()]
        wva = sb("wva", NKC * 256, BF16)
        wmisc = sb("wmisc", NKC * 80, BF16)
        rwtm = Res()
        stg32 = [sb("stg32_%d" % i, 512, F32) for i in range(3)]
        rstg32 = [Res() for _ in range(3)]
        stgb = [sb("stgb%d" % i, 512, BF16) for i in range(3)]
        rstgb = [Res() for _ in range(3)]
        vstg = [sb("vstg%d" % i, 2 * 257, BF16) for i in range(2)]
        rvstg = [Res(), Res()]
        tstg = [sb("tstg%d" % i, 320, F32) for i in range(2)]
        rtstg = [Res(), Res()]
        tstgb = [sb("tstgb%d" % i, 256, BF16) for i in range(2)]
        rtstgb = [Res(), Res()]
        tstw = [sb("tstw%d" % i, 8, F32) for i in range(2)]
        rtstw = [Res(), Res()]
        kn32 = sb("kn32", 2 * 512, F32)
        rkn = Res()
        kout = [sb("kout%d" % i, 256, F32) for i in range(2)]
        rkout = [Res(), Res()]
        hsq = [sb("hsq%d" % i, 512, BF16) for i in range(2)]
        rhsq = [Res(), Res()]
        hr = [sb("hr%d" % i, 512, F32) for i in range(2)]
        rhr = [Res(), Res()]
        cstg = sb("cstg", 512, F32)
        rcstg = Res()
        gb_ = sb("gateb", 1, F32)
        qng = sb("qng", 1, F32)
        kng = sb("kng", 1, F32)
        eps128 = C.epsb
        rsm = Res()
        P.add("sp", lambda e: e.dma_start(out=gb_.v(0, (1, 1), np_=8), in_=I["gate_b"]), writes=[rsm], dma=True)
        P.add("sp", lambda e: e.dma_start(out=qng.v(0, (1, 1)), in_=I["qng"]), writes=[rsm], dma=True)
        P.add("sp", lambda e: e.dma_start(out=kng.v(0, (1, 1)), in_=I["kng"]), writes=[rsm], dma=True)
        for k in range(2):
            P.add("dve", lambda e, k=k: e.memset(vstg[k].v(0, (1, 2 * 257)), 1.0), writes=[rvstg[k]])
        P.add("pool", lambda e: e.dma_start(out=wva.v(0, (256, NKC), (1, 256)),
                                            in_=dv(w_in, 5384, (PW, 128), (128 * PW, NKC), (1, 256))),
              writes=[rwtm], dma=True)
        wm32 = sb("wm32", NKC * 80, F32)
        rwm32 = Res()
        P.add("sp", lambda e: e.dma_start(out=wm32.v(0, (80, NKC), (1, 72)),
                                          in_=dv(w_in, 6152, (PW, 128), (128 * PW, NKC), (1, 72))),
              writes=[rwm32], dma=True)
        P.add("sp", lambda e: e.dma_start(out=wm32.v(72, (80, NKC), (1, 8)),
                                          in_=dv(w_in, 4096, (PW, 128), (128 * PW, NKC), (1, 8))),
              writes=[rwm32], dma=True)
        P.add("dve", lambda e: e.tensor_copy(wmisc.v(0, (1, NKC * 80)), wm32.v(0, (1, NKC * 80))),
              reads=[rwm32], writes=[rwtm])
        cnt = {"s32": 0, "sb": 0, "fm": 0, "ev": 0, "hs": 0, "ko": 0, "w": 0, "tb": 0}

        def evac(out_ap, in_ap, reads, writes):
            cnt["ev"] += 1
            if cnt["ev"] % 2 == 0:
                P.add("act", lambda e: e.activation(out=out_ap, in_=in_ap, func=AF.Copy), reads=reads, writes=writes)
            else:
                P.add("dve", lambda e: e.tensor_copy(out_ap, in_ap), reads=reads, writes=writes)

        for ti, (t0, T) in enumerate(TILES):
            subs = subs_of(t0)
            if ti > 0:
                P.barrier()
            for si, (o, n, samp) in enumerate(subs):
                P.add("sp", lambda e, o=o, n=n: e.dma_start(
                    out=xsb.v(0, (512, NKC), (1, n)),
                    in_=dv(S["X1"], t0 + o, (NKC * NT, 128), (NT, NKC), (1, n))),
                    reads=[C.rX1[ti]], writes=rxsb, dma=True)
                norm_sub(C, N, xsb, 0, 512, rxsb, ht, o, rh[si], o, n, samp)
            if ti == 0:
                C.dump("pj_ht0", ht, rh)
                C.dump("pj_xsb0", xsb, rxsb)
                C.dump("pj_rstd0", N.rstd, [N.rrstd])
                C.dump("pj_A17", N.A17, [N.rA])
            blocks = []
            for si, (o, n, samp) in enumerate(subs):
                if samp:
                    blocks.append((o, 64, si, True, 0))
                else:
                    for b in range(n // 128):
                        blocks.append((o + b * 128, 128, si, False, t0 + o + b * 128))
            for gi, (kind, col0, nch, cbase) in enumerate(FM_GROUPS):
                if ("skip_" + kind) in C.dbg:
                    continue
                ws = cnt["w"] % 2
                cnt["w"] += 1
                ncol = nch * 128
                P.add("pool", lambda e, ws=ws, col0=col0, ncol=ncol: e.dma_start(
                    out=wbuf[ws].v(0, (512, NKC), (1, ncol)),
                    in_=dv(w_in, col0, (PW, 128), (128 * PW, NKC), (1, ncol))), writes=[rwb[ws]], dma=True)
                if kind == "vm":
                    half = cbase
                    for (bo, nb, si, samp, trow) in blocks:
                        tb = cnt["tb"] % 2
                        cnt["tb"] += 1
                        grow = (NP_TOK if samp else trow)
                        pb = 2 + tb
                        for kc in range(NKC):
                            P.add("pe", lambda e, ws=ws, kc=kc, bo=bo, nb=nb, pb=pb: e.matmul(
                                C.ps[pb].v(0, (1, 512), np_=nb), ht.v(kc * TS + bo, (1, nb)),
                                wbuf[ws].v(kc * 512, (1, 512)), start=(kc == 0), stop=(kc == NKC - 1)),
                                reads=[rwb[ws], rh[si]], writes=[C.rps[pb]])
                        evac(vstg[tb].v(0, (257, 2), (1, 256), np_=nb),
                             C.ps[pb].v(0, (256, 2), (1, 256), np_=nb), [C.rps[pb]], [rvstg[tb]])
                        P.add("sp", lambda e, tb=tb, nb=nb, grow=grow, half=half: e.dma_start(
                            out=dv(S["VM"], grow * 1028 + half * 514, (1028, nb), (1, 514)),
                            in_=vstg[tb].v(0, (1, 514), np_=nb)), reads=[rvstg[tb]], dma=True)
                    continue
                for si, (o, n, samp) in enumerate(subs):
                    for j in range(nch):
                        pb = cnt["fm"] % 2
                        cnt["fm"] += 1
                        for kc in range(NKC):
                            P.add("pe", lambda e, ws=ws, j=j, kc=kc, o=o, n=n, pb=pb: e.matmul(
                                C.ps[pb].v(0, (1, n)), wbuf[ws].v(kc * 512 + j * 128, (1, 128)),
                                ht.v(kc * TS + o, (1, n)), start=(kc == 0), stop=(kc == NKC - 1)),
                                reads=[rwb[ws], rh[si]], writes=[C.rps[pb]])
                        cidx = cbase + j
                        if kind == "qk":
                            k = cnt["s32"] % 3
                            cnt["s32"] += 1
                            evac(stg32[k].v(0, (1, n)), C.ps[pb].v(0, (1, n)), [C.rps[pb]], [rstg32[k]])
                            P.add("sp", lambda e, k=k, cidx=cidx, o=o, n=n: e.dma_start(
                                out=dv(S["QK"], cidx * NT + t0 + o, (NKC * NT, 128), (1, n)),
                                in_=stg32[k].v(0, (1, n))), reads=[rstg32[k]], writes=[], dma=True)
                        elif kind in ("om", "qi"):
                            k = cnt["sb"] % 3
                            cnt["sb"] += 1
                            dst = S["OM"] if kind == "om" else S["QI"]
                            nchk = 8 if kind == "om" else 4
                            rdst = None
                            evac(stgb[k].v(0, (1, n)), C.ps[pb].v(0, (1, n)), [C.rps[pb]], [rstgb[k]])
                            P.add("sp", lambda e, k=k, cidx=cidx, o=o, n=n, dst=dst, nchk=nchk: e.dma_start(
                                out=dv(dst, cidx * NT + t0 + o, (nchk * NT, 128), (1, n)),
                                in_=stgb[k].v(0, (1, n))), reads=[rstgb[k]], dma=True)
                        else:
                            hs = cnt["hs"] % 2
                            cnt["hs"] += 1
                            gcol = qng if kind == "qa" else kng
                            P.add("act", lambda e, hs=hs, pb=pb, n=n: e.activation(
                                out=hsq[hs].v(0, (1, n)), in_=C.ps[pb].v(0, (1, n)), func=AF.Square),
                                reads=[C.rps[pb]], writes=[rhsq[hs]])
                            P.add("pe", lambda e, hs=hs, n=n: e.matmul(
                                C.ps[6].v(0, (1, n)), C.ones_b.v(0, (1, 128)), hsq[hs].v(0, (1, n)),
                                start=True, stop=True), reads=[rhsq[hs], C.rconst], writes=[C.rps[6]])
                            P.add("act", lambda e, hs=hs, n=n: e.activation(
                                out=hr[hs].v(0, (1, n)), in_=C.ps[6].v(0, (1, n)), func=AF.Sqrt,
                                bias=eps128.v(0, (1, 1)), scale=1.0 / 128), reads=[C.rps[6], C.rconst],
                                writes=[rhr[hs]])
                            P.add("dve", lambda e, hs=hs, n=n: e.reciprocal(hr[hs].v(0, (1, n)), hr[hs].v(0, (1, n))),
                                  reads=[rhr[hs]], writes=[rhr[hs]])
                            if kind == "qa":
                                k = cnt["sb"] % 3
                                cnt["sb"] += 1
                                P.add("dve", lambda e, hs=hs, k=k, pb=pb, n=n, gcol=gcol: e.scalar_tensor_tensor(
                                    out=stgb[k].v(0, (1, n)), in0=C.ps[pb].v(0, (1, n)), scalar=gcol.v(0, (1, 1)),
                                    in1=hr[hs].v(0, (1, n)), op0=ALU.mult, op1=ALU.mult),
                                    reads=[C.rps[pb], rhr[hs], rsm], writes=[rstgb[k]])
                                P.add("sp", lambda e, k=k, cidx=cidx, o=o, n=n: e.dma_start(
                                    out=dv(S["QA"], cidx * NT + t0 + o, (8 * NT, 128), (1, n)),
                                    in_=stgb[k].v(0, (1, n))), reads=[rstgb[k]], writes=[], dma=True)
                            else:
                                P.add("dve", lambda e, hs=hs, pb=pb, n=n, j=j, gcol=gcol: e.scalar_tensor_tensor(
                                    out=kn32.v(j * 512, (1, n)), in0=C.ps[pb].v(0, (1, n)), scalar=gcol.v(0, (1, 1)),
                                    in1=hr[hs].v(0, (1, n)), op0=ALU.mult, op1=ALU.mult),
                                    reads=[C.rps[pb], rhr[hs], rsm], writes=[rkn])
                                k = cnt["sb"] % 3
                                cnt["sb"] += 1
                                P.add("act", lambda e, k=k, j=j, n=n: e.activation(
                                    out=stgb[k].v(0, (1, n)), in_=kn32.v(j * 512, (1, n)), func=AF.Copy),
                                    reads=[rkn], writes=[rstgb[k]])
                                P.add("sp", lambda e, k=k, cidx=cidx, o=o, n=n: e.dma_start(
                                    out=dv(S["KA"], cidx * NT + t0 + o, (2 * NT, 128), (1, n)),
                                    in_=stgb[k].v(0, (1, n))), reads=[rstgb[k]], writes=[], dma=True)
                                if j == 1:
                                    nb = min(n, 128)
                                    for b in range(max(1, n // 128)):
                                        for hh in range(2):
                                            P.add("pe", lambda e, b=b, hh=hh, nb=nb: e.transpose(
                                                C.ps[5].v(hh * 128, (1, 128), np_=nb),
                                                kn32.v(hh * 512 + b * 128, (1, nb)), C.ident.v(0, (1, 128))),
                                                reads=[rkn, C.rconst], writes=[C.rps[5]])
                                        ko = cnt["ko"] % 2
                                        cnt["ko"] += 1
                                        evac(kout[ko].v(0, (1, 256), np_=nb), C.ps[5].v(0, (1, 256), np_=nb),
                                             [C.rps[5]], [rkout[ko]])
                                        if samp:
                                            dst, doff = O["k_s"], 0
                                        else:
                                            dst, doff = O["k_p"], (t0 + o + b * 128) * 256
                                        P.add("sp", lambda e, ko=ko, nb=nb, dst=dst, doff=doff: e.dma_start(
                                            out=dv(dst, doff, (256, nb), (1, 256)), in_=kout[ko].v(0, (1, 256), np_=nb)),
                                            reads=[rkout[ko]], dma=True)
                if kind == "qk" and ti == 1 and "skip_conv" not in C.dbg:
                    for kc in range(NKC):
                        P.add("pe", lambda e, ws=ws, kc=kc: e.matmul(
                            C.ps[5].v(0, (1, 512), np_=67), ht.v(kc * TS + 1021, (1, 67)),
                            wbuf[ws].v(kc * 512, (1, 512)), start=(kc == 0), stop=(kc == NKC - 1)),
                            reads=[rwb[ws], rh[1], rh[2]], writes=[C.rps[5]])
                    P.add("act", lambda e: e.activation(out=cstg.v(0, (1, 512), np_=67),
                                                        in_=C.ps[5].v(0, (1, 512), np_=67), func=AF.Copy),
                          reads=[C.rps[5]], writes=[rcstg])
                    P.add("sp", lambda e, col0=col0: e.dma_start(
                        out=dv(O["conv_p"], col0, (D, 3), (1, 512)), in_=cstg.v(0, (1, 512), np_=3)),
                        reads=[rcstg], dma=True)
                    for b in range(NSB if "skip_convs" not in C.dbg else 0):
                        P.add("sp", lambda e, col0=col0, b=b: e.dma_start(
                            out=dv(O["conv_s"], b * 3 * D + col0, (D, 3), (1, 512)),
                            in_=cstg.v(0, (1, 512), p0=3 + 4 * b + 1, np_=3)), reads=[rcstg], dma=True)
            for si, (o, n, samp) in enumerate(subs):
                if "skip_misc" in C.dbg:
                    continue
                pb = cnt["fm"] % 2
                cnt["fm"] += 1
                for dup in range(2):
                    for kc in range(NKC):
                        P.add("pe", lambda e, kc=kc, o=o, n=n, pb=pb, dup=dup: e.matmul(
                            C.ps[pb].v(0, (1, n), p0=64 * dup, np_=64), wmisc.v(kc * 80, (1, 64)),
                            ht.v(kc * TS + o, (1, n)), start=(kc == 0), stop=(kc == NKC - 1)),
                            reads=[rwtm, rh[si]], writes=[C.rps[pb]])
                k = cnt["sb"] % 3
                cnt["sb"] += 1
                evac(stgb[k].v(0, (1, n)), C.ps[pb].v(0, (1, n)), [C.rps[pb]], [rstgb[k]])
                P.add("sp", lambda e, k=k, o=o, n=n: e.dma_start(
                    out=dv(S["KI2"], t0 + o, (NT, 128), (1, n)), in_=stgb[k].v(0, (1, n))),
                    reads=[rstgb[k]], writes=[], dma=True)
                pb = cnt["fm"] % 2
                cnt["fm"] += 1
                for kc in range(NKC):
                    P.add("pe", lambda e, kc=kc, o=o, n=n, pb=pb: e.matmul(
                        C.ps[pb].v(0, (1, n), np_=8), wmisc.v(kc * 80 + 72, (1, 8)),
                        ht.v(kc * TS + o, (1, n)), start=(kc == 0), stop=(kc == NKC - 1)),
                        reads=[rwtm, rh[si]], writes=[C.rps[pb]])
                k = cnt["s32"] % 3
                cnt["s32"] += 1
                P.add("act", lambda e, k=k, pb=pb, n=n: e.activation(
                    out=stg32[k].v(0, (1, n), np_=8), in_=C.ps[pb].v(0, (1, n), np_=8), func=AF.Identity,
                    bias=gb_.v(0, (1, 1), np_=8), scale=1.0), reads=[C.rps[pb], rsm], writes=[rstg32[k]])
                P.add("sp", lambda e, k=k, o=o, n=n: e.dma_start(
                    out=dv(S["GT"], t0 + o, (NT, 8), (1, n)), in_=stg32[k].v(0, (1, n), np_=8)),
                    reads=[rstg32[k]], writes=[], dma=True)
            for (bo, nb, si, samp, trow) in blocks:
                if "skip_tm" in C.dbg:
                    continue
                tb = cnt["tb"] % 2
                cnt["tb"] += 1
                grow = (NP_TOK if samp else trow)
                for kc in range(NKC):
                    P.add("pe", lambda e, kc=kc, bo=bo, nb=nb: e.matmul(
                        C.ps[TMB].v(0, (1, 256), np_=nb), ht.v(kc * TS + bo, (1, nb)),
                        wva.v(kc * 256, (1, 256)), start=(kc == 0), stop=(kc == NKC - 1)),
                        reads=[rwtm, rh[si]], writes=[C.rps[TMB]])
                for kc in range(NKC if "tm_nokiwi" not in C.dbg else 0):
                    P.add("pe", lambda e, kc=kc, bo=bo, nb=nb: e.matmul(
                        C.ps[TMB].v(256, (1, 72), np_=nb), ht.v(kc * TS + bo, (1, nb)),
                        wmisc.v(kc * 80, (1, 72)), start=(kc == 0), stop=(kc == NKC - 1)),
                        reads=[rwtm, rh[si]], writes=[C.rps[TMB]])
                if "tm_nodve" not in C.dbg:
                    P.add("act", lambda e, tb=tb, nb=nb: e.activation(
                        out=tstg[tb].v(0, (1, 320), np_=nb), in_=C.ps[TMB].v(0, (1, 320), np_=nb), func=AF.Copy),
                        reads=[C.rps[TMB]], writes=[rtstg[tb]])
                if "tm_noact" not in C.dbg:
                    P.add("act", lambda e, tb=tb, nb=nb: e.activation(
                        out=tstgb[tb].v(0, (1, 256), np_=nb), in_=C.ps[TMB].v(0, (1, 256), np_=nb), func=AF.Copy),
                        reads=[C.rps[TMB]], writes=[rtstgb[tb]])
                if "skip_wi" not in C.dbg:
                    P.add("act", lambda e, tb=tb, nb=nb: e.activation(
                        out=tstw[tb].v(0, (1, 8), np_=nb), in_=C.ps[TMB].v(320, (1, 8), np_=nb), func=AF.Copy,
                        scale=WI_SCALE), reads=[C.rps[TMB]], writes=[rtstw[tb]])
                vdst, vrow = (O["v_s"], 0) if samp else (O["v_p"], trow)
                idst = O["idxk_s"] if samp else O["idxk_p"]
                if "tm_novout" not in C.dbg:
                    P.add("sp", lambda e, tb=tb, nb=nb, vdst=vdst, vrow=vrow: e.dma_start(
                        out=dv(vdst, vrow * 256, (256, nb), (1, 256)), in_=tstg[tb].v(0, (1, 256), np_=nb)),
                        reads=[rtstg[tb]], dma=True)
                if "skip_idxk" not in C.dbg:
                    P.add("sp", lambda e, tb=tb, nb=nb, idst=idst, vrow=vrow: e.dma_start(
                        out=dv(idst, vrow * 64, (64, nb), (1, 64)), in_=tstg[tb].v(256, (1, 64), np_=nb)),
                        reads=[rtstg[tb]], dma=True)
                if "tm_nova" not in C.dbg:
                    P.add("sp", lambda e, tb=tb, nb=nb, grow=grow: e.dma_start(
                        out=dv(S["VA"], grow * 256, (256, nb), (1, 256)), in_=tstgb[tb].v(0, (1, 256), np_=nb)),
                        reads=[rtstgb[tb]], writes=[], dma=True)
                if "skip_wi" not in C.dbg:
                    P.add("sp", lambda e, tb=tb, nb=nb, grow=grow: e.dma_start(
                        out=dv(S["WI"], grow * 8, (8, nb), (1, 8)), in_=tstw[tb].v(0, (1, 8), np_=nb)),
                        reads=[rtstw[tb]], writes=[], dma=True)
        P.barrier()


LN16 = 2.772588722239781


def phase_mlstm(C):
    P, I, S, O, nc = C.P, C.I, C.S, C.O, C.nc
    with contextlib.ExitStack() as es:
        def sb(name, width, dt):
            return SB(es.enter_context(nc.sbuf_tensor("ml_" + name, [128, padw(width, dt)], dt)), padw(width, dt), dt)
        C.sb_save = C.sb
        C.sb = sb
        psb = [SB(p.t.bitcast(BF16), 1024, BF16) for p in C.ps]
        cw = sb("cw", 64, F32)
        rcw = Res()
        load_featmajor_small(C, cw, I["conv_w"], 64, rcw, "tmp_cw")
        outg = sb("outg", 8, F32)
        load_featmajor_small(C, outg, I["out_g"], 8, rcw, "tmp_og")
        tri = sb("tri", 64, F32)
        sel8 = sb("sel8", 1024, F32)
        mk64 = sb("mk64", 512, F32)
        mks = sb("mks", 3 * 64, F32)
        m0r = sb("m0r", 64, F32)
        rk = Res()
        P.add("sp", lambda e: e.dma_start(out=tri.v(0, (1, 64), np_=64), in_=I["tri"]), writes=[rk], dma=True)
        P.add("sp", lambda e: e.dma_start(out=sel8.v(0, (1, 1024), np_=8), in_=I["sel8"]), writes=[rk], dma=True)
        P.add("sp", lambda e: e.dma_start(out=mk64.v(0, (1, 512)), in_=dv(I["mk64"], 0, (0, 128), (1, 512))),
              writes=[rk], dma=True)
        P.add("sp", lambda e: e.dma_start(out=mks.v(0, (1, 192)), in_=dv(I["mks"], 0, (0, 128), (1, 192))),
              writes=[rk], dma=True)
        for h in range(4):
            P.add("sp", lambda e, h=h: e.dma_start(out=m0r.v(h * 16, (1, 16)),
                                                   in_=dv(I["st_m"], h, (0, 128), (4, 16))), writes=[rk], dma=True)
        NG = 512
        xc = sb("xc", NKC * 515, F32)
        rxc = Res()
        ycv = [sb("ycv%d" % i, 512, F32) for i in range(2)]
        rycv = [Res(), Res()]
        qk = sb("qk", NKC * NG, BF16)
        rqk = Res()
        gtg = sb("gtg", NG, F32)
        rgtg = Res()
        R = [sb("R%d" % i, 4 * NG, F32) for i in range(6)]
        rR = [Res() for _ in range(6)]
        carry = sb("carry", 4, F32)
        rcarry = Res()
        hbuf = sb("hbuf", 8 * NG, F32)
        rhb = Res()
        omt = sb("omt", 8 * NG, BF16)
        romt = Res()
        mot = sb("mot", 8 * NG, BF16)
        rmot = Res()
        Cst = [sb("Cst%d" % i, 4 * 514, F32) for i in range(2)]
        rCst = [[Res() for _ in range(4)] for _ in range(2)]
        Cb = [sb("Cb%d" % i, 4 * 514, BF16) for i in range(2)]
        rCb = [[Res() for _ in range(4)] for _ in range(2)]
        vt = [sb("vt%d" % i, 1028, BF16) for i in range(2)]
        rvt = [Res(), Res()]
        acol = [sb("acol%d" % i, 1, F32) for i in range(2)]
        racol = [Res(), Res()]
        wl = [sb("wl%d" % i, 1, F32) for i in range(2)]
        rwl = [Res(), Res()]
        DT = [sb("DT%d" % i, 64, F32) for i in range(2)]
        rDT = [Res(), Res()]
        Wt = [sb("Wt%d" % i, 64, BF16) for i in range(2)]
        rWt = [Res(), Res()]
        qs = [sb("qs%d" % i, 128, BF16) for i in range(2)]
        rqs = [Res(), Res()]
        kw = [sb("kw%d" % i, 256, BF16) for i in range(2)]
        rkw = [Res(), Res()]
        rr = [sb("rr%d" % i, 64, F32) for i in range(2)]
        rrr = [Res(), Res()]
        sq = [sb("sq%d" % i, NG, BF16) for i in range(2)]
        rsq = [Res(), Res()]
        rs_ = sb("rs_", NG, F32)
        rrs = Res()
        sg = [sb("sg%d" % i, NG, BF16) for i in range(2)]
        rsg = [Res(), Res()]
        tmpd = [sb("tmpd%d" % i, NG, F32) for i in range(2)]
        rtmpd = [Res(), Res()]
        stc = SBV(xc, 0)
        xs7 = SBV(xc, 2048)
        xsn = SBV(xc, 2048 + NKC * NSB * 7)
        rxs7 = rxc
        rxsn = rxc
        rstc = rxc
        P.add("dve", lambda e: e.memset(Cst[0].v(0, (1, 4 * 514)), 0.0), writes=rCst[0])
        P.add("dve", lambda e: e.memset(Cb[0].v(0, (1, 4 * 514)), 0.0), writes=rCb[0])
        it = {"n": 0}

        def run_group(t0, n, L, sample):
            nch = n // L
            if not sample:
                if t0 == 0:
                    P.add("dve", lambda e: e.memset(xc.v(0, (515, NKC), (1, 3)), 0.0), writes=[rxc])
                    P.add("sp", lambda e: e.dma_start(out=xc.v(3, (515, NKC), (1, 512)),
                                                      in_=dv(S["QK"], 0, (NKC * NT, 128), (NT, NKC), (1, 512))),
                          writes=[rxc], dma=True)
                else:
                    P.add("sp", lambda e: e.dma_start(out=xc.v(0, (515, NKC), (1, 515)),
                                                      in_=dv(S["QK"], t0 - 3, (NKC * NT, 128), (NT, NKC), (1, 515))),
                          writes=[rxc], dma=True)
                for c in range(NKC):
                    s = c % 2
                    for j in (3, 2, 1, 0):
                        if j == 3:
                            P.add("dve", lambda e, s=s, c=c, j=j: e.tensor_scalar(
                                ycv[s].v(0, (1, 512)), xc.v(c * 515 + j, (1, 512)), cw.v(j * 16 + c, (1, 1)), None,
                                op0=ALU.mult), reads=[rxc, rcw], writes=[rycv[s]])
                        else:
                            P.add("dve", lambda e, s=s, c=c, j=j: e.scalar_tensor_tensor(
                                out=ycv[s].v(0, (1, 512)), in0=xc.v(c * 515 + j, (1, 512)),
                                scalar=cw.v(j * 16 + c, (1, 1)), in1=ycv[s].v(0, (1, 512)),
                                op0=ALU.mult, op1=ALU.add), reads=[rxc, rcw, rycv[s]], writes=[rycv[s]])
                    P.add("act", lambda e, s=s, c=c: e.activation(
                        out=qk.v(c * NG, (1, 512)), in_=ycv[s].v(0, (1, 512)), func=AF.Silu),
                        reads=[rycv[s]], writes=[rqk])
            else:
                P.add("sp", lambda e: e.dma_start(out=stc.v(0, (1, D), np_=48), in_=I["st_conv"]),
                      writes=[rstc], dma=True)
                P.add("sp", lambda e: e.dma_start(out=xsn.v(0, (64, NKC), (1, 64)),
                                                  in_=dv(S["QK"], NP_TOK, (NKC * NT, 128), (NT, NKC), (1, 64))),
                      writes=[rxsn], dma=True)
                for c4 in range(4):
                    for j in range(4):
                        c = c4 * 4 + j
                        P.add("pe", lambda e, c=c, j=j: e.transpose(
                            C.ps[0].v(j * 48, (1, 48)), stc.v(c * 128, (1, 128), np_=48),
                            C.ident.v(0, (1, 48), np_=48)), reads=[rstc, C.rconst], writes=[C.rps[0]])
                    P.add("dve", lambda e, c4=c4: e.tensor_copy(
                        xs7.v(c4 * 4 * 112, (112, 4), (7, NSB), (1, 3)), C.ps[0].v(0, (48, 4), (3, NSB), (1, 3))),
                        reads=[C.rps[0]], writes=[rxs7])
                P.add("dve", lambda e: e.tensor_copy(xs7.v(3, (112, NKC), (7, NSB), (1, 4)),
                                                     xsn.v(0, (64, NKC), (4, NSB), (1, 4))),
                      reads=[rxsn, rxs7], writes=[rxs7])
                for c in range(NKC):
                    s = c % 2
                    for j in (3, 2, 1, 0):
                        if j == 3:
                            P.add("dve", lambda e, s=s, c=c, j=j: e.tensor_scalar(
                                ycv[s].v(0, (4, NSB), (1, 4)), xs7.v(c * 112 + j, (7, NSB), (1, 4)),
                                cw.v(j * 16 + c, (1, 1)), None, op0=ALU.mult),
                                reads=[rxs7, rcw], writes=[rycv[s]])
                        else:
                            P.add("dve", lambda e, s=s, c=c, j=j: e.scalar_tensor_tensor(
                                out=ycv[s].v(0, (4, NSB), (1, 4)), in0=xs7.v(c * 112 + j, (7, NSB), (1, 4)),
                                scalar=cw.v(j * 16 + c, (1, 1)), in1=ycv[s].v(0, (4, NSB), (1, 4)),
                                op0=ALU.mult, op1=ALU.add), reads=[rxs7, rcw, rycv[s]], writes=[rycv[s]])
                    P.add("act", lambda e, s=s, c=c: e.activation(
                        out=qk.v(c * NG, (1, 64)), in_=ycv[s].v(0, (1, 64)), func=AF.Silu),
                        reads=[rycv[s]], writes=[rqk])
            P.add("sp", lambda e: e.dma_start(out=gtg.v(0, (1, n), np_=8), in_=dv(S["GT"], t0, (NT, 8), (1, n))),
                  writes=[rgtg], dma=True)
            for r_ in range(8):
                bank = r_ % 2
                P.add("pe", lambda e, r_=r_, bank=bank: e.matmul(
                    C.ps[bank].v(0, (1, n)), sel8.v(r_ * 128, (1, 128), np_=8), gtg.v(0, (1, n), np_=8),
                    start=True, stop=True), reads=[rgtg, rk], writes=[C.rps[bank]])
                dst = R[0] if r_ < 4 else R[1]
                rd = rR[0] if r_ < 4 else rR[1]
                P.add("act", lambda e, r_=r_, bank=bank, dst=dst: e.activation(
                    out=dst.v((r_ % 4) * NG, (1, n)), in_=C.ps[bank].v(0, (1, n)), func=AF.Copy),
                    reads=[C.rps[bank]], writes=[rd])

            def all4(Rt):
                return Rt.v(0, (NG, 4), (1, n))

            P.add("dve", lambda e: e.scalar_tensor_tensor(out=all4(R[2]), in0=all4(R[1]), scalar=-1.0, in1=all4(R[1]),
                                                          op0=ALU.mult, op1=ALU.max), reads=[rR[1]], writes=[rR[2]])
            P.add("act", lambda e: e.activation(out=all4(R[2]), in_=all4(R[2]), func=AF.Exp, scale=-1.0),
                  reads=[rR[2]], writes=[rR[2]])
            P.add("act", lambda e: e.activation(out=all4(R[2]), in_=all4(R[2]), func=AF.Ln, bias=C.oneb.v(0, (1, 1)),
                                                scale=1.0), reads=[rR[2], C.rconst], writes=[rR[2]])
            P.add("dve", lambda e: e.scalar_tensor_tensor(out=all4(R[1]), in0=all4(R[1]), scalar=0.0, in1=all4(R[2]),
                                                          op0=ALU.min, op1=ALU.subtract),
                  reads=[rR[1], rR[2]], writes=[rR[1]])
            for h in range(4):
                mask = mks.v(0, (1, n)) if sample else mk64.v(0, (1, n))
                P.add("dve", lambda e, h=h, mask=mask: e.tensor_tensor_scan(
                    R[2].v(h * NG, (1, n)), mask, R[1].v(h * NG, (1, n)), 0.0, op0=ALU.mult, op1=ALU.add),
                    reads=[rR[1], rk], writes=[rR[2]])
            if sample:
                P.add("dve", lambda e: e.tensor_copy(R[5].v(0, (NG, 4), (4, NSB), (1, 4)),
                                                     m0r.v(0, (16, 4), (1, NSB), (0, 4))), reads=[rk], writes=[rR[5]])
                P.add("dve", lambda e: e.tensor_tensor(out=all4(R[4]), in0=all4(R[5]), in1=all4(R[1]), op=ALU.add),
                      reads=[rR[5], rR[1]], writes=[rR[4]])
                P.add("dve", lambda e: e.tensor_tensor(out=all4(R[4]), in0=all4(R[4]),
                                                       in1=mks.v(64, (0, 4), (1, n)), op=ALU.add),
                      reads=[rR[4], rk], writes=[rR[4]])
                P.add("dve", lambda e: e.tensor_tensor(out=all4(R[4]), in0=all4(R[4]), in1=all4(R[0]), op=ALU.max),
                      reads=[rR[4], rR[0]], writes=[rR[4]])
                P.add("dve", lambda e: e.tensor_tensor(out=all4(R[3]), in0=all4(R[1]),
                                                       in1=mks.v(128, (0, 4), (1, n)), op=ALU.add),
                      reads=[rR[1], rk], writes=[rR[3]])
                for h in range(4):
                    P.add("dve", lambda e, h=h: e.tensor_tensor_scan(
                        R[3].v(h * NG, (1, n)), R[3].v(h * NG, (1, n)), R[4].v(h * NG, (1, n)), 0.0,
                        op0=ALU.add, op1=ALU.max), reads=[rR[3], rR[4]], writes=[rR[3]])
            else:
                for h in range(4):
                    init = 0.0 if t0 == 0 else carry.v(h, (1, 1))
                    P.add("dve", lambda e, h=h, init=init: e.tensor_tensor_scan(
                        R[3].v(h * NG, (1, n)), R[1].v(h * NG, (1, n)), R[0].v(h * NG, (1, n)), init,
                        op0=ALU.add, op1=ALU.max), reads=[rR[1], rR[0], rcarry], writes=[rR[3]])
                if t0 == 0:
                    P.add("dve", lambda e: e.memset(R[5].v(0, (NG, 4), (1, L)), 0.0), writes=[rR[5]])
                else:
                    P.add("dve", lambda e: e.tensor_copy(R[5].v(0, (NG, 4), (1, L)), carry.v(0, (1, 4), (0, L))),
                          reads=[rcarry], writes=[rR[5]])
                P.add("dve", lambda e: e.tensor_copy(R[5].v(L, (NG, 4), (L, nch - 1), (1, L)),
                                                     R[3].v(L - 1, (NG, 4), (L, nch - 1), (0, L))),
                      reads=[rR[3], rR[5]], writes=[rR[5]])
                P.add("dve", lambda e: e.tensor_copy(carry.v(0, (1, 4)), R[3].v(n - 1, (NG, 4))),
                      reads=[rR[3], rR[5]], writes=[rcarry])
            P.add("dve", lambda e: e.scalar_tensor_tensor(out=all4(R[4]), in0=all4(R[0]), scalar=-LN16, in1=all4(R[2]),
                                                          op0=ALU.add, op1=ALU.subtract),
                  reads=[rR[0], rR[2], rR[3]], writes=[rR[4]])
            P.add("dve", lambda e: e.tensor_tensor(out=all4(R[2]), in0=all4(R[2]), in1=all4(R[3]), op=ALU.subtract),
                  reads=[rR[2], rR[3], rR[4]], writes=[rR[2]])
            P.add("dve", lambda e: e.tensor_tensor(out=all4(R[1]), in0=all4(R[5]), in1=all4(R[2]), op=ALU.add),
                  reads=[rR[5], rR[2], rR[3]], writes=[rR[1]])
            P.add("act", lambda e: e.activation(out=all4(R[1]), in_=all4(R[1]), func=AF.Exp),
                  reads=[rR[1]], writes=[rR[1]])
            P.add("act", lambda e: e.activation(out=all4(R[5]), in_=all4(R[3]), func=AF.Exp, scale=-1.0),
                  reads=[rR[3], rR[1]], writes=[rR[5]])
            if sample:
                for h in range(4):
                    P.add("sp", lambda e, h=h: e.dma_start(out=dv(O["m_s"], h, (64, 1), (4, NSB)),
                                                           in_=R[3].v(h * NG + 3, (4, NSB), np_=1)),
                          reads=[rR[3]], dma=True)
            elif t0 + n == NP_TOK:
                P.add("sp", lambda e: e.dma_start(out=dv(O["m_p"], 0, (4, 1), (1, 4)),
                                                  in_=R[3].v(n - 1, (NG, 4), np_=1)), reads=[rR[3]], dma=True)
            P.add("sp", lambda e: e.dma_start(out=omt.v(0, (NG, 8), (1, n)),
                                              in_=dv(S["OM"], t0, (8 * NT, 128), (NT, 8), (1, n))),
                  writes=[romt], dma=True)
            for c in range(nch):
                cs = c % 2
                tok = t0 + c * L
                P.add("sp", lambda e, cs=cs, tok=tok: e.dma_start(
                    out=vt[cs].v(0, (1, 1028), np_=L), in_=dv(S["VM"], tok * 1028, (1028, L), (1, 1028))),
                    writes=[rvt[cs]], dma=True)
                if sample:
                    st = c % 2
                    P.add("sp", lambda e, st=st, c=c: e.dma_start(
                        out=Cst[st].v(0, (514, 4), (257, 2), (1, 256)),
                        in_=dv(I["st_C"], c * 4 * 65536, (256, 128), (65536, 4), (128 * 256, 2), (1, 256))),
                        writes=rCst[st], dma=True)
                    P.add("sp", lambda e, st=st, c=c: e.dma_start(
                        out=Cst[st].v(256, (514, 4), (257, 2), (1, 1)),
                        in_=dv(I["st_n"], c * 1024, (1, 128), (256, 4), (128, 2), (1, 1))),
                        writes=rCst[st], dma=True)
                    P.add("act", lambda e, st=st: e.activation(out=Cb[st].v(0, (1, 4 * 514)),
                                                               in_=Cst[st].v(0, (1, 4 * 514)), func=AF.Copy),
                          reads=rCst[st], writes=rCb[st])
                else:
                    st = 0
                for h in range(4):
                    s = it["n"] % 2
                    it["n"] += 1
                    b0 = 4 * s
                    col = h * NG + c * L
                    qc, kc_ = 2 * h, 8 + 2 * h
                    P.add("pe", lambda e, col=col, b0=b0: e.transpose(
                        C.ps[b0].v(64, (1, 128), np_=L), R[4].v(col, (1, L)), C.ident.v(0, (1, 128))),
                        reads=[rR[4], C.rconst], writes=[C.rps[b0]])
                    P.add("dve", lambda e, s=s, b0=b0: e.tensor_copy(acol[s].v(0, (1, 1), np_=L),
                                                                     C.ps[b0].v(64, (1, 1), np_=L)),
                          reads=[C.rps[b0]], writes=[racol[s]])
                    for dc in range(2):
                        P.add("pe", lambda e, dc=dc, b0=b0, c=c: e.matmul(
                            C.ps[b0].v(0, (1, L), np_=L), qk.v((kc_ + dc) * NG + c * L, (1, L)),
                            qk.v((qc + dc) * NG + c * L, (1, L)), start=(dc == 0), stop=(dc == 1)),
                            reads=[rqk], writes=[C.rps[b0]])
                    P.add("act", lambda e, s=s, col=col: e.activation(
                        out=DT[s].v(0, (1, L), np_=L), in_=R[2].v(col, (1, L), np_=L), func=AF.Exp,
                        bias=acol[s].v(0, (1, 1), np_=L), scale=1.0), reads=[rR[2], racol[s]], writes=[rDT[s]])
                    P.add("pool", lambda e, s=s: e.tensor_tensor(
                        out=DT[s].v(0, (1, L), np_=L), in0=DT[s].v(0, (1, L), np_=L), in1=tri.v(0, (1, L), np_=L),
                        op=ALU.mult), reads=[rDT[s], rk], writes=[rDT[s]])
                    P.add("dve", lambda e, s=s, b0=b0: e.tensor_tensor(
                        out=Wt[s].v(0, (1, L), np_=L), in0=C.ps[b0].v(0, (1, L), np_=L), in1=DT[s].v(0, (1, L), np_=L),
                        op=ALU.mult), reads=[C.rps[b0], rDT[s]], writes=[rWt[s]])
                    P.add("dve", lambda e, s=s, col=col, c=c: e.tensor_tensor(
                        out=qs[s].v(0, (64, 2), (1, L)), in0=qk.v(qc * NG + c * L, (NG, 2), (1, L)),
                        in1=R[1].v(col, (0, 2), (1, L)), op=ALU.mult), reads=[rqk, rR[1]], writes=[rqs[s]])
                    for ec in range(2):
                        P.add("pe", lambda e, s=s, cs=cs, ec=ec, b0=b0: e.matmul(
                            C.ps[b0 + 1].v(ec * 64, (1, L)), vt[cs].v(h * 257 + ec * 128, (1, 128), np_=L),
                            Wt[s].v(0, (1, L), np_=L), start=True, stop=False),
                            reads=[rvt[cs], rWt[s]], writes=[C.rps[b0 + 1]])
                        for dc in range(2):
                            P.add("pe", lambda e, s=s, st=st, ec=ec, dc=dc, b0=b0: e.matmul(
                                C.ps[b0 + 1].v(ec * 64, (1, L)), Cb[st].v(h * 514 + dc * 257 + ec * 128, (1, 128)),
                                qs[s].v(dc * 64, (1, L)), start=False, stop=(dc == 1)),
                                reads=[rCb[st][h], rqs[s]], writes=[C.rps[b0 + 1]])
                    P.add("pe", lambda e, s=s, b0=b0: e.matmul(
                        C.ps[b0 + 1].v(128, (1, L)), C.ones_b.v(0, (1, 128), np_=L), Wt[s].v(0, (1, L), np_=L),
                        start=True, stop=False), reads=[rWt[s], C.rconst], writes=[C.rps[b0 + 1]])
                    for dc in range(2):
                        P.add("pe", lambda e, s=s, st=st, dc=dc, b0=b0: e.matmul(
                            C.ps[b0 + 1].v(128, (1, L)), Cb[st].v(h * 514 + dc * 257 + 256, (0, 128)),
                            qs[s].v(dc * 64, (1, L)), start=False, stop=(dc == 1)),
                            reads=[rCb[st][h], rqs[s]], writes=[C.rps[b0 + 1]])
                    P.add("act", lambda e, s=s, b0=b0: e.activation(
                        out=rr[s].v(0, (1, L)), in_=C.ps[b0 + 1].v(128, (1, L)), func=AF.Abs),
                        reads=[C.rps[b0 + 1]], writes=[rrr[s]])
                    P.add("dve", lambda e, s=s, col=col: e.tensor_tensor(
                        out=rr[s].v(0, (1, L)), in0=rr[s].v(0, (1, L)), in1=R[5].v(col, (1, L)), op=ALU.max),
                        reads=[rrr[s], rR[5]], writes=[rrr[s]])
                    P.add("dve", lambda e, s=s: e.reciprocal(rr[s].v(0, (1, L)), rr[s].v(0, (1, L))),
                          reads=[rrr[s]], writes=[rrr[s]])
                    P.add("dve", lambda e, s=s, b0=b0, c=c: e.tensor_tensor(
                        out=hbuf.v(2 * h * NG + c * L, (NG, 2), (1, L)), in0=C.ps[b0 + 1].v(0, (64, 2), (1, L)),
                        in1=rr[s].v(0, (0, 2), (1, L)), op=ALU.mult), reads=[C.rps[b0 + 1], rrr[s]], writes=[rhb])
                    for dc in range(2):
                        P.add("pe", lambda e, dc=dc, b0=b0, c=c: e.transpose(
                            psb[b0].v(512 + dc * 128, (1, 128), np_=L), qk.v((kc_ + dc) * NG + c * L, (1, L)),
                            C.identb.v(0, (1, 128))), reads=[rqk, C.rconst], writes=[C.rps[b0]])
                    P.add("act", lambda e, s=s, col=col: e.activation(
                        out=wl[s].v(0, (1, 1), np_=L), in_=acol[s].v(0, (1, 1), np_=L), func=AF.Exp,
                        bias=R[2].v(col + L - 1, (1, 1), np_=L), scale=1.0), reads=[racol[s], rR[2]], writes=[rwl[s]])
                    P.add("dve", lambda e, s=s, b0=b0: e.tensor_scalar(
                        kw[s].v(0, (1, 256), np_=L), psb[b0].v(512, (1, 256), np_=L), wl[s].v(0, (1, 1), np_=L), None,
                        op0=ALU.mult), reads=[C.rps[b0], rwl[s]], writes=[rkw[s]])
                    for dc in range(2):
                        P.add("pe", lambda e, s=s, cs=cs, dc=dc, b0=b0: e.matmul(
                            C.ps[b0 + 2 + dc].v(0, (1, 257)), kw[s].v(dc * 128, (1, 128), np_=L),
                            vt[cs].v(h * 257, (1, 257), np_=L), start=True, stop=True),
                            reads=[rkw[s], rvt[cs]], writes=[C.rps[b0 + 2 + dc]])
                        P.add("dve", lambda e, st=st, dc=dc, b0=b0, col=col: e.scalar_tensor_tensor(
                            out=Cst[st].v(h * 514 + dc * 257, (1, 257)), in0=Cst[st].v(h * 514 + dc * 257, (1, 257)),
                            scalar=R[1].v(col + L - 1, (1, 1)), in1=C.ps[b0 + 2 + dc].v(0, (1, 257)),
                            op0=ALU.mult, op1=ALU.add), reads=[rCst[st][h], rR[1], C.rps[b0 + 2 + dc]],
                            writes=[rCst[st][h]])
                    if not sample:
                        P.add("act", lambda e, st=st: e.activation(
                            out=Cb[st].v(h * 514, (1, 514)), in_=Cst[st].v(h * 514, (1, 514)), func=AF.Copy),
                            reads=[rCst[st][h]], writes=[rCb[st][h]])
                if sample:
                    P.add("sp", lambda e, st=st, c=c: e.dma_start(
                        out=dv(O["C_s"], c * 4 * 65536, (256, 128), (65536, 4), (128 * 256, 2), (1, 256)),
                        in_=Cst[st].v(0, (514, 4), (257, 2), (1, 256))), reads=rCst[st], dma=True)
                    P.add("sp", lambda e, st=st, c=c: e.dma_start(
                        out=dv(O["n_s"], c * 1024, (1, 128), (256, 4), (128, 2), (1, 1)),
                        in_=Cst[st].v(256, (514, 4), (257, 2), (1, 1))), reads=rCst[st], dma=True)
            if (not sample) and t0 + n == NP_TOK:
                P.add("sp", lambda e: e.dma_start(
                    out=dv(O["C_p"], 0, (256, 128), (65536, 4), (128 * 256, 2), (1, 256)),
                    in_=Cst[0].v(0, (514, 4), (257, 2), (1, 256))), reads=rCst[0], dma=True)
                P.add("sp", lambda e: e.dma_start(
                    out=dv(O["n_p"], 0, (1, 128), (256, 4), (128, 2), (1, 1)),
                    in_=Cst[0].v(256, (514, 4), (257, 2), (1, 1))), reads=rCst[0], dma=True)
            for h in range(4):
                for ec in range(2):
                    P.add("act", lambda e, h=h, ec=ec: e.activation(
                        out=sq[ec].v(0, (1, n)), in_=hbuf.v((2 * h + ec) * NG, (1, n)), func=AF.Square),
                        reads=[rhb], writes=[rsq[ec]])
                    P.add("pe", lambda e, ec=ec: e.matmul(
                        C.ps[0].v(0, (1, n)), C.ones_b.v(0, (1, 128)), sq[ec].v(0, (1, n)),
                        start=(ec == 0), stop=(ec == 1)), reads=[rsq[ec], C.rconst], writes=[C.rps[0]])
                P.add("act", lambda e: e.activation(out=rs_.v(0, (1, n)), in_=C.ps[0].v(0, (1, n)), func=AF.Sqrt,
                                                    bias=C.epsb.v(0, (1, 1)), scale=1.0 / 256),
                      reads=[C.rps[0], C.rconst], writes=[rrs])
                P.add("dve", lambda e: e.reciprocal(rs_.v(0, (1, n)), rs_.v(0, (1, n))), reads=[rrs], writes=[rrs])
                for ec in range(2):
                    ch = 2 * h + ec
                    P.add("act", lambda e, ch=ch, ec=ec: e.activation(
                        out=sg[ec].v(0, (1, n)), in_=omt.v(ch * NG, (1, n)), func=AF.Sigmoid),
                        reads=[romt], writes=[rsg[ec]])
                    P.add("dve", lambda e, ch=ch, ec=ec: e.scalar_tensor_tensor(
                        out=tmpd[ec].v(0, (1, n)), in0=hbuf.v(ch * NG, (1, n)), scalar=outg.v(ch, (1, 1)),
                        in1=rs_.v(0, (1, n)), op0=ALU.mult, op1=ALU.mult), reads=[rhb, rrs, rcw], writes=[rtmpd[ec]])
                    P.add("dve", lambda e, ch=ch, ec=ec: e.tensor_tensor(
                        out=mot.v(ch * NG, (1, n)), in0=tmpd[ec].v(0, (1, n)), in1=sg[ec].v(0, (1, n)), op=ALU.mult),
                        reads=[rtmpd[ec], rsg[ec]], writes=[rmot])
            P.add("sp", lambda e: e.dma_start(out=dv(S["MOAO"], t0, (NKC * NT, 128), (NT, 8), (1, n)),
                                              in_=mot.v(0, (NG, 8), (1, n))), reads=[rmot], dma=True)

        for g in range(4):
            run_group(g * 512, 512, 64, False)
        run_group(NP_TOK, 64, 4, True)
        P.barrier()
        C.sb = C.sb_save


NEGB = -1.0e30
ATT_SCALE = 128 ** -0.5


def t5_bucket_np(d):
    d = np.maximum(np.asarray(d), 0)
    ratio = np.log(np.maximum(d, 1).astype(np.float32) / np.float32(16)) / np.float32(np.log(8.0))
    large = 16 + (ratio * np.float32(16)).astype(np.int32)
    large = np.minimum(large, 31)
    return np.where(d < 16, d, large)


def host_bias_tables(t5):
    s = np.arange(128)[:, None]
    t = np.arange(128)[None, :]
    kinds = [t5_bucket_np(t - s), t5_bucket_np(t - s + 128), np.full((128, 128), 31)]
    bdp = np.zeros((2, 3, 128, 4, 128), np.float32)
    for g in range(2):
        for k in range(3):
            for h in range(4):
                bdp[g, k, :, h, :] = t5[kinds[k], 4 * g + h]
    sk = np.arange(128)[:, None, None]
    kb = np.arange(16)[None, :, None]
    tt = np.arange(4)[None, None, :]
    bk = t5_bucket_np(2048 + tt - (kb * 128 + sk))
    bsp = np.zeros((2, 128, 16, 4, 4), np.float32)
    bsn = np.zeros((2, 4, 4, 4), np.float32)
    bn = t5_bucket_np(np.arange(4)[None, :] - np.arange(4)[:, None])
    for g in range(2):
        for h in range(4):
            bsp[g, :, :, h, :] = t5[bk, 4 * g + h]
            bsn[g, :, h, :] = t5[bn, 4 * g + h]
    return bdp.reshape(6, 128, 512), bsp.reshape(2, 128, 256), bsn.reshape(2, 4, 16)


def phase_dsa(C):
    P, I, S, O, nc = C.P, C.I, C.S, C.O, C.nc
    with contextlib.ExitStack() as es:
        def sb(name, width, dt):
            return SB(es.enter_context(nc.sbuf_tensor("ds_" + name, [128, padw(width, dt)], dt)), padw(width, dt), dt)
        psb = [SB(p.t.bitcast(BF16), 1024, BF16) for p in C.ps]
        ki2 = sb("ki2", NT, BF16)
        ka = sb("ka", 2 * NT, BF16)
        va = sb("va", 16 * 256, BF16)
        qi = sb("qi", 4 * NT, BF16)
        qa = sb("qa", 8 * NT, BF16)
        wi = sb("wi", 17 * 8, F32)
        bdp = sb("bdp", 6 * 512, F32)
        negtri = sb("negtri", 128, F32)
        rin = Res()
        P.add("sp", lambda e: e.dma_start(out=ki2.v(0, (1, NT)), in_=S["KI2"]), writes=[rin], dma=True)
        P.add("sp", lambda e: e.dma_start(out=ka.v(0, (1, 2 * NT)), in_=dv(S["KA"], 0, (2 * NT, 128), (1, 2 * NT))),
              writes=[rin], dma=True)
        P.add("sp", lambda e: e.dma_start(out=va.v(0, (256, 16), (1, 256)),
                                          in_=dv(S["VA"], 0, (256, 128), (128 * 256, 16), (1, 256))),
              writes=[rin], dma=True)
        P.add("sp", lambda e: e.dma_start(out=qi.v(0, (1, 4 * NT)), in_=dv(S["QI"], 0, (4 * NT, 128), (1, 4 * NT))),
              writes=[rin], dma=True)
        P.add("sp", lambda e: e.dma_start(out=qa.v(0, (1, 8 * NT)), in_=dv(S["QA"], 0, (8 * NT, 128), (1, 8 * NT))),
              writes=[rin], dma=True)
        P.add("sp", lambda e: e.dma_start(out=wi.v(0, (8, 16), (1, 8)),
                                          in_=dv(S["WI"], 0, (8, 128), (128 * 8, 16), (1, 8))), writes=[rin], dma=True)
        P.add("sp", lambda e: e.dma_start(out=wi.v(128, (1, 8), np_=64), in_=dv(S["WI"], NP_TOK * 8, (8, 64), (1, 8))),
              writes=[rin], dma=True)
        P.add("sp", lambda e: e.dma_start(out=bdp.v(0, (512, 6), (1, 512)),
                                          in_=dv(I["bdp"], 0, (512, 128), (128 * 512, 6), (1, 512))),
              writes=[rin], dma=True)
        P.add("sp", lambda e: e.dma_start(out=negtri.v(0, (1, 128)), in_=I["negtri"]), writes=[rin], dma=True)
        acc = sb("acc", 2048 + 64, F32)
        racc = Res()
        work = sb("work", 2048 + 64, F32)
        rwork = Res()
        rl = [sb("rl%d" % i, 512, F32) for i in range(2)]
        rrl = [Res(), Res()]
        mx = sb("mx", 8, F32)
        rmx = Res()
        msk = sb("msk", 2048 + 128, BF16)
        rmsk = Res()
        mskT = sb("mskT", 17 * 128, BF16)
        rmskT = Res()
        lg = [sb("lg%d" % i, 512, F32) for i in range(2)]
        rlg = [Res(), Res()]
        ex = [sb("ex%d" % i, 512, BF16) for i in range(2)]
        rex = [Res(), Res()]
        pT = [sb("pT%d" % i, 512, BF16) for i in range(2)]
        rpT = [Res(), Res()]
        rden = sb("rden", 512, F32)
        rrden = Res()
        aot = [sb("aot%d" % i, 8 * 128, BF16) for i in range(2)]
        raot = [Res(), Res()]
        cnt = {"e": 0, "l": 0}
        if "dsa_noprompt" not in C.dbg:
            for i in range(16):
                q0 = i * 128
                S_i = (i + 1) * 128
                npieces = (S_i + 511) // 512
                for pc in range(npieces):
                    k0 = pc * 512
                    kn = min(512, S_i - k0)
                    for j in range(8):
                        pbase = 64 * (j % 2)
                        bank = cnt["e"] % 2
                        s = cnt["e"] % 2
                        cnt["e"] += 1
                        P.add("pe", lambda e, j=j, pbase=pbase, bank=bank, k0=k0, kn=kn, q0=q0: e.matmul(
                            C.ps[bank].v(0, (1, kn)), qi.v((j // 2) * NT + q0, (1, 128), p0=pbase, np_=64),
                            ki2.v(k0, (1, kn), p0=pbase, np_=64), start=True, stop=True),
                            reads=[rin], writes=[C.rps[bank]])
                        P.add("act", lambda e, s=s, bank=bank, kn=kn: e.activation(
                            out=rl[s].v(0, (1, kn)), in_=C.ps[bank].v(0, (1, kn)), func=AF.Relu),
                            reads=[C.rps[bank]], writes=[rrl[s]])
                        if j == 0:
                            P.add("dve", lambda e, s=s, k0=k0, kn=kn, i=i, j=j: e.tensor_scalar(
                                acc.v(k0, (1, kn)), rl[s].v(0, (1, kn)), wi.v(i * 8 + j, (1, 1)), None, op0=ALU.mult),
                                reads=[rrl[s], rin], writes=[racc])
                        else:
                            P.add("dve", lambda e, s=s, k0=k0, kn=kn, i=i, j=j: e.scalar_tensor_tensor(
                                out=acc.v(k0, (1, kn)), in0=rl[s].v(0, (1, kn)), scalar=wi.v(i * 8 + j, (1, 1)),
                                in1=acc.v(k0, (1, kn)), op0=ALU.mult, op1=ALU.add),
                                reads=[rrl[s], rin, racc], writes=[racc])
                P.add("dve", lambda e, q0=q0: e.tensor_tensor(
                    out=acc.v(q0, (1, 128)), in0=acc.v(q0, (1, 128)), in1=negtri.v(0, (1, 128)), op=ALU.add),
                    reads=[racc, rin], writes=[racc])
                if i >= 2:
                    for r_ in range(32):
                        src = acc if r_ == 0 else work
                        rsrc = racc if r_ == 0 else rwork
                        P.add("dve", lambda e, src=src, S_i=S_i: e.max(out=mx.v(0, (1, 8)), in_=src.v(0, (1, S_i))),
                              reads=[rsrc], writes=[rmx])
                        if r_ < 31:
                            P.add("dve", lambda e, src=src, S_i=S_i: e.match_replace(
                                out=work.v(0, (1, S_i)), in_to_replace=mx.v(0, (1, 8)), in_values=src.v(0, (1, S_i)),
                                imm_value=NEGB), reads=[rsrc, rmx], writes=[rwork])
                    P.add("dve", lambda e, S_i=S_i: e.tensor_scalar(
                        msk.v(0, (1, S_i)), acc.v(0, (1, S_i)), mx.v(7, (1, 1)), None, op0=ALU.is_ge),
                        reads=[racc, rmx], writes=[rmsk])
                else:
                    P.add("dve", lambda e, S_i=S_i: e.tensor_scalar(
                        msk.v(0, (1, S_i)), acc.v(0, (1, S_i)), -1.0e29, None, op0=ALU.is_ge),
                        reads=[racc], writes=[rmsk])
                for kb in range(i + 1):
                    P.add("pe", lambda e, kb=kb: e.transpose(
                        psb[kb // 8].v((kb % 8) * 128, (1, 128)), msk.v(kb * 128, (1, 128)), C.identb.v(0, (1, 128))),
                        reads=[rmsk, C.rconst], writes=[C.rps[kb // 8]])
                for half in range((i + 8) // 8):
                    nb_ = min(8, i + 1 - half * 8)
                    P.add("act", lambda e, half=half, nb_=nb_: e.activation(
                        out=mskT.v(half * 1024, (1, nb_ * 128)), in_=psb[half].v(0, (1, nb_ * 128)), func=AF.Copy),
                        reads=[C.rps[half]], writes=[rmskT])
                ao_s = i % 2
                for g in range(2):
                    ob, db = (4, 5) if g == 0 else (6, 7)
                    for kb in range(i + 1):
                        kind = 0 if kb == i else (1 if kb == i - 1 else 2)
                        l = cnt["l"] % 2
                        cnt["l"] += 1
                        lb = 2 + l
                        P.add("pe", lambda e, g=g, kb=kb, lb=lb, q0=q0: e.matmul(
                            C.ps[lb].v(0, (128, 4), (1, 128)), ka.v(g * NT + kb * 128, (1, 128)),
                            qa.v(4 * g * NT + q0, (NT, 4), (1, 128)), start=True, stop=True),
                            reads=[rin], writes=[C.rps[lb]])
                        P.add("dve", lambda e, l=l, lb=lb, g=g, kind=kind: e.scalar_tensor_tensor(
                            out=lg[l].v(0, (1, 512)), in0=C.ps[lb].v(0, (1, 512)), scalar=ATT_SCALE,
                            in1=bdp.v((g * 3 + kind) * 512, (1, 512)), op0=ALU.mult, op1=ALU.add),
                            reads=[C.rps[lb], rin], writes=[rlg[l]])
                        P.add("act", lambda e, l=l: e.activation(out=ex[l].v(0, (1, 512)), in_=lg[l].v(0, (1, 512)),
                                                                 func=AF.Exp), reads=[rlg[l]], writes=[rex[l]])
                        P.add("dve", lambda e, l=l, kb=kb: e.tensor_tensor(
                            out=pT[l].v(0, (128, 4), (1, 128)), in0=ex[l].v(0, (128, 4), (1, 128)),
                            in1=mskT.v(kb * 128, (0, 4), (1, 128)), op=ALU.mult),
                            reads=[rex[l], rmskT], writes=[rpT[l]])
                        P.add("pe", lambda e, l=l, g=g, kb=kb, ob=ob, i=i: e.matmul(
                            C.ps[ob].v(0, (1, 512)), va.v(kb * 256 + g * 128, (1, 128)), pT[l].v(0, (1, 512)),
                            start=(kb == 0), stop=(kb == i)), reads=[rin, rpT[l]], writes=[C.rps[ob]])
                        P.add("pe", lambda e, l=l, kb=kb, db=db, i=i: e.matmul(
                            C.ps[db].v(0, (1, 512)), C.ones_b.v(0, (1, 128)), pT[l].v(0, (1, 512)),
                            start=(kb == 0), stop=(kb == i)), reads=[C.rconst, rpT[l]], writes=[C.rps[db]])
                    P.add("dve", lambda e, db=db: e.reciprocal(rden.v(0, (1, 512)), C.ps[db].v(0, (1, 512))),
                          reads=[C.rps[db]], writes=[rrden])
                    P.add("dve", lambda e, ob=ob, g=g, ao_s=ao_s: e.tensor_tensor(
                        out=aot[ao_s].v(4 * g * 128, (1, 512)), in0=C.ps[ob].v(0, (1, 512)), in1=rden.v(0, (1, 512)),
                        op=ALU.mult), reads=[C.rps[ob], rrden], writes=[raot[ao_s]])
                P.add("sp", lambda e, ao_s=ao_s, q0=q0: e.dma_start(
                    out=dv(S["MOAO"], 8 * NT + q0, (NKC * NT, 128), (NT, 8), (1, 128)),
                    in_=aot[ao_s].v(0, (128, 8), (1, 128))), reads=[raot[ao_s]], dma=True)
        P.barrier()


def phase_dsa_sample(C):
    P, I, S, O, nc = C.P, C.I, C.S, C.O, C.nc
    NK = 2048
    with contextlib.ExitStack() as es:
        def sb(name, width, dt):
            return SB(es.enter_context(nc.sbuf_tensor("dq_" + name, [128, padw(width, dt)], dt)), padw(width, dt), dt)
        psb = [SB(p.t.bitcast(BF16), 1024, BF16) for p in C.ps]
        ki2s = sb("ki2s", 64, BF16)
        kas = sb("kas", 2 * 64, BF16)
        qis = sb("qis", 4 * 64, BF16)
        qas = sb("qas", 8 * 64, BF16)
        wis = sb("wis", 16 * 8, F32)
        vns = sb("vns", 16 * 256, BF16)
        bsp = sb("bsp", 2 * 256, F32)
        bsn = sb("bsn", 2 * 16, F32)
        negtri = sb("negtri", 128, F32)
        ptr = sb("ptr", 256, I32)
        pidx = sb("pidx", 1, F32)
        idx = sb("idx", 256, U32)
        rin = Res()
        t0 = NP_TOK
        P.add("sp", lambda e: e.dma_start(out=ki2s.v(0, (1, 64)), in_=dv(S["KI2"], t0, (NT, 128), (1, 64))),
              writes=[rin], dma=True)
        P.add("sp", lambda e: e.dma_start(out=kas.v(0, (64, 2), (1, 64)),
                                          in_=dv(S["KA"], t0, (2 * NT, 128), (NT, 2), (1, 64))), writes=[rin], dma=True)
        P.add("sp", lambda e: e.dma_start(out=qis.v(0, (64, 4), (1, 64)),
                                          in_=dv(S["QI"], t0, (4 * NT, 128), (NT, 4), (1, 64))), writes=[rin], dma=True)
        P.add("sp", lambda e: e.dma_start(out=qas.v(0, (64, 8), (1, 64)),
                                          in_=dv(S["QA"], t0, (8 * NT, 128), (NT, 8), (1, 64))), writes=[rin], dma=True)
        P.add("sp", lambda e: e.dma_start(out=wis.v(0, (8, 16), (1, 8), np_=4),
                                          in_=dv(S["WI"], t0 * 8, (8, 4), (32, 16), (1, 8))), writes=[rin], dma=True)
        P.add("sp", lambda e: e.dma_start(out=vns.v(0, (256, 16), (1, 256), np_=4),
                                          in_=dv(S["VA"], t0 * 256, (256, 4), (1024, 16), (1, 256))),
              writes=[rin], dma=True)
        P.add("sp", lambda e: e.dma_start(out=bsp.v(0, (256, 2), (1, 256)),
                                          in_=dv(I["bsp"], 0, (256, 128), (128 * 256, 2), (1, 256))),
              writes=[rin], dma=True)
        P.add("sp", lambda e: e.dma_start(out=bsn.v(0, (16, 2), (1, 16), np_=4),
                                          in_=dv(I["bsn"], 0, (16, 4), (64, 2), (1, 16))), writes=[rin], dma=True)
        P.add("sp", lambda e: e.dma_start(out=negtri.v(0, (1, 128)), in_=I["negtri"]), writes=[rin], dma=True)
        P.add("sp", lambda e: e.dma_start(out=ptr.v(0, (1, 256)), in_=dv(I["ptab"], 0, (0, 128), (1, 256))),
              writes=[rin], dma=True)
        P.add("sp", lambda e: e.dma_start(out=pidx.v(0, (1, 1)), in_=I["pidx"]), writes=[rin], dma=True)
        ridx = Res()
        P.add("dve", lambda e: e.scalar_tensor_tensor(out=idx.v(0, (1, 256)), in0=ptr.v(0, (1, 256)), scalar=128.0,
                                                      in1=pidx.v(0, (0, 256)), op0=ALU.mult, op1=ALU.add),
              reads=[rin], writes=[ridx])
        kis = [sb("kis%d" % i, 16 * 128, F32) for i in range(2)]
        rkis = [Res(), Res()]
        kiT = sb("kiT", NK + 64, BF16)
        rkiT = Res()
        accb = [sb("accb%d" % i, NK + 64, F32) for i in range(2)]
        raccb = [Res(), Res()]
        rl = [sb("rl%d" % i, 512, F32) for i in range(2)]
        rrl = [Res(), Res()]
        accS = sb("accS", NK + 64, F32)
        raccS = Res()
        work = sb("work", NK + 64, F32)
        rwork = Res()
        mx = sb("mx", 8, F32)
        rmx = Res()
        msk = sb("msk", NK + 64, BF16)
        rmsk = Res()
        mskT = sb("mskT", 17 * 64, BF16)
        rmskT = Res()
        kcs = [sb("kcs%d" % i, 16 * 256, BF16) for i in range(2)]
        rkcs = [Res(), Res()]
        vcs = [sb("vcs%d" % i, 16 * 256, BF16) for i in range(2)]
        rvcs = [Res(), Res()]
        kTs = sb("kTs", 2 * NK, BF16)
        rkTs = Res()
        lgS = sb("lgS", 256 + 16, F32)
        rlgS = Res()
        lgn = sb("lgn", 16, F32)
        rlgn = Res()
        exS = sb("exS", 256, BF16)
        rexS = Res()
        exn = sb("exn", 16, BF16)
        rexn = Res()
        pTs = sb("pTs", 256, BF16)
        rpTs = Res()
        pTn = sb("pTn", 16, BF16)
        rpTn = Res()
        rden = sb("rden", 16, F32)
        rrden = Res()
        aoS = sb("aoS", 8 * 64, BF16)
        raoS = Res()
        cnt = {"e": 0}
        IO = bass.IndirectOffsetOnAxis
        for b in range(NSB):
            ks = b % 2
            for pg in range(16):
                P.add("pool", lambda e, ks=ks, pg=pg, b=b: e.indirect_dma_start(
                    out=kis[ks].v(pg * 128, (1, 64)), out_offset=None, in_=I["cidx"],
                    in_offset=IO(ap=idx.v(b * 16 + pg, (1, 1)), axis=0)), reads=[ridx], writes=[rkis[ks]], dma=True)
            P.add("act", lambda e, ks=ks: e.activation(out=kis[ks].v(64, (128, 16), (1, 64)),
                                                       in_=kis[ks].v(0, (128, 16), (1, 64)), func=AF.Copy),
                  reads=[rkis[ks]], writes=[rkis[ks]])
            for q4 in range(4):
                bank = 2 + q4 % 2
                for j in range(4):
                    pg = q4 * 4 + j
                    P.add("pe", lambda e, ks=ks, pg=pg, j=j, bank=bank: e.transpose(
                        C.ps[bank].v(j * 128, (1, 128)), kis[ks].v(pg * 128, (1, 128)), C.ident.v(0, (1, 128))),
                        reads=[rkis[ks], C.rconst], writes=[C.rps[bank]])
                P.add("act", lambda e, q4=q4, bank=bank: e.activation(
                    out=kiT.v(q4 * 512, (1, 512)), in_=C.ps[bank].v(0, (1, 512)), func=AF.Copy),
                    reads=[C.rps[bank]], writes=[rkiT])
            P.add("dve", lambda e, b=b: e.tensor_copy(kiT.v(NK, (1, 4)), ki2s.v(4 * b, (1, 4))),
                  reads=[rin, rkiT], writes=[rkiT])
            ab = b % 2
            for (k0, kn) in [(0, 512), (512, 512), (1024, 512), (1536, 512), (NK, 4)]:
                for j in range(8):
                    pbase = 64 * (j % 2)
                    bank = cnt["e"] % 2
                    s = cnt["e"] % 2
                    cnt["e"] += 1
                    P.add("pe", lambda e, j=j, pbase=pbase, bank=bank, k0=k0, kn=kn, b=b: e.matmul(
                        C.ps[bank].v(0, (1, kn), np_=4), qis.v((j // 2) * 64 + 4 * b, (1, 4), p0=pbase, np_=64),
                        kiT.v(k0, (1, kn), p0=pbase, np_=64), start=True, stop=True),
                        reads=[rin, rkiT], writes=[C.rps[bank]])
                    P.add("act", lambda e, s=s, bank=bank, kn=kn: e.activation(
                        out=rl[s].v(0, (1, kn), np_=4), in_=C.ps[bank].v(0, (1, kn), np_=4), func=AF.Relu),
                        reads=[C.rps[bank]], writes=[rrl[s]])
                    if j == 0:
                        P.add("dve", lambda e, s=s, k0=k0, kn=kn, b=b, j=j, ab=ab: e.tensor_scalar(
                            accb[ab].v(k0, (1, kn), np_=4), rl[s].v(0, (1, kn), np_=4),
                            wis.v(b * 8 + j, (1, 1), np_=4), None, op0=ALU.mult),
                            reads=[rrl[s], rin], writes=[raccb[ab]])
                    else:
                        P.add("dve", lambda e, s=s, k0=k0, kn=kn, b=b, j=j, ab=ab: e.scalar_tensor_tensor(
                            out=accb[ab].v(k0, (1, kn), np_=4), in0=rl[s].v(0, (1, kn), np_=4),
                            scalar=wis.v(b * 8 + j, (1, 1), np_=4), in1=accb[ab].v(k0, (1, kn), np_=4),
                            op0=ALU.mult, op1=ALU.add), reads=[rrl[s], rin, raccb[ab]], writes=[raccb[ab]])
            P.add("dve", lambda e, ab=ab: e.tensor_tensor(
                out=accb[ab].v(NK, (1, 4), np_=4), in0=accb[ab].v(NK, (1, 4), np_=4), in1=negtri.v(0, (1, 4), np_=4),
                op=ALU.add), reads=[raccb[ab], rin], writes=[raccb[ab]])
            P.add("sp", lambda e, ab=ab, b=b: e.dma_start(out=accS.v(0, (1, NK + 4), p0=4 * b, np_=4),
                                                          in_=accb[ab].v(0, (1, NK + 4), np_=4)),
                  reads=[raccb[ab]], writes=[raccS], dma=True)
        SS = NK + 4
        for r_ in range(32):
            src = accS if r_ == 0 else work
            rsrc = raccS if r_ == 0 else rwork
            P.add("dve", lambda e, src=src: e.max(out=mx.v(0, (1, 8), np_=64), in_=src.v(0, (1, SS), np_=64)),
                  reads=[rsrc], writes=[rmx])
            if r_ < 31:
                P.add("dve", lambda e, src=src: e.match_replace(
                    out=work.v(0, (1, SS), np_=64), in_to_replace=mx.v(0, (1, 8), np_=64),
                    in_values=src.v(0, (1, SS), np_=64), imm_value=NEGB), reads=[rsrc, rmx], writes=[rwork])
        P.add("dve", lambda e: e.tensor_scalar(msk.v(0, (1, SS), np_=64), accS.v(0, (1, SS), np_=64),
                                               mx.v(7, (1, 1), np_=64), None, op0=ALU.is_ge),
              reads=[raccS, rmx], writes=[rmsk])
        for kb in range(16):
            P.add("pe", lambda e, kb=kb: e.transpose(
                psb[kb // 8].v((kb % 8) * 64, (1, 64)), msk.v(kb * 128, (1, 128), np_=64),
                C.identb.v(0, (1, 64), np_=64)), reads=[rmsk, C.rconst], writes=[C.rps[kb // 8]])
        for half in range(2):
            P.add("act", lambda e, half=half: e.activation(
                out=mskT.v(half * 512, (1, 512)), in_=psb[half].v(0, (1, 512)), func=AF.Copy),
                reads=[C.rps[half]], writes=[rmskT])
        P.add("pe", lambda e: e.transpose(psb[2].v(0, (1, 64), np_=4), msk.v(NK, (1, 4), np_=64),
                                          C.identb.v(0, (1, 64), np_=64)), reads=[rmsk, C.rconst], writes=[C.rps[2]])
        P.add("act", lambda e: e.activation(out=mskT.v(16 * 64, (1, 64), np_=4), in_=psb[2].v(0, (1, 64), np_=4),
                                            func=AF.Copy), reads=[C.rps[2]], writes=[rmskT])
        for b in range(NSB):
            cs = b % 2
            for pg in range(16):
                P.add("pool", lambda e, cs=cs, pg=pg, b=b: e.indirect_dma_start(
                    out=kcs[cs].v(pg * 256, (1, 256)), out_offset=None, in_=I["ck"],
                    in_offset=IO(ap=idx.v(b * 16 + pg, (1, 1)), axis=0)), reads=[ridx], writes=[rkcs[cs]], dma=True)
                P.add("pool", lambda e, cs=cs, pg=pg, b=b: e.indirect_dma_start(
                    out=vcs[cs].v(pg * 256, (1, 256)), out_offset=None, in_=I["cv"],
                    in_offset=IO(ap=idx.v(b * 16 + pg, (1, 1)), axis=0)), reads=[ridx], writes=[rvcs[cs]], dma=True)
            for g in range(2):
                for half in range(2):
                    bank = 2 + half
                    for j in range(8):
                        pg = half * 8 + j
                        P.add("pe", lambda e, cs=cs, pg=pg, j=j, g=g, bank=bank: e.transpose(
                            psb[bank].v(j * 128, (1, 128)), kcs[cs].v(pg * 256 + g * 128, (1, 128)),
                            C.identb.v(0, (1, 128))), reads=[rkcs[cs], C.rconst], writes=[C.rps[bank]])
                    P.add("act" if half == 0 else "dve",
                          (lambda e, g=g, half=half, bank=bank: e.activation(
                              out=kTs.v(g * NK + half * 1024, (1, 1024)), in_=psb[bank].v(0, (1, 1024)), func=AF.Copy))
                          if half == 0 else
                          (lambda e, g=g, half=half, bank=bank: e.tensor_copy(
                              kTs.v(g * NK + half * 1024, (1, 1024)), psb[bank].v(0, (1, 1024)))),
                          reads=[C.rps[bank]], writes=[rkTs])
                lb = 4 + g
                for kb in range(16):
                    P.add("pe", lambda e, g=g, kb=kb, lb=lb, b=b: e.matmul(
                        C.ps[lb].v(kb * 16, (4, 4), (1, 4)), kTs.v(g * NK + kb * 128, (1, 128)),
                        qas.v(4 * g * 64 + 4 * b, (64, 4), (1, 4)), start=True, stop=True),
                        reads=[rkTs, rin], writes=[C.rps[lb]])
                P.add("pe", lambda e, g=g, lb=lb, b=b: e.matmul(
                    C.ps[lb].v(256, (4, 4), (1, 4), np_=4), kas.v(g * 64 + 4 * b, (1, 4)),
                    qas.v(4 * g * 64 + 4 * b, (64, 4), (1, 4)), start=True, stop=True),
                    reads=[rin], writes=[C.rps[lb]])
                P.add("dve", lambda e, g=g, lb=lb: e.scalar_tensor_tensor(
                    out=lgS.v(0, (1, 256)), in0=C.ps[lb].v(0, (1, 256)), scalar=ATT_SCALE,
                    in1=bsp.v(g * 256, (1, 256)), op0=ALU.mult, op1=ALU.add),
                    reads=[C.rps[lb], rin], writes=[rlgS])
                P.add("dve", lambda e, g=g, lb=lb: e.scalar_tensor_tensor(
                    out=lgn.v(0, (1, 16), np_=4), in0=C.ps[lb].v(256, (1, 16), np_=4), scalar=ATT_SCALE,
                    in1=bsn.v(g * 16, (1, 16), np_=4), op0=ALU.mult, op1=ALU.add),
                    reads=[C.rps[lb], rin], writes=[rlgn])
                P.add("act", lambda e: e.activation(out=exS.v(0, (1, 256)), in_=lgS.v(0, (1, 256)), func=AF.Exp),
                      reads=[rlgS], writes=[rexS])
                P.add("act", lambda e: e.activation(out=exn.v(0, (1, 16), np_=4), in_=lgn.v(0, (1, 16), np_=4),
                                                    func=AF.Exp), reads=[rlgn], writes=[rexn])
                P.add("dve", lambda e, b=b: e.tensor_tensor(
                    out=pTs.v(0, (16, 16), (4, 4), (1, 4)), in0=exS.v(0, (16, 16), (4, 4), (1, 4)),
                    in1=mskT.v(4 * b, (64, 16), (0, 4), (1, 4)), op=ALU.mult), reads=[rexS, rmskT], writes=[rpTs])
                P.add("dve", lambda e, b=b: e.tensor_tensor(
                    out=pTn.v(0, (4, 4), (1, 4), np_=4), in0=exn.v(0, (4, 4), (1, 4), np_=4),
                    in1=mskT.v(16 * 64 + 4 * b, (0, 4), (1, 4), np_=4), op=ALU.mult),
                    reads=[rexn, rmskT], writes=[rpTn])
                for kb in range(16):
                    P.add("pe", lambda e, cs=cs, g=g, kb=kb: e.matmul(
                        C.ps[6].v(0, (1, 16)), vcs[cs].v(kb * 256 + g * 128, (1, 128)), pTs.v(kb * 16, (1, 16)),
                        start=(kb == 0), stop=False), reads=[rvcs[cs], rpTs], writes=[C.rps[6]])
                P.add("pe", lambda e, g=g, b=b: e.matmul(
                    C.ps[6].v(0, (1, 16)), vns.v(b * 256 + g * 128, (1, 128), np_=4), pTn.v(0, (1, 16), np_=4),
                    start=False, stop=True), reads=[rin, rpTn], writes=[C.rps[6]])
                for kb in range(16):
                    P.add("pe", lambda e, kb=kb: e.matmul(
                        C.ps[7].v(0, (1, 16)), C.ones_b.v(0, (1, 128)), pTs.v(kb * 16, (1, 16)),
                        start=(kb == 0), stop=False), reads=[C.rconst, rpTs], writes=[C.rps[7]])
                P.add("pe", lambda e: e.matmul(
                    C.ps[7].v(0, (1, 16)), C.ones_b.v(0, (1, 128), np_=4), pTn.v(0, (1, 16), np_=4),
                    start=False, stop=True), reads=[C.rconst, rpTn], writes=[C.rps[7]])
                P.add("dve", lambda e: e.reciprocal(rden.v(0, (1, 16)), C.ps[7].v(0, (1, 16))),
                      reads=[C.rps[7]], writes=[rrden])
                P.add("dve", lambda e, g=g, b=b: e.tensor_tensor(
                    out=aoS.v(4 * g * 64 + 4 * b, (64, 4), (1, 4)), in0=C.ps[6].v(0, (4, 4), (1, 4)),
                    in1=rden.v(0, (4, 4), (1, 4)), op=ALU.mult), reads=[C.rps[6], rrden], writes=[raoS])
        P.add("sp", lambda e: e.dma_start(out=dv(S["MOAO"], 8 * NT + NP_TOK, (NKC * NT, 128), (NT, 8), (1, 64)),
                                          in_=aoS.v(0, (64, 8), (1, 64))), reads=[raoS], dma=True)
        P.barrier()


OUT_ORDER = ["y_p", "y_s", "k_p", "v_p", "idxk_p", "C_p", "n_p", "m_p", "conv_p",
             "k_s", "v_s", "idxk_s", "C_s", "n_s", "m_s", "conv_s"]


def kernel(**inputs):
    inp = {k: np.asarray(v) for k, v in inputs.items()}
    nc = build_program()
    sh = prep_shared(inp)
    in_maps = []
    for core in range(8):
        m = prep_core(inp, core, sh)
        in_maps.append({k: np.ascontiguousarray(v) for k, v in m.items()})
    res = run_bass_kernel_spmd(nc, in_maps, core_ids=list(range(8)))
    R = res.results

    def cat(name, shape_p):
        return np.stack([np.asarray(R[c][name], dtype=np.float32).reshape(shape_p) for c in range(8)], 0)

    y_p = cat("y_p", (2048, 2048))
    y_s = cat("y_s", (16, 4, 2048)).reshape(128, 4, 2048)
    k_p = cat("k_p", (2048, 2, 128))[None]
    v_p = cat("v_p", (2048, 2, 128))[None]
    i_p = cat("idxk_p", (2048, 64))[None]
    C_p = cat("C_p", (4, 256, 256))[None]
    n_p = cat("n_p", (4, 256))[None]
    m_p = cat("m_p", (4,))[None]
    cv_p = cat("conv_p", (3, 2048))[None]
    k_s = cat("k_s", (16, 4, 2, 128)).reshape(128, 4, 2, 128)[None]
    v_s = cat("v_s", (16, 4, 2, 128)).reshape(128, 4, 2, 128)[None]
    i_s = cat("idxk_s", (16, 4, 64)).reshape(128, 4, 64)[None]
    C_s = cat("C_s", (16, 4, 256, 256)).reshape(128, 4, 256, 256)[None]
    n_s = cat("n_s", (16, 4, 256)).reshape(128, 4, 256)[None]
    m_s = cat("m_s", (16, 4)).reshape(128, 4)[None]
    cv_s = cat("conv_s", (16, 3, 2048)).reshape(128, 3, 2048)[None]
    return (y_p, y_s, k_p, v_p, i_p, C_p, n_p, m_p, cv_p, k_s, v_s, i_s, C_s, n_s, m_s, cv_s)
```

```python
import contextlib
import types
import numpy as np
import concourse.bass as bass
import concourse.mybir as mybir
from concourse.bass_utils import run_bass_kernel_spmd

F32 = mybir.dt.float32
BF16 = mybir.dt.bfloat16
I32 = mybir.dt.int32
U32 = mybir.dt.uint32
AF = mybir.ActivationFunctionType
ALU = mybir.AluOpType
AX = mybir.AxisListType


class Res:
    __slots__ = ("name", "w", "rs", "rd")

    def __init__(self, name=""):
        self.name = name
        self.w = None
        self.rs = {}
        self.rd = []


def _freeze(fn):
    cl = fn.__closure__
    if not cl:
        return fn
    cells = []
    for c in cl:
        try:
            cells.append(types.CellType(c.cell_contents))
        except ValueError:
            cells.append(c)
    g = types.FunctionType(fn.__code__, fn.__globals__, fn.__name__, fn.__defaults__, tuple(cells))
    g.__kwdefaults__ = fn.__kwdefaults__
    return g


class Op:
    __slots__ = ("eng", "fn", "dma", "n", "deps", "need_inc", "sem", "val", "prev_val")

    def __init__(self, eng, fn, dma, n):
        self.eng = eng
        self.fn = fn
        self.dma = dma
        self.n = n
        self.deps = []
        self.need_inc = False
        self.sem = None
        self.val = 0
        self.prev_val = 0


ENGS = ("sp", "act", "dve", "pool", "pe")
NDSEM = 20


class Prog:
    def __init__(self, nc):
        self.nc = nc
        self.ops = []
        self.last = {}
        self.pend_dma = []

    def add(self, eng, fn, reads=(), writes=(), dma=False, n=1):
        op = Op(eng, _freeze(fn), dma, n)
        deps = []
        for r in reads:
            if r.w is not None:
                deps.append((r.w, 0))
        for w in writes:
            if w.w is not None:
                deps.append((w.w, 1))
            for o in w.rs.values():
                deps.append((o, 2))
            for o in w.rd:
                deps.append((o, 2))
        for r in reads:
            if dma:
                r.rd.append(op)
            else:
                r.rs[eng] = op
        for w in writes:
            w.w = op
            w.rs = {}
            w.rd = []
        seen = set()
        for p, kind in deps:
            if p is op or id(p) in seen:
                continue
            if p.eng == eng and not p.dma and not dma:
                if eng == "pe" or kind != 0:
                    continue
            seen.add(id(p))
            op.deps.append(p)
            p.need_inc = True
        self.ops.append(op)
        if dma:
            self.pend_dma.append(op)
        else:
            self.last[eng] = op
        return op

    def barrier(self):
        lasts = dict(self.last)
        dmas = list(self.pend_dma)
        self.pend_dma = []
        for e in ENGS:
            op = Op(e, lambda eng: eng.nop(), False, 1)
            for x, p in lasts.items():
                if x != e:
                    op.deps.append(p)
                    p.need_inc = True
            for p in dmas:
                op.deps.append(p)
            self.ops.append(op)
            self.last[e] = op

    def emit(self):
        nc = self.nc
        with contextlib.ExitStack() as es:
            csem = {e: es.enter_context(nc.semaphore("c_" + e)) for e in ENGS}
            dsem = {e: [es.enter_context(nc.semaphore("d_%s%d" % (e, i))) for i in range(NDSEM)]
                    for e in ("sp", "act", "pool")}
            cnt = {e: 0 for e in ENGS}
            dcount = {e: 0 for e in dsem}
            dval = {e: [0] * NDSEM for e in dsem}
            for op in self.ops:
                if op.dma:
                    q = op.eng
                    slot = dcount[q] % NDSEM
                    dcount[q] += 1
                    op.sem = dsem[q][slot]
                    op.prev_val = dval[q][slot]
                    dval[q][slot] += 16 * op.n
                    op.val = dval[q][slot]
                elif op.need_inc:
                    cnt[op.eng] += 1
                    op.val = cnt[op.eng]
                    op.sem = csem[op.eng]
            per = {e: [o for o in self.ops if o.eng == e] for e in ENGS}

            def run(ename, e):
                waited = {}

                def wait(sem, val):
                    k = id(sem)
                    if waited.get(k, 0) < val:
                        e.wait_ge(sem, val)
                        waited[k] = val

                for op in per[ename]:
                    for p in op.deps:
                        wait(p.sem, p.val)
                    if op.dma:
                        if op.prev_val > 0:
                            wait(op.sem, op.prev_val)
                        ins = op.fn(e)
                        if not isinstance(ins, (list, tuple)):
                            ins = [ins]
                        assert len(ins) == op.n, (len(ins), op.n)
                        for i in ins:
                            i.then_inc(op.sem, 16)
                    else:
                        ins = op.fn(e)
                        if op.need_inc:
                            ins.then_inc(op.sem, 1)
                if ename in dsem:
                    for s, v in zip(dsem[ename], dval[ename]):
                        if v > 0:
                            wait(s, v)

            with nc.allow_non_contiguous_dma(reason="small strided state / index transfers"), nc.Block() as block:
                @block.sync
                def _(e):
                    run("sp", e)

                @block.scalar
                def _(e):
                    run("act", e)

                @block.vector
                def _(e):
                    run("dve", e)

                @block.gpsimd
                def _(e):
                    run("pool", e)

                @block.tensor
                def _(e):
                    run("pe", e)


D = 2048
NKC = 16
DFF = 5632
NFG = 22
NP_TOK = 2048
NS_TOK = 64
NT = NP_TOK + NS_TOK
NSB = 16
PW = 6224
EPS = 1e-6
TILES = [(0, 1024), (1024, 1088)]
TS = 1088


def subs_of(t0):
    if t0 == 0:
        return [(0, 512, False), (512, 512, False)]
    return [(0, 512, False), (512, 512, False), (1024, 64, True)]


def padw(width, dt):
    per = 64 // mybir.dt.size(dt)
    return ((width + per - 1) // per) * per


class SB:
    def __init__(self, t, width, dt):
        self.t = t
        self.W = width
        self.dt = dt

    def v(self, off, *dims, p0=0, np_=128):
        return bass.AP(self.t, p0 * self.W + off, [[self.W, np_]] + [list(d) for d in dims])


class SBV(SB):
    def __init__(self, base, off0):
        self.t = base.t
        self.W = base.W
        self.dt = base.dt
        self.off0 = off0

    def v(self, off, *dims, p0=0, np_=128):
        return bass.AP(self.t, p0 * self.W + self.off0 + off, [[self.W, np_]] + [list(d) for d in dims])


def dv(ap, off, *dims):
    return bass.AP(ap.tensor, off, [list(d) for d in dims])


class Ctx:
    def dump(self, name, sbt, reads):
        if ("dump_" + name) not in self.dbg:
            return
        o = self.nc.dram_tensor("dump_" + name, [128, sbt.W], sbt.dt, kind="ExternalOutput").ap()
        self.P.add("sp", lambda e: e.dma_start(out=o, in_=sbt.v(0, (1, sbt.W))), reads=list(reads), dma=True)


def build_program(dbg=None):
    dbg = dbg or set()
    nc = bass.Bass("TRN2", target_bir_lowering=False)
    P = Prog(nc)
    C = Ctx()
    C.nc = nc
    C.P = P
    C.dbg = dbg

    def din(name, shape, dt=F32):
        return nc.dram_tensor(name, list(shape), dt, kind="ExternalInput").ap()

    def dout(name, shape, dt=F32):
        return nc.dram_tensor(name, list(shape), dt, kind="ExternalOutput").ap()

    def dscr(name, shape, dt=F32):
        return nc.dram_tensor(name, list(shape), dt, kind="Internal").ap()

    I = {}
    I["xp"] = din("xp", [NP_TOK, D])
    I["xs"] = din("xs", [NS_TOK, D])
    I["c17"] = din("c17", [17, D])
    I["ident"] = din("ident", [128, 128])
    for nm in ("ffn1", "ffn2"):
        I[nm + "_g"] = din(nm + "_norm_g", [NKC, 128])
        I[nm + "_wg"] = din(nm + "_w_gate", [D, DFF])
        I[nm + "_wu"] = din(nm + "_w_up", [D, DFF])
        I[nm + "_wd"] = din(nm + "_w_down", [DFF, D])
    I["mix_g"] = din("mix_norm_g", [NKC, 128])
    I["w_ada"] = din("w_ada", [D, 9 * D])
    I["b_ada"] = din("b_ada", [144, 128])
    I["w_in"] = din("w_in", [D, PW])
    I["w_out"] = din("w_out", [D, D])
    I["gate_b"] = din("gate_b", [8, 1])
    I["qng"] = din("qng", [128, 1])
    I["kng"] = din("kng", [128, 1])
    I["conv_w"] = din("conv_w", [64, 128])
    I["out_g"] = din("out_g", [8, 128])
    I["tri"] = din("tri", [64, 64])
    I["sel8"] = din("sel8", [8, 1024])
    I["mk64"] = din("mk64", [1, 512])
    I["mks"] = din("mks", [1, 192])
    I["st_C"] = din("st_C", [NSB, 4, 256, 256])
    I["st_n"] = din("st_n", [NSB, 4, 256])
    I["st_m"] = din("st_m", [NSB, 4])
    I["st_conv"] = din("st_conv", [NSB * 3, D])
    I["bdp"] = din("bdp", [6, 128, 512])
    I["bsp"] = din("bsp", [2, 128, 256])
    I["bsn"] = din("bsn", [2, 4, 16])
    I["negtri"] = din("negtri", [128, 128])
    I["ptab"] = din("ptab", [1, 256], I32)
    I["pidx"] = din("pidx", [128, 1])
    NPOOL = 2560
    I["ck"] = din("ck", [NPOOL * 128, 256])
    I["cv"] = din("cv", [NPOOL * 128, 256])
    I["cidx"] = din("cidx", [NPOOL * 128, 64])
    C.I = I
    O = {}
    C.O = O
    S = {}
    C.S = S
    S["X1"] = din("X1", [128, NKC, NT]) if "noffn" in dbg else dscr("X1", [128, NKC, NT])
    S["QK"] = dscr("QK", [128, NKC, NT])
    S["OM"] = dscr("OM", [128, 8, NT], BF16)
    S["QI"] = dscr("QI", [128, 4, NT], BF16)
    S["QA"] = dscr("QA", [128, 8, NT], BF16)
    S["KA"] = dscr("KA", [128, 2, NT], BF16)
    S["KI2"] = dscr("KI2", [128, NT], BF16)
    S["GT"] = dscr("GT", [8, NT])
    S["VM"] = dscr("VM", [NT, 1028], BF16)
    S["VA"] = dscr("VA", [NT, 256], BF16)
    S["WI"] = dscr("WI", [NT, 8])
    O["k_p"] = dout("k_p", [NP_TOK, 256])
    O["v_p"] = dout("v_p", [NP_TOK, 256])
    O["idxk_p"] = dout("idxk_p", [NP_TOK, 64])
    O["k_s"] = dout("k_s", [NS_TOK, 256])
    O["v_s"] = dout("v_s", [NS_TOK, 256])
    O["idxk_s"] = dout("idxk_s", [NS_TOK, 64])
    O["conv_p"] = dout("conv_p", [3, D])
    O["conv_s"] = dout("conv_s", [NSB, 3, D])
    O["y_p"] = dout("y_p", [NP_TOK, D])
    O["y_s"] = dout("y_s", [NS_TOK, D])
    O["C_p"] = dout("C_p", [4, 256, 256])
    O["n_p"] = dout("n_p", [4, 256])
    O["m_p"] = dout("m_p", [1, 4])
    O["C_s"] = dout("C_s", [NSB, 4, 256, 256])
    O["n_s"] = dout("n_s", [NSB, 4, 256])
    O["m_s"] = dout("m_s", [NSB, 4])
    S["MOAO"] = dscr("MOAO", [128, NKC, NT], BF16)
    if "MOAO" in dbg:
        O["dbg_MOAO"] = dout("dbg_MOAO", [128, NKC, NT], BF16)
    DBG_SCR = {"QK": F32, "OM": BF16, "QI": BF16, "QA": BF16, "KA": BF16, "KI2": BF16, "GT": F32,
               "VM": BF16, "VA": BF16, "WI": F32}
    for k_, dt_ in DBG_SCR.items():
        if k_ in dbg:
            O["dbg_" + k_] = dout("dbg_" + k_, list(S[k_].shape), dt_)
    if "X1" in dbg:
        O["dbg_X1"] = dout("dbg_X1", [128, NKC, NT])
    if "mods" in dbg:
        O["dbg_mods"] = dout("dbg_mods", [128, 144 * 17])

    with contextlib.ExitStack() as es:
        def sb(name, width, dt):
            return SB(es.enter_context(nc.sbuf_tensor("s_" + name, [128, padw(width, dt)], dt)), padw(width, dt), dt)

        C.sb = sb
        C.ps = [SB(es.enter_context(nc.psum_tensor("ps%d" % i, [128, 512], F32)), 512, F32) for i in range(8)]
        C.rps = [Res("ps%d" % i) for i in range(8)]
        C.ident = sb("ident", 128, F32)
        C.identb = sb("identb", 128, BF16)
        C.ones_b = sb("ones_b", 128, BF16)
        C.rconst = Res("const")
        P.add("sp", lambda e: e.dma_start(out=C.ident.v(0, (1, 128)), in_=I["ident"]), writes=[C.rconst], dma=True)
        P.add("dve", lambda e: e.tensor_copy(C.identb.v(0, (1, 128)), C.ident.v(0, (1, 128))),
              reads=[C.rconst], writes=[C.rconst])
        P.add("dve", lambda e: e.memset(C.ones_b.v(0, (1, 128)), 1.0), writes=[C.rconst])
        C.oneb = sb("oneb", 1, F32)
        P.add("dve", lambda e: e.memset(C.oneb.v(0, (1, 1)), 1.0), writes=[C.rconst])
        C.epsb = sb("epsb", 1, F32)
        P.add("dve", lambda e: e.memset(C.epsb.v(0, (1, 1)), EPS), writes=[C.rconst])
        C.mods = sb("mods", 144 * 17, F32)
        C.rmods = Res("mods")
        C.gn = {}
        for k in ("ffn1_g", "mix_g", "ffn2_g"):
            C.gn[k] = sb("gn_" + k, NKC, F32)
        C.rgn = Res("gn")

        phase_mods(C)
        if "mods" in dbg:
            P.add("sp", lambda e: e.dma_start(out=O["dbg_mods"], in_=C.mods.v(0, (1, 144 * 17))),
                  reads=[C.rmods], dma=True)
        rX1 = [Res("X1_%d" % i) for i in range(len(TILES))]
        C.rX1 = rX1

        def load_x_in(ti, xt, rxs):
            load_x_tokenmajor(C, ti, xt, rxs)

        def store_x1(ti, xt, rxs):
            t0, T = TILES[ti]
            P.add("sp", lambda e: e.dma_start(out=dv(S["X1"], t0, (NKC * NT, 128), (NT, NKC), (1, T)),
                                              in_=xt.v(0, (TS, NKC), (1, T))),
                  reads=[r for row in rxs for r in row], writes=[rX1[ti]], dma=True)

        if "noffn" not in dbg:
            phase_ffn(C, "ffn1", 0, load_x_in, store_x1)
        if "X1" in dbg:
            for ti, (t0, T) in enumerate(TILES):
                pass
            P.add("sp", lambda e: e.dma_start(out=O["dbg_X1"], in_=S["X1"]), reads=rX1, dma=True)
        if "noproj" not in dbg:
            phase_proj(C)
        for k_ in DBG_SCR:
            if k_ in dbg:
                P.add("sp", lambda e, k_=k_: e.dma_start(out=O["dbg_" + k_], in_=S[k_]), dma=True)
        if "nomlstm" not in dbg:
            phase_mlstm(C)
        if "nodsa" not in dbg:
            phase_dsa(C)
            if "dsa_nosample" not in dbg:
                phase_dsa_sample(C)
        if "MOAO" in dbg:
            P.add("sp", lambda e: e.dma_start(out=O["dbg_MOAO"], in_=S["MOAO"]), dma=True)

        def load_x1(ti, xt, rxs):
            t0, T = TILES[ti]
            P.add("sp", lambda e: e.dma_start(out=xt.v(0, (TS, NKC), (1, T)),
                                              in_=dv(S["X1"], t0, (NKC * NT, 128), (NT, NKC), (1, T))),
                  reads=[rX1[ti]], writes=[r for row in rxs for r in row], dma=True)

        def store_y(ti, xt, rxs):
            t0, T = TILES[ti]
            stage, rstage = C.xstage, C.rxstage
            blocks = []
            for si, (o, n, samp) in enumerate(subs_of(t0)):
                if samp:
                    blocks.append((o, 64, O["y_s"], 0, si))
                else:
                    for b in range(n // 128):
                        blocks.append((o + b * 128, 128, O["y_p"], (t0 + o + b * 128) * D, si))
            for (o, n, dst, doff, si) in blocks:
                for half in range(2):
                    for q in range(2):
                        bank = q + 2 * half
                        for j in range(4):
                            c = half * 8 + q * 4 + j
                            P.add("pe", lambda e, n=n, o=o, c=c, j=j, bank=bank: e.transpose(
                                C.ps[bank].v(j * 128, (1, 128), np_=n), xt.v(c * TS + o, (1, n)),
                                C.ident.v(0, (1, 128))), reads=[rxs[c][si], C.rconst], writes=[C.rps[bank]])
                        P.add("act", lambda e, half=half, q=q, n=n, bank=bank: e.activation(
                            out=stage[half].v(q * 512, (1, 512), np_=n), in_=C.ps[bank].v(0, (1, 512), np_=n),
                            func=AF.Copy), reads=[C.rps[bank]], writes=[rstage[half]])
                    P.add("sp", lambda e, half=half, n=n, dst=dst, doff=doff: e.dma_start(
                        out=dv(dst, doff + half * 1024, (D, n), (1, 1024)), in_=stage[half].v(0, (1, 1024), np_=n)),
                        reads=[rstage[half]], dma=True)

        if "noffn2" not in dbg:
            phase_ffn(C, "ffn2", 2, load_x1, store_y)
        P.emit()
    return nc


def load_featmajor_small(C, dst, src_ap, nrows, rdst, tmpname):
    P = C.P
    tmp = C.sb(tmpname, 128, F32)
    done = 0
    blk = 0
    rt = Res()
    while done < nrows:
        n = min(128, nrows - done)
        P.add("sp", lambda e, done=done, n=n: e.dma_start(out=tmp.v(0, (1, 128), np_=n),
                                                          in_=dv(src_ap, done * 128, (128, n), (1, 128))),
              writes=[rt], dma=True)
        bank = 7
        P.add("pe", lambda e, n=n: e.transpose(C.ps[bank].v(0, (1, n)), tmp.v(0, (1, 128), np_=n),
                                               C.ident.v(0, (1, n), np_=n)),
              reads=[rt, C.rconst], writes=[C.rps[bank]])
        P.add("dve", lambda e, done=done, n=n: e.tensor_copy(dst.v(done, (1, n)), C.ps[bank].v(0, (1, n))),
              reads=[C.rps[bank]], writes=[rdst, rt])
        done += n
        blk += 1


def phase_mods(C):
    P, I, nc = C.P, C.I, C.nc
    with contextlib.ExitStack() as es:
        def sb(name, width, dt):
            return SB(es.enter_context(nc.sbuf_tensor("s_" + name, [128, padw(width, dt)], dt)), padw(width, dt), dt)
        sbo = C.sb
        C.sb = sb
        for k in ("ffn1_g", "mix_g", "ffn2_g"):
            load_featmajor_small(C, C.gn[k], I[k], NKC, C.rgn, "tmp_" + k)
        bada = sb("bada", 144, F32)
        rb = Res("bada")
        load_featmajor_small(C, bada, I["b_ada"], 144, rb, "tmp_bada")
        c_tm = sb("c_tm", D, F32)
        rc = Res()
        P.add("sp", lambda e: e.dma_start(out=c_tm.v(0, (1, D), np_=17), in_=I["c17"]), writes=[rc], dma=True)
        cT = sb("cT", NKC * 17, BF16)
        rcT = Res()
        for kc in range(NKC):
            P.add("pe", lambda e, kc=kc: e.transpose(C.ps[6].v(kc * 17, (1, 17)),
                                                     c_tm.v(kc * 128, (1, 128), np_=17),
                                                     C.ident.v(0, (1, 17), np_=17)),
                  reads=[rc, C.rconst], writes=[C.rps[6]])
        P.add("dve", lambda e: e.tensor_copy(cT.v(0, (1, NKC * 17)), C.ps[6].v(0, (1, NKC * 17))),
              reads=[C.rps[6]], writes=[rcT])
        wb = [sb("wada%d" % i, NKC * 512, BF16) for i in range(2)]
        rwb = [Res(), Res()]
        for blk in range(36):
            s = blk % 2
            P.add("pool", lambda e, blk=blk, s=s: e.dma_start(
                out=wb[s].v(0, (512, NKC), (1, 512)),
                in_=dv(I["w_ada"], blk * 512, (9 * D, 128), (128 * 9 * D, NKC), (1, 512))),
                writes=[rwb[s]], dma=True)
            bank = 4 + (blk % 2)
            for cc in range(4):
                j = blk * 4 + cc
                for kc in range(NKC):
                    P.add("pe", lambda e, s=s, cc=cc, kc=kc, bank=bank: e.matmul(
                        C.ps[bank].v(cc * 17, (1, 17)),
                        wb[s].v(kc * 512 + cc * 128, (1, 128)),
                        cT.v(kc * 17, (1, 17)), start=(kc == 0), stop=(kc == NKC - 1)),
                        reads=[rwb[s], rcT], writes=[C.rps[bank]])
            for cc in range(4):
                j = blk * 4 + cc
                P.add("act", lambda e, j=j, cc=cc, bank=bank: e.activation(
                    out=C.mods.v(j * 17, (1, 17)), in_=C.ps[bank].v(cc * 17, (1, 17)),
                    func=AF.Identity, bias=bada.v(j, (1, 1)), scale=1.0),
                    reads=[C.rps[bank], rb], writes=[C.rmods])
        P.barrier()
        C.sb = sbo


def load_x_tokenmajor(C, ti, xt, rxs):
    P, I = C.P, C.I
    t0, T = TILES[ti]
    stage, rstage = C.xstage, C.rxstage
    blocks = []
    for si, (o, n, samp) in enumerate(subs_of(t0)):
        if samp:
            blocks.append((o, 64, I["xs"], 0, si))
        else:
            for b in range(n // 128):
                blocks.append((o + b * 128, 128, I["xp"], (t0 + o + b * 128) * D, si))
    for bi, (o, n, src, soff, si) in enumerate(blocks):
        for half in range(2):
            s = (bi * 2 + half) % 2
            P.add("sp", lambda e, s=s, n=n, src=src, soff=soff, half=half: e.dma_start(
                out=stage[s].v(0, (1, 1024), np_=n),
                in_=dv(src, soff + half * 1024, (D, n), (1, 1024))), writes=[rstage[s]], dma=True)
            for q in range(2):
                bank = q + 2 * half
                for j in range(4):
                    P.add("pe", lambda e, s=s, n=n, q=q, j=j, bank=bank: e.transpose(
                        C.ps[bank].v(j * 128, (1, n)),
                        stage[s].v((q * 4 + j) * 128, (1, 128), np_=n),
                        C.ident.v(0, (1, n), np_=n)),
                        reads=[rstage[s], C.rconst], writes=[C.rps[bank]])
                c0 = half * 8 + q * 4
                eng = "act" if q == 0 else "dve"
                if eng == "act":
                    fn = lambda e, c0=c0, o=o, n=n, bank=bank: e.activation(
                        out=xt.v(c0 * TS + o, (TS, 4), (1, n)), in_=C.ps[bank].v(0, (128, 4), (1, n)),
                        func=AF.Copy)
                else:
                    fn = lambda e, c0=c0, o=o, n=n, bank=bank: e.tensor_copy(
                        xt.v(c0 * TS + o, (TS, 4), (1, n)), C.ps[bank].v(0, (128, 4), (1, n)))
                P.add(eng, fn, reads=[C.rps[bank]], writes=[rxs[c][si] for c in range(c0, c0 + 4)])


def phase_ffn(C, nm, sl, load_x, store_x):
    P, I, nc = C.P, C.I, C.nc
    wg, wu, wd = I[nm + "_wg"], I[nm + "_wu"], I[nm + "_wd"]
    gn = C.gn[nm + "_g"]
    shb, scb, gtb = (3 * sl) * 16, (3 * sl + 1) * 16, (3 * sl + 2) * 16
    TM = 1088
    with contextlib.ExitStack() as es:
        def sb(name, width, dt):
            return SB(es.enter_context(nc.sbuf_tensor(nm + "_" + name, [128, padw(width, dt)], dt)), padw(width, dt), dt)
        xt = sb("xt", NKC * TM, F32)
        ht = sb("ht", NKC * TM, BF16)
        wgu = [[sb("wg%d" % i, NKC * 256, BF16), sb("wu%d" % i, NKC * 256, BF16)] for i in range(2)]
        wdb = [sb("wd%d" % i, 2 * D, BF16) for i in range(2)]
        actg = [sb("actg%d" % i, 2 * TM, BF16) for i in range(2)]
        sg = [sb("sg%d" % i, 512, BF16) for i in range(2)]
        rstd = sb("rstd", TM, F32)
        xsq = [sb("xsq%d" % i, 512, BF16) for i in range(2)]
        tmp = [sb("tmp%d" % i, 512, F32) for i in range(2)]
        A17 = sb("A17", NKC * 17, F32)
        G17 = sb("G17", NKC * 17, F32)
        E1 = sb("E1", NKC * 64, F32)
        E2 = sb("E2", NKC * 64, F32)
        T1 = sb("T1", NKC * 64, F32)
        C.xstage = [sb("xstage%d" % i, 1024, F32) for i in range(2)]
        C.rxstage = [Res(), Res()]
        rwgu = [Res(), Res()]
        rwd = [Res(), Res()]
        rsg = [Res(), Res()]
        rxsq = [Res(), Res()]
        rtmp = [Res(), Res()]
        rA, rG, rE1, rE2, rT1, rrstd = Res(), Res(), Res(), Res(), Res(), Res()
        P.add("dve", lambda e: e.tensor_scalar(A17.v(0, (1, NKC * 17)), C.mods.v(scb * 17, (1, NKC * 17)),
                                               1.0, None, op0=ALU.add), reads=[C.rmods], writes=[rA])
        P.add("dve", lambda e: e.tensor_tensor(out=A17.v(0, (17, NKC), (1, 17)), in0=A17.v(0, (17, NKC), (1, 17)),
                                               in1=gn.v(0, (1, NKC), (0, 17)), op=ALU.mult),
              reads=[rA, C.rgn], writes=[rA])
        P.add("dve", lambda e: e.tensor_scalar(G17.v(0, (1, NKC * 17)), C.mods.v(gtb * 17, (1, NKC * 17)),
                                               0.5, None, op0=ALU.mult), reads=[C.rmods], writes=[rG])
        P.add("dve", lambda e: e.tensor_copy(E1.v(0, (64, NKC), (4, NSB), (1, 4)),
                                             A17.v(1, (17, NKC), (1, NSB), (0, 4))), reads=[rA], writes=[rE1])
        P.add("dve", lambda e: e.tensor_copy(E2.v(0, (64, NKC), (4, NSB), (1, 4)),
                                             C.mods.v(shb * 17 + 1, (17, NKC), (1, NSB), (0, 4))),
              reads=[C.rmods], writes=[rE2])
        fgc = 0
        dbank = 0
        if nm == "ffn2":
            Gs2 = sb("Gs2", NKC * 64, F32)
            rGs2 = Res()
            P.add("dve", lambda e: e.tensor_copy(Gs2.v(0, (64, NKC), (4, NSB), (1, 4)),
                                                 C.mods.v(80 * 17 + 1, (17, NKC), (1, NSB), (0, 4))),
                  reads=[C.rmods], writes=[rGs2])
        rxs = [[Res() for _ in range(3)] for _ in range(NKC)]
        rh = [Res() for _ in range(3)]
        ract_all = [[Res() for _ in range(3)] for _ in range(2)]
        for ti, (t0, T) in enumerate(TILES):
            subs = subs_of(t0)
            if ti > 0:
                P.barrier()
            load_x(ti, xt, rxs)
            if nm == "ffn2":
                for si, (o, n, samp) in enumerate(subs):
                    P.add("sp", lambda e, o=o, n=n, t0=t0: e.dma_start(
                        out=ht.v(o, (TS, NKC), (1, n)),
                        in_=dv(C.S["MOAO"], t0 + o, (NKC * NT, 128), (NT, NKC), (1, n))), writes=[rh[si]], dma=True)
                for dp in range(8):
                    ws = fgc % 2
                    fgc += 1
                    P.add("pool", lambda e, ws=ws, dp=dp: e.dma_start(
                        out=wgu[ws][0].v(0, (256, NKC), (1, 256)),
                        in_=dv(I["w_out"], dp * 256, (D, 128), (128 * D, NKC), (1, 256))), writes=[rwgu[ws]], dma=True)
                    for j in range(2):
                        dc = dp * 2 + j
                        for si, (o, n, samp) in enumerate(subs):
                            db = 4 + dbank % 3
                            dbank += 1
                            for kc in range(NKC):
                                P.add("pe", lambda e, ws=ws, j=j, kc=kc, o=o, n=n, db=db: e.matmul(
                                    C.ps[db].v(0, (1, n)), wgu[ws][0].v(kc * 256 + j * 128, (1, 128)),
                                    ht.v(kc * TS + o, (1, n)), start=(kc == 0), stop=(kc == NKC - 1)),
                                    reads=[rwgu[ws], rh[si]], writes=[C.rps[db]])
                            if not samp:
                                P.add("dve", lambda e, dc=dc, o=o, n=n, db=db: e.scalar_tensor_tensor(
                                    out=xt.v(dc * TS + o, (1, n)), in0=C.ps[db].v(0, (1, n)),
                                    scalar=C.mods.v((80 + dc) * 17, (1, 1)), in1=xt.v(dc * TS + o, (1, n)),
                                    op0=ALU.mult, op1=ALU.add),
                                    reads=[C.rps[db], C.rmods, rxs[dc][si]], writes=[rxs[dc][si]])
                            else:
                                P.add("dve", lambda e, dc=dc, db=db: e.tensor_tensor(
                                    out=T1.v(dc * 64, (1, 64)), in0=C.ps[db].v(0, (1, 64)),
                                    in1=Gs2.v(dc * 64, (1, 64)), op=ALU.mult),
                                    reads=[C.rps[db], rGs2], writes=[rT1])
                                P.add("dve", lambda e, dc=dc, o=o: e.tensor_tensor(
                                    out=xt.v(dc * TS + o, (1, 64)), in0=xt.v(dc * TS + o, (1, 64)),
                                    in1=T1.v(dc * 64, (1, 64)), op=ALU.add),
                                    reads=[rT1, rxs[dc][si]], writes=[rxs[dc][si]])
            for si, (o, n, samp) in enumerate(subs):
                for c in range(NKC):
                    s = c % 2
                    P.add("act", lambda e, s=s, c=c, o=o, n=n: e.activation(
                        out=xsq[s].v(0, (1, n)), in_=xt.v(c * TS + o, (1, n)), func=AF.Square),
                        reads=[rxs[c][si]], writes=[rxsq[s]])
                    P.add("pe", lambda e, s=s, c=c, n=n: e.matmul(
                        C.ps[7].v(0, (1, n)), C.ones_b.v(0, (1, 128)), xsq[s].v(0, (1, n)),
                        start=(c == 0), stop=(c == NKC - 1)), reads=[rxsq[s], C.rconst], writes=[C.rps[7]])
                P.add("act", lambda e, o=o, n=n: e.activation(
                    out=rstd.v(o, (1, n)), in_=C.ps[7].v(0, (1, n)), func=AF.Sqrt, bias=C.epsb.v(0, (1, 1)),
                    scale=1.0 / D), reads=[C.rps[7], C.rconst], writes=[rrstd])
                P.add("dve", lambda e, o=o, n=n: e.reciprocal(rstd.v(o, (1, n)), rstd.v(o, (1, n))),
                      reads=[rrstd], writes=[rrstd])
                if not samp:
                    for c in range(NKC):
                        s = c % 2
                        P.add("dve", lambda e, s=s, c=c, o=o, n=n: e.scalar_tensor_tensor(
                            out=tmp[s].v(0, (1, n)), in0=xt.v(c * TS + o, (1, n)), scalar=A17.v(c * 17, (1, 1)),
                            in1=rstd.v(o, (1, n)), op0=ALU.mult, op1=ALU.mult),
                            reads=[rxs[c][si], rA, rrstd], writes=[rtmp[s]])
                        P.add("act", lambda e, s=s, c=c, o=o, n=n: e.activation(
                            out=ht.v(c * TS + o, (1, n)), in_=tmp[s].v(0, (1, n)), func=AF.Identity,
                            bias=C.mods.v((shb + c) * 17, (1, 1)), scale=1.0),
                            reads=[rtmp[s], C.rmods], writes=[rh[si]])
                else:
                    P.add("dve", lambda e, o=o: e.tensor_tensor(
                        out=T1.v(0, (64, NKC), (1, 64)), in0=xt.v(o, (TS, NKC), (1, 64)),
                        in1=rstd.v(o, (0, NKC), (1, 64)), op=ALU.mult),
                        reads=[rxs[c][si] for c in range(NKC)] + [rrstd], writes=[rT1])
                    P.add("dve", lambda e: e.tensor_tensor(
                        out=T1.v(0, (1, NKC * 64)), in0=T1.v(0, (1, NKC * 64)), in1=E1.v(0, (1, NKC * 64)),
                        op=ALU.mult), reads=[rT1, rE1], writes=[rT1])
                    P.add("dve", lambda e, o=o: e.tensor_tensor(
                        out=ht.v(o, (TS, NKC), (1, 64)), in0=T1.v(0, (64, NKC), (1, 64)),
                        in1=E2.v(0, (64, NKC), (1, 64)), op=ALU.add), reads=[rT1, rE2], writes=[rh[si]])
                    P.add("dve", lambda e: e.tensor_copy(E1.v(0, (64, NKC), (4, NSB), (1, 4)),
                                                         G17.v(1, (17, NKC), (1, NSB), (0, 4))),
                          reads=[rG, rT1], writes=[rE1])
            for fg in range(NFG):
                ws = fgc % 2
                fgc += 1
                P.add("pool", lambda e, ws=ws, fg=fg: e.dma_start(
                    out=wgu[ws][0].v(0, (256, NKC), (1, 256)),
                    in_=dv(wg, fg * 256, (DFF, 128), (128 * DFF, NKC), (1, 256))), writes=[rwgu[ws]], dma=True)
                P.add("pool", lambda e, ws=ws, fg=fg: e.dma_start(
                    out=wgu[ws][1].v(0, (256, NKC), (1, 256)),
                    in_=dv(wu, fg * 256, (DFF, 128), (128 * DFF, NKC), (1, 256))), writes=[rwgu[ws]], dma=True)
                P.add("pool", lambda e, ws=ws, fg=fg: e.dma_start(
                    out=wdb[ws].v(0, (D, 2), (1, D)),
                    in_=dv(wd, fg * 256 * D, (D, 128), (128 * D, 2), (1, D))), writes=[rwd[ws]], dma=True)
                ract = ract_all[ws]
                for fc in range(2):
                    for si, (o, n, samp) in enumerate(subs):
                        gb = (fc * len(subs) + si) % 2
                        ub = 2 + gb
                        for kc in range(NKC):
                            P.add("pe", lambda e, ws=ws, fc=fc, kc=kc, o=o, n=n, gb=gb: e.matmul(
                                C.ps[gb].v(0, (1, n)), wgu[ws][0].v(kc * 256 + fc * 128, (1, 128)),
                                ht.v(kc * TS + o, (1, n)), start=(kc == 0), stop=(kc == NKC - 1)),
                                reads=[rwgu[ws], rh[si]], writes=[C.rps[gb]])
                        for kc in range(NKC):
                            P.add("pe", lambda e, ws=ws, fc=fc, kc=kc, o=o, n=n, ub=ub: e.matmul(
                                C.ps[ub].v(0, (1, n)), wgu[ws][1].v(kc * 256 + fc * 128, (1, 128)),
                                ht.v(kc * TS + o, (1, n)), start=(kc == 0), stop=(kc == NKC - 1)),
                                reads=[rwgu[ws], rh[si]], writes=[C.rps[ub]])
                        P.add("act", lambda e, gb=gb, n=n: e.activation(
                            out=sg[gb].v(0, (1, n)), in_=C.ps[gb].v(0, (1, n)), func=AF.Silu),
                            reads=[C.rps[gb]], writes=[rsg[gb]])
                        P.add("dve", lambda e, ws=ws, fc=fc, o=o, n=n, gb=gb, ub=ub: e.tensor_tensor(
                            out=actg[ws].v(fc * TS + o, (1, n)), in0=sg[gb].v(0, (1, n)),
                            in1=C.ps[ub].v(0, (1, n)), op=ALU.mult),
                            reads=[rsg[gb], C.rps[ub]], writes=[ract[si]])
                for si, (o, n, samp) in enumerate(subs):
                    for dc in range(NKC):
                        db = 4 + dbank % 3
                        dbank += 1
                        for fc in range(2):
                            P.add("pe", lambda e, ws=ws, fc=fc, dc=dc, o=o, n=n, db=db: e.matmul(
                                C.ps[db].v(0, (1, n)), wdb[ws].v(fc * D + dc * 128, (1, 128)),
                                actg[ws].v(fc * TS + o, (1, n)), start=(fc == 0), stop=(fc == 1)),
                                reads=[rwd[ws], ract[si]], writes=[C.rps[db]])
                        if not samp:
                            P.add("dve", lambda e, dc=dc, o=o, n=n, db=db: e.scalar_tensor_tensor(
                                out=xt.v(dc * TS + o, (1, n)), in0=C.ps[db].v(0, (1, n)),
                                scalar=G17.v(dc * 17, (1, 1)), in1=xt.v(dc * TS + o, (1, n)),
                                op0=ALU.mult, op1=ALU.add),
                                reads=[C.rps[db], rG, rxs[dc][si]], writes=[rxs[dc][si]])
                        else:
                            P.add("dve", lambda e, dc=dc, db=db: e.tensor_tensor(
                                out=T1.v(dc * 64, (1, 64)), in0=C.ps[db].v(0, (1, 64)),
                                in1=E1.v(dc * 64, (1, 64)), op=ALU.mult),
                                reads=[C.rps[db], rE1], writes=[rT1])
                            P.add("dve", lambda e, dc=dc, o=o: e.tensor_tensor(
                                out=xt.v(dc * TS + o, (1, 64)), in0=xt.v(dc * TS + o, (1, 64)),
                                in1=T1.v(dc * 64, (1, 64)), op=ALU.add),
                                reads=[rT1, rxs[dc][si]], writes=[rxs[dc][si]])
            store_x(ti, xt, rxs)
        P.barrier()


def prep_shared(inp):
    sh = {}
    sh["ident"] = np.eye(128, dtype=np.float32)
    for nm in ("ffn1", "ffn2"):
        sh[nm + "_norm_g"] = np.ascontiguousarray(inp[nm + "_norm_g"][0].reshape(NKC, 128))
        sh[nm + "_w_gate"] = inp[nm + "_w_gate"][0]
        sh[nm + "_w_up"] = inp[nm + "_w_up"][0]
        sh[nm + "_w_down"] = inp[nm + "_w_down"][0]
    sh["mix_norm_g"] = np.ascontiguousarray(inp["mix_norm_g"][0].reshape(NKC, 128))
    sh["w_ada"] = inp["w_ada"][0]
    sh["b_ada"] = np.ascontiguousarray(inp["b_ada"][0].reshape(144, 128))
    sh["w_in"] = inp["w_in"][0]
    sh["w_out"] = inp["w_out"][0]
    sh["gate_b"] = np.ascontiguousarray(inp["mlstm_gate_b"][0].reshape(8, 1))
    sh["qng"] = np.ascontiguousarray(inp["q_norm_g"][0].reshape(128, 1))
    sh["kng"] = np.ascontiguousarray(inp["k_norm_g"][0].reshape(128, 1))
    sh["conv_w"] = np.ascontiguousarray(inp["mlstm_conv_w"][0].reshape(4, NKC, 128).reshape(64, 128))
    sh["out_g"] = np.ascontiguousarray(inp["mlstm_out_g"][0].reshape(8, 128))
    sh["tri"] = np.triu(np.ones((64, 64), np.float32))
    sel = np.zeros((8, 8, 128), np.float32)
    for r_ in range(8):
        sel[r_, r_, :] = 1.0
    sh["sel8"] = sel.reshape(8, 1024)
    mk = np.ones((1, 512), np.float32)
    mk[0, ::64] = 0.0
    sh["mk64"] = mk
    first = (np.arange(64) % 4 == 0)
    mks = np.zeros((3, 64), np.float32)
    mks[0] = np.where(first, 0.0, 1.0)
    mks[1] = np.where(first, 0.0, -1e30)
    mks[2] = np.where(first, -1e30, 0.0)
    sh["mks"] = mks.reshape(1, 192)
    sh["bdp"], sh["bsp"], sh["bsn"] = host_bias_tables(np.asarray(inp["t5_bias"]))
    sh["pidx"] = np.arange(128, dtype=np.float32).reshape(128, 1)
    sh["ck"] = inp["cache_k"][0].reshape(-1, 256)
    sh["cv"] = inp["cache_v"][0].reshape(-1, 256)
    sh["cidx"] = inp["cache_idx_k"][0].reshape(-1, 64)
    sh["negtri"] = np.where(np.arange(128)[None, :] > np.arange(128)[:, None], np.float32(NEGB), np.float32(0.0)).astype(np.float32)
    return sh


def prep_core(inp, core, sh):
    m = dict(sh)
    m["xp"] = inp["x_prompt"][core]
    m["xs"] = np.ascontiguousarray(inp["x_sample"][NSB * core:NSB * (core + 1)].reshape(NS_TOK, D))
    sl = slice(NSB * core, NSB * (core + 1))
    m["ptab"] = np.ascontiguousarray(np.asarray(inp["page_table"][sl]).astype(np.int32).reshape(1, 256))
    m["st_C"] = np.ascontiguousarray(inp["state_C"][0, sl])
    m["st_n"] = np.ascontiguousarray(inp["state_n"][0, sl])
    m["st_m"] = np.ascontiguousarray(inp["state_m"][0, sl])
    m["st_conv"] = np.ascontiguousarray(inp["state_conv"][0, sl].reshape(NSB * 3, D))
    m["c17"] = np.ascontiguousarray(np.concatenate(
        [inp["c_prompt"][core:core + 1], inp["c_sample"][NSB * core:NSB * (core + 1)]], axis=0))
    return m


class NormBufs:
    pass


def norm_setup(C, sb, gn, sl, with_gate, gate_scale):
    P = C.P
    N = NormBufs()
    N.shb, N.scb, N.gtb = (3 * sl) * 16, (3 * sl + 1) * 16, (3 * sl + 2) * 16
    N.A17 = sb("A17", NKC * 17, F32)
    N.G17 = sb("G17", NKC * 17, F32)
    N.E1 = sb("E1", NKC * 64, F32)
    N.E2 = sb("E2", NKC * 64, F32)
    N.T1 = sb("T1", NKC * 64, F32)
    N.rstd = sb("rstd", TS, F32)
    N.xsq = [sb("xsq%d" % i, 512, BF16) for i in range(2)]
    N.tmp = [sb("tmp%d" % i, 512, F32) for i in range(2)]
    N.rxsq = [Res(), Res()]
    N.rtmp = [Res(), Res()]
    N.rA, N.rG, N.rE1, N.rE2, N.rT1, N.rrstd = Res(), Res(), Res(), Res(), Res(), Res()
    A17, G17, E1, E2 = N.A17, N.G17, N.E1, N.E2
    scb, gtb, shb = N.scb, N.gtb, N.shb
    P.add("dve", lambda e: e.tensor_scalar(A17.v(0, (1, NKC * 17)), C.mods.v(scb * 17, (1, NKC * 17)),
                                           1.0, None, op0=ALU.add), reads=[C.rmods], writes=[N.rA])
    P.add("dve", lambda e: e.tensor_tensor(out=A17.v(0, (17, NKC), (1, 17)), in0=A17.v(0, (17, NKC), (1, 17)),
                                           in1=gn.v(0, (1, NKC), (0, 17)), op=ALU.mult),
          reads=[N.rA, C.rgn], writes=[N.rA])
    P.add("dve", lambda e: e.tensor_scalar(G17.v(0, (1, NKC * 17)), C.mods.v(gtb * 17, (1, NKC * 17)),
                                           gate_scale, None, op0=ALU.mult), reads=[C.rmods], writes=[N.rG])
    P.add("dve", lambda e: e.tensor_copy(E1.v(0, (64, NKC), (4, NSB), (1, 4)),
                                         A17.v(1, (17, NKC), (1, NSB), (0, 4))), reads=[N.rA], writes=[N.rE1])
    P.add("dve", lambda e: e.tensor_copy(E2.v(0, (64, NKC), (4, NSB), (1, 4)),
                                         C.mods.v(shb * 17 + 1, (17, NKC), (1, NSB), (0, 4))),
          reads=[C.rmods], writes=[N.rE2])
    return N


def norm_sub(C, N, xt, xoff, xstride, rx_list, ht, hoff, rh, o, n, samp):
    P = C.P
    rstd, xsq, tmp, A17, E1, E2, T1 = N.rstd, N.xsq, N.tmp, N.A17, N.E1, N.E2, N.T1
    for c in range(NKC):
        s = c % 2
        P.add("act", lambda e, s=s, c=c: e.activation(
            out=xsq[s].v(0, (1, n)), in_=xt.v(c * xstride + xoff, (1, n)), func=AF.Square),
            reads=[rx_list[c]], writes=[N.rxsq[s]])
        P.add("pe", lambda e, s=s, c=c: e.matmul(
            C.ps[7].v(0, (1, n)), C.ones_b.v(0, (1, 128)), xsq[s].v(0, (1, n)),
            start=(c == 0), stop=(c == NKC - 1)), reads=[N.rxsq[s], C.rconst], writes=[C.rps[7]])
    P.add("act", lambda e: e.activation(
        out=rstd.v(o, (1, n)), in_=C.ps[7].v(0, (1, n)), func=AF.Sqrt, bias=C.epsb.v(0, (1, 1)),
        scale=1.0 / D), reads=[C.rps[7], C.rconst], writes=[N.rrstd])
    P.add("dve", lambda e: e.reciprocal(rstd.v(o, (1, n)), rstd.v(o, (1, n))),
          reads=[N.rrstd], writes=[N.rrstd])
    if not samp:
        for c in range(NKC):
            s = c % 2
            P.add("dve", lambda e, s=s, c=c: e.scalar_tensor_tensor(
                out=tmp[s].v(0, (1, n)), in0=xt.v(c * xstride + xoff, (1, n)), scalar=A17.v(c * 17, (1, 1)),
                in1=rstd.v(o, (1, n)), op0=ALU.mult, op1=ALU.mult),
                reads=[rx_list[c], N.rA, N.rrstd], writes=[N.rtmp[s]])
            P.add("act", lambda e, s=s, c=c: e.activation(
                out=ht.v(c * TS + hoff, (1, n)), in_=tmp[s].v(0, (1, n)), func=AF.Identity,
                bias=C.mods.v((N.shb + c) * 17, (1, 1)), scale=1.0),
                reads=[N.rtmp[s], C.rmods], writes=[rh])
    else:
        P.add("dve", lambda e: e.tensor_tensor(
            out=T1.v(0, (64, NKC), (1, 64)), in0=xt.v(xoff, (xstride, NKC), (1, 64)),
            in1=rstd.v(o, (0, NKC), (1, 64)), op=ALU.mult),
            reads=list(rx_list) + [N.rrstd], writes=[N.rT1])
        P.add("dve", lambda e: e.tensor_tensor(
            out=T1.v(0, (1, NKC * 64)), in0=T1.v(0, (1, NKC * 64)), in1=E1.v(0, (1, NKC * 64)),
            op=ALU.mult), reads=[N.rT1, N.rE1], writes=[N.rT1])
        P.add("dve", lambda e: e.tensor_tensor(
            out=ht.v(hoff, (TS, NKC), (1, 64)), in0=T1.v(0, (64, NKC), (1, 64)),
            in1=E2.v(0, (64, NKC), (1, 64)), op=ALU.add), reads=[N.rT1, N.rE2], writes=[rh])


WI_SCALE = (8 ** -0.5) * (64 ** -0.5)
TMB = 2
FM_GROUPS = [("qk", 0, 4, 0), ("qk", 512, 4, 4), ("qk", 1024, 4, 8), ("qk", 1536, 4, 12),
             ("om", 3072, 4, 0), ("om", 3584, 4, 4),
             ("qa", 4104, 4, 0), ("qa", 4616, 4, 4),
             ("ka", 5128, 2, 0), ("qi", 5640, 4, 0), ("vm", 2048, 4, 0), ("vm", 2560, 4, 1)]


def phase_proj(C):
    P, I, S, O, nc = C.P, C.I, C.S, C.O, C.nc
    w_in = I["w_in"]
    with contextlib.ExitStack() as es:
        def sb(name, width, dt):
            return SB(es.enter_context(nc.sbuf_tensor("pj_" + name, [128, padw(width, dt)], dt)), padw(width, dt), dt)
        N = norm_setup(C, sb, C.gn["mix_g"], 1, False, 1.0)
        xsb = sb("xsb", NKC * 512, F32)
        rxsb = [Res() for _ in range(NKC)]
        ht = sb("ht", NKC * TS, BF16)
        rh = [Res() for _ in range(3)]
        wbuf = [sb("wb%d" % i, NKC * 512, BF16) for i in range(2)]
        rwb = [Res(), Res()]
        wva = sb("wva", NKC * 256, BF16)
        wmisc = sb("wmisc", NKC * 80, BF16)
        rwtm = Res()
        stg32 = [sb("stg32_%d" % i, 512, F32) for i in range(3)]
        rstg32 = [Res() for _ in range(3)]
        stgb = [sb("stgb%d" % i, 512, BF16) for i in range(3)]
        rstgb = [Res() for _ in range(3)]
        vstg = [sb("vstg%d" % i, 2 * 257, BF16) for i in range(2)]
        rvstg = [Res(), Res()]
        tstg = [sb("tstg%d" % i, 320, F32) for i in range(2)]
        rtstg = [Res(), Res()]
        tstgb = [sb("tstgb%d" % i, 256, BF16) for i in range(2)]
        rtstgb = [Res(), Res()]
        tstw = [sb("tstw%d" % i, 8, F32) for i in range(2)]
        rtstw = [Res(), Res()]
        kn32 = sb("kn32", 2 * 512, F32)
        rkn = Res()
        kout = [sb("kout%d" % i, 256, F32) for i in range(2)]
        rkout = [Res(), Res()]
        hsq = [sb("hsq%d" % i, 512, BF16) for i in range(2)]
        rhsq = [Res(), Res()]
        hr = [sb("hr%d" % i, 512, F32) for i in range(2)]
        rhr = [Res(), Res()]
        cstg = sb("cstg", 512, F32)
        rcstg = Res()
        gb_ = sb("gateb", 1, F32)
        qng = sb("qng", 1, F32)
        kng = sb("kng", 1, F32)
        eps128 = C.epsb
        rsm = Res()
        P.add("sp", lambda e: e.dma_start(out=gb_.v(0, (1, 1), np_=8), in_=I["gate_b"]), writes=[rsm], dma=True)
        P.add("sp", lambda e: e.dma_start(out=qng.v(0, (1, 1)), in_=I["qng"]), writes=[rsm], dma=True)
        P.add("sp", lambda e: e.dma_start(out=kng.v(0, (1, 1)), in_=I["kng"]), writes=[rsm], dma=True)
        for k in range(2):
            P.add("dve", lambda e, k=k: e.memset(vstg[k].v(0, (1, 2 * 257)), 1.0), writes=[rvstg[k]])
        P.add("pool", lambda e: e.dma_start(out=wva.v(0, (256, NKC), (1, 256)),
                                            in_=dv(w_in, 5384, (PW, 128), (128 * PW, NKC), (1, 256))),
              writes=[rwtm], dma=True)
        wm32 = sb("wm32", NKC * 80, F32)
        rwm32 = Res()
        P.add("sp", lambda e: e.dma_start(out=wm32.v(0, (80, NKC), (1, 72)),
                                          in_=dv(w_in, 6152, (PW, 128), (128 * PW, NKC), (1, 72))),
              writes=[rwm32], dma=True)
        P.add("sp", lambda e: e.dma_start(out=wm32.v(72, (80, NKC), (1, 8)),
                                          in_=dv(w_in, 4096, (PW, 128), (128 * PW, NKC), (1, 8))),
              writes=[rwm32], dma=True)
        P.add("dve", lambda e: e.tensor_copy(wmisc.v(0, (1, NKC * 80)), wm32.v(0, (1, NKC * 80))),
              reads=[rwm32], writes=[rwtm])
        cnt = {"s32": 0, "sb": 0, "fm": 0, "ev": 0, "hs": 0, "ko": 0, "w": 0, "tb": 0}

        def evac(out_ap, in_ap, reads, writes):
            cnt["ev"] += 1
            if cnt["ev"] % 2 == 0:
                P.add("act", lambda e: e.activation(out=out_ap, in_=in_ap, func=AF.Copy), reads=reads, writes=writes)
            else:
                P.add("dve", lambda e: e.tensor_copy(out_ap, in_ap), reads=reads, writes=writes)

        for ti, (t0, T) in enumerate(TILES):
            subs = subs_of(t0)
            if ti > 0:
                P.barrier()
            for si, (o, n, samp) in enumerate(subs):
                P.add("sp", lambda e, o=o, n=n: e.dma_start(
                    out=xsb.v(0, (512, NKC), (1, n)),
                    in_=dv(S["X1"], t0 + o, (NKC * NT, 128), (NT, NKC), (1, n))),
                    reads=[C.rX1[ti]], writes=rxsb, dma=True)
                norm_sub(C, N, xsb, 0, 512, rxsb, ht, o, rh[si], o, n, samp)
            if ti == 0:
                C.dump("pj_ht0", ht, rh)
                C.dump("pj_xsb0", xsb, rxsb)
                C.dump("pj_rstd0", N.rstd, [N.rrstd])
                C.dump("pj_A17", N.A17, [N.rA])
            blocks = []
            for si, (o, n, samp) in enumerate(subs):
                if samp:
                    blocks.append((o, 64, si, True, 0))
                else:
                    for b in range(n // 128):
                        blocks.append((o + b * 128, 128, si, False, t0 + o + b * 128))
            for gi, (kind, col0, nch, cbase) in enumerate(FM_GROUPS):
                if ("skip_" + kind) in C.dbg:
                    continue
                ws = cnt["w"] % 2
                cnt["w"] += 1
                ncol = nch * 128
                P.add("pool", lambda e, ws=ws, col0=col0, ncol=ncol: e.dma_start(
                    out=wbuf[ws].v(0, (512, NKC), (1, ncol)),
                    in_=dv(w_in, col0, (PW, 128), (128 * PW, NKC), (1, ncol))), writes=[rwb[ws]], dma=True)
                if kind == "vm":
                    half = cbase
                    for (bo, nb, si, samp, trow) in blocks:
                        tb = cnt["tb"] % 2
                        cnt["tb"] += 1
                        grow = (NP_TOK if samp else trow)
                        pb = 2 + tb
                        for kc in range(NKC):
                            P.add("pe", lambda e, ws=ws, kc=kc, bo=bo, nb=nb, pb=pb: e.matmul(
                                C.ps[pb].v(0, (1, 512), np_=nb), ht.v(kc * TS + bo, (1, nb)),
                                wbuf[ws].v(kc * 512, (1, 512)), start=(kc == 0), stop=(kc == NKC - 1)),
                                reads=[rwb[ws], rh[si]], writes=[C.rps[pb]])
                        evac(vstg[tb].v(0, (257, 2), (1, 256), np_=nb),
                             C.ps[pb].v(0, (256, 2), (1, 256), np_=nb), [C.rps[pb]], [rvstg[tb]])
                        P.add("sp", lambda e, tb=tb, nb=nb, grow=grow, half=half: e.dma_start(
                            out=dv(S["VM"], grow * 1028 + half * 514, (1028, nb), (1, 514)),
                            in_=vstg[tb].v(0, (1, 514), np_=nb)), reads=[rvstg[tb]], dma=True)
                    continue
                for si, (o, n, samp) in enumerate(subs):
                    for j in range(nch):
                        pb = cnt["fm"] % 2
                        cnt["fm"] += 1
                        for kc in range(NKC):
                            P.add("pe", lambda e, ws=ws, j=j, kc=kc, o=o, n=n, pb=pb: e.matmul(
                                C.ps[pb].v(0, (1, n)), wbuf[ws].v(kc * 512 + j * 128, (1, 128)),
                                ht.v(kc * TS + o, (1, n)), start=(kc == 0), stop=(kc == NKC - 1)),
                                reads=[rwb[ws], rh[si]], writes=[C.rps[pb]])
                        cidx = cbase + j
                        if kind == "qk":
                            k = cnt["s32"] % 3
                            cnt["s32"] += 1
                            evac(stg32[k].v(0, (1, n)), C.ps[pb].v(0, (1, n)), [C.rps[pb]], [rstg32[k]])
                            P.add("sp", lambda e, k=k, cidx=cidx, o=o, n=n: e.dma_start(
                                out=dv(S["QK"], cidx * NT + t0 + o, (NKC * NT, 128), (1, n)),
                                in_=stg32[k].v(0, (1, n))), reads=[rstg32[k]], writes=[], dma=True)
                        elif kind in ("om", "qi"):
                            k = cnt["sb"] % 3
                            cnt["sb"] += 1
                            dst = S["OM"] if kind == "om" else S["QI"]
                            nchk = 8 if kind == "om" else 4
                            rdst = None
                            evac(stgb[k].v(0, (1, n)), C.ps[pb].v(0, (1, n)), [C.rps[pb]], [rstgb[k]])
                            P.add("sp", lambda e, k=k, cidx=cidx, o=o, n=n, dst=dst, nchk=nchk: e.dma_start(
                                out=dv(dst, cidx * NT + t0 + o, (nchk * NT, 128), (1, n)),
                                in_=stgb[k].v(0, (1, n))), reads=[rstgb[k]], dma=True)
                        else:
                            hs = cnt["hs"] % 2
                            cnt["hs"] += 1
                            gcol = qng if kind == "qa" else kng
                            P.add("act", lambda e, hs=hs, pb=pb, n=n: e.activation(
                                out=hsq[hs].v(0, (1, n)), in_=C.ps[pb].v(0, (1, n)), func=AF.Square),
                                reads=[C.rps[pb]], writes=[rhsq[hs]])
                            P.add("pe", lambda e, hs=hs, n=n: e.matmul(
                                C.ps[6].v(0, (1, n)), C.ones_b.v(0, (1, 128)), hsq[hs].v(0, (1, n)),
                                start=True, stop=True), reads=[rhsq[hs], C.rconst], writes=[C.rps[6]])
                            P.add("act", lambda e, hs=hs, n=n: e.activation(
                                out=hr[hs].v(0, (1, n)), in_=C.ps[6].v(0, (1, n)), func=AF.Sqrt,
                                bias=eps128.v(0, (1, 1)), scale=1.0 / 128), reads=[C.rps[6], C.rconst],
                                writes=[rhr[hs]])
                            P.add("dve", lambda e, hs=hs, n=n: e.reciprocal(hr[hs].v(0, (1, n)), hr[hs].v(0, (1, n))),
                                  reads=[rhr[hs]], writes=[rhr[hs]])
                            if kind == "qa":
                                k = cnt["sb"] % 3
                                cnt["sb"] += 1
                                P.add("dve", lambda e, hs=hs, k=k, pb=pb, n=n, gcol=gcol: e.scalar_tensor_tensor(
                                    out=stgb[k].v(0, (1, n)), in0=C.ps[pb].v(0, (1, n)), scalar=gcol.v(0, (1, 1)),
                                    in1=hr[hs].v(0, (1, n)), op0=ALU.mult, op1=ALU.mult),
                                    reads=[C.rps[pb], rhr[hs], rsm], writes=[rstgb[k]])
                                P.add("sp", lambda e, k=k, cidx=cidx, o=o, n=n: e.dma_start(
                                    out=dv(S["QA"], cidx * NT + t0 + o, (8 * NT, 128), (1, n)),
                                    in_=stgb[k].v(0, (1, n))), reads=[rstgb[k]], writes=[], dma=True)
                            else:
                                P.add("dve", lambda e, hs=hs, pb=pb, n=n, j=j, gcol=gcol: e.scalar_tensor_tensor(
                                    out=kn32.v(j * 512, (1, n)), in0=C.ps[pb].v(0, (1, n)), scalar=gcol.v(0, (1, 1)),
                                    in1=hr[hs].v(0, (1, n)), op0=ALU.mult, op1=ALU.mult),
                                    reads=[C.rps[pb], rhr[hs], rsm], writes=[rkn])
                                k = cnt["sb"] % 3
                                cnt["sb"] += 1
                                P.add("act", lambda e, k=k, j=j, n=n: e.activation(
                                    out=stgb[k].v(0, (1, n)), in_=kn32.v(j * 512, (1, n)), func=AF.Copy),
                                    reads=[rkn], writes=[rstgb[k]])
                                P.add("sp", lambda e, k=k, cidx=cidx, o=o, n=n: e.dma_start(
                                    out=dv(S["KA"], cidx * NT + t0 + o, (2 * NT, 128), (1, n)),
                                    in_=stgb[k].v(0, (1, n))), reads=[rstgb[k]], writes=[], dma=True)
                                if j == 1:
                                    nb = min(n, 128)
                                    for b in range(max(1, n // 128)):
                                        for hh in range(2):
                                            P.add("pe", lambda e, b=b, hh=hh, nb=nb: e.transpose(
                                                C.ps[5].v(hh * 128, (1, 128), np_=nb),
                                                kn32.v(hh * 512 + b * 128, (1, nb)), C.ident.v(0, (1, 128))),
                                                reads=[rkn, C.rconst], writes=[C.rps[5]])
                                        ko = cnt["ko"] % 2
                                        cnt["ko"] += 1
                                        evac(kout[ko].v(0, (1, 256), np_=nb), C.ps[5].v(0, (1, 256), np_=nb),
                                             [C.rps[5]], [rkout[ko]])
                                        if samp:
                                            dst, doff = O["k_s"], 0
                                        else:
                                            dst, doff = O["k_p"], (t0 + o + b * 128) * 256
                                        P.add("sp", lambda e, ko=ko, nb=nb, dst=dst, doff=doff: e.dma_start(
                                            out=dv(dst, doff, (256, nb), (1, 256)), in_=kout[ko].v(0, (1, 256), np_=nb)),
                                            reads=[rkout[ko]], dma=True)
                if kind == "qk" and ti == 1 and "skip_conv" not in C.dbg:
                    for kc in range(NKC):
                        P.add("pe", lambda e, ws=ws, kc=kc: e.matmul(
                            C.ps[5].v(0, (1, 512), np_=67), ht.v(kc * TS + 1021, (1, 67)),
                            wbuf[ws].v(kc * 512, (1, 512)), start=(kc == 0), stop=(kc == NKC - 1)),
                            reads=[rwb[ws], rh[1], rh[2]], writes=[C.rps[5]])
                    P.add("act", lambda e: e.activation(out=cstg.v(0, (1, 512), np_=67),
                                                        in_=C.ps[5].v(0, (1, 512), np_=67), func=AF.Copy),
                          reads=[C.rps[5]], writes=[rcstg])
                    P.add("sp", lambda e, col0=col0: e.dma_start(
                        out=dv(O["conv_p"], col0, (D, 3), (1, 512)), in_=cstg.v(0, (1, 512), np_=3)),
                        reads=[rcstg], dma=True)
                    for b in range(NSB if "skip_convs" not in C.dbg else 0):
                        P.add("sp", lambda e, col0=col0, b=b: e.dma_start(
                            out=dv(O["conv_s"], b * 3 * D + col0, (D, 3), (1, 512)),
                            in_=cstg.v(0, (1, 512), p0=3 + 4 * b + 1, np_=3)), reads=[rcstg], dma=True)
            for si, (o, n, samp) in enumerate(subs):
                if "skip_misc" in C.dbg:
                    continue
                pb = cnt["fm"] % 2
                cnt["fm"] += 1
                for dup in range(2):
                    for kc in range(NKC):
                        P.add("pe", lambda e, kc=kc, o=o, n=n, pb=pb, dup=dup: e.matmul(
                            C.ps[pb].v(0, (1, n), p0=64 * dup, np_=64), wmisc.v(kc * 80, (1, 64)),
                            ht.v(kc * TS + o, (1, n)), start=(kc == 0), stop=(kc == NKC - 1)),
                            reads=[rwtm, rh[si]], writes=[C.rps[pb]])
                k = cnt["sb"] % 3
                cnt["sb"] += 1
                evac(stgb[k].v(0, (1, n)), C.ps[pb].v(0, (1, n)), [C.rps[pb]], [rstgb[k]])
                P.add("sp", lambda e, k=k, o=o, n=n: e.dma_start(
                    out=dv(S["KI2"], t0 + o, (NT, 128), (1, n)), in_=stgb[k].v(0, (1, n))),
                    reads=[rstgb[k]], writes=[], dma=True)
                pb = cnt["fm"] % 2
                cnt["fm"] += 1
                for kc in range(NKC):
                    P.add("pe", lambda e, kc=kc, o=o, n=n, pb=pb: e.matmul(
                        C.ps[pb].v(0, (1, n), np_=8), wmisc.v(kc * 80 + 72, (1, 8)),
                        ht.v(kc * TS + o, (1, n)), start=(kc == 0), stop=(kc == NKC - 1)),
                        reads=[rwtm, rh[si]], writes=[C.rps[pb]])
                k = cnt["s32"] % 3
                cnt["s32"] += 1
                P.add("act", lambda e, k=k, pb=pb, n=n: e.activation(
                    out=stg32[k].v(0, (1, n), np_=8), in_=C.ps[pb].v(0, (1, n), np_=8), func=AF.Identity,
                    bias=gb_.v(0, (1, 1), np_=8), scale=1.0), reads=[C.rps[pb], rsm], writes=[rstg32[k]])
                P.add("sp", lambda e, k=k, o=o, n=n: e.dma_start(
                    out=dv(S["GT"], t0 + o, (NT, 8), (1, n)), in_=stg32[k].v(0, (1, n), np_=8)),
                    reads=[rstg32[k]], writes=[], dma=True)
            for (bo, nb, si, samp, trow) in blocks:
                if "skip_tm" in C.dbg:
                    continue
                tb = cnt["tb"] % 2
                cnt["tb"] += 1
                grow = (NP_TOK if samp else trow)
                for kc in range(NKC):
                    P.add("pe", lambda e, kc=kc, bo=bo, nb=nb: e.matmul(
                        C.ps[TMB].v(0, (1, 256), np_=nb), ht.v(kc * TS + bo, (1, nb)),
                        wva.v(kc * 256, (1, 256)), start=(kc == 0), stop=(kc == NKC - 1)),
                        reads=[rwtm, rh[si]], writes=[C.rps[TMB]])
                for kc in range(NKC if "tm_nokiwi" not in C.dbg else 0):
                    P.add("pe", lambda e, kc=kc, bo=bo, nb=nb: e.matmul(
                        C.ps[TMB].v(256, (1, 72), np_=nb), ht.v(kc * TS + bo, (1, nb)),
                        wmisc.v(kc * 80, (1, 72)), start=(kc == 0), stop=(kc == NKC - 1)),
                        reads=[rwtm, rh[si]], writes=[C.rps[TMB]])
                if "tm_nodve" not in C.dbg:
                    P.add("act", lambda e, tb=tb, nb=nb: e.activation(
                        out=tstg[tb].v(0, (1, 320), np_=nb), in_=C.ps[TMB].v(0, (1, 320), np_=nb), func=AF.Copy),
                        reads=[C.rps[TMB]], writes=[rtstg[tb]])
                if "tm_noact" not in C.dbg:
                    P.add("act", lambda e, tb=tb, nb=nb: e.activation(
                        out=tstgb[tb].v(0, (1, 256), np_=nb), in_=C.ps[TMB].v(0, (1, 256), np_=nb), func=AF.Copy),
                        reads=[C.rps[TMB]], writes=[rtstgb[tb]])
                if "skip_wi" not in C.dbg:
                    P.add("act", lambda e, tb=tb, nb=nb: e.activation(
                        out=tstw[tb].v(0, (1, 8), np_=nb), in_=C.ps[TMB].v(320, (1, 8), np_=nb), func=AF.Copy,
                        scale=WI_SCALE), reads=[C.rps[TMB]], writes=[rtstw[tb]])
                vdst, vrow = (O["v_s"], 0) if samp else (O["v_p"], trow)
                idst = O["idxk_s"] if samp else O["idxk_p"]
                if "tm_novout" not in C.dbg:
                    P.add("sp", lambda e, tb=tb, nb=nb, vdst=vdst, vrow=vrow: e.dma_start(
                        out=dv(vdst, vrow * 256, (256, nb), (1, 256)), in_=tstg[tb].v(0, (1, 256), np_=nb)),
                        reads=[rtstg[tb]], dma=True)
                if "skip_idxk" not in C.dbg:
                    P.add("sp", lambda e, tb=tb, nb=nb, idst=idst, vrow=vrow: e.dma_start(
                        out=dv(idst, vrow * 64, (64, nb), (1, 64)), in_=tstg[tb].v(256, (1, 64), np_=nb)),
                        reads=[rtstg[tb]], dma=True)
                if "tm_nova" not in C.dbg:
                    P.add("sp", lambda e, tb=tb, nb=nb, grow=grow: e.dma_start(
                        out=dv(S["VA"], grow * 256, (256, nb), (1, 256)), in_=tstgb[tb].v(0, (1, 256), np_=nb)),
                        reads=[rtstgb[tb]], writes=[], dma=True)
                if "skip_wi" not in C.dbg:
                    P.add("sp", lambda e, tb=tb, nb=nb, grow=grow: e.dma_start(
                        out=dv(S["WI"], grow * 8, (8, nb), (1, 8)), in_=tstw[tb].v(0, (1, 8), np_=nb)),
                        reads=[rtstw[tb]], writes=[], dma=True)
        P.barrier()


LN16 = 2.772588722239781


def phase_mlstm(C):
    P, I, S, O, nc = C.P, C.I, C.S, C.O, C.nc
    with contextlib.ExitStack() as es:
        def sb(name, width, dt):
            return SB(es.enter_context(nc.sbuf_tensor("ml_" + name, [128, padw(width, dt)], dt)), padw(width, dt), dt)
        C.sb_save = C.sb
        C.sb = sb
        psb = [SB(p.t.bitcast(BF16), 1024, BF16) for p in C.ps]
        cw = sb("cw", 64, F32)
        rcw = Res()
        load_featmajor_small(C, cw, I["conv_w"], 64, rcw, "tmp_cw")
        outg = sb("outg", 8, F32)
        load_featmajor_small(C, outg, I["out_g"], 8, rcw, "tmp_og")
        tri = sb("tri", 64, F32)
        sel8 = sb("sel8", 1024, F32)
        mk64 = sb("mk64", 512, F32)
        mks = sb("mks", 3 * 64, F32)
        m0r = sb("m0r", 64, F32)
        rk = Res()
        P.add("sp", lambda e: e.dma_start(out=tri.v(0, (1, 64), np_=64), in_=I["tri"]), writes=[rk], dma=True)
        P.add("sp", lambda e: e.dma_start(out=sel8.v(0, (1, 1024), np_=8), in_=I["sel8"]), writes=[rk], dma=True)
        P.add("sp", lambda e: e.dma_start(out=mk64.v(0, (1, 512)), in_=dv(I["mk64"], 0, (0, 128), (1, 512))),
              writes=[rk], dma=True)
        P.add("sp", lambda e: e.dma_start(out=mks.v(0, (1, 192)), in_=dv(I["mks"], 0, (0, 128), (1, 192))),
              writes=[rk], dma=True)
        for h in range(4):
            P.add("sp", lambda e, h=h: e.dma_start(out=m0r.v(h * 16, (1, 16)),
                                                   in_=dv(I["st_m"], h, (0, 128), (4, 16))), writes=[rk], dma=True)
        NG = 512
        xc = sb("xc", NKC * 515, F32)
        rxc = Res()
        ycv = [sb("ycv%d" % i, 512, F32) for i in range(2)]
        rycv = [Res(), Res()]
        qk = sb("qk", NKC * NG, BF16)
        rqk = Res()
        gtg = sb("gtg", NG, F32)
        rgtg = Res()
        R = [sb("R%d" % i, 4 * NG, F32) for i in range(6)]
        rR = [Res() for _ in range(6)]
        carry = sb("carry", 4, F32)
        rcarry = Res()
        hbuf = sb("hbuf", 8 * NG, F32)
        rhb = Res()
        omt = sb("omt", 8 * NG, BF16)
        romt = Res()
        mot = sb("mot", 8 * NG, BF16)
        rmot = Res()
        Cst = [sb("Cst%d" % i, 4 * 514, F32) for i in range(2)]
        rCst = [[Res() for _ in range(4)] for _ in range(2)]
        Cb = [sb("Cb%d" % i, 4 * 514, BF16) for i in range(2)]
        rCb = [[Res() for _ in range(4)] for _ in range(2)]
        vt = [sb("vt%d" % i, 1028, BF16) for i in range(2)]
        rvt = [Res(), Res()]
        acol = [sb("acol%d" % i, 1, F32) for i in range(2)]
        racol = [Res(), Res()]
        wl = [sb("wl%d" % i, 1, F32) for i in range(2)]
        rwl = [Res(), Res()]
        DT = [sb("DT%d" % i, 64, F32) for i in range(2)]
        rDT = [Res(), Res()]
        Wt = [sb("Wt%d" % i, 64, BF16) for i in range(2)]
        rWt = [Res(), Res()]
        qs = [sb("qs%d" % i, 128, BF16) for i in range(2)]
        rqs = [Res(), Res()]
        kw = [sb("kw%d" % i, 256, BF16) for i in range(2)]
        rkw = [Res(), Res()]
        rr = [sb("rr%d" % i, 64, F32) for i in range(2)]
        rrr = [Res(), Res()]
        sq = [sb("sq%d" % i, NG, BF16) for i in range(2)]
        rsq = [Res(), Res()]
        rs_ = sb("rs_", NG, F32)
        rrs = Res()
        sg = [sb("sg%d" % i, NG, BF16) for i in range(2)]
        rsg = [Res(), Res()]
        tmpd = [sb("tmpd%d" % i, NG, F32) for i in range(2)]
        rtmpd = [Res(), Res()]
        stc = SBV(xc, 0)
        xs7 = SBV(xc, 2048)
        xsn = SBV(xc, 2048 + NKC * NSB * 7)
        rxs7 = rxc
        rxsn = rxc
        rstc = rxc
        P.add("dve", lambda e: e.memset(Cst[0].v(0, (1, 4 * 514)), 0.0), writes=rCst[0])
        P.add("dve", lambda e: e.memset(Cb[0].v(0, (1, 4 * 514)), 0.0), writes=rCb[0])
        it = {"n": 0}

        def run_group(t0, n, L, sample):
            nch = n // L
            if not sample:
                if t0 == 0:
                    P.add("dve", lambda e: e.memset(xc.v(0, (515, NKC), (1, 3)), 0.0), writes=[rxc])
                    P.add("sp", lambda e: e.dma_start(out=xc.v(3, (515, NKC), (1, 512)),
                                                      in_=dv(S["QK"], 0, (NKC * NT, 128), (NT, NKC), (1, 512))),
                          writes=[rxc], dma=True)
                else:
                    P.add("sp", lambda e: e.dma_start(out=xc.v(0, (515, NKC), (1, 515)),
                                                      in_=dv(S["QK"], t0 - 3, (NKC * NT, 128), (NT, NKC), (1, 515))),
                          writes=[rxc], dma=True)
                for c in range(NKC):
                    s = c % 2
                    for j in (3, 2, 1, 0):
                        if j == 3:
                            P.add("dve", lambda e, s=s, c=c, j=j: e.tensor_scalar(
                                ycv[s].v(0, (1, 512)), xc.v(c * 515 + j, (1, 512)), cw.v(j * 16 + c, (1, 1)), None,
                                op0=ALU.mult), reads=[rxc, rcw], writes=[rycv[s]])
                        else:
                            P.add("dve", lambda e, s=s, c=c, j=j: e.scalar_tensor_tensor(
                                out=ycv[s].v(0, (1, 512)), in0=xc.v(c * 515 + j, (1, 512)),
                                scalar=cw.v(j * 16 + c, (1, 1)), in1=ycv[s].v(0, (1, 512)),
                                op0=ALU.mult, op1=ALU.add), reads=[rxc, rcw, rycv[s]], writes=[rycv[s]])
                    P.add("act", lambda e, s=s, c=c: e.activation(
                        out=qk.v(c * NG, (1, 512)), in_=ycv[s].v(0, (1, 512)), func=AF.Silu),
                        reads=[rycv[s]], writes=[rqk])
            else:
                P.add("sp", lambda e: e.dma_start(out=stc.v(0, (1, D), np_=48), in_=I["st_conv"]),
                      writes=[rstc], dma=True)
                P.add("sp", lambda e: e.dma_start(out=xsn.v(0, (64, NKC), (1, 64)),
                                                  in_=dv(S["QK"], NP_TOK, (NKC * NT, 128), (NT, NKC), (1, 64))),
                      writes=[rxsn], dma=True)
                for c4 in range(4):
                    for j in range(4):
                        c = c4 * 4 + j
                        P.add("pe", lambda e, c=c, j=j: e.transpose(
                            C.ps[0].v(j * 48, (1, 48)), stc.v(c * 128, (1, 128), np_=48),
                            C.ident.v(0, (1, 48), np_=48)), reads=[rstc, C.rconst], writes=[C.rps[0]])
                    P.add("dve", lambda e, c4=c4: e.tensor_copy(
                        xs7.v(c4 * 4 * 112, (112, 4), (7, NSB), (1, 3)), C.ps[0].v(0, (48, 4), (3, NSB), (1, 3))),
                        reads=[C.rps[0]], writes=[rxs7])
                P.add("dve", lambda e: e.tensor_copy(xs7.v(3, (112, NKC), (7, NSB), (1, 4)),
                                                     xsn.v(0, (64, NKC), (4, NSB), (1, 4))),
                      reads=[rxsn, rxs7], writes=[rxs7])
                for c in range(NKC):
                    s = c % 2
                    for j in (3, 2, 1, 0):
                        if j == 3:
                            P.add("dve", lambda e, s=s, c=c, j=j: e.tensor_scalar(
                                ycv[s].v(0, (4, NSB), (1, 4)), xs7.v(c * 112 + j, (7, NSB), (1, 4)),
                                cw.v(j * 16 + c, (1, 1)), None, op0=ALU.mult),
                                reads=[rxs7, rcw], writes=[rycv[s]])
                        else:
                            P.add("dve", lambda e, s=s, c=c, j=j: e.scalar_tensor_tensor(
                                out=ycv[s].v(0, (4, NSB), (1, 4)), in0=xs7.v(c * 112 + j, (7, NSB), (1, 4)),
                                scalar=cw.v(j * 16 + c, (1, 1)), in1=ycv[s].v(0, (4, NSB), (1, 4)),
                                op0=ALU.mult, op1=ALU.add), reads=[rxs7, rcw, rycv[s]], writes=[rycv[s]])
                    P.add("act", lambda e, s=s, c=c: e.activation(
                        out=qk.v(c * NG, (1, 64)), in_=ycv[s].v(0, (1, 64)), func=AF.Silu),
                        reads=[rycv[s]], writes=[rqk])
            P.add("sp", lambda e: e.dma_start(out=gtg.v(0, (1, n), np_=8), in_=dv(S["GT"], t0, (NT, 8), (1, n))),
                  writes=[rgtg], dma=True)
            for r_ in range(8):
                bank = r_ % 2
                P.add("pe", lambda e, r_=r_, bank=bank: e.matmul(
                    C.ps[bank].v(0, (1, n)), sel8.v(r_ * 128, (1, 128), np_=8), gtg.v(0, (1, n), np_=8),
                    start=True, stop=True), reads=[rgtg, rk], writes=[C.rps[bank]])
                dst = R[0] if r_ < 4 else R[1]
                rd = rR[0] if r_ < 4 else rR[1]
                P.add("act", lambda e, r_=r_, bank=bank, dst=dst: e.activation(
                    out=dst.v((r_ % 4) * NG, (1, n)), in_=C.ps[bank].v(0, (1, n)), func=AF.Copy),
                    reads=[C.rps[bank]], writes=[rd])

            def all4(Rt):
                return Rt.v(0, (NG, 4), (1, n))

            P.add("dve", lambda e: e.scalar_tensor_tensor(out=all4(R[2]), in0=all4(R[1]), scalar=-1.0, in1=all4(R[1]),
                                                          op0=ALU.mult, op1=ALU.max), reads=[rR[1]], writes=[rR[2]])
            P.add("act", lambda e: e.activation(out=all4(R[2]), in_=all4(R[2]), func=AF.Exp, scale=-1.0),
                  reads=[rR[2]], writes=[rR[2]])
            P.add("act", lambda e: e.activation(out=all4(R[2]), in_=all4(R[2]), func=AF.Ln, bias=C.oneb.v(0, (1, 1)),
                                                scale=1.0), reads=[rR[2], C.rconst], writes=[rR[2]])
            P.add("dve", lambda e: e.scalar_tensor_tensor(out=all4(R[1]), in0=all4(R[1]), scalar=0.0, in1=all4(R[2]),
                                                          op0=ALU.min, op1=ALU.subtract),
                  reads=[rR[1], rR[2]], writes=[rR[1]])
            for h in range(4):
                mask = mks.v(0, (1, n)) if sample else mk64.v(0, (1, n))
                P.add("dve", lambda e, h=h, mask=mask: e.tensor_tensor_scan(
                    R[2].v(h * NG, (1, n)), mask, R[1].v(h * NG, (1, n)), 0.0, op0=ALU.mult, op1=ALU.add),
                    reads=[rR[1], rk], writes=[rR[2]])
            if sample:
                P.add("dve", lambda e: e.tensor_copy(R[5].v(0, (NG, 4), (4, NSB), (1, 4)),
                                                     m0r.v(0, (16, 4), (1, NSB), (0, 4))), reads=[rk], writes=[rR[5]])
                P.add("dve", lambda e: e.tensor_tensor(out=all4(R[4]), in0=all4(R[5]), in1=all4(R[1]), op=ALU.add),
                      reads=[rR[5], rR[1]], writes=[rR[4]])
                P.add("dve", lambda e: e.tensor_tensor(out=all4(R[4]), in0=all4(R[4]),
                                                       in1=mks.v(64, (0, 4), (1, n)), op=ALU.add),
                      reads=[rR[4], rk], writes=[rR[4]])
                P.add("dve", lambda e: e.tensor_tensor(out=all4(R[4]), in0=all4(R[4]), in1=all4(R[0]), op=ALU.max),
                      reads=[rR[4], rR[0]], writes=[rR[4]])
                P.add("dve", lambda e: e.tensor_tensor(out=all4(R[3]), in0=all4(R[1]),
                                                       in1=mks.v(128, (0, 4), (1, n)), op=ALU.add),
                      reads=[rR[1], rk], writes=[rR[3]])
                for h in range(4):
                    P.add("dve", lambda e, h=h: e.tensor_tensor_scan(
                        R[3].v(h * NG, (1, n)), R[3].v(h * NG, (1, n)), R[4].v(h * NG, (1, n)), 0.0,
                        op0=ALU.add, op1=ALU.max), reads=[rR[3], rR[4]], writes=[rR[3]])
            else:
                for h in range(4):
                    init = 0.0 if t0 == 0 else carry.v(h, (1, 1))
                    P.add("dve", lambda e, h=h, init=init: e.tensor_tensor_scan(
                        R[3].v(h * NG, (1, n)), R[1].v(h * NG, (1, n)), R[0].v(h * NG, (1, n)), init,
                        op0=ALU.add, op1=ALU.max), reads=[rR[1], rR[0], rcarry], writes=[rR[3]])
                if t0 == 0:
                    P.add("dve", lambda e: e.memset(R[5].v(0, (NG, 4), (1, L)), 0.0), writes=[rR[5]])
                else:
                    P.add("dve", lambda e: e.tensor_copy(R[5].v(0, (NG, 4), (1, L)), carry.v(0, (1, 4), (0, L))),
                          reads=[rcarry], writes=[rR[5]])
                P.add("dve", lambda e: e.tensor_copy(R[5].v(L, (NG, 4), (L, nch - 1), (1, L)),
                                                     R[3].v(L - 1, (NG, 4), (L, nch - 1), (0, L))),
                      reads=[rR[3], rR[5]], writes=[rR[5]])
                P.add("dve", lambda e: e.tensor_copy(carry.v(0, (1, 4)), R[3].v(n - 1, (NG, 4))),
                      reads=[rR[3], rR[5]], writes=[rcarry])
            P.add("dve", lambda e: e.scalar_tensor_tensor(out=all4(R[4]), in0=all4(R[0]), scalar=-LN16, in1=all4(R[2]),
                                                          op0=ALU.add, op1=ALU.subtract),
                  reads=[rR[0], rR[2], rR[3]], writes=[rR[4]])
            P.add("dve", lambda e: e.tensor_tensor(out=all4(R[2]), in0=all4(R[2]), in1=all4(R[3]), op=ALU.subtract),
                  reads=[rR[2], rR[3], rR[4]], writes=[rR[2]])
            P.add("dve", lambda e: e.tensor_tensor(out=all4(R[1]), in0=all4(R[5]), in1=all4(R[2]), op=ALU.add),
                  reads=[rR[5], rR[2], rR[3]], writes=[rR[1]])
            P.add("act", lambda e: e.activation(out=all4(R[1]), in_=all4(R[1]), func=AF.Exp),
                  reads=[rR[1]], writes=[rR[1]])
            P.add("act", lambda e: e.activation(out=all4(R[5]), in_=all4(R[3]), func=AF.Exp, scale=-1.0),
                  reads=[rR[3], rR[1]], writes=[rR[5]])
            if sample:
                for h in range(4):
                    P.add("sp", lambda e, h=h: e.dma_start(out=dv(O["m_s"], h, (64, 1), (4, NSB)),
                                                           in_=R[3].v(h * NG + 3, (4, NSB), np_=1)),
                          reads=[rR[3]], dma=True)
            elif t0 + n == NP_TOK:
                P.add("sp", lambda e: e.dma_start(out=dv(O["m_p"], 0, (4, 1), (1, 4)),
                                                  in_=R[3].v(n - 1, (NG, 4), np_=1)), reads=[rR[3]], dma=True)
            P.add("sp", lambda e: e.dma_start(out=omt.v(0, (NG, 8), (1, n)),
                                              in_=dv(S["OM"], t0, (8 * NT, 128), (NT, 8), (1, n))),
                  writes=[romt], dma=True)
            for c in range(nch):
                cs = c % 2
                tok = t0 + c * L
                P.add("sp", lambda e, cs=cs, tok=tok: e.dma_start(
                    out=vt[cs].v(0, (1, 1028), np_=L), in_=dv(S["VM"], tok * 1028, (1028, L), (1, 1028))),
                    writes=[rvt[cs]], dma=True)
                if sample:
                    st = c % 2
                    P.add("sp", lambda e, st=st, c=c: e.dma_start(
                        out=Cst[st].v(0, (514, 4), (257, 2), (1, 256)),
                        in_=dv(I["st_C"], c * 4 * 65536, (256, 128), (65536, 4), (128 * 256, 2), (1, 256))),
                        writes=rCst[st], dma=True)
                    P.add("sp", lambda e, st=st, c=c: e.dma_start(
                        out=Cst[st].v(256, (514, 4), (257, 2), (1, 1)),
                        in_=dv(I["st_n"], c * 1024, (1, 128), (256, 4), (128, 2), (1, 1))),
                        writes=rCst[st], dma=True)
                    P.add("act", lambda e, st=st: e.activation(out=Cb[st].v(0, (1, 4 * 514)),
                                                               in_=Cst[st].v(0, (1, 4 * 514)), func=AF.Copy),
                          reads=rCst[st], writes=rCb[st])
                else:
                    st = 0
                for h in range(4):
                    s = it["n"] % 2
                    it["n"] += 1
                    b0 = 4 * s
                    col = h * NG + c * L
                    qc, kc_ = 2 * h, 8 + 2 * h
                    P.add("pe", lambda e, col=col, b0=b0: e.transpose(
                        C.ps[b0].v(64, (1, 128), np_=L), R[4].v(col, (1, L)), C.ident.v(0, (1, 128))),
                        reads=[rR[4], C.rconst], writes=[C.rps[b0]])
                    P.add("dve", lambda e, s=s, b0=b0: e.tensor_copy(acol[s].v(0, (1, 1), np_=L),
                                                                     C.ps[b0].v(64, (1, 1), np_=L)),
                          reads=[C.rps[b0]], writes=[racol[s]])
                    for dc in range(2):
                        P.add("pe", lambda e, dc=dc, b0=b0, c=c: e.matmul(
                            C.ps[b0].v(0, (1, L), np_=L), qk.v((kc_ + dc) * NG + c * L, (1, L)),
                            qk.v((qc + dc) * NG + c * L, (1, L)), start=(dc == 0), stop=(dc == 1)),
                            reads=[rqk], writes=[C.rps[b0]])
                    P.add("act", lambda e, s=s, col=col: e.activation(
                        out=DT[s].v(0, (1, L), np_=L), in_=R[2].v(col, (1, L), np_=L), func=AF.Exp,
                        bias=acol[s].v(0, (1, 1), np_=L), scale=1.0), reads=[rR[2], racol[s]], writes=[rDT[s]])
                    P.add("pool", lambda e, s=s: e.tensor_tensor(
                        out=DT[s].v(0, (1, L), np_=L), in0=DT[s].v(0, (1, L), np_=L), in1=tri.v(0, (1, L), np_=L),
                        op=ALU.mult), reads=[rDT[s], rk], writes=[rDT[s]])
                    P.add("dve", lambda e, s=s, b0=b0: e.tensor_tensor(
                        out=Wt[s].v(0, (1, L), np_=L), in0=C.ps[b0].v(0, (1, L), np_=L), in1=DT[s].v(0, (1, L), np_=L),
                        op=ALU.mult), reads=[C.rps[b0], rDT[s]], writes=[rWt[s]])
                    P.add("dve", lambda e, s=s, col=col, c=c: e.tensor_tensor(
                        out=qs[s].v(0, (64, 2), (1, L)), in0=qk.v(qc * NG + c * L, (NG, 2), (1, L)),
                        in1=R[1].v(col, (0, 2), (1, L)), op=ALU.mult), reads=[rqk, rR[1]], writes=[rqs[s]])
                    for ec in range(2):
                        P.add("pe", lambda e, s=s, cs=cs, ec=ec, b0=b0: e.matmul(
                            C.ps[b0 + 1].v(ec * 64, (1, L)), vt[cs].v(h * 257 + ec * 128, (1, 128), np_=L),
                            Wt[s].v(0, (1, L), np_=L), start=True, stop=False),
                            reads=[rvt[cs], rWt[s]], writes=[C.rps[b0 + 1]])
                        for dc in range(2):
                            P.add("pe", lambda e, s=s, st=st, ec=ec, dc=dc, b0=b0: e.matmul(
                                C.ps[b0 + 1].v(ec * 64, (1, L)), Cb[st].v(h * 514 + dc * 257 + ec * 128, (1, 128)),
                                qs[s].v(dc * 64, (1, L)), start=False, stop=(dc == 1)),
                                reads=[rCb[st][h], rqs[s]], writes=[C.rps[b0 + 1]])
                    P.add("pe", lambda e, s=s, b0=b0: e.matmul(
                        C.ps[b0 + 1].v(128, (1, L)), C.ones_b.v(0, (1, 128), np_=L), Wt[s].v(0, (1, L), np_=L),
                        start=True, stop=False), reads=[rWt[s], C.rconst], writes=[C.rps[b0 + 1]])
                    for dc in range(2):
                        P.add("pe", lambda e, s=s, st=st, dc=dc, b0=b0: e.matmul(
                            C.ps[b0 + 1].v(128, (1, L)), Cb[st].v(h * 514 + dc * 257 + 256, (0, 128)),
                            qs[s].v(dc * 64, (1, L)), start=False, stop=(dc == 1)),
                            reads=[rCb[st][h], rqs[s]], writes=[C.rps[b0 + 1]])
                    P.add("act", lambda e, s=s, b0=b0: e.activation(
                        out=rr[s].v(0, (1, L)), in_=C.ps[b0 + 1].v(128, (1, L)), func=AF.Abs),
                        reads=[C.rps[b0 + 1]], writes=[rrr[s]])
                    P.add("dve", lambda e, s=s, col=col: e.tensor_tensor(
                        out=rr[s].v(0, (1, L)), in0=rr[s].v(0, (1, L)), in1=R[5].v(col, (1, L)), op=ALU.max),
                        reads=[rrr[s], rR[5]], writes=[rrr[s]])
                    P.add("dve", lambda e, s=s: e.reciprocal(rr[s].v(0, (1, L)), rr[s].v(0, (1, L))),
                          reads=[rrr[s]], writes=[rrr[s]])
                    P.add("dve", lambda e, s=s, b0=b0, c=c: e.tensor_tensor(
                        out=hbuf.v(2 * h * NG + c * L, (NG, 2), (1, L)), in0=C.ps[b0 + 1].v(0, (64, 2), (1, L)),
                        in1=rr[s].v(0, (0, 2), (1, L)), op=ALU.mult), reads=[C.rps[b0 + 1], rrr[s]], writes=[rhb])
                    for dc in range(2):
                        P.add("pe", lambda e, dc=dc, b0=b0, c=c: e.transpose(
                            psb[b0].v(512 + dc * 128, (1, 128), np_=L), qk.v((kc_ + dc) * NG + c * L, (1, L)),
                            C.identb.v(0, (1, 128))), reads=[rqk, C.rconst], writes=[C.rps[b0]])
                    P.add("act", lambda e, s=s, col=col: e.activation(
                        out=wl[s].v(0, (1, 1), np_=L), in_=acol[s].v(0, (1, 1), np_=L), func=AF.Exp,
                        bias=R[2].v(col + L - 1, (1, 1), np_=L), scale=1.0), reads=[racol[s], rR[2]], writes=[rwl[s]])
                    P.add("dve", lambda e, s=s, b0=b0: e.tensor_scalar(
                        kw[s].v(0, (1, 256), np_=L), psb[b0].v(512, (1, 256), np_=L), wl[s].v(0, (1, 1), np_=L), None,
                        op0=ALU.mult), reads=[C.rps[b0], rwl[s]], writes=[rkw[s]])
                    for dc in range(2):
                        P.add("pe", lambda e, s=s, cs=cs, dc=dc, b0=b0: e.matmul(
                            C.ps[b0 + 2 + dc].v(0, (1, 257)), kw[s].v(dc * 128, (1, 128), np_=L),
                            vt[cs].v(h * 257, (1, 257), np_=L), start=True, stop=True),
                            reads=[rkw[s], rvt[cs]], writes=[C.rps[b0 + 2 + dc]])
                        P.add("dve", lambda e, st=st, dc=dc, b0=b0, col=col: e.scalar_tensor_tensor(
                            out=Cst[st].v(h * 514 + dc * 257, (1, 257)), in0=Cst[st].v(h * 514 + dc * 257, (1, 257)),
                            scalar=R[1].v(col + L - 1, (1, 1)), in1=C.ps[b0 + 2 + dc].v(0, (1, 257)),
                            op0=ALU.mult, op1=ALU.add), reads=[rCst[st][h], rR[1], C.rps[b0 + 2 + dc]],
                            writes=[rCst[st][h]])
                    if not sample:
                        P.add("act", lambda e, st=st: e.activation(
                            out=Cb[st].v(h * 514, (1, 514)), in_=Cst[st].v(h * 514, (1, 514)), func=AF.Copy),
                            reads=[rCst[st][h]], writes=[rCb[st][h]])
                if sample:
                    P.add("sp", lambda e, st=st, c=c: e.dma_start(
                        out=dv(O["C_s"], c * 4 * 65536, (256, 128), (65536, 4), (128 * 256, 2), (1, 256)),
                        in_=Cst[st].v(0, (514, 4), (257, 2), (1, 256))), reads=rCst[st], dma=True)
                    P.add("sp", lambda e, st=st, c=c: e.dma_start(
                        out=dv(O["n_s"], c * 1024, (1, 128), (256, 4), (128, 2), (1, 1)),
                        in_=Cst[st].v(256, (514, 4), (257, 2), (1, 1))), reads=rCst[st], dma=True)
            if (not sample) and t0 + n == NP_TOK:
                P.add("sp", lambda e: e.dma_start(
                    out=dv(O["C_p"], 0, (256, 128), (65536, 4), (128 * 256, 2), (1, 256)),
                    in_=Cst[0].v(0, (514, 4), (257, 2), (1, 256))), reads=rCst[0], dma=True)
                P.add("sp", lambda e: e.dma_start(
                    out=dv(O["n_p"], 0, (1, 128), (256, 4), (128, 2), (1, 1)),
                    in_=Cst[0].v(256, (514, 4), (257, 2), (1, 1))), reads=rCst[0], dma=True)
            for h in range(4):
                for ec in range(2):
                    P.add("act", lambda e, h=h, ec=ec: e.activation(
                        out=sq[ec].v(0, (1, n)), in_=hbuf.v((2 * h + ec) * NG, (1, n)), func=AF.Square),
                        reads=[rhb], writes=[rsq[ec]])
                    P.add("pe", lambda e, ec=ec: e.matmul(
                        C.ps[0].v(0, (1, n)), C.ones_b.v(0, (1, 128)), sq[ec].v(0, (1, n)),
                        start=(ec == 0), stop=(ec == 1)), reads=[rsq[ec], C.rconst], writes=[C.rps[0]])
                P.add("act", lambda e: e.activation(out=rs_.v(0, (1, n)), in_=C.ps[0].v(0, (1, n)), func=AF.Sqrt,
                                                    bias=C.epsb.v(0, (1, 1)), scale=1.0 / 256),
                      reads=[C.rps[0], C.rconst], writes=[rrs])
                P.add("dve", lambda e: e.reciprocal(rs_.v(0, (1, n)), rs_.v(0, (1, n))), reads=[rrs], writes=[rrs])
                for ec in range(2):
                    ch = 2 * h + ec
                    P.add("act", lambda e, ch=ch, ec=ec: e.activation(
                        out=sg[ec].v(0, (1, n)), in_=omt.v(ch * NG, (1, n)), func=AF.Sigmoid),
                        reads=[romt], writes=[rsg[ec]])
                    P.add("dve", lambda e, ch=ch, ec=ec: e.scalar_tensor_tensor(
                        out=tmpd[ec].v(0, (1, n)), in0=hbuf.v(ch * NG, (1, n)), scalar=outg.v(ch, (1, 1)),
                        in1=rs_.v(0, (1, n)), op0=ALU.mult, op1=ALU.mult), reads=[rhb, rrs, rcw], writes=[rtmpd[ec]])
                    P.add("dve", lambda e, ch=ch, ec=ec: e.tensor_tensor(
                        out=mot.v(ch * NG, (1, n)), in0=tmpd[ec].v(0, (1, n)), in1=sg[ec].v(0, (1, n)), op=ALU.mult),
                        reads=[rtmpd[ec], rsg[ec]], writes=[rmot])
            P.add("sp", lambda e: e.dma_start(out=dv(S["MOAO"], t0, (NKC * NT, 128), (NT, 8), (1, n)),
                                              in_=mot.v(0, (NG, 8), (1, n))), reads=[rmot], dma=True)

        for g in range(4):
            run_group(g * 512, 512, 64, False)
        run_group(NP_TOK, 64, 4, True)
        P.barrier()
        C.sb = C.sb_save


NEGB = -1.0e30
ATT_SCALE = 128 ** -0.5


def t5_bucket_np(d):
    d = np.maximum(np.asarray(d), 0)
    ratio = np.log(np.maximum(d, 1).astype(np.float32) / np.float32(16)) / np.float32(np.log(8.0))
    large = 16 + (ratio * np.float32(16)).astype(np.int32)
    large = np.minimum(large, 31)
    return np.where(d < 16, d, large)


def host_bias_tables(t5):
    s = np.arange(128)[:, None]
    t = np.arange(128)[None, :]
    kinds = [t5_bucket_np(t - s), t5_bucket_np(t - s + 128), np.full((128, 128), 31)]
    bdp = np.zeros((2, 3, 128, 4, 128), np.float32)
    for g in range(2):
        for k in range(3):
            for h in range(4):
                bdp[g, k, :, h, :] = t5[kinds[k], 4 * g + h]
    sk = np.arange(128)[:, None, None]
    kb = np.arange(16)[None, :, None]
    tt = np.arange(4)[None, None, :]
    bk = t5_bucket_np(2048 + tt - (kb * 128 + sk))
    bsp = np.zeros((2, 128, 16, 4, 4), np.float32)
    bsn = np.zeros((2, 4, 4, 4), np.float32)
    bn = t5_bucket_np(np.arange(4)[None, :] - np.arange(4)[:, None])
    for g in range(2):
        for h in range(4):
            bsp[g, :, :, h, :] = t5[bk, 4 * g + h]
            bsn[g, :, h, :] = t5[bn, 4 * g + h]
    return bdp.reshape(6, 128, 512), bsp.reshape(2, 128, 256), bsn.reshape(2, 4, 16)


def phase_dsa(C):
    P, I, S, O, nc = C.P, C.I, C.S, C.O, C.nc
    with contextlib.ExitStack() as es:
        def sb(name, width, dt):
            return SB(es.enter_context(nc.sbuf_tensor("ds_" + name, [128, padw(width, dt)], dt)), padw(width, dt), dt)
        psb = [SB(p.t.bitcast(BF16), 1024, BF16) for p in C.ps]
        ki2 = sb("ki2", NT, BF16)
        ka = sb("ka", 2 * NT, BF16)
        va = sb("va", 16 * 256, BF16)
        qi = sb("qi", 4 * NT, BF16)
        qa = sb("qa", 8 * NT, BF16)
        wi = sb("wi", 17 * 8, F32)
        bdp = sb("bdp", 6 * 512, F32)
        negtri = sb("negtri", 128, F32)
        rin = Res()
        P.add("sp", lambda e: e.dma_start(out=ki2.v(0, (1, NT)), in_=S["KI2"]), writes=[rin], dma=True)
        P.add("sp", lambda e: e.dma_start(out=ka.v(0, (1, 2 * NT)), in_=dv(S["KA"], 0, (2 * NT, 128), (1, 2 * NT))),
              writes=[rin], dma=True)
        P.add("sp", lambda e: e.dma_start(out=va.v(0, (256, 16), (1, 256)),
                                          in_=dv(S["VA"], 0, (256, 128), (128 * 256, 16), (1, 256))),
              writes=[rin], dma=True)
        P.add("sp", lambda e: e.dma_start(out=qi.v(0, (1, 4 * NT)), in_=dv(S["QI"], 0, (4 * NT, 128), (1, 4 * NT))),
              writes=[rin], dma=True)
        P.add("sp", lambda e: e.dma_start(out=qa.v(0, (1, 8 * NT)), in_=dv(S["QA"], 0, (8 * NT, 128), (1, 8 * NT))),
              writes=[rin], dma=True)
        P.add("sp", lambda e: e.dma_start(out=wi.v(0, (8, 16), (1, 8)),
                                          in_=dv(S["WI"], 0, (8, 128), (128 * 8, 16), (1, 8))), writes=[rin], dma=True)
        P.add("sp", lambda e: e.dma_start(out=wi.v(128, (1, 8), np_=64), in_=dv(S["WI"], NP_TOK * 8, (8, 64), (1, 8))),
              writes=[rin], dma=True)
        P.add("sp", lambda e: e.dma_start(out=bdp.v(0, (512, 6), (1, 512)),
                                          in_=dv(I["bdp"], 0, (512, 128), (128 * 512, 6), (1, 512))),
              writes=[rin], dma=True)
        P.add("sp", lambda e: e.dma_start(out=negtri.v(0, (1, 128)), in_=I["negtri"]), writes=[rin], dma=True)
        acc = sb("acc", 2048 + 64, F32)
        racc = Res()
        work = sb("work", 2048 + 64, F32)
        rwork = Res()
        rl = [sb("rl%d" % i, 512, F32) for i in range(2)]
        rrl = [Res(), Res()]
        mx = sb("mx", 8, F32)
        rmx = Res()
        msk = sb("msk", 2048 + 128, BF16)
        rmsk = Res()
        mskT = sb("mskT", 17 * 128, BF16)
        rmskT = Res()
        lg = [sb("lg%d" % i, 512, F32) for i in range(2)]
        rlg = [Res(), Res()]
        ex = [sb("ex%d" % i, 512, BF16) for i in range(2)]
        rex = [Res(), Res()]
        pT = [sb("pT%d" % i, 512, BF16) for i in range(2)]
        rpT = [Res(), Res()]
        rden = sb("rden", 512, F32)
        rrden = Res()
        aot = [sb("aot%d" % i, 8 * 128, BF16) for i in range(2)]
        raot = [Res(), Res()]
        cnt = {"e": 0, "l": 0}
        def scores(i):
            q0 = i * 128
            S_i = (i + 1) * 128
            npieces = (S_i + 511) // 512
            for pc in range(npieces):
                k0 = pc * 512
                kn = min(512, S_i - k0)
                for j in range(8):
                    pbase = 64 * (j % 2)
                    bank = cnt["e"] % 2
                    s = cnt["e"] % 2
                    cnt["e"] += 1
                    P.add("pe", lambda e, j=j, pbase=pbase, bank=bank, k0=k0, kn=kn, q0=q0: e.matmul(
                        C.ps[bank].v(0, (1, kn)), qi.v((j // 2) * NT + q0, (1, 128), p0=pbase, np_=64),
                        ki2.v(k0, (1, kn), p0=pbase, np_=64), start=True, stop=True),
                        reads=[rin], writes=[C.rps[bank]])
                    P.add("act", lambda e, s=s, bank=bank, kn=kn: e.activation(
                        out=rl[s].v(0, (1, kn)), in_=C.ps[bank].v(0, (1, kn)), func=AF.Relu),
                        reads=[C.rps[bank]], writes=[rrl[s]])
                    if j == 0:
                        P.add("dve", lambda e, s=s, k0=k0, kn=kn, i=i, j=j: e.tensor_scalar(
                            acc.v(k0, (1, kn)), rl[s].v(0, (1, kn)), wi.v(i * 8 + j, (1, 1)), None, op0=ALU.mult),
                            reads=[rrl[s], rin], writes=[racc])
                    else:
                        P.add("dve", lambda e, s=s, k0=k0, kn=kn, i=i, j=j: e.scalar_tensor_tensor(
                            out=acc.v(k0, (1, kn)), in0=rl[s].v(0, (1, kn)), scalar=wi.v(i * 8 + j, (1, 1)),
                            in1=acc.v(k0, (1, kn)), op0=ALU.mult, op1=ALU.add),
                            reads=[rrl[s], rin, racc], writes=[racc])
            P.add("dve", lambda e, q0=q0: e.tensor_tensor(
                out=acc.v(q0, (1, 128)), in0=acc.v(q0, (1, 128)), in1=negtri.v(0, (1, 128)), op=ALU.add),
                reads=[racc, rin], writes=[racc])

        def topk_gen(i):
            S_i = (i + 1) * 128
            if i >= 2:
                for r_ in range(32):
                    src = acc if r_ == 0 else work
                    rsrc = racc if r_ == 0 else rwork
                    P.add("dve", lambda e, src=src, S_i=S_i: e.max(out=mx.v(0, (1, 8)), in_=src.v(0, (1, S_i))),
                          reads=[rsrc], writes=[rmx])
                    if r_ < 31:
                        P.add("dve", lambda e, src=src, S_i=S_i: e.match_replace(
                            out=work.v(0, (1, S_i)), in_to_replace=mx.v(0, (1, 8)), in_values=src.v(0, (1, S_i)),
                            imm_value=NEGB), reads=[rsrc, rmx], writes=[rwork])
                    yield
                P.add("dve", lambda e, S_i=S_i: e.tensor_scalar(
                    msk.v(0, (1, S_i)), acc.v(0, (1, S_i)), mx.v(7, (1, 1)), None, op0=ALU.is_ge),
                    reads=[racc, rmx], writes=[rmsk])
            else:
                P.add("dve", lambda e, S_i=S_i: e.tensor_scalar(
                    msk.v(0, (1, S_i)), acc.v(0, (1, S_i)), -1.0e29, None, op0=ALU.is_ge),
                    reads=[racc], writes=[rmsk])
            yield

        def masks(i):
            for kb in range(i + 1):
                P.add("pe", lambda e, kb=kb: e.transpose(
                    psb[kb // 8].v((kb % 8) * 128, (1, 128)), msk.v(kb * 128, (1, 128)), C.identb.v(0, (1, 128))),
                    reads=[rmsk, C.rconst], writes=[C.rps[kb // 8]])
            for half in range((i + 8) // 8):
                nb_ = min(8, i + 1 - half * 8)
                P.add("act", lambda e, half=half, nb_=nb_: e.activation(
                    out=mskT.v(half * 1024, (1, nb_ * 128)), in_=psb[half].v(0, (1, nb_ * 128)), func=AF.Copy),
                    reads=[C.rps[half]], writes=[rmskT])

        def attn_gen(i):
            q0 = i * 128
            ao_s = i % 2
            for g in range(2):
                ob, db = (4, 5) if g == 0 else (6, 7)
                for kb in range(i + 1):
                    kind = 0 if kb == i else (1 if kb == i - 1 else 2)
                    l = cnt["l"] % 2
                    cnt["l"] += 1
                    lb = 2 + l
                    P.add("pe", lambda e, g=g, kb=kb, lb=lb, q0=q0: e.matmul(
                        C.ps[lb].v(0, (128, 4), (1, 128)), ka.v(g * NT + kb * 128, (1, 128)),
                        qa.v(4 * g * NT + q0, (NT, 4), (1, 128)), start=True, stop=True),
                        reads=[rin], writes=[C.rps[lb]])
                    P.add("dve", lambda e, l=l, lb=lb, g=g, kind=kind: e.scalar_tensor_tensor(
                        out=lg[l].v(0, (1, 512)), in0=C.ps[lb].v(0, (1, 512)), scalar=ATT_SCALE,
                        in1=bdp.v((g * 3 + kind) * 512, (1, 512)), op0=ALU.mult, op1=ALU.add),
                        reads=[C.rps[lb], rin], writes=[rlg[l]])
                    P.add("act", lambda e, l=l: e.activation(out=ex[l].v(0, (1, 512)), in_=lg[l].v(0, (1, 512)),
                                                             func=AF.Exp), reads=[rlg[l]], writes=[rex[l]])
                    P.add("dve", lambda e, l=l, kb=kb: e.tensor_tensor(
                        out=pT[l].v(0, (128, 4), (1, 128)), in0=ex[l].v(0, (128, 4), (1, 128)),
                        in1=mskT.v(kb * 128, (0, 4), (1, 128)), op=ALU.mult),
                        reads=[rex[l], rmskT], writes=[rpT[l]])
                    P.add("pe", lambda e, l=l, g=g, kb=kb, ob=ob, i=i: e.matmul(
                        C.ps[ob].v(0, (1, 512)), va.v(kb * 256 + g * 128, (1, 128)), pT[l].v(0, (1, 512)),
                        start=(kb == 0), stop=(kb == i)), reads=[rin, rpT[l]], writes=[C.rps[ob]])
                    P.add("pe", lambda e, l=l, kb=kb, db=db, i=i: e.matmul(
                        C.ps[db].v(0, (1, 512)), C.ones_b.v(0, (1, 128)), pT[l].v(0, (1, 512)),
                        start=(kb == 0), stop=(kb == i)), reads=[C.rconst, rpT[l]], writes=[C.rps[db]])
                    yield
                P.add("dve", lambda e, db=db: e.reciprocal(rden.v(0, (1, 512)), C.ps[db].v(0, (1, 512))),
                      reads=[C.rps[db]], writes=[rrden])
                P.add("dve", lambda e, ob=ob, g=g, ao_s=ao_s: e.tensor_tensor(
                    out=aot[ao_s].v(4 * g * 128, (1, 512)), in0=C.ps[ob].v(0, (1, 512)), in1=rden.v(0, (1, 512)),
                    op=ALU.mult), reads=[C.rps[ob], rrden], writes=[raot[ao_s]])
                yield
            P.add("sp", lambda e, ao_s=ao_s, q0=q0: e.dma_start(
                out=dv(S["MOAO"], 8 * NT + q0, (NKC * NT, 128), (NT, 8), (1, 128)),
                in_=aot[ao_s].v(0, (128, 8), (1, 128))), reads=[raot[ao_s]], dma=True)
            yield

        if "dsa_noprompt" not in C.dbg:
            scores(0)
            for _ in topk_gen(0):
                pass
            for i in range(16):
                masks(i)
                ga = attn_gen(i)
                if i + 1 < 16:
                    scores(i + 1)
                    gt = topk_gen(i + 1)
                    nt = 33 if i + 1 >= 2 else 1
                else:
                    gt = iter(())
                    nt = 0
                na = 2 * (i + 1) + 3
                a_done = t_done = 0
                while a_done < na or t_done < nt:
                    if t_done >= nt or (a_done < na and a_done * max(nt, 1) <= t_done * na):
                        next(ga, None)
                        a_done += 1
                    else:
                        next(gt, None)
                        t_done += 1
                for _ in ga:
                    pass
                for _ in gt:
                    pass
        P.barrier()


def phase_dsa_sample(C):
    P, I, S, O, nc = C.P, C.I, C.S, C.O, C.nc
    NK = 2048
    with contextlib.ExitStack() as es:
        def sb(name, width, dt):
            return SB(es.enter_context(nc.sbuf_tensor("dq_" + name, [128, padw(width, dt)], dt)), padw(width, dt), dt)
        psb = [SB(p.t.bitcast(BF16), 1024, BF16) for p in C.ps]
        ki2s = sb("ki2s", 64, BF16)
        kas = sb("kas", 2 * 64, BF16)
        qis = sb("qis", 4 * 64, BF16)
        qas = sb("qas", 8 * 64, BF16)
        wis = sb("wis", 16 * 8, F32)
        vns = sb("vns", 16 * 256, BF16)
        bsp = sb("bsp", 2 * 256, F32)
        bsn = sb("bsn", 2 * 16, F32)
        negtri = sb("negtri", 128, F32)
        ptr = sb("ptr", 256, I32)
        pidx = sb("pidx", 1, F32)
        idx = sb("idx", 256, U32)
        rin = Res()
        t0 = NP_TOK
        P.add("sp", lambda e: e.dma_start(out=ki2s.v(0, (1, 64)), in_=dv(S["KI2"], t0, (NT, 128), (1, 64))),
              writes=[rin], dma=True)
        P.add("sp", lambda e: e.dma_start(out=kas.v(0, (64, 2), (1, 64)),
                                          in_=dv(S["KA"], t0, (2 * NT, 128), (NT, 2), (1, 64))), writes=[rin], dma=True)
        P.add("sp", lambda e: e.dma_start(out=qis.v(0, (64, 4), (1, 64)),
                                          in_=dv(S["QI"], t0, (4 * NT, 128), (NT, 4), (1, 64))), writes=[rin], dma=True)
        P.add("sp", lambda e: e.dma_start(out=qas.v(0, (64, 8), (1, 64)),
                                          in_=dv(S["QA"], t0, (8 * NT, 128), (NT, 8), (1, 64))), writes=[rin], dma=True)
        P.add("sp", lambda e: e.dma_start(out=wis.v(0, (8, 16), (1, 8), np_=4),
                                          in_=dv(S["WI"], t0 * 8, (8, 4), (32, 16), (1, 8))), writes=[rin], dma=True)
        P.add("sp", lambda e: e.dma_start(out=vns.v(0, (256, 16), (1, 256), np_=4),
                                          in_=dv(S["VA"], t0 * 256, (256, 4), (1024, 16), (1, 256))),
              writes=[rin], dma=True)
        P.add("sp", lambda e: e.dma_start(out=bsp.v(0, (256, 2), (1, 256)),
                                          in_=dv(I["bsp"], 0, (256, 128), (128 * 256, 2), (1, 256))),
              writes=[rin], dma=True)
        P.add("sp", lambda e: e.dma_start(out=bsn.v(0, (16, 2), (1, 16), np_=4),
                                          in_=dv(I["bsn"], 0, (16, 4), (64, 2), (1, 16))), writes=[rin], dma=True)
        P.add("sp", lambda e: e.dma_start(out=negtri.v(0, (1, 128)), in_=I["negtri"]), writes=[rin], dma=True)
        P.add("sp", lambda e: e.dma_start(out=ptr.v(0, (1, 256)), in_=dv(I["ptab"], 0, (0, 128), (1, 256))),
              writes=[rin], dma=True)
        P.add("sp", lambda e: e.dma_start(out=pidx.v(0, (1, 1)), in_=I["pidx"]), writes=[rin], dma=True)
        ridx = Res()
        P.add("dve", lambda e: e.scalar_tensor_tensor(out=idx.v(0, (1, 256)), in0=ptr.v(0, (1, 256)), scalar=128.0,
                                                      in1=pidx.v(0, (0, 256)), op0=ALU.mult, op1=ALU.add),
              reads=[rin], writes=[ridx])
        kis = [sb("kis%d" % i, 16 * 128, F32) for i in range(2)]
        rkis = [Res(), Res()]
        kiT2 = [sb("kiT%d" % i, NK + 64, BF16) for i in range(2)]
        rkiT2 = [Res(), Res()]
        accb = [sb("accb%d" % i, NK + 64, F32) for i in range(2)]
        raccb = [Res(), Res()]
        rl = [sb("rl%d" % i, 512, F32) for i in range(2)]
        rrl = [Res(), Res()]
        accS = sb("accS", NK + 64, F32)
        raccS = Res()
        work = sb("work", NK + 64, F32)
        rwork = Res()
        mx = sb("mx", 8, F32)
        rmx = Res()
        msk = sb("msk", NK + 64, BF16)
        rmsk = Res()
        mskT = sb("mskT", 17 * 64, BF16)
        rmskT = Res()
        kcs = [sb("kcs%d" % i, 16 * 256, BF16) for i in range(2)]
        rkcs = [Res(), Res()]
        vcs = [sb("vcs%d" % i, 16 * 256, BF16) for i in range(2)]
        rvcs = [Res(), Res()]
        kTs2 = [sb("kTs%d" % i, 2 * NK, BF16) for i in range(2)]
        rkTs2 = [Res(), Res()]
        lgS = sb("lgS", 256 + 16, F32)
        rlgS = Res()
        lgn = sb("lgn", 16, F32)
        rlgn = Res()
        exS = sb("exS", 256, BF16)
        rexS = Res()
        exn = sb("exn", 16, BF16)
        rexn = Res()
        pTs = sb("pTs", 256, BF16)
        rpTs = Res()
        pTn = sb("pTn", 16, BF16)
        rpTn = Res()
        rden = sb("rden", 16, F32)
        rrden = Res()
        aoS = sb("aoS", 8 * 64, BF16)
        raoS = Res()
        cnt = {"e": 0}
        IO = bass.IndirectOffsetOnAxis
        for b in range(NSB):
            ks = b % 2
            kiT, rkiT = kiT2[ks], rkiT2[ks]
            for pg in range(16):
                P.add("pool", lambda e, ks=ks, pg=pg, b=b: e.indirect_dma_start(
                    out=kis[ks].v(pg * 128, (1, 64)), out_offset=None, in_=I["cidx"],
                    in_offset=IO(ap=idx.v(b * 16 + pg, (1, 1)), axis=0)), reads=[ridx], writes=[rkis[ks]], dma=True)
            P.add("act", lambda e, ks=ks: e.activation(out=kis[ks].v(64, (128, 16), (1, 64)),
                                                       in_=kis[ks].v(0, (128, 16), (1, 64)), func=AF.Copy),
                  reads=[rkis[ks]], writes=[rkis[ks]])
            for q4 in range(4):
                bank = 2 + q4 % 2
                for j in range(4):
                    pg = q4 * 4 + j
                    P.add("pe", lambda e, ks=ks, pg=pg, j=j, bank=bank: e.transpose(
                        C.ps[bank].v(j * 128, (1, 128)), kis[ks].v(pg * 128, (1, 128)), C.ident.v(0, (1, 128))),
                        reads=[rkis[ks], C.rconst], writes=[C.rps[bank]])
                P.add("act", lambda e, q4=q4, bank=bank: e.activation(
                    out=kiT.v(q4 * 512, (1, 512)), in_=C.ps[bank].v(0, (1, 512)), func=AF.Copy),
                    reads=[C.rps[bank]], writes=[rkiT])
            P.add("dve", lambda e, b=b: e.tensor_copy(kiT.v(NK, (1, 4)), ki2s.v(4 * b, (1, 4))),
                  reads=[rin, rkiT], writes=[rkiT])
            ab = b % 2
            for (k0, kn) in [(0, 512), (512, 512), (1024, 512), (1536, 512), (NK, 4)]:
                for j in range(8):
                    pbase = 64 * (j % 2)
                    bank = cnt["e"] % 2
                    s = cnt["e"] % 2
                    cnt["e"] += 1
                    P.add("pe", lambda e, j=j, pbase=pbase, bank=bank, k0=k0, kn=kn, b=b: e.matmul(
                        C.ps[bank].v(0, (1, kn), np_=4), qis.v((j // 2) * 64 + 4 * b, (1, 4), p0=pbase, np_=64),
                        kiT.v(k0, (1, kn), p0=pbase, np_=64), start=True, stop=True),
                        reads=[rin, rkiT], writes=[C.rps[bank]])
                    P.add("act", lambda e, s=s, bank=bank, kn=kn: e.activation(
                        out=rl[s].v(0, (1, kn), np_=4), in_=C.ps[bank].v(0, (1, kn), np_=4), func=AF.Relu),
                        reads=[C.rps[bank]], writes=[rrl[s]])
                    if j == 0:
                        P.add("dve", lambda e, s=s, k0=k0, kn=kn, b=b, j=j, ab=ab: e.tensor_scalar(
                            accb[ab].v(k0, (1, kn), np_=4), rl[s].v(0, (1, kn), np_=4),
                            wis.v(b * 8 + j, (1, 1), np_=4), None, op0=ALU.mult),
                            reads=[rrl[s], rin], writes=[raccb[ab]])
                    else:
                        P.add("dve", lambda e, s=s, k0=k0, kn=kn, b=b, j=j, ab=ab: e.scalar_tensor_tensor(
                            out=accb[ab].v(k0, (1, kn), np_=4), in0=rl[s].v(0, (1, kn), np_=4),
                            scalar=wis.v(b * 8 + j, (1, 1), np_=4), in1=accb[ab].v(k0, (1, kn), np_=4),
                            op0=ALU.mult, op1=ALU.add), reads=[rrl[s], rin, raccb[ab]], writes=[raccb[ab]])
            P.add("dve", lambda e, ab=ab: e.tensor_tensor(
                out=accb[ab].v(NK, (1, 4), np_=4), in0=accb[ab].v(NK, (1, 4), np_=4), in1=negtri.v(0, (1, 4), np_=4),
                op=ALU.add), reads=[raccb[ab], rin], writes=[raccb[ab]])
            P.add("sp", lambda e, ab=ab, b=b: e.dma_start(out=accS.v(0, (1, NK + 4), p0=4 * b, np_=4),
                                                          in_=accb[ab].v(0, (1, NK + 4), np_=4)),
                  reads=[raccb[ab]], writes=[raccS], dma=True)
        SS = NK + 4
        for r_ in range(32):
            src = accS if r_ == 0 else work
            rsrc = raccS if r_ == 0 else rwork
            P.add("dve", lambda e, src=src: e.max(out=mx.v(0, (1, 8), np_=64), in_=src.v(0, (1, SS), np_=64)),
                  reads=[rsrc], writes=[rmx])
            if r_ < 31:
                P.add("dve", lambda e, src=src: e.match_replace(
                    out=work.v(0, (1, SS), np_=64), in_to_replace=mx.v(0, (1, 8), np_=64),
                    in_values=src.v(0, (1, SS), np_=64), imm_value=NEGB), reads=[rsrc, rmx], writes=[rwork])
        P.add("dve", lambda e: e.tensor_scalar(msk.v(0, (1, SS), np_=64), accS.v(0, (1, SS), np_=64),
                                               mx.v(7, (1, 1), np_=64), None, op0=ALU.is_ge),
              reads=[raccS, rmx], writes=[rmsk])
        for kb in range(16):
            P.add("pe", lambda e, kb=kb: e.transpose(
                psb[kb // 8].v((kb % 8) * 64, (1, 64)), msk.v(kb * 128, (1, 128), np_=64),
                C.identb.v(0, (1, 64), np_=64)), reads=[rmsk, C.rconst], writes=[C.rps[kb // 8]])
        for half in range(2):
            P.add("act", lambda e, half=half: e.activation(
                out=mskT.v(half * 512, (1, 512)), in_=psb[half].v(0, (1, 512)), func=AF.Copy),
                reads=[C.rps[half]], writes=[rmskT])
        P.add("pe", lambda e: e.transpose(psb[2].v(0, (1, 64), np_=4), msk.v(NK, (1, 4), np_=64),
                                          C.identb.v(0, (1, 64), np_=64)), reads=[rmsk, C.rconst], writes=[C.rps[2]])
        P.add("act", lambda e: e.activation(out=mskT.v(16 * 64, (1, 64), np_=4), in_=psb[2].v(0, (1, 64), np_=4),
                                            func=AF.Copy), reads=[C.rps[2]], writes=[rmskT])
        for b in range(NSB):
            cs = b % 2
            kTs, rkTs = kTs2[cs], rkTs2[cs]
            for pg in range(16):
                P.add("pool", lambda e, cs=cs, pg=pg, b=b: e.indirect_dma_start(
                    out=kcs[cs].v(pg * 256, (1, 256)), out_offset=None, in_=I["ck"],
                    in_offset=IO(ap=idx.v(b * 16 + pg, (1, 1)), axis=0)), reads=[ridx], writes=[rkcs[cs]], dma=True)
                P.add("pool", lambda e, cs=cs, pg=pg, b=b: e.indirect_dma_start(
                    out=vcs[cs].v(pg * 256, (1, 256)), out_offset=None, in_=I["cv"],
                    in_offset=IO(ap=idx.v(b * 16 + pg, (1, 1)), axis=0)), reads=[ridx], writes=[rvcs[cs]], dma=True)
            for g in range(2):
                for half in range(2):
                    bank = 2 + half
                    for j in range(8):
                        pg = half * 8 + j
                        P.add("pe", lambda e, cs=cs, pg=pg, j=j, g=g, bank=bank: e.transpose(
                            psb[bank].v(j * 128, (1, 128)), kcs[cs].v(pg * 256 + g * 128, (1, 128)),
                            C.identb.v(0, (1, 128))), reads=[rkcs[cs], C.rconst], writes=[C.rps[bank]])
                    P.add("act" if half == 0 else "dve",
                          (lambda e, g=g, half=half, bank=bank: e.activation(
                              out=kTs.v(g * NK + half * 1024, (1, 1024)), in_=psb[bank].v(0, (1, 1024)), func=AF.Copy))
                          if half == 0 else
                          (lambda e, g=g, half=half, bank=bank: e.tensor_copy(
                              kTs.v(g * NK + half * 1024, (1, 1024)), psb[bank].v(0, (1, 1024)))),
                          reads=[C.rps[bank]], writes=[rkTs])
                lb = 4 + g
                for kb in range(16):
                    P.add("pe", lambda e, g=g, kb=kb, lb=lb, b=b: e.matmul(
                        C.ps[lb].v(kb * 16, (4, 4), (1, 4)), kTs.v(g * NK + kb * 128, (1, 128)),
                        qas.v(4 * g * 64 + 4 * b, (64, 4), (1, 4)), start=True, stop=True),
                        reads=[rkTs, rin], writes=[C.rps[lb]])
                P.add("pe", lambda e, g=g, lb=lb, b=b: e.matmul(
                    C.ps[lb].v(256, (4, 4), (1, 4), np_=4), kas.v(g * 64 + 4 * b, (1, 4)),
                    qas.v(4 * g * 64 + 4 * b, (64, 4), (1, 4)), start=True, stop=True),
                    reads=[rin], writes=[C.rps[lb]])
                P.add("dve", lambda e, g=g, lb=lb: e.scalar_tensor_tensor(
                    out=lgS.v(0, (1, 256)), in0=C.ps[lb].v(0, (1, 256)), scalar=ATT_SCALE,
                    in1=bsp.v(g * 256, (1, 256)), op0=ALU.mult, op1=ALU.add),
                    reads=[C.rps[lb], rin], writes=[rlgS])
                P.add("dve", lambda e, g=g, lb=lb: e.scalar_tensor_tensor(
                    out=lgn.v(0, (1, 16), np_=4), in0=C.ps[lb].v(256, (1, 16), np_=4), scalar=ATT_SCALE,
                    in1=bsn.v(g * 16, (1, 16), np_=4), op0=ALU.mult, op1=ALU.add),
                    reads=[C.rps[lb], rin], writes=[rlgn])
                P.add("act", lambda e: e.activation(out=exS.v(0, (1, 256)), in_=lgS.v(0, (1, 256)), func=AF.Exp),
                      reads=[rlgS], writes=[rexS])
                P.add("act", lambda e: e.activation(out=exn.v(0, (1, 16), np_=4), in_=lgn.v(0, (1, 16), np_=4),
                                                    func=AF.Exp), reads=[rlgn], writes=[rexn])
                P.add("dve", lambda e, b=b: e.tensor_tensor(
                    out=pTs.v(0, (16, 16), (4, 4), (1, 4)), in0=exS.v(0, (16, 16), (4, 4), (1, 4)),
                    in1=mskT.v(4 * b, (64, 16), (0, 4), (1, 4)), op=ALU.mult), reads=[rexS, rmskT], writes=[rpTs])
                P.add("dve", lambda e, b=b: e.tensor_tensor(
                    out=pTn.v(0, (4, 4), (1, 4), np_=4), in0=exn.v(0, (4, 4), (1, 4), np_=4),
                    in1=mskT.v(16 * 64 + 4 * b, (0, 4), (1, 4), np_=4), op=ALU.mult),
                    reads=[rexn, rmskT], writes=[rpTn])
                for kb in range(16):
                    P.add("pe", lambda e, cs=cs, g=g, kb=kb: e.matmul(
                        C.ps[6].v(0, (1, 16)), vcs[cs].v(kb * 256 + g * 128, (1, 128)), pTs.v(kb * 16, (1, 16)),
                        start=(kb == 0), stop=False), reads=[rvcs[cs], rpTs], writes=[C.rps[6]])
                P.add("pe", lambda e, g=g, b=b: e.matmul(
                    C.ps[6].v(0, (1, 16)), vns.v(b * 256 + g * 128, (1, 128), np_=4), pTn.v(0, (1, 16), np_=4),
                    start=False, stop=True), reads=[rin, rpTn], writes=[C.rps[6]])
                for kb in range(16):
                    P.add("pe", lambda e, kb=kb: e.matmul(
                        C.ps[7].v(0, (1, 16)), C.ones_b.v(0, (1, 128)), pTs.v(kb * 16, (1, 16)),
                        start=(kb == 0), stop=False), reads=[C.rconst, rpTs], writes=[C.rps[7]])
                P.add("pe", lambda e: e.matmul(
                    C.ps[7].v(0, (1, 16)), C.ones_b.v(0, (1, 128), np_=4), pTn.v(0, (1, 16), np_=4),
                    start=False, stop=True), reads=[C.rconst, rpTn], writes=[C.rps[7]])
                P.add("dve", lambda e: e.reciprocal(rden.v(0, (1, 16)), C.ps[7].v(0, (1, 16))),
                      reads=[C.rps[7]], writes=[rrden])
                P.add("dve", lambda e, g=g, b=b: e.tensor_tensor(
                    out=aoS.v(4 * g * 64 + 4 * b, (64, 4), (1, 4)), in0=C.ps[6].v(0, (4, 4), (1, 4)),
                    in1=rden.v(0, (4, 4), (1, 4)), op=ALU.mult), reads=[C.rps[6], rrden], writes=[raoS])
        P.add("sp", lambda e: e.dma_start(out=dv(S["MOAO"], 8 * NT + NP_TOK, (NKC * NT, 128), (NT, 8), (1, 64)),
                                          in_=aoS.v(0, (64, 8), (1, 64))), reads=[raoS], dma=True)
        P.barrier()


OUT_ORDER = ["y_p", "y_s", "k_p", "v_p", "idxk_p", "C_p", "n_p", "m_p", "conv_p",
             "k_s", "v_s", "idxk_s", "C_s", "n_s", "m_s", "conv_s"]


def kernel(**inputs):
    inp = {k: np.asarray(v) for k, v in inputs.items()}
    nc = build_program()
    sh = prep_shared(inp)
    in_maps = []
    for core in range(8):
        m = prep_core(inp, core, sh)
        in_maps.append({k: np.ascontiguousarray(v) for k, v in m.items()})
    res = run_bass_kernel_spmd(nc, in_maps, core_ids=list(range(8)))
    R = res.results

    def cat(name, shape_p):
        return np.stack([np.asarray(R[c][name], dtype=np.float32).reshape(shape_p) for c in range(8)], 0)

    y_p = cat("y_p", (2048, 2048))
    y_s = cat("y_s", (16, 4, 2048)).reshape(128, 4, 2048)
    k_p = cat("k_p", (2048, 2, 128))[None]
    v_p = cat("v_p", (2048, 2, 128))[None]
    i_p = cat("idxk_p", (2048, 64))[None]
    C_p = cat("C_p", (4, 256, 256))[None]
    n_p = cat("n_p", (4, 256))[None]
    m_p = cat("m_p", (4,))[None]
    cv_p = cat("conv_p", (3, 2048))[None]
    k_s = cat("k_s", (16, 4, 2, 128)).reshape(128, 4, 2, 128)[None]
    v_s = cat("v_s", (16, 4, 2, 128)).reshape(128, 4, 2, 128)[None]
    i_s = cat("idxk_s", (16, 4, 64)).reshape(128, 4, 64)[None]
    C_s = cat("C_s", (16, 4, 256, 256)).reshape(128, 4, 256, 256)[None]
    n_s = cat("n_s", (16, 4, 256)).reshape(128, 4, 256)[None]
    m_s = cat("m_s", (16, 4)).reshape(128, 4)[None]
    cv_s = cat("conv_s", (16, 3, 2048)).reshape(128, 3, 2048)[None]
    return (y_p, y_s, k_p, v_p, i_p, C_p, n_p, m_p, cv_p, k_s, v_s, i_s, C_s, n_s, m_s, cv_s)
```

```python
import contextlib
import types
import numpy as np
import concourse.bass as bass
import concourse.mybir as mybir
from concourse.bass_utils import run_bass_kernel_spmd

F32 = mybir.dt.float32
BF16 = mybir.dt.bfloat16
I32 = mybir.dt.int32
U32 = mybir.dt.uint32
AF = mybir.ActivationFunctionType
ALU = mybir.AluOpType
AX = mybir.AxisListType


class Res:
    __slots__ = ("name", "w", "rs", "rd")

    def __init__(self, name=""):
        self.name = name
        self.w = None
        self.rs = {}
        self.rd = []


def _freeze(fn):
    cl = fn.__closure__
    if not cl:
        return fn
    cells = []
    for c in cl:
        try:
            cells.append(types.CellType(c.cell_contents))
        except ValueError:
            cells.append(c)
    g = types.FunctionType(fn.__code__, fn.__globals__, fn.__name__, fn.__defaults__, tuple(cells))
    g.__kwdefaults__ = fn.__kwdefaults__
    return g


class Op:
    __slots__ = ("eng", "fn", "dma", "n", "deps", "need_inc", "sem", "val", "prev_val")

    def __init__(self, eng, fn, dma, n):
        self.eng = eng
        self.fn = fn
        self.dma = dma
        self.n = n
        self.deps = []
        self.need_inc = False
        self.sem = None
        self.val = 0
        self.prev_val = 0


ENGS = ("sp", "act", "dve", "pool", "pe")
NDSEM = 20


class Prog:
    def __init__(self, nc):
        self.nc = nc
        self.ops = []
        self.last = {}
        self.pend_dma = []

    def add(self, eng, fn, reads=(), writes=(), dma=False, n=1):
        op = Op(eng, _freeze(fn), dma, n)
        deps = []
        for r in reads:
            if r.w is not None:
                deps.append((r.w, 0))
        for w in writes:
            if w.w is not None:
                deps.append((w.w, 1))
            for o in w.rs.values():
                deps.append((o, 2))
            for o in w.rd:
                deps.append((o, 2))
        for r in reads:
            if dma:
                r.rd.append(op)
            else:
                r.rs[eng] = op
        for w in writes:
            w.w = op
            w.rs = {}
            w.rd = []
        seen = set()
        for p, kind in deps:
            if p is op or id(p) in seen:
                continue
            if p.eng == eng and not p.dma and not dma:
                if eng == "pe" or kind != 0:
                    continue
            seen.add(id(p))
            op.deps.append(p)
            p.need_inc = True
        self.ops.append(op)
        if dma:
            self.pend_dma.append(op)
        else:
            self.last[eng] = op
        return op

    def barrier(self):
        lasts = dict(self.last)
        dmas = list(self.pend_dma)
        self.pend_dma = []
        for e in ENGS:
            op = Op(e, lambda eng: eng.nop(), False, 1)
            for x, p in lasts.items():
                if x != e:
                    op.deps.append(p)
                    p.need_inc = True
            for p in dmas:
                op.deps.append(p)
            self.ops.append(op)
            self.last[e] = op

    def emit(self):
        nc = self.nc
        with contextlib.ExitStack() as es:
            csem = {e: es.enter_context(nc.semaphore("c_" + e)) for e in ENGS}
            dsem = {e: [es.enter_context(nc.semaphore("d_%s%d" % (e, i))) for i in range(NDSEM)]
                    for e in ("sp", "act", "pool")}
            cnt = {e: 0 for e in ENGS}
            dcount = {e: 0 for e in dsem}
            dval = {e: [0] * NDSEM for e in dsem}
            for op in self.ops:
                if op.dma:
                    q = op.eng
                    slot = dcount[q] % NDSEM
                    dcount[q] += 1
                    op.sem = dsem[q][slot]
                    op.prev_val = dval[q][slot]
                    dval[q][slot] += 16 * op.n
                    op.val = dval[q][slot]
                elif op.need_inc:
                    cnt[op.eng] += 1
                    op.val = cnt[op.eng]
                    op.sem = csem[op.eng]
            per = {e: [o for o in self.ops if o.eng == e] for e in ENGS}

            def run(ename, e):
                waited = {}

                def wait(sem, val):
                    k = id(sem)
                    if waited.get(k, 0) < val:
                        e.wait_ge(sem, val)
                        waited[k] = val

                for op in per[ename]:
                    for p in op.deps:
                        wait(p.sem, p.val)
                    if op.dma:
                        if op.prev_val > 0:
                            wait(op.sem, op.prev_val)
                        ins = op.fn(e)
                        if not isinstance(ins, (list, tuple)):
                            ins = [ins]
                        assert len(ins) == op.n, (len(ins), op.n)
                        for i in ins:
                            i.then_inc(op.sem, 16)
                    else:
                        ins = op.fn(e)
                        if op.need_inc:
                            ins.then_inc(op.sem, 1)
                if ename in dsem:
                    for s, v in zip(dsem[ename], dval[ename]):
                        if v > 0:
                            wait(s, v)

            with nc.allow_non_contiguous_dma(reason="small strided state / index transfers"), nc.Block() as block:
                @block.sync
                def _(e):
                    run("sp", e)

                @block.scalar
                def _(e):
                    run("act", e)

                @block.vector
                def _(e):
                    run("dve", e)

                @block.gpsimd
                def _(e):
                    run("pool", e)

                @block.tensor
                def _(e):
                    run("pe", e)


D = 2048
NKC = 16
DFF = 5632
NFG = 22
NP_TOK = 2048
NS_TOK = 64
NT = NP_TOK + NS_TOK
NSB = 16
PW = 6224
EPS = 1e-6
TILES = [(0, 1024), (1024, 1088)]
TS = 1088


def subs_of(t0):
    if t0 == 0:
        return [(0, 512, False), (512, 512, False)]
    return [(0, 512, False), (512, 512, False), (1024, 64, True)]


def padw(width, dt):
    per = 64 // mybir.dt.size(dt)
    return ((width + per - 1) // per) * per


class SB:
    def __init__(self, t, width, dt):
        self.t = t
        self.W = width
        self.dt = dt

    def v(self, off, *dims, p0=0, np_=128):
        return bass.AP(self.t, p0 * self.W + off, [[self.W, np_]] + [list(d) for d in dims])


class SBV(SB):
    def __init__(self, base, off0):
        self.t = base.t
        self.W = base.W
        self.dt = base.dt
        self.off0 = off0

    def v(self, off, *dims, p0=0, np_=128):
        return bass.AP(self.t, p0 * self.W + self.off0 + off, [[self.W, np_]] + [list(d) for d in dims])


def dv(ap, off, *dims):
    return bass.AP(ap.tensor, off, [list(d) for d in dims])


class Ctx:
    def dump(self, name, sbt, reads):
        if ("dump_" + name) not in self.dbg:
            return
        o = self.nc.dram_tensor("dump_" + name, [128, sbt.W], sbt.dt, kind="ExternalOutput").ap()
        self.P.add("sp", lambda e: e.dma_start(out=o, in_=sbt.v(0, (1, sbt.W))), reads=list(reads), dma=True)


def build_program(dbg=None):
    dbg = dbg or set()
    nc = bass.Bass("TRN2", target_bir_lowering=False)
    P = Prog(nc)
    C = Ctx()
    C.nc = nc
    C.P = P
    C.dbg = dbg

    def din(name, shape, dt=F32):
        return nc.dram_tensor(name, list(shape), dt, kind="ExternalInput").ap()

    def dout(name, shape, dt=F32):
        return nc.dram_tensor(name, list(shape), dt, kind="ExternalOutput").ap()

    def dscr(name, shape, dt=F32):
        return nc.dram_tensor(name, list(shape), dt, kind="Internal").ap()

    I = {}
    I["xp"] = din("xp", [NP_TOK, D])
    I["xs"] = din("xs", [NS_TOK, D])
    I["c17"] = din("c17", [17, D])
    I["ident"] = din("ident", [128, 128])
    for nm in ("ffn1", "ffn2"):
        I[nm + "_g"] = din(nm + "_norm_g", [NKC, 128])
        I[nm + "_wg"] = din(nm + "_w_gate", [D, DFF])
        I[nm + "_wu"] = din(nm + "_w_up", [D, DFF])
        I[nm + "_wd"] = din(nm + "_w_down", [DFF, D])
    I["mix_g"] = din("mix_norm_g", [NKC, 128])
    I["w_ada"] = din("w_ada", [D, 9 * D])
    I["b_ada"] = din("b_ada", [144, 128])
    I["w_in"] = din("w_in", [D, PW])
    I["w_out"] = din("w_out", [D, D])
    I["gate_b"] = din("gate_b", [8, 1])
    I["qng"] = din("qng", [128, 1])
    I["kng"] = din("kng", [128, 1])
    I["conv_w"] = din("conv_w", [64, 128])
    I["out_g"] = din("out_g", [8, 128])
    I["tri"] = din("tri", [64, 64])
    I["sel8"] = din("sel8", [8, 1024])
    I["mk64"] = din("mk64", [1, 512])
    I["mks"] = din("mks", [1, 192])
    I["st_C"] = din("st_C", [NSB, 4, 256, 256])
    I["st_n"] = din("st_n", [NSB, 4, 256])
    I["st_m"] = din("st_m", [NSB, 4])
    I["st_conv"] = din("st_conv", [NSB * 3, D])
    I["bdp"] = din("bdp", [6, 128, 512])
    I["bsp"] = din("bsp", [2, 128, 256])
    I["bsn"] = din("bsn", [2, 4, 16])
    I["negtri"] = din("negtri", [128, 128])
    I["ptab"] = din("ptab", [1, 256], I32)
    I["pidx"] = din("pidx", [128, 1])
    NPOOL = 2560
    I["ck"] = din("ck", [NPOOL * 128, 256])
    I["cv"] = din("cv", [NPOOL * 128, 256])
    I["cidx"] = din("cidx", [NPOOL * 128, 64])
    C.I = I
    O = {}
    C.O = O
    S = {}
    C.S = S
    S["X1"] = din("X1", [128, NKC, NT]) if "noffn" in dbg else dscr("X1", [128, NKC, NT])
    S["QK"] = dscr("QK", [128, NKC, NT])
    S["OM"] = dscr("OM", [128, 8, NT], BF16)
    S["QI"] = dscr("QI", [128, 4, NT], BF16)
    S["QA"] = dscr("QA", [128, 8, NT], BF16)
    S["KA"] = dscr("KA", [128, 2, NT], BF16)
    S["KI2"] = dscr("KI2", [128, NT], BF16)
    S["GT"] = dscr("GT", [8, NT])
    S["VM"] = dscr("VM", [NT, 1028], BF16)
    S["VA"] = dscr("VA", [NT, 256], BF16)
    S["WI"] = dscr("WI", [NT, 8])
    O["k_p"] = dout("k_p", [NP_TOK, 256])
    O["v_p"] = dout("v_p", [NP_TOK, 256])
    O["idxk_p"] = dout("idxk_p", [NP_TOK, 64])
    O["k_s"] = dout("k_s", [NS_TOK, 256])
    O["v_s"] = dout("v_s", [NS_TOK, 256])
    O["idxk_s"] = dout("idxk_s", [NS_TOK, 64])
    O["conv_p"] = dout("conv_p", [3, D])
    O["conv_s"] = dout("conv_s", [NSB, 3, D])
    O["y_p"] = dout("y_p", [NP_TOK, D])
    O["y_s"] = dout("y_s", [NS_TOK, D])
    O["C_p"] = dout("C_p", [4, 256, 256])
    O["n_p"] = dout("n_p", [4, 256])
    O["m_p"] = dout("m_p", [1, 4])
    O["C_s"] = dout("C_s", [NSB, 4, 256, 256])
    O["n_s"] = dout("n_s", [NSB, 4, 256])
    O["m_s"] = dout("m_s", [NSB, 4])
    S["MOAO"] = dscr("MOAO", [128, NKC, NT], BF16)
    if "MOAO" in dbg:
        O["dbg_MOAO"] = dout("dbg_MOAO", [128, NKC, NT], BF16)
    DBG_SCR = {"QK": F32, "OM": BF16, "QI": BF16, "QA": BF16, "KA": BF16, "KI2": BF16, "GT": F32,
               "VM": BF16, "VA": BF16, "WI": F32}
    for k_, dt_ in DBG_SCR.items():
        if k_ in dbg:
            O["dbg_" + k_] = dout("dbg_" + k_, list(S[k_].shape), dt_)
    if "X1" in dbg:
        O["dbg_X1"] = dout("dbg_X1", [128, NKC, NT])
    if "mods" in dbg:
        O["dbg_mods"] = dout("dbg_mods", [128, 144 * 17])

    with contextlib.ExitStack() as es:
        def sb(name, width, dt):
            return SB(es.enter_context(nc.sbuf_tensor("s_" + name, [128, padw(width, dt)], dt)), padw(width, dt), dt)

        C.sb = sb
        C.ps = [SB(es.enter_context(nc.psum_tensor("ps%d" % i, [128, 512], F32)), 512, F32) for i in range(8)]
        C.rps = [Res("ps%d" % i) for i in range(8)]
        C.ident = sb("ident", 128, F32)
        C.identb = sb("identb", 128, BF16)
        C.ones_b = sb("ones_b", 128, BF16)
        C.rconst = Res("const")
        P.add("sp", lambda e: e.dma_start(out=C.ident.v(0, (1, 128)), in_=I["ident"]), writes=[C.rconst], dma=True)
        P.add("dve", lambda e: e.tensor_copy(C.identb.v(0, (1, 128)), C.ident.v(0, (1, 128))),
              reads=[C.rconst], writes=[C.rconst])
        P.add("dve", lambda e: e.memset(C.ones_b.v(0, (1, 128)), 1.0), writes=[C.rconst])
        C.oneb = sb("oneb", 1, F32)
        P.add("dve", lambda e: e.memset(C.oneb.v(0, (1, 1)), 1.0), writes=[C.rconst])
        C.epsb = sb("epsb", 1, F32)
        P.add("dve", lambda e: e.memset(C.epsb.v(0, (1, 1)), EPS), writes=[C.rconst])
        C.mods = sb("mods", 144 * 17, F32)
        C.rmods = Res("mods")
        C.gn = {}
        for k in ("ffn1_g", "mix_g", "ffn2_g"):
            C.gn[k] = sb("gn_" + k, NKC, F32)
        C.rgn = Res("gn")

        phase_mods(C)
        if "mods" in dbg:
            P.add("sp", lambda e: e.dma_start(out=O["dbg_mods"], in_=C.mods.v(0, (1, 144 * 17))),
                  reads=[C.rmods], dma=True)
        rX1 = [Res("X1_%d" % i) for i in range(len(TILES))]
        C.rX1 = rX1

        def load_x_in(ti, xt, rxs):
            load_x_tokenmajor(C, ti, xt, rxs)

        def store_x1(ti, xt, rxs):
            t0, T = TILES[ti]
            P.add("sp", lambda e: e.dma_start(out=dv(S["X1"], t0, (NKC * NT, 128), (NT, NKC), (1, T)),
                                              in_=xt.v(0, (TS, NKC), (1, T))),
                  reads=[r for row in rxs for r in row], writes=[rX1[ti]], dma=True)

        if "noffn" not in dbg:
            phase_ffn(C, "ffn1", 0, load_x_in, store_x1)
        if "X1" in dbg:
            for ti, (t0, T) in enumerate(TILES):
                pass
            P.add("sp", lambda e: e.dma_start(out=O["dbg_X1"], in_=S["X1"]), reads=rX1, dma=True)
        if "noproj" not in dbg:
            phase_proj(C)
        for k_ in DBG_SCR:
            if k_ in dbg:
                P.add("sp", lambda e, k_=k_: e.dma_start(out=O["dbg_" + k_], in_=S[k_]), dma=True)
        if "nomlstm" not in dbg:
            phase_mlstm(C)
        if "nodsa" not in dbg:
            phase_dsa(C)
            if "dsa_nosample" not in dbg:
                phase_dsa_sample(C)
        if "MOAO" in dbg:
            P.add("sp", lambda e: e.dma_start(out=O["dbg_MOAO"], in_=S["MOAO"]), dma=True)

        def load_x1(ti, xt, rxs):
            t0, T = TILES[ti]
            P.add("sp", lambda e: e.dma_start(out=xt.v(0, (TS, NKC), (1, T)),
                                              in_=dv(S["X1"], t0, (NKC * NT, 128), (NT, NKC), (1, T))),
                  reads=[rX1[ti]], writes=[r for row in rxs for r in row], dma=True)

        def store_y(ti, xt, rxs):
            t0, T = TILES[ti]
            stage, rstage = C.xstage, C.rxstage
            blocks = []
            for si, (o, n, samp) in enumerate(subs_of(t0)):
                if samp:
                    blocks.append((o, 64, O["y_s"], 0, si))
                else:
                    for b in range(n // 128):
                        blocks.append((o + b * 128, 128, O["y_p"], (t0 + o + b * 128) * D, si))
            for (o, n, dst, doff, si) in blocks:
                for half in range(2):
                    for q in range(2):
                        bank = q + 2 * half
                        for j in range(4):
                            c = half * 8 + q * 4 + j
                            P.add("pe", lambda e, n=n, o=o, c=c, j=j, bank=bank: e.transpose(
                                C.ps[bank].v(j * 128, (1, 128), np_=n), xt.v(c * TS + o, (1, n)),
                                C.ident.v(0, (1, 128))), reads=[rxs[c][si], C.rconst], writes=[C.rps[bank]])
                        P.add("act", lambda e, half=half, q=q, n=n, bank=bank: e.activation(
                            out=stage[half].v(q * 512, (1, 512), np_=n), in_=C.ps[bank].v(0, (1, 512), np_=n),
                            func=AF.Copy), reads=[C.rps[bank]], writes=[rstage[half]])
                    P.add("sp", lambda e, half=half, n=n, dst=dst, doff=doff: e.dma_start(
                        out=dv(dst, doff + half * 1024, (D, n), (1, 1024)), in_=stage[half].v(0, (1, 1024), np_=n)),
                        reads=[rstage[half]], dma=True)

        if "noffn2" not in dbg:
            phase_ffn(C, "ffn2", 2, load_x1, store_y)
        P.emit()
    return nc


def load_featmajor_small(C, dst, src_ap, nrows, rdst, tmpname):
    P = C.P
    tmp = C.sb(tmpname, 128, F32)
    done = 0
    blk = 0
    rt = Res()
    while done < nrows:
        n = min(128, nrows - done)
        P.add("sp", lambda e, done=done, n=n: e.dma_start(out=tmp.v(0, (1, 128), np_=n),
                                                          in_=dv(src_ap, done * 128, (128, n), (1, 128))),
              writes=[rt], dma=True)
        bank = 7
        P.add("pe", lambda e, n=n: e.transpose(C.ps[bank].v(0, (1, n)), tmp.v(0, (1, 128), np_=n),
                                               C.ident.v(0, (1, n), np_=n)),
              reads=[rt, C.rconst], writes=[C.rps[bank]])
        P.add("dve", lambda e, done=done, n=n: e.tensor_copy(dst.v(done, (1, n)), C.ps[bank].v(0, (1, n))),
              reads=[C.rps[bank]], writes=[rdst, rt])
        done += n
        blk += 1


def phase_mods(C):
    P, I, nc = C.P, C.I, C.nc
    with contextlib.ExitStack() as es:
        def sb(name, width, dt):
            return SB(es.enter_context(nc.sbuf_tensor("s_" + name, [128, padw(width, dt)], dt)), padw(width, dt), dt)
        sbo = C.sb
        C.sb = sb
        for k in ("ffn1_g", "mix_g", "ffn2_g"):
            load_featmajor_small(C, C.gn[k], I[k], NKC, C.rgn, "tmp_" + k)
        bada = sb("bada", 144, F32)
        rb = Res("bada")
        load_featmajor_small(C, bada, I["b_ada"], 144, rb, "tmp_bada")
        c_tm = sb("c_tm", D, F32)
        rc = Res()
        P.add("sp", lambda e: e.dma_start(out=c_tm.v(0, (1, D), np_=17), in_=I["c17"]), writes=[rc], dma=True)
        cT = sb("cT", NKC * 17, BF16)
        rcT = Res()
        for kc in range(NKC):
            P.add("pe", lambda e, kc=kc: e.transpose(C.ps[6].v(kc * 17, (1, 17)),
                                                     c_tm.v(kc * 128, (1, 128), np_=17),
                                                     C.ident.v(0, (1, 17), np_=17)),
                  reads=[rc, C.rconst], writes=[C.rps[6]])
        P.add("dve", lambda e: e.tensor_copy(cT.v(0, (1, NKC * 17)), C.ps[6].v(0, (1, NKC * 17))),
              reads=[C.rps[6]], writes=[rcT])
        wb = [sb("wada%d" % i, NKC * 512, BF16) for i in range(2)]
        rwb = [Res(), Res()]
        for blk in range(36):
            s = blk % 2
            P.add("pool", lambda e, blk=blk, s=s: e.dma_start(
                out=wb[s].v(0, (512, NKC), (1, 512)),
                in_=dv(I["w_ada"], blk * 512, (9 * D, 128), (128 * 9 * D, NKC), (1, 512))),
                writes=[rwb[s]], dma=True)
            bank = 4 + (blk % 2)
            for cc in range(4):
                j = blk * 4 + cc
                for kc in range(NKC):
                    P.add("pe", lambda e, s=s, cc=cc, kc=kc, bank=bank: e.matmul(
                        C.ps[bank].v(cc * 17, (1, 17)),
                        wb[s].v(kc * 512 + cc * 128, (1, 128)),
                        cT.v(kc * 17, (1, 17)), start=(kc == 0), stop=(kc == NKC - 1)),
                        reads=[rwb[s], rcT], writes=[C.rps[bank]])
            for cc in range(4):
                j = blk * 4 + cc
                P.add("act", lambda e, j=j, cc=cc, bank=bank: e.activation(
                    out=C.mods.v(j * 17, (1, 17)), in_=C.ps[bank].v(cc * 17, (1, 17)),
                    func=AF.Identity, bias=bada.v(j, (1, 1)), scale=1.0),
                    reads=[C.rps[bank], rb], writes=[C.rmods])
        P.barrier()
        C.sb = sbo


def load_x_tokenmajor(C, ti, xt, rxs):
    P, I = C.P, C.I
    t0, T = TILES[ti]
    stage, rstage = C.xstage, C.rxstage
    blocks = []
    for si, (o, n, samp) in enumerate(subs_of(t0)):
        if samp:
            blocks.append((o, 64, I["xs"], 0, si))
        else:
            for b in range(n // 128):
                blocks.append((o + b * 128, 128, I["xp"], (t0 + o + b * 128) * D, si))
    for bi, (o, n, src, soff, si) in enumerate(blocks):
        for half in range(2):
            s = (bi * 2 + half) % 2
            P.add("sp", lambda e, s=s, n=n, src=src, soff=soff, half=half: e.dma_start(
                out=stage[s].v(0, (1, 1024), np_=n),
                in_=dv(src, soff + half * 1024, (D, n), (1, 1024))), writes=[rstage[s]], dma=True)
            for q in range(2):
                bank = q + 2 * half
                for j in range(4):
                    P.add("pe", lambda e, s=s, n=n, q=q, j=j, bank=bank: e.transpose(
                        C.ps[bank].v(j * 128, (1, n)),
                        stage[s].v((q * 4 + j) * 128, (1, 128), np_=n),
                        C.ident.v(0, (1, n), np_=n)),
                        reads=[rstage[s], C.rconst], writes=[C.rps[bank]])
                c0 = half * 8 + q * 4
                eng = "act" if q == 0 else "dve"
                if eng == "act":
                    fn = lambda e, c0=c0, o=o, n=n, bank=bank: e.activation(
                        out=xt.v(c0 * TS + o, (TS, 4), (1, n)), in_=C.ps[bank].v(0, (128, 4), (1, n)),
                        func=AF.Copy)
                else:
                    fn = lambda e, c0=c0, o=o, n=n, bank=bank: e.tensor_copy(
                        xt.v(c0 * TS + o, (TS, 4), (1, n)), C.ps[bank].v(0, (128, 4), (1, n)))
                P.add(eng, fn, reads=[C.rps[bank]], writes=[rxs[c][si] for c in range(c0, c0 + 4)])


def phase_ffn(C, nm, sl, load_x, store_x):
    P, I, nc = C.P, C.I, C.nc
    wg, wu, wd = I[nm + "_wg"], I[nm + "_wu"], I[nm + "_wd"]
    gn = C.gn[nm + "_g"]
    shb, scb, gtb = (3 * sl) * 16, (3 * sl + 1) * 16, (3 * sl + 2) * 16
    TM = 1088
    with contextlib.ExitStack() as es:
        def sb(name, width, dt):
            return SB(es.enter_context(nc.sbuf_tensor(nm + "_" + name, [128, padw(width, dt)], dt)), padw(width, dt), dt)
        xt = sb("xt", NKC * TM, F32)
        ht = sb("ht", NKC * TM, BF16)
        wgu = [[sb("wg%d" % i, NKC * 256, BF16), sb("wu%d" % i, NKC * 256, BF16)] for i in range(2)]
        wdb = [sb("wd%d" % i, 2 * D, BF16) for i in range(2)]
        actg = [sb("actg%d" % i, 2 * TM, BF16) for i in range(2)]
        sg = [sb("sg%d" % i, 512, BF16) for i in range(2)]
        rstd = sb("rstd", TM, F32)
        xsq = [sb("xsq%d" % i, 512, BF16) for i in range(2)]
        tmp = [sb("tmp%d" % i, 512, F32) for i in range(2)]
        A17 = sb("A17", NKC * 17, F32)
        G17 = sb("G17", NKC * 17, F32)
        E1 = sb("E1", NKC * 64, F32)
        E2 = sb("E2", NKC * 64, F32)
        T1 = sb("T1", NKC * 64, F32)
        C.xstage = [sb("xstage%d" % i, 1024, F32) for i in range(2)]
        C.rxstage = [Res(), Res()]
        rwgu = [Res(), Res()]
        rwd = [Res(), Res()]
        rsg = [Res(), Res()]
        rxsq = [Res(), Res()]
        rtmp = [Res(), Res()]
        rA, rG, rE1, rE2, rT1, rrstd = Res(), Res(), Res(), Res(), Res(), Res()
        P.add("dve", lambda e: e.tensor_scalar(A17.v(0, (1, NKC * 17)), C.mods.v(scb * 17, (1, NKC * 17)),
                                               1.0, None, op0=ALU.add), reads=[C.rmods], writes=[rA])
        P.add("dve", lambda e: e.tensor_tensor(out=A17.v(0, (17, NKC), (1, 17)), in0=A17.v(0, (17, NKC), (1, 17)),
                                               in1=gn.v(0, (1, NKC), (0, 17)), op=ALU.mult),
              reads=[rA, C.rgn], writes=[rA])
        P.add("dve", lambda e: e.tensor_scalar(G17.v(0, (1, NKC * 17)), C.mods.v(gtb * 17, (1, NKC * 17)),
                                               0.5, None, op0=ALU.mult), reads=[C.rmods], writes=[rG])
        P.add("dve", lambda e: e.tensor_copy(E1.v(0, (64, NKC), (4, NSB), (1, 4)),
                                             A17.v(1, (17, NKC), (1, NSB), (0, 4))), reads=[rA], writes=[rE1])
        P.add("dve", lambda e: e.tensor_copy(E2.v(0, (64, NKC), (4, NSB), (1, 4)),
                                             C.mods.v(shb * 17 + 1, (17, NKC), (1, NSB), (0, 4))),
              reads=[C.rmods], writes=[rE2])
        fgc = 0
        dbank = 0
        if nm == "ffn2":
            Gs2 = sb("Gs2", NKC * 64, F32)
            rGs2 = Res()
            P.add("dve", lambda e: e.tensor_copy(Gs2.v(0, (64, NKC), (4, NSB), (1, 4)),
                                                 C.mods.v(80 * 17 + 1, (17, NKC), (1, NSB), (0, 4))),
                  reads=[C.rmods], writes=[rGs2])
        rxs = [[Res() for _ in range(3)] for _ in range(NKC)]
        rh = [Res() for _ in range(3)]
        ract_all = [[Res() for _ in range(3)] for _ in range(2)]
        for ti, (t0, T) in enumerate(TILES):
            subs = subs_of(t0)
            if ti > 0:
                P.barrier()
            load_x(ti, xt, rxs)
            if nm == "ffn2":
                for si, (o, n, samp) in enumerate(subs):
                    P.add("sp", lambda e, o=o, n=n, t0=t0: e.dma_start(
                        out=ht.v(o, (TS, NKC), (1, n)),
                        in_=dv(C.S["MOAO"], t0 + o, (NKC * NT, 128), (NT, NKC), (1, n))), writes=[rh[si]], dma=True)
                for dp in range(8):
                    ws = fgc % 2
                    fgc += 1
                    P.add("pool", lambda e, ws=ws, dp=dp: e.dma_start(
                        out=wgu[ws][0].v(0, (256, NKC), (1, 256)),
                        in_=dv(I["w_out"], dp * 256, (D, 128), (128 * D, NKC), (1, 256))), writes=[rwgu[ws]], dma=True)
                    for j in range(2):
                        dc = dp * 2 + j
                        for si, (o, n, samp) in enumerate(subs):
                            db = 4 + dbank % 3
                            dbank += 1
                            for kc in range(NKC):
                                P.add("pe", lambda e, ws=ws, j=j, kc=kc, o=o, n=n, db=db: e.matmul(
                                    C.ps[db].v(0, (1, n)), wgu[ws][0].v(kc * 256 + j * 128, (1, 128)),
                                    ht.v(kc * TS + o, (1, n)), start=(kc == 0), stop=(kc == NKC - 1)),
                                    reads=[rwgu[ws], rh[si]], writes=[C.rps[db]])
                            if not samp:
                                P.add("dve", lambda e, dc=dc, o=o, n=n, db=db: e.scalar_tensor_tensor(
                                    out=xt.v(dc * TS + o, (1, n)), in0=C.ps[db].v(0, (1, n)),
                                    scalar=C.mods.v((80 + dc) * 17, (1, 1)), in1=xt.v(dc * TS + o, (1, n)),
                                    op0=ALU.mult, op1=ALU.add),
                                    reads=[C.rps[db], C.rmods, rxs[dc][si]], writes=[rxs[dc][si]])
                            else:
                                P.add("dve", lambda e, dc=dc, db=db: e.tensor_tensor(
                                    out=T1.v(dc * 64, (1, 64)), in0=C.ps[db].v(0, (1, 64)),
                                    in1=Gs2.v(dc * 64, (1, 64)), op=ALU.mult),
                                    reads=[C.rps[db], rGs2], writes=[rT1])
                                P.add("dve", lambda e, dc=dc, o=o: e.tensor_tensor(
                                    out=xt.v(dc * TS + o, (1, 64)), in0=xt.v(dc * TS + o, (1, 64)),
                                    in1=T1.v(dc * 64, (1, 64)), op=ALU.add),
                                    reads=[rT1, rxs[dc][si]], writes=[rxs[dc][si]])
            for si, (o, n, samp) in enumerate(subs):
                for c in range(NKC):
                    s = c % 2
                    P.add("act", lambda e, s=s, c=c, o=o, n=n: e.activation(
                        out=xsq[s].v(0, (1, n)), in_=xt.v(c * TS + o, (1, n)), func=AF.Square),
                        reads=[rxs[c][si]], writes=[rxsq[s]])
                    P.add("pe", lambda e, s=s, c=c, n=n: e.matmul(
                        C.ps[7].v(0, (1, n)), C.ones_b.v(0, (1, 128)), xsq[s].v(0, (1, n)),
                        start=(c == 0), stop=(c == NKC - 1)), reads=[rxsq[s], C.rconst], writes=[C.rps[7]])
                P.add("act", lambda e, o=o, n=n: e.activation(
                    out=rstd.v(o, (1, n)), in_=C.ps[7].v(0, (1, n)), func=AF.Sqrt, bias=C.epsb.v(0, (1, 1)),
                    scale=1.0 / D), reads=[C.rps[7], C.rconst], writes=[rrstd])
                P.add("dve", lambda e, o=o, n=n: e.reciprocal(rstd.v(o, (1, n)), rstd.v(o, (1, n))),
                      reads=[rrstd], writes=[rrstd])
                if not samp:
                    for c in range(NKC):
                        s = c % 2
                        P.add("dve", lambda e, s=s, c=c, o=o, n=n: e.scalar_tensor_tensor(
                            out=tmp[s].v(0, (1, n)), in0=xt.v(c * TS + o, (1, n)), scalar=A17.v(c * 17, (1, 1)),
                            in1=rstd.v(o, (1, n)), op0=ALU.mult, op1=ALU.mult),
                            reads=[rxs[c][si], rA, rrstd], writes=[rtmp[s]])
                        P.add("act", lambda e, s=s, c=c, o=o, n=n: e.activation(
                            out=ht.v(c * TS + o, (1, n)), in_=tmp[s].v(0, (1, n)), func=AF.Identity,
                            bias=C.mods.v((shb + c) * 17, (1, 1)), scale=1.0),
                            reads=[rtmp[s], C.rmods], writes=[rh[si]])
                else:
                    P.add("dve", lambda e, o=o: e.tensor_tensor(
                        out=T1.v(0, (64, NKC), (1, 64)), in0=xt.v(o, (TS, NKC), (1, 64)),
                        in1=rstd.v(o, (0, NKC), (1, 64)), op=ALU.mult),
                        reads=[rxs[c][si] for c in range(NKC)] + [rrstd], writes=[rT1])
                    P.add("dve", lambda e: e.tensor_tensor(
                        out=T1.v(0, (1, NKC * 64)), in0=T1.v(0, (1, NKC * 64)), in1=E1.v(0, (1, NKC * 64)),
                        op=ALU.mult), reads=[rT1, rE1], writes=[rT1])
                    P.add("dve", lambda e, o=o: e.tensor_tensor(
                        out=ht.v(o, (TS, NKC), (1, 64)), in0=T1.v(0, (64, NKC), (1, 64)),
                        in1=E2.v(0, (64, NKC), (1, 64)), op=ALU.add), reads=[rT1, rE2], writes=[rh[si]])
                    P.add("dve", lambda e: e.tensor_copy(E1.v(0, (64, NKC), (4, NSB), (1, 4)),
                                                         G17.v(1, (17, NKC), (1, NSB), (0, 4))),
                          reads=[rG, rT1], writes=[rE1])
            for fg in range(NFG):
                ws = fgc % 2
                fgc += 1
                P.add("pool", lambda e, ws=ws, fg=fg: e.dma_start(
                    out=wgu[ws][0].v(0, (256, NKC), (1, 256)),
                    in_=dv(wg, fg * 256, (DFF, 128), (128 * DFF, NKC), (1, 256))), writes=[rwgu[ws]], dma=True)
                P.add("pool", lambda e, ws=ws, fg=fg: e.dma_start(
                    out=wgu[ws][1].v(0, (256, NKC), (1, 256)),
                    in_=dv(wu, fg * 256, (DFF, 128), (128 * DFF, NKC), (1, 256))), writes=[rwgu[ws]], dma=True)
                P.add("pool", lambda e, ws=ws, fg=fg: e.dma_start(
                    out=wdb[ws].v(0, (D, 2), (1, D)),
                    in_=dv(wd, fg * 256 * D, (D, 128), (128 * D, 2), (1, D))), writes=[rwd[ws]], dma=True)
                ract = ract_all[ws]
                for fc in range(2):
                    for si, (o, n, samp) in enumerate(subs):
                        gb = (fc * len(subs) + si) % 2
                        ub = 2 + gb
                        for kc in range(NKC):
                            P.add("pe", lambda e, ws=ws, fc=fc, kc=kc, o=o, n=n, gb=gb: e.matmul(
                                C.ps[gb].v(0, (1, n)), wgu[ws][0].v(kc * 256 + fc * 128, (1, 128)),
                                ht.v(kc * TS + o, (1, n)), start=(kc == 0), stop=(kc == NKC - 1)),
                                reads=[rwgu[ws], rh[si]], writes=[C.rps[gb]])
                        for kc in range(NKC):
                            P.add("pe", lambda e, ws=ws, fc=fc, kc=kc, o=o, n=n, ub=ub: e.matmul(
                                C.ps[ub].v(0, (1, n)), wgu[ws][1].v(kc * 256 + fc * 128, (1, 128)),
                                ht.v(kc * TS + o, (1, n)), start=(kc == 0), stop=(kc == NKC - 1)),
                                reads=[rwgu[ws], rh[si]], writes=[C.rps[ub]])
                        P.add("act", lambda e, gb=gb, n=n: e.activation(
                            out=sg[gb].v(0, (1, n)), in_=C.ps[gb].v(0, (1, n)), func=AF.Silu),
                            reads=[C.rps[gb]], writes=[rsg[gb]])
                        P.add("dve", lambda e, ws=ws, fc=fc, o=o, n=n, gb=gb, ub=ub: e.tensor_tensor(
                            out=actg[ws].v(fc * TS + o, (1, n)), in0=sg[gb].v(0, (1, n)),
                            in1=C.ps[ub].v(0, (1, n)), op=ALU.mult),
                            reads=[rsg[gb], C.rps[ub]], writes=[ract[si]])
                for si, (o, n, samp) in enumerate(subs):
                    for dc in range(NKC):
                        db = 4 + dbank % 3
                        dbank += 1
                        for fc in range(2):
                            P.add("pe", lambda e, ws=ws, fc=fc, dc=dc, o=o, n=n, db=db: e.matmul(
                                C.ps[db].v(0, (1, n)), wdb[ws].v(fc * D + dc * 128, (1, 128)),
                                actg[ws].v(fc * TS + o, (1, n)), start=(fc == 0), stop=(fc == 1)),
                                reads=[rwd[ws], ract[si]], writes=[C.rps[db]])
                        if not samp:
                            P.add("dve", lambda e, dc=dc, o=o, n=n, db=db: e.scalar_tensor_tensor(
                                out=xt.v(dc * TS + o, (1, n)), in0=C.ps[db].v(0, (1, n)),
                                scalar=G17.v(dc * 17, (1, 1)), in1=xt.v(dc * TS + o, (1, n)),
                                op0=ALU.mult, op1=ALU.add),
                                reads=[C.rps[db], rG, rxs[dc][si]], writes=[rxs[dc][si]])
                        else:
                            P.add("dve", lambda e, dc=dc, db=db: e.tensor_tensor(
                                out=T1.v(dc * 64, (1, 64)), in0=C.ps[db].v(0, (1, 64)),
                                in1=E1.v(dc * 64, (1, 64)), op=ALU.mult),
                                reads=[C.rps[db], rE1], writes=[rT1])
                            P.add("dve", lambda e, dc=dc, o=o: e.tensor_tensor(
                                out=xt.v(dc * TS + o, (1, 64)), in0=xt.v(dc * TS + o, (1, 64)),
                                in1=T1.v(dc * 64, (1, 64)), op=ALU.add),
                                reads=[rT1, rxs[dc][si]], writes=[rxs[dc][si]])
            store_x(ti, xt, rxs)
        P.barrier()


def prep_shared(inp):
    sh = {}
    sh["ident"] = np.eye(128, dtype=np.float32)
    for nm in ("ffn1", "ffn2"):
        sh[nm + "_norm_g"] = np.ascontiguousarray(inp[nm + "_norm_g"][0].reshape(NKC, 128))
        sh[nm + "_w_gate"] = inp[nm + "_w_gate"][0]
        sh[nm + "_w_up"] = inp[nm + "_w_up"][0]
        sh[nm + "_w_down"] = inp[nm + "_w_down"][0]
    sh["mix_norm_g"] = np.ascontiguousarray(inp["mix_norm_g"][0].reshape(NKC, 128))
    sh["w_ada"] = inp["w_ada"][0]
    sh["b_ada"] = np.ascontiguousarray(inp["b_ada"][0].reshape(144, 128))
    sh["w_in"] = inp["w_in"][0]
    sh["w_out"] = inp["w_out"][0]
    sh["gate_b"] = np.ascontiguousarray(inp["mlstm_gate_b"][0].reshape(8, 1))
    sh["qng"] = np.ascontiguousarray(inp["q_norm_g"][0].reshape(128, 1))
    sh["kng"] = np.ascontiguousarray(inp["k_norm_g"][0].reshape(128, 1))
    sh["conv_w"] = np.ascontiguousarray(inp["mlstm_conv_w"][0].reshape(4, NKC, 128).reshape(64, 128))
    sh["out_g"] = np.ascontiguousarray(inp["mlstm_out_g"][0].reshape(8, 128))
    sh["tri"] = np.triu(np.ones((64, 64), np.float32))
    sel = np.zeros((8, 8, 128), np.float32)
    for r_ in range(8):
        sel[r_, r_, :] = 1.0
    sh["sel8"] = sel.reshape(8, 1024)
    mk = np.ones((1, 512), np.float32)
    mk[0, ::64] = 0.0
    sh["mk64"] = mk
    first = (np.arange(64) % 4 == 0)
    mks = np.zeros((3, 64), np.float32)
    mks[0] = np.where(first, 0.0, 1.0)
    mks[1] = np.where(first, 0.0, -1e30)
    mks[2] = np.where(first, -1e30, 0.0)
    sh["mks"] = mks.reshape(1, 192)
    sh["bdp"], sh["bsp"], sh["bsn"] = host_bias_tables(np.asarray(inp["t5_bias"]))
    sh["pidx"] = np.arange(128, dtype=np.float32).reshape(128, 1)
    sh["ck"] = inp["cache_k"][0].reshape(-1, 256)
    sh["cv"] = inp["cache_v"][0].reshape(-1, 256)
    sh["cidx"] = inp["cache_idx_k"][0].reshape(-1, 64)
    sh["negtri"] = np.where(np.arange(128)[None, :] > np.arange(128)[:, None], np.float32(NEGB), np.float32(0.0)).astype(np.float32)
    return sh


def prep_core(inp, core, sh):
    m = dict(sh)
    m["xp"] = inp["x_prompt"][core]
    m["xs"] = np.ascontiguousarray(inp["x_sample"][NSB * core:NSB * (core + 1)].reshape(NS_TOK, D))
    sl = slice(NSB * core, NSB * (core + 1))
    m["ptab"] = np.ascontiguousarray(np.asarray(inp["page_table"][sl]).astype(np.int32).reshape(1, 256))
    m["st_C"] = np.ascontiguousarray(inp["state_C"][0, sl])
    m["st_n"] = np.ascontiguousarray(inp["state_n"][0, sl])
    m["st_m"] = np.ascontiguousarray(inp["state_m"][0, sl])
    m["st_conv"] = np.ascontiguousarray(inp["state_conv"][0, sl].reshape(NSB * 3, D))
    m["c17"] = np.ascontiguousarray(np.concatenate(
        [inp["c_prompt"][core:core + 1], inp["c_sample"][NSB * core:NSB * (core + 1)]], axis=0))
    return m


class NormBufs:
    pass


def norm_setup(C, sb, gn, sl, with_gate, gate_scale):
    P = C.P
    N = NormBufs()
    N.shb, N.scb, N.gtb = (3 * sl) * 16, (3 * sl + 1) * 16, (3 * sl + 2) * 16
    N.A17 = sb("A17", NKC * 17, F32)
    N.G17 = sb("G17", NKC * 17, F32)
    N.E1 = sb("E1", NKC * 64, F32)
    N.E2 = sb("E2", NKC * 64, F32)
    N.T1 = sb("T1", NKC * 64, F32)
    N.rstd = sb("rstd", TS, F32)
    N.xsq = [sb("xsq%d" % i, 512, BF16) for i in range(2)]
    N.tmp = [sb("tmp%d" % i, 512, F32) for i in range(2)]
    N.rxsq = [Res(), Res()]
    N.rtmp = [Res(), Res()]
    N.rA, N.rG, N.rE1, N.rE2, N.rT1, N.rrstd = Res(), Res(), Res(), Res(), Res(), Res()
    A17, G17, E1, E2 = N.A17, N.G17, N.E1, N.E2
    scb, gtb, shb = N.scb, N.gtb, N.shb
    P.add("dve", lambda e: e.tensor_scalar(A17.v(0, (1, NKC * 17)), C.mods.v(scb * 17, (1, NKC * 17)),
                                           1.0, None, op0=ALU.add), reads=[C.rmods], writes=[N.rA])
    P.add("dve", lambda e: e.tensor_tensor(out=A17.v(0, (17, NKC), (1, 17)), in0=A17.v(0, (17, NKC), (1, 17)),
                                           in1=gn.v(0, (1, NKC), (0, 17)), op=ALU.mult),
          reads=[N.rA, C.rgn], writes=[N.rA])
    P.add("dve", lambda e: e.tensor_scalar(G17.v(0, (1, NKC * 17)), C.mods.v(gtb * 17, (1, NKC * 17)),
                                           gate_scale, None, op0=ALU.mult), reads=[C.rmods], writes=[N.rG])
    P.add("dve", lambda e: e.tensor_copy(E1.v(0, (64, NKC), (4, NSB), (1, 4)),
                                         A17.v(1, (17, NKC), (1, NSB), (0, 4))), reads=[N.rA], writes=[N.rE1])
    P.add("dve", lambda e: e.tensor_copy(E2.v(0, (64, NKC), (4, NSB), (1, 4)),
                                         C.mods.v(shb * 17 + 1, (17, NKC), (1, NSB), (0, 4))),
          reads=[C.rmods], writes=[N.rE2])
    return N


def norm_sub(C, N, xt, xoff, xstride, rx_list, ht, hoff, rh, o, n, samp):
    P = C.P
    rstd, xsq, tmp, A17, E1, E2, T1 = N.rstd, N.xsq, N.tmp, N.A17, N.E1, N.E2, N.T1
    for c in range(NKC):
        s = c % 2
        P.add("act", lambda e, s=s, c=c: e.activation(
            out=xsq[s].v(0, (1, n)), in_=xt.v(c * xstride + xoff, (1, n)), func=AF.Square),
            reads=[rx_list[c]], writes=[N.rxsq[s]])
        P.add("pe", lambda e, s=s, c=c: e.matmul(
            C.ps[7].v(0, (1, n)), C.ones_b.v(0, (1, 128)), xsq[s].v(0, (1, n)),
            start=(c == 0), stop=(c == NKC - 1)), reads=[N.rxsq[s], C.rconst], writes=[C.rps[7]])
    P.add("act", lambda e: e.activation(
        out=rstd.v(o, (1, n)), in_=C.ps[7].v(0, (1, n)), func=AF.Sqrt, bias=C.epsb.v(0, (1, 1)),
        scale=1.0 / D), reads=[C.rps[7], C.rconst], writes=[N.rrstd])
    P.add("dve", lambda e: e.reciprocal(rstd.v(o, (1, n)), rstd.v(o, (1, n))),
          reads=[N.rrstd], writes=[N.rrstd])
    if not samp:
        for c in range(NKC):
            s = c % 2
            P.add("dve", lambda e, s=s, c=c: e.scalar_tensor_tensor(
                out=tmp[s].v(0, (1, n)), in0=xt.v(c * xstride + xoff, (1, n)), scalar=A17.v(c * 17, (1, 1)),
                in1=rstd.v(o, (1, n)), op0=ALU.mult, op1=ALU.mult),
                reads=[rx_list[c], N.rA, N.rrstd], writes=[N.rtmp[s]])
            P.add("act", lambda e, s=s, c=c: e.activation(
                out=ht.v(c * TS + hoff, (1, n)), in_=tmp[s].v(0, (1, n)), func=AF.Identity,
                bias=C.mods.v((N.shb + c) * 17, (1, 1)), scale=1.0),
                reads=[N.rtmp[s], C.rmods], writes=[rh])
    else:
        P.add("dve", lambda e: e.tensor_tensor(
            out=T1.v(0, (64, NKC), (1, 64)), in0=xt.v(xoff, (xstride, NKC), (1, 64)),
            in1=rstd.v(o, (0, NKC), (1, 64)), op=ALU.mult),
            reads=list(rx_list) + [N.rrstd], writes=[N.rT1])
        P.add("dve", lambda e: e.tensor_tensor(
            out=T1.v(0, (1, NKC * 64)), in0=T1.v(0, (1, NKC * 64)), in1=E1.v(0, (1, NKC * 64)),
            op=ALU.mult), reads=[N.rT1, N.rE1], writes=[N.rT1])
        P.add("dve", lambda e: e.tensor_tensor(
            out=ht.v(hoff, (TS, NKC), (1, 64)), in0=T1.v(0, (64, NKC), (1, 64)),
            in1=E2.v(0, (64, NKC), (1, 64)), op=ALU.add), reads=[N.rT1, N.rE2], writes=[rh])


WI_SCALE = (8 ** -0.5) * (64 ** -0.5)
TMB = 2
FM_GROUPS = [("qk", 0, 4, 0), ("qk", 512, 4, 4), ("qk", 1024, 4, 8), ("qk", 1536, 4, 12),
             ("om", 3072, 4, 0), ("om", 3584, 4, 4),
             ("qa", 4104, 4, 0), ("qa", 4616, 4, 4),
             ("ka", 5128, 2, 0), ("qi", 5640, 4, 0), ("vm", 2048, 4, 0), ("vm", 2560, 4, 1)]


def phase_proj(C):
    P, I, S, O, nc = C.P, C.I, C.S, C.O, C.nc
    w_in = I["w_in"]
    with contextlib.ExitStack() as es:
        def sb(name, width, dt):
            return SB(es.enter_context(nc.sbuf_tensor("pj_" + name, [128, padw(width, dt)], dt)), padw(width, dt), dt)
        N = norm_setup(C, sb, C.gn["mix_g"], 1, False, 1.0)
        xsb = sb("xsb", NKC * 512, F32)
        rxsb = [Res() for _ in range(NKC)]
        ht = sb("ht", NKC * TS, BF16)
        rh = [Res() for _ in range(3)]
        wbuf = [sb("wb%d" % i, NKC * 512, BF16) for i in range(2)]
        rwb = [Res(), Res()]
        wva = sb("wva", NKC * 256, BF16)
        wmisc = sb("wmisc", NKC * 80, BF16)
        rwtm = Res()
        stg32 = [sb("stg32_%d" % i, 512, F32) for i in range(3)]
        rstg32 = [Res() for _ in range(3)]
        stgb = [sb("stgb%d" % i, 512, BF16) for i in range(3)]
        rstgb = [Res() for _ in range(3)]
        vstg = [sb("vstg%d" % i, 2 * 257, BF16) for i in range(2)]
        rvstg = [Res(), Res()]
        tstg = [sb("tstg%d" % i, 320, F32) for i in range(2)]
        rtstg = [Res(), Res()]
        tstgb = [sb("tstgb%d" % i, 256, BF16) for i in range(2)]
        rtstgb = [Res(), Res()]
        tstw = [sb("tstw%d" % i, 8, F32) for i in range(2)]
        rtstw = [Res(), Res()]
        kn32 = sb("kn32", 2 * 512, F32)
        rkn = Res()
        kout = [sb("kout%d" % i, 256, F32) for i in range(2)]
        rkout = [Res(), Res()]
        hsq = [sb("hsq%d" % i, 512, BF16) for i in range(2)]
        rhsq = [Res(), Res()]
        hr = [sb("hr%d" % i, 512, F32) for i in range(2)]
        rhr = [Res(), Res()]
        cstg = sb("cstg", 512, F32)
        rcstg = Res()
        gb_ = sb("gateb", 1, F32)
        qng = sb("qng", 1, F32)
        kng = sb("kng", 1, F32)
        eps128 = C.epsb
        rsm = Res()
        P.add("sp", lambda e: e.dma_start(out=gb_.v(0, (1, 1), np_=8), in_=I["gate_b"]), writes=[rsm], dma=True)
        P.add("sp", lambda e: e.dma_start(out=qng.v(0, (1, 1)), in_=I["qng"]), writes=[rsm], dma=True)
        P.add("sp", lambda e: e.dma_start(out=kng.v(0, (1, 1)), in_=I["kng"]), writes=[rsm], dma=True)
        for k in range(2):
            P.add("dve", lambda e, k=k: e.memset(vstg[k].v(0, (1, 2 * 257)), 1.0), writes=[rvstg[k]])
        P.add("pool", lambda e: e.dma_start(out=wva.v(0, (256, NKC), (1, 256)),
                                            in_=dv(w_in, 5384, (PW, 128), (128 * PW, NKC), (1, 256))),
              writes=[rwtm], dma=True)
        wm32 = sb("wm32", NKC * 80, F32)
        rwm32 = Res()
        P.add("sp", lambda e: e.dma_start(out=wm32.v(0, (80, NKC), (1, 72)),
                                          in_=dv(w_in, 6152, (PW, 128), (128 * PW, NKC), (1, 72))),
              writes=[rwm32], dma=True)
        P.add("sp", lambda e: e.dma_start(out=wm32.v(72, (80, NKC), (1, 8)),
                                          in_=dv(w_in, 4096, (PW, 128), (128 * PW, NKC), (1, 8))),
              writes=[rwm32], dma=True)
        P.add("dve", lambda e: e.tensor_copy(wmisc.v(0, (1, NKC * 80)), wm32.v(0, (1, NKC * 80))),
              reads=[rwm32], writes=[rwtm])
        cnt = {"s32": 0, "sb": 0, "fm": 0, "ev": 0, "hs": 0, "ko": 0, "w": 0, "tb": 0}

        def evac(out_ap, in_ap, reads, writes):
            cnt["ev"] += 1
            if cnt["ev"] % 2 == 0:
                P.add("act", lambda e: e.activation(out=out_ap, in_=in_ap, func=AF.Copy), reads=reads, writes=writes)
            else:
                P.add("dve", lambda e: e.tensor_copy(out_ap, in_ap), reads=reads, writes=writes)

        for ti, (t0, T) in enumerate(TILES):
            subs = subs_of(t0)
            if ti > 0:
                P.barrier()
            for si, (o, n, samp) in enumerate(subs):
                P.add("sp", lambda e, o=o, n=n: e.dma_start(
                    out=xsb.v(0, (512, NKC), (1, n)),
                    in_=dv(S["X1"], t0 + o, (NKC * NT, 128), (NT, NKC), (1, n))),
                    reads=[C.rX1[ti]], writes=rxsb, dma=True)
                norm_sub(C, N, xsb, 0, 512, rxsb, ht, o, rh[si], o, n, samp)
            if ti == 0:
                C.dump("pj_ht0", ht, rh)
                C.dump("pj_xsb0", xsb, rxsb)
                C.dump("pj_rstd0", N.rstd, [N.rrstd])
                C.dump("pj_A17", N.A17, [N.rA])
            blocks = []
            for si, (o, n, samp) in enumerate(subs):
                if samp:
                    blocks.append((o, 64, si, True, 0))
                else:
                    for b in range(n // 128):
                        blocks.append((o + b * 128, 128, si, False, t0 + o + b * 128))
            for gi, (kind, col0, nch, cbase) in enumerate(FM_GROUPS):
                if ("skip_" + kind) in C.dbg:
                    continue
                ws = cnt["w"] % 2
                cnt["w"] += 1
                ncol = nch * 128
                P.add("pool", lambda e, ws=ws, col0=col0, ncol=ncol: e.dma_start(
                    out=wbuf[ws].v(0, (512, NKC), (1, ncol)),
                    in_=dv(w_in, col0, (PW, 128), (128 * PW, NKC), (1, ncol))), writes=[rwb[ws]], dma=True)
                if kind == "vm":
                    half = cbase
                    for (bo, nb, si, samp, trow) in blocks:
                        tb = cnt["tb"] % 2
                        cnt["tb"] += 1
                        grow = (NP_TOK if samp else trow)
                        pb = 2 + tb
                        for kc in range(NKC):
                            P.add("pe", lambda e, ws=ws, kc=kc, bo=bo, nb=nb, pb=pb: e.matmul(
                                C.ps[pb].v(0, (1, 512), np_=nb), ht.v(kc * TS + bo, (1, nb)),
                                wbuf[ws].v(kc * 512, (1, 512)), start=(kc == 0), stop=(kc == NKC - 1)),
                                reads=[rwb[ws], rh[si]], writes=[C.rps[pb]])
                        evac(vstg[tb].v(0, (257, 2), (1, 256), np_=nb),
                             C.ps[pb].v(0, (256, 2), (1, 256), np_=nb), [C.rps[pb]], [rvstg[tb]])
                        P.add("sp", lambda e, tb=tb, nb=nb, grow=grow, half=half: e.dma_start(
                            out=dv(S["VM"], grow * 1028 + half * 514, (1028, nb), (1, 514)),
                            in_=vstg[tb].v(0, (1, 514), np_=nb)), reads=[rvstg[tb]], dma=True)
                    continue
                for si, (o, n, samp) in enumerate(subs):
                    for j in range(nch):
                        pb = cnt["fm"] % 2
                        cnt["fm"] += 1
                        for kc in range(NKC):
                            P.add("pe", lambda e, ws=ws, j=j, kc=kc, o=o, n=n, pb=pb: e.matmul(
                                C.ps[pb].v(0, (1, n)), wbuf[ws].v(kc * 512 + j * 128, (1, 128)),
                                ht.v(kc * TS + o, (1, n)), start=(kc == 0), stop=(kc == NKC - 1)),
                                reads=[rwb[ws], rh[si]], writes=[C.rps[pb]])
                        cidx = cbase + j
                        if kind == "qk":
                            k = cnt["s32"] % 3
                            cnt["s32"] += 1
                            evac(stg32[k].v(0, (1, n)), C.ps[pb].v(0, (1, n)), [C.rps[pb]], [rstg32[k]])
                            P.add("sp", lambda e, k=k, cidx=cidx, o=o, n=n: e.dma_start(
                                out=dv(S["QK"], cidx * NT + t0 + o, (NKC * NT, 128), (1, n)),
                                in_=stg32[k].v(0, (1, n))), reads=[rstg32[k]], writes=[], dma=True)
                        elif kind in ("om", "qi"):
                            k = cnt["sb"] % 3
                            cnt["sb"] += 1
                            dst = S["OM"] if kind == "om" else S["QI"]
                            nchk = 8 if kind == "om" else 4
                            rdst = None
                            evac(stgb[k].v(0, (1, n)), C.ps[pb].v(0, (1, n)), [C.rps[pb]], [rstgb[k]])
                            P.add("sp", lambda e, k=k, cidx=cidx, o=o, n=n, dst=dst, nchk=nchk: e.dma_start(
                                out=dv(dst, cidx * NT + t0 + o, (nchk * NT, 128), (1, n)),
                                in_=stgb[k].v(0, (1, n))), reads=[rstgb[k]], dma=True)
                        else:
                            hs = cnt["hs"] % 2
                            cnt["hs"] += 1
                            gcol = qng if kind == "qa" else kng
                            P.add("act", lambda e, hs=hs, pb=pb, n=n: e.activation(
                                out=hsq[hs].v(0, (1, n)), in_=C.ps[pb].v(0, (1, n)), func=AF.Square),
                                reads=[C.rps[pb]], writes=[rhsq[hs]])
                            P.add("pe", lambda e, hs=hs, n=n: e.matmul(
                                C.ps[6].v(0, (1, n)), C.ones_b.v(0, (1, 128)), hsq[hs].v(0, (1, n)),
                                start=True, stop=True), reads=[rhsq[hs], C.rconst], writes=[C.rps[6]])
                            P.add("act", lambda e, hs=hs, n=n: e.activation(
                                out=hr[hs].v(0, (1, n)), in_=C.ps[6].v(0, (1, n)), func=AF.Sqrt,
                                bias=eps128.v(0, (1, 1)), scale=1.0 / 128), reads=[C.rps[6], C.rconst],
                                writes=[rhr[hs]])
                            P.add("dve", lambda e, hs=hs, n=n: e.reciprocal(hr[hs].v(0, (1, n)), hr[hs].v(0, (1, n))),
                                  reads=[rhr[hs]], writes=[rhr[hs]])
                            if kind == "qa":
                                k = cnt["sb"] % 3
                                cnt["sb"] += 1
                                P.add("dve", lambda e, hs=hs, k=k, pb=pb, n=n, gcol=gcol: e.scalar_tensor_tensor(
                                    out=stgb[k].v(0, (1, n)), in0=C.ps[pb].v(0, (1, n)), scalar=gcol.v(0, (1, 1)),
                                    in1=hr[hs].v(0, (1, n)), op0=ALU.mult, op1=ALU.mult),
                                    reads=[C.rps[pb], rhr[hs], rsm], writes=[rstgb[k]])
                                P.add("sp", lambda e, k=k, cidx=cidx, o=o, n=n: e.dma_start(
                                    out=dv(S["QA"], cidx * NT + t0 + o, (8 * NT, 128), (1, n)),
                                    in_=stgb[k].v(0, (1, n))), reads=[rstgb[k]], writes=[], dma=True)
                            else:
                                P.add("dve", lambda e, hs=hs, pb=pb, n=n, j=j, gcol=gcol: e.scalar_tensor_tensor(
                                    out=kn32.v(j * 512, (1, n)), in0=C.ps[pb].v(0, (1, n)), scalar=gcol.v(0, (1, 1)),
                                    in1=hr[hs].v(0, (1, n)), op0=ALU.mult, op1=ALU.mult),
                                    reads=[C.rps[pb], rhr[hs], rsm], writes=[rkn])
                                k = cnt["sb"] % 3
                                cnt["sb"] += 1
                                P.add("act", lambda e, k=k, j=j, n=n: e.activation(
                                    out=stgb[k].v(0, (1, n)), in_=kn32.v(j * 512, (1, n)), func=AF.Copy),
                                    reads=[rkn], writes=[rstgb[k]])
                                P.add("sp", lambda e, k=k, cidx=cidx, o=o, n=n: e.dma_start(
                                    out=dv(S["KA"], cidx * NT + t0 + o, (2 * NT, 128), (1, n)),
                                    in_=stgb[k].v(0, (1, n))), reads=[rstgb[k]], writes=[], dma=True)
                                if j == 1:
                                    nb = min(n, 128)
                                    for b in range(max(1, n // 128)):
                                        for hh in range(2):
                                            P.add("pe", lambda e, b=b, hh=hh, nb=nb: e.transpose(
                                                C.ps[5].v(hh * 128, (1, 128), np_=nb),
                                                kn32.v(hh * 512 + b * 128, (1, nb)), C.ident.v(0, (1, 128))),
                                                reads=[rkn, C.rconst], writes=[C.rps[5]])
                                        ko = cnt["ko"] % 2
                                        cnt["ko"] += 1
                                        evac(kout[ko].v(0, (1, 256), np_=nb), C.ps[5].v(0, (1, 256), np_=nb),
                                             [C.rps[5]], [rkout[ko]])
                                        if samp:
                                            dst, doff = O["k_s"], 0
                                        else:
                                            dst, doff = O["k_p"], (t0 + o + b * 128) * 256
                                        P.add("sp", lambda e, ko=ko, nb=nb, dst=dst, doff=doff: e.dma_start(
                                            out=dv(dst, doff, (256, nb), (1, 256)), in_=kout[ko].v(0, (1, 256), np_=nb)),
                                            reads=[rkout[ko]], dma=True)
                if kind == "qk" and ti == 1 and "skip_conv" not in C.dbg:
                    for kc in range(NKC):
                        P.add("pe", lambda e, ws=ws, kc=kc: e.matmul(
                            C.ps[5].v(0, (1, 512), np_=67), ht.v(kc * TS + 1021, (1, 67)),
                            wbuf[ws].v(kc * 512, (1, 512)), start=(kc == 0), stop=(kc == NKC - 1)),
                            reads=[rwb[ws], rh[1], rh[2]], writes=[C.rps[5]])
                    P.add("act", lambda e: e.activation(out=cstg.v(0, (1, 512), np_=67),
                                                        in_=C.ps[5].v(0, (1, 512), np_=67), func=AF.Copy),
                          reads=[C.rps[5]], writes=[rcstg])
                    P.add("sp", lambda e, col0=col0: e.dma_start(
                        out=dv(O["conv_p"], col0, (D, 3), (1, 512)), in_=cstg.v(0, (1, 512), np_=3)),
                        reads=[rcstg], dma=True)
                    for b in range(NSB if "skip_convs" not in C.dbg else 0):
                        P.add("sp", lambda e, col0=col0, b=b: e.dma_start(
                            out=dv(O["conv_s"], b * 3 * D + col0, (D, 3), (1, 512)),
                            in_=cstg.v(0, (1, 512), p0=3 + 4 * b + 1, np_=3)), reads=[rcstg], dma=True)
            for si, (o, n, samp) in enumerate(subs):
                if "skip_misc" in C.dbg:
                    continue
                pb = cnt["fm"] % 2
                cnt["fm"] += 1
                for dup in range(2):
                    for kc in range(NKC):
                        P.add("pe", lambda e, kc=kc, o=o, n=n, pb=pb, dup=dup: e.matmul(
                            C.ps[pb].v(0, (1, n), p0=64 * dup, np_=64), wmisc.v(kc * 80, (1, 64)),
                            ht.v(kc * TS + o, (1, n)), start=(kc == 0), stop=(kc == NKC - 1)),
                            reads=[rwtm, rh[si]], writes=[C.rps[pb]])
                k = cnt["sb"] % 3
                cnt["sb"] += 1
                evac(stgb[k].v(0, (1, n)), C.ps[pb].v(0, (1, n)), [C.rps[pb]], [rstgb[k]])
                P.add("sp", lambda e, k=k, o=o, n=n: e.dma_start(
                    out=dv(S["KI2"], t0 + o, (NT, 128), (1, n)), in_=stgb[k].v(0, (1, n))),
                    reads=[rstgb[k]], writes=[], dma=True)
                pb = cnt["fm"] % 2
                cnt["fm"] += 1
                for kc in range(NKC):
                    P.add("pe", lambda e, kc=kc, o=o, n=n, pb=pb: e.matmul(
                        C.ps[pb].v(0, (1, n), np_=8), wmisc.v(kc * 80 + 72, (1, 8)),
                        ht.v(kc * TS + o, (1, n)), start=(kc == 0), stop=(kc == NKC - 1)),
                        reads=[rwtm, rh[si]], writes=[C.rps[pb]])
                k = cnt["s32"] % 3
                cnt["s32"] += 1
                P.add("act", lambda e, k=k, pb=pb, n=n: e.activation(
                    out=stg32[k].v(0, (1, n), np_=8), in_=C.ps[pb].v(0, (1, n), np_=8), func=AF.Identity,
                    bias=gb_.v(0, (1, 1), np_=8), scale=1.0), reads=[C.rps[pb], rsm], writes=[rstg32[k]])
                P.add("sp", lambda e, k=k, o=o, n=n: e.dma_start(
                    out=dv(S["GT"], t0 + o, (NT, 8), (1, n)), in_=stg32[k].v(0, (1, n), np_=8)),
                    reads=[rstg32[k]], writes=[], dma=True)
            for (bo, nb, si, samp, trow) in blocks:
                if "skip_tm" in C.dbg:
                    continue
                tb = cnt["tb"] % 2
                cnt["tb"] += 1
                grow = (NP_TOK if samp else trow)
                for kc in range(NKC):
                    P.add("pe", lambda e, kc=kc, bo=bo, nb=nb: e.matmul(
                        C.ps[TMB].v(0, (1, 256), np_=nb), ht.v(kc * TS + bo, (1, nb)),
                        wva.v(kc * 256, (1, 256)), start=(kc == 0), stop=(kc == NKC - 1)),
                        reads=[rwtm, rh[si]], writes=[C.rps[TMB]])
                for kc in range(NKC if "tm_nokiwi" not in C.dbg else 0):
                    P.add("pe", lambda e, kc=kc, bo=bo, nb=nb: e.matmul(
                        C.ps[TMB].v(256, (1, 72), np_=nb), ht.v(kc * TS + bo, (1, nb)),
                        wmisc.v(kc * 80, (1, 72)), start=(kc == 0), stop=(kc == NKC - 1)),
                        reads=[rwtm, rh[si]], writes=[C.rps[TMB]])
                if "tm_nodve" not in C.dbg:
                    P.add("act", lambda e, tb=tb, nb=nb: e.activation(
                        out=tstg[tb].v(0, (1, 320), np_=nb), in_=C.ps[TMB].v(0, (1, 320), np_=nb), func=AF.Copy),
                        reads=[C.rps[TMB]], writes=[rtstg[tb]])
                if "tm_noact" not in C.dbg:
                    P.add("act", lambda e, tb=tb, nb=nb: e.activation(
                        out=tstgb[tb].v(0, (1, 256), np_=nb), in_=C.ps[TMB].v(0, (1, 256), np_=nb), func=AF.Copy),
                        reads=[C.rps[TMB]], writes=[rtstgb[tb]])
                if "skip_wi" not in C.dbg:
                    P.add("act", lambda e, tb=tb, nb=nb: e.activation(
                        out=tstw[tb].v(0, (1, 8), np_=nb), in_=C.ps[TMB].v(320, (1, 8), np_=nb), func=AF.Copy,
                        scale=WI_SCALE), reads=[C.rps[TMB]], writes=[rtstw[tb]])
                vdst, vrow = (O["v_s"], 0) if samp else (O["v_p"], trow)
                idst = O["idxk_s"] if samp else O["idxk_p"]
                if "tm_novout" not in C.dbg:
                    P.add("sp", lambda e, tb=tb, nb=nb, vdst=vdst, vrow=vrow: e.dma_start(
                        out=dv(vdst, vrow * 256, (256, nb), (1, 256)), in_=tstg[tb].v(0, (1, 256), np_=nb)),
                        reads=[rtstg[tb]], dma=True)
                if "skip_idxk" not in C.dbg:
                    P.add("sp", lambda e, tb=tb, nb=nb, idst=idst, vrow=vrow: e.dma_start(
                        out=dv(idst, vrow * 64, (64, nb), (1, 64)), in_=tstg[tb].v(256, (1, 64), np_=nb)),
                        reads=[rtstg[tb]], dma=True)
                if "tm_nova" not in C.dbg:
                    P.add("sp", lambda e, tb=tb, nb=nb, grow=grow: e.dma_start(
                        out=dv(S["VA"], grow * 256, (256, nb), (1, 256)), in_=tstgb[tb].v(0, (1, 256), np_=nb)),
                        reads=[rtstgb[tb]], writes=[], dma=True)
                if "skip_wi" not in C.dbg:
                    P.add("sp", lambda e, tb=tb, nb=nb, grow=grow: e.dma_start(
                        out=dv(S["WI"], grow * 8, (8, nb), (1, 8)), in_=tstw[tb].v(0, (1, 8), np_=nb)),
                        reads=[rtstw[tb]], writes=[], dma=True)
        P.barrier()


LN16 = 2.772588722239781


def phase_mlstm(C):
    P, I, S, O, nc = C.P, C.I, C.S, C.O, C.nc
    with contextlib.ExitStack() as es:
        def sb(name, width, dt):
            return SB(es.enter_context(nc.sbuf_tensor("ml_" + name, [128, padw(width, dt)], dt)), padw(width, dt), dt)
        C.sb_save = C.sb
        C.sb = sb
        psb = [SB(p.t.bitcast(BF16), 1024, BF16) for p in C.ps]
        cw = sb("cw", 64, F32)
        rcw = Res()
        load_featmajor_small(C, cw, I["conv_w"], 64, rcw, "tmp_cw")
        outg = sb("outg", 8, F32)
        load_featmajor_small(C, outg, I["out_g"], 8, rcw, "tmp_og")
        tri = sb("tri", 64, F32)
        sel8 = sb("sel8", 1024, F32)
        mk64 = sb("mk64", 512, F32)
        mks = sb("mks", 3 * 64, F32)
        m0r = sb("m0r", 64, F32)
        rk = Res()
        P.add("sp", lambda e: e.dma_start(out=tri.v(0, (1, 64), np_=64), in_=I["tri"]), writes=[rk], dma=True)
        P.add("sp", lambda e: e.dma_start(out=sel8.v(0, (1, 1024), np_=8), in_=I["sel8"]), writes=[rk], dma=True)
        P.add("sp", lambda e: e.dma_start(out=mk64.v(0, (1, 512)), in_=dv(I["mk64"], 0, (0, 128), (1, 512))),
              writes=[rk], dma=True)
        P.add("sp", lambda e: e.dma_start(out=mks.v(0, (1, 192)), in_=dv(I["mks"], 0, (0, 128), (1, 192))),
              writes=[rk], dma=True)
        for h in range(4):
            P.add("sp", lambda e, h=h: e.dma_start(out=m0r.v(h * 16, (1, 16)),
                                                   in_=dv(I["st_m"], h, (0, 128), (4, 16))), writes=[rk], dma=True)
        NG = 512
        xc = sb("xc", NKC * 515, F32)
        rxc = Res()
        ycv = [sb("ycv%d" % i, 512, F32) for i in range(2)]
        rycv = [Res(), Res()]
        qk = sb("qk", NKC * NG, BF16)
        rqk = Res()
        gtg = sb("gtg", NG, F32)
        rgtg = Res()
        R = [sb("R%d" % i, 4 * NG, F32) for i in range(6)]
        rR = [Res() for _ in range(6)]
        carry = sb("carry", 4, F32)
        rcarry = Res()
        hbuf = sb("hbuf", 8 * NG, F32)
        rhb = Res()
        omt = sb("omt", 8 * NG, BF16)
        romt = Res()
        mot = sb("mot", 8 * NG, BF16)
        rmot = Res()
        Cst = [sb("Cst%d" % i, 4 * 514, F32) for i in range(2)]
        rCst = [[Res() for _ in range(4)] for _ in range(2)]
        Cb = [sb("Cb%d" % i, 4 * 514, BF16) for i in range(2)]
        rCb = [[Res() for _ in range(4)] for _ in range(2)]
        vt = [sb("vt%d" % i, 1028, BF16) for i in range(2)]
        rvt = [Res(), Res()]
        acol = [sb("acol%d" % i, 1, F32) for i in range(2)]
        racol = [Res(), Res()]
        wl = [sb("wl%d" % i, 1, F32) for i in range(2)]
        rwl = [Res(), Res()]
        DT = [sb("DT%d" % i, 64, F32) for i in range(2)]
        rDT = [Res(), Res()]
        Wt = [sb("Wt%d" % i, 64, BF16) for i in range(2)]
        rWt = [Res(), Res()]
        qs = [sb("qs%d" % i, 128, BF16) for i in range(2)]
        rqs = [Res(), Res()]
        kw = [sb("kw%d" % i, 256, BF16) for i in range(2)]
        rkw = [Res(), Res()]
        rr = [sb("rr%d" % i, 64, F32) for i in range(2)]
        rrr = [Res(), Res()]
        sq = [sb("sq%d" % i, NG, BF16) for i in range(2)]
        rsq = [Res(), Res()]
        rs_ = sb("rs_", NG, F32)
        rrs = Res()
        sg = [sb("sg%d" % i, NG, BF16) for i in range(2)]
        rsg = [Res(), Res()]
        tmpd = [sb("tmpd%d" % i, NG, F32) for i in range(2)]
        rtmpd = [Res(), Res()]
        stc = SBV(xc, 0)
        xs7 = SBV(xc, 2048)
        xsn = SBV(xc, 2048 + NKC * NSB * 7)
        rxs7 = rxc
        rxsn = rxc
        rstc = rxc
        P.add("dve", lambda e: e.memset(Cst[0].v(0, (1, 4 * 514)), 0.0), writes=rCst[0])
        P.add("dve", lambda e: e.memset(Cb[0].v(0, (1, 4 * 514)), 0.0), writes=rCb[0])
        it = {"n": 0}

        def run_group(t0, n, L, sample):
            nch = n // L
            if not sample:
                if t0 == 0:
                    P.add("dve", lambda e: e.memset(xc.v(0, (515, NKC), (1, 3)), 0.0), writes=[rxc])
                    P.add("sp", lambda e: e.dma_start(out=xc.v(3, (515, NKC), (1, 512)),
                                                      in_=dv(S["QK"], 0, (NKC * NT, 128), (NT, NKC), (1, 512))),
                          writes=[rxc], dma=True)
                else:
                    P.add("sp", lambda e: e.dma_start(out=xc.v(0, (515, NKC), (1, 515)),
                                                      in_=dv(S["QK"], t0 - 3, (NKC * NT, 128), (NT, NKC), (1, 515))),
                          writes=[rxc], dma=True)
                for c in range(NKC):
                    s = c % 2
                    for j in (3, 2, 1, 0):
                        if j == 3:
                            P.add("dve", lambda e, s=s, c=c, j=j: e.tensor_scalar(
                                ycv[s].v(0, (1, 512)), xc.v(c * 515 + j, (1, 512)), cw.v(j * 16 + c, (1, 1)), None,
                                op0=ALU.mult), reads=[rxc, rcw], writes=[rycv[s]])
                        else:
                            P.add("dve", lambda e, s=s, c=c, j=j: e.scalar_tensor_tensor(
                                out=ycv[s].v(0, (1, 512)), in0=xc.v(c * 515 + j, (1, 512)),
                                scalar=cw.v(j * 16 + c, (1, 1)), in1=ycv[s].v(0, (1, 512)),
                                op0=ALU.mult, op1=ALU.add), reads=[rxc, rcw, rycv[s]], writes=[rycv[s]])
                    P.add("act", lambda e, s=s, c=c: e.activation(
                        out=qk.v(c * NG, (1, 512)), in_=ycv[s].v(0, (1, 512)), func=AF.Silu),
                        reads=[rycv[s]], writes=[rqk])
            else:
                P.add("sp", lambda e: e.dma_start(out=stc.v(0, (1, D), np_=48), in_=I["st_conv"]),
                      writes=[rstc], dma=True)
                P.add("sp", lambda e: e.dma_start(out=xsn.v(0, (64, NKC), (1, 64)),
                                                  in_=dv(S["QK"], NP_TOK, (NKC * NT, 128), (NT, NKC), (1, 64))),
                      writes=[rxsn], dma=True)
                for c4 in range(4):
                    for j in range(4):
                        c = c4 * 4 + j
                        P.add("pe", lambda e, c=c, j=j: e.transpose(
                            C.ps[0].v(j * 48, (1, 48)), stc.v(c * 128, (1, 128), np_=48),
                            C.ident.v(0, (1, 48), np_=48)), reads=[rstc, C.rconst], writes=[C.rps[0]])
                    P.add("dve", lambda e, c4=c4: e.tensor_copy(
                        xs7.v(c4 * 4 * 112, (112, 4), (7, NSB), (1, 3)), C.ps[0].v(0, (48, 4), (3, NSB), (1, 3))),
                        reads=[C.rps[0]], writes=[rxs7])
                P.add("dve", lambda e: e.tensor_copy(xs7.v(3, (112, NKC), (7, NSB), (1, 4)),
                                                     xsn.v(0, (64, NKC), (4, NSB), (1, 4))),
                      reads=[rxsn, rxs7], writes=[rxs7])
                for c in range(NKC):
                    s = c % 2
                    for j in (3, 2, 1, 0):
                        if j == 3:
                            P.add("dve", lambda e, s=s, c=c, j=j: e.tensor_scalar(
                                ycv[s].v(0, (4, NSB), (1, 4)), xs7.v(c * 112 + j, (7, NSB), (1, 4)),
                                cw.v(j * 16 + c, (1, 1)), None, op0=ALU.mult),
                                reads=[rxs7, rcw], writes=[rycv[s]])
                        else:
                            P.add("dve", lambda e, s=s, c=c, j=j: e.scalar_tensor_tensor(
                                out=ycv[s].v(0, (4, NSB), (1, 4)), in0=xs7.v(c * 112 + j, (7, NSB), (1, 4)),
                                scalar=cw.v(j * 16 + c, (1, 1)), in1=ycv[s].v(0, (4, NSB), (1, 4)),
                                op0=ALU.mult, op1=ALU.add), reads=[rxs7, rcw, rycv[s]], writes=[rycv[s]])
                    P.add("act", lambda e, s=s, c=c: e.activation(
                        out=qk.v(c * NG, (1, 64)), in_=ycv[s].v(0, (1, 64)), func=AF.Silu),
                        reads=[rycv[s]], writes=[rqk])
            P.add("sp", lambda e: e.dma_start(out=gtg.v(0, (1, n), np_=8), in_=dv(S["GT"], t0, (NT, 8), (1, n))),
                  writes=[rgtg], dma=True)
            for r_ in range(8):
                bank = r_ % 2
                P.add("pe", lambda e, r_=r_, bank=bank: e.matmul(
                    C.ps[bank].v(0, (1, n)), sel8.v(r_ * 128, (1, 128), np_=8), gtg.v(0, (1, n), np_=8),
                    start=True, stop=True), reads=[rgtg, rk], writes=[C.rps[bank]])
                dst = R[0] if r_ < 4 else R[1]
                rd = rR[0] if r_ < 4 else rR[1]
                P.add("act", lambda e, r_=r_, bank=bank, dst=dst: e.activation(
                    out=dst.v((r_ % 4) * NG, (1, n)), in_=C.ps[bank].v(0, (1, n)), func=AF.Copy),
                    reads=[C.rps[bank]], writes=[rd])

            def all4(Rt):
                return Rt.v(0, (NG, 4), (1, n))

            P.add("dve", lambda e: e.scalar_tensor_tensor(out=all4(R[2]), in0=all4(R[1]), scalar=-1.0, in1=all4(R[1]),
                                                          op0=ALU.mult, op1=ALU.max), reads=[rR[1]], writes=[rR[2]])
            P.add("act", lambda e: e.activation(out=all4(R[2]), in_=all4(R[2]), func=AF.Exp, scale=-1.0),
                  reads=[rR[2]], writes=[rR[2]])
            P.add("act", lambda e: e.activation(out=all4(R[2]), in_=all4(R[2]), func=AF.Ln, bias=C.oneb.v(0, (1, 1)),
                                                scale=1.0), reads=[rR[2], C.rconst], writes=[rR[2]])
            P.add("dve", lambda e: e.scalar_tensor_tensor(out=all4(R[1]), in0=all4(R[1]), scalar=0.0, in1=all4(R[2]),
                                                          op0=ALU.min, op1=ALU.subtract),
                  reads=[rR[1], rR[2]], writes=[rR[1]])
            for h in range(4):
                mask = mks.v(0, (1, n)) if sample else mk64.v(0, (1, n))
                P.add("dve", lambda e, h=h, mask=mask: e.tensor_tensor_scan(
                    R[2].v(h * NG, (1, n)), mask, R[1].v(h * NG, (1, n)), 0.0, op0=ALU.mult, op1=ALU.add),
                    reads=[rR[1], rk], writes=[rR[2]])
            if sample:
                P.add("dve", lambda e: e.tensor_copy(R[5].v(0, (NG, 4), (4, NSB), (1, 4)),
                                                     m0r.v(0, (16, 4), (1, NSB), (0, 4))), reads=[rk], writes=[rR[5]])
                P.add("dve", lambda e: e.tensor_tensor(out=all4(R[4]), in0=all4(R[5]), in1=all4(R[1]), op=ALU.add),
                      reads=[rR[5], rR[1]], writes=[rR[4]])
                P.add("dve", lambda e: e.tensor_tensor(out=all4(R[4]), in0=all4(R[4]),
                                                       in1=mks.v(64, (0, 4), (1, n)), op=ALU.add),
                      reads=[rR[4], rk], writes=[rR[4]])
                P.add("dve", lambda e: e.tensor_tensor(out=all4(R[4]), in0=all4(R[4]), in1=all4(R[0]), op=ALU.max),
                      reads=[rR[4], rR[0]], writes=[rR[4]])
                P.add("dve", lambda e: e.tensor_tensor(out=all4(R[3]), in0=all4(R[1]),
                                                       in1=mks.v(128, (0, 4), (1, n)), op=ALU.add),
                      reads=[rR[1], rk], writes=[rR[3]])
                for h in range(4):
                    P.add("dve", lambda e, h=h: e.tensor_tensor_scan(
                        R[3].v(h * NG, (1, n)), R[3].v(h * NG, (1, n)), R[4].v(h * NG, (1, n)), 0.0,
                        op0=ALU.add, op1=ALU.max), reads=[rR[3], rR[4]], writes=[rR[3]])
            else:
                for h in range(4):
                    init = 0.0 if t0 == 0 else carry.v(h, (1, 1))
                    P.add("dve", lambda e, h=h, init=init: e.tensor_tensor_scan(
                        R[3].v(h * NG, (1, n)), R[1].v(h * NG, (1, n)), R[0].v(h * NG, (1, n)), init,
                        op0=ALU.add, op1=ALU.max), reads=[rR[1], rR[0], rcarry], writes=[rR[3]])
                if t0 == 0:
                    P.add("dve", lambda e: e.memset(R[5].v(0, (NG, 4), (1, L)), 0.0), writes=[rR[5]])
                else:
                    P.add("dve", lambda e: e.tensor_copy(R[5].v(0, (NG, 4), (1, L)), carry.v(0, (1, 4), (0, L))),
                          reads=[rcarry], writes=[rR[5]])
                P.add("dve", lambda e: e.tensor_copy(R[5].v(L, (NG, 4), (L, nch - 1), (1, L)),
                                                     R[3].v(L - 1, (NG, 4), (L, nch - 1), (0, L))),
                      reads=[rR[3], rR[5]], writes=[rR[5]])
                P.add("dve", lambda e: e.tensor_copy(carry.v(0, (1, 4)), R[3].v(n - 1, (NG, 4))),
                      reads=[rR[3], rR[5]], writes=[rcarry])
            P.add("dve", lambda e: e.scalar_tensor_tensor(out=all4(R[4]), in0=all4(R[0]), scalar=-LN16, in1=all4(R[2]),
                                                          op0=ALU.add, op1=ALU.subtract),
                  reads=[rR[0], rR[2], rR[3]], writes=[rR[4]])
            P.add("dve", lambda e: e.tensor_tensor(out=all4(R[2]), in0=all4(R[2]), in1=all4(R[3]), op=ALU.subtract),
                  reads=[rR[2], rR[3], rR[4]], writes=[rR[2]])
            P.add("dve", lambda e: e.tensor_tensor(out=all4(R[1]), in0=all4(R[5]), in1=all4(R[2]), op=ALU.add),
                  reads=[rR[5], rR[2], rR[3]], writes=[rR[1]])
            P.add("act", lambda e: e.activation(out=all4(R[1]), in_=all4(R[1]), func=AF.Exp),
                  reads=[rR[1]], writes=[rR[1]])
            P.add("act", lambda e: e.activation(out=all4(R[5]), in_=all4(R[3]), func=AF.Exp, scale=-1.0),
                  reads=[rR[3], rR[1]], writes=[rR[5]])
            if sample:
                for h in range(4):
                    P.add("sp", lambda e, h=h: e.dma_start(out=dv(O["m_s"], h, (64, 1), (4, NSB)),
                                                           in_=R[3].v(h * NG + 3, (4, NSB), np_=1)),
                          reads=[rR[3]], dma=True)
            elif t0 + n == NP_TOK:
                P.add("sp", lambda e: e.dma_start(out=dv(O["m_p"], 0, (4, 1), (1, 4)),
                                                  in_=R[3].v(n - 1, (NG, 4), np_=1)), reads=[rR[3]], dma=True)
            P.add("sp", lambda e: e.dma_start(out=omt.v(0, (NG, 8), (1, n)),
                                              in_=dv(S["OM"], t0, (8 * NT, 128), (NT, 8), (1, n))),
                  writes=[romt], dma=True)
            for c in range(nch):
                cs = c % 2
                tok = t0 + c * L
                P.add("sp", lambda e, cs=cs, tok=tok: e.dma_start(
                    out=vt[cs].v(0, (1, 1028), np_=L), in_=dv(S["VM"], tok * 1028, (1028, L), (1, 1028))),
                    writes=[rvt[cs]], dma=True)
                if sample:
                    st = c % 2
                    P.add("sp", lambda e, st=st, c=c: e.dma_start(
                        out=Cst[st].v(0, (514, 4), (257, 2), (1, 256)),
                        in_=dv(I["st_C"], c * 4 * 65536, (256, 128), (65536, 4), (128 * 256, 2), (1, 256))),
                        writes=rCst[st], dma=True)
                    P.add("sp", lambda e, st=st, c=c: e.dma_start(
                        out=Cst[st].v(256, (514, 4), (257, 2), (1, 1)),
                        in_=dv(I["st_n"], c * 1024, (1, 128), (256, 4), (128, 2), (1, 1))),
                        writes=rCst[st], dma=True)
                    P.add("act", lambda e, st=st: e.activation(out=Cb[st].v(0, (1, 4 * 514)),
                                                               in_=Cst[st].v(0, (1, 4 * 514)), func=AF.Copy),
                          reads=rCst[st], writes=rCb[st])
                else:
                    st = 0
                def hgen(h, c=c, cs=cs, st=st):
                    s = h % 2
                    b0 = 4 * s
                    col = h * NG + c * L
                    qc, kc_ = 2 * h, 8 + 2 * h
                    P.add("pe", lambda e, col=col, b0=b0: e.transpose(
                        C.ps[b0].v(64, (1, 128), np_=L), R[4].v(col, (1, L)), C.ident.v(0, (1, 128))),
                        reads=[rR[4], C.rconst], writes=[C.rps[b0]])
                    P.add("dve", lambda e, s=s, b0=b0: e.tensor_copy(acol[s].v(0, (1, 1), np_=L),
                                                                     C.ps[b0].v(64, (1, 1), np_=L)),
                          reads=[C.rps[b0]], writes=[racol[s]])
                    yield
                    for dc in range(2):
                        P.add("pe", lambda e, dc=dc, b0=b0, c=c: e.matmul(
                            C.ps[b0].v(0, (1, L), np_=L), qk.v((kc_ + dc) * NG + c * L, (1, L)),
                            qk.v((qc + dc) * NG + c * L, (1, L)), start=(dc == 0), stop=(dc == 1)),
                            reads=[rqk], writes=[C.rps[b0]])
                    yield
                    P.add("act", lambda e, s=s, col=col: e.activation(
                        out=DT[s].v(0, (1, L), np_=L), in_=R[2].v(col, (1, L), np_=L), func=AF.Exp,
                        bias=acol[s].v(0, (1, 1), np_=L), scale=1.0), reads=[rR[2], racol[s]], writes=[rDT[s]])
                    yield
                    P.add("pool", lambda e, s=s: e.tensor_tensor(
                        out=DT[s].v(0, (1, L), np_=L), in0=DT[s].v(0, (1, L), np_=L), in1=tri.v(0, (1, L), np_=L),
                        op=ALU.mult), reads=[rDT[s], rk], writes=[rDT[s]])
                    yield
                    P.add("dve", lambda e, s=s, b0=b0: e.tensor_tensor(
                        out=Wt[s].v(0, (1, L), np_=L), in0=C.ps[b0].v(0, (1, L), np_=L), in1=DT[s].v(0, (1, L), np_=L),
                        op=ALU.mult), reads=[C.rps[b0], rDT[s]], writes=[rWt[s]])
                    P.add("dve", lambda e, s=s, col=col, c=c: e.tensor_tensor(
                        out=qs[s].v(0, (64, 2), (1, L)), in0=qk.v(qc * NG + c * L, (NG, 2), (1, L)),
                        in1=R[1].v(col, (0, 2), (1, L)), op=ALU.mult), reads=[rqk, rR[1]], writes=[rqs[s]])
                    yield
                    for ec in range(2):
                        P.add("pe", lambda e, s=s, cs=cs, ec=ec, b0=b0: e.matmul(
                            C.ps[b0 + 1].v(ec * 64, (1, L)), vt[cs].v(h * 257 + ec * 128, (1, 128), np_=L),
                            Wt[s].v(0, (1, L), np_=L), start=True, stop=False),
                            reads=[rvt[cs], rWt[s]], writes=[C.rps[b0 + 1]])
                        for dc in range(2):
                            P.add("pe", lambda e, s=s, st=st, ec=ec, dc=dc, b0=b0: e.matmul(
                                C.ps[b0 + 1].v(ec * 64, (1, L)), Cb[st].v(h * 514 + dc * 257 + ec * 128, (1, 128)),
                                qs[s].v(dc * 64, (1, L)), start=False, stop=(dc == 1)),
                                reads=[rCb[st][h], rqs[s]], writes=[C.rps[b0 + 1]])
                    P.add("pe", lambda e, s=s, b0=b0: e.matmul(
                        C.ps[b0 + 1].v(128, (1, L)), C.ones_b.v(0, (1, 128), np_=L), Wt[s].v(0, (1, L), np_=L),
                        start=True, stop=False), reads=[rWt[s], C.rconst], writes=[C.rps[b0 + 1]])
                    for dc in range(2):
                        P.add("pe", lambda e, s=s, st=st, dc=dc, b0=b0: e.matmul(
                            C.ps[b0 + 1].v(128, (1, L)), Cb[st].v(h * 514 + dc * 257 + 256, (0, 128)),
                            qs[s].v(dc * 64, (1, L)), start=False, stop=(dc == 1)),
                            reads=[rCb[st][h], rqs[s]], writes=[C.rps[b0 + 1]])
                    yield
                    P.add("act", lambda e, s=s, b0=b0: e.activation(
                        out=rr[s].v(0, (1, L)), in_=C.ps[b0 + 1].v(128, (1, L)), func=AF.Abs),
                        reads=[C.rps[b0 + 1]], writes=[rrr[s]])
                    P.add("dve", lambda e, s=s, col=col: e.tensor_tensor(
                        out=rr[s].v(0, (1, L)), in0=rr[s].v(0, (1, L)), in1=R[5].v(col, (1, L)), op=ALU.max),
                        reads=[rrr[s], rR[5]], writes=[rrr[s]])
                    P.add("dve", lambda e, s=s: e.reciprocal(rr[s].v(0, (1, L)), rr[s].v(0, (1, L))),
                          reads=[rrr[s]], writes=[rrr[s]])
                    P.add("dve", lambda e, s=s, b0=b0, c=c: e.tensor_tensor(
                        out=hbuf.v(2 * h * NG + c * L, (NG, 2), (1, L)), in0=C.ps[b0 + 1].v(0, (64, 2), (1, L)),
                        in1=rr[s].v(0, (0, 2), (1, L)), op=ALU.mult), reads=[C.rps[b0 + 1], rrr[s]], writes=[rhb])
                    yield
                    for dc in range(2):
                        P.add("pe", lambda e, dc=dc, b0=b0, c=c: e.transpose(
                            psb[b0].v(512 + dc * 128, (1, 128), np_=L), qk.v((kc_ + dc) * NG + c * L, (1, L)),
                            C.identb.v(0, (1, 128))), reads=[rqk, C.rconst], writes=[C.rps[b0]])
                    P.add("act", lambda e, s=s, col=col: e.activation(
                        out=wl[s].v(0, (1, 1), np_=L), in_=acol[s].v(0, (1, 1), np_=L), func=AF.Exp,
                        bias=R[2].v(col + L - 1, (1, 1), np_=L), scale=1.0), reads=[racol[s], rR[2]], writes=[rwl[s]])
                    yield
                    P.add("dve", lambda e, s=s, b0=b0: e.tensor_scalar(
                        kw[s].v(0, (1, 256), np_=L), psb[b0].v(512, (1, 256), np_=L), wl[s].v(0, (1, 1), np_=L), None,
                        op0=ALU.mult), reads=[C.rps[b0], rwl[s]], writes=[rkw[s]])
                    yield
                    for dc in range(2):
                        P.add("pe", lambda e, s=s, cs=cs, dc=dc, b0=b0: e.matmul(
                            C.ps[b0 + 2 + dc].v(0, (1, 257)), kw[s].v(dc * 128, (1, 128), np_=L),
                            vt[cs].v(h * 257, (1, 257), np_=L), start=True, stop=True),
                            reads=[rkw[s], rvt[cs]], writes=[C.rps[b0 + 2 + dc]])
                        P.add("dve", lambda e, st=st, dc=dc, b0=b0, col=col: e.scalar_tensor_tensor(
                            out=Cst[st].v(h * 514 + dc * 257, (1, 257)), in0=Cst[st].v(h * 514 + dc * 257, (1, 257)),
                            scalar=R[1].v(col + L - 1, (1, 1)), in1=C.ps[b0 + 2 + dc].v(0, (1, 257)),
                            op0=ALU.mult, op1=ALU.add), reads=[rCst[st][h], rR[1], C.rps[b0 + 2 + dc]],
                            writes=[rCst[st][h]])
                    if not sample:
                        P.add("act", lambda e, st=st: e.activation(
                            out=Cb[st].v(h * 514, (1, 514)), in_=Cst[st].v(h * 514, (1, 514)), func=AF.Copy),
                            reads=[rCst[st][h]], writes=[rCb[st][h]])
                for pair in ((0, 1), (2, 3)):
                    gens = [hgen(h) for h in pair]
                    alive = True
                    while alive:
                        alive = False
                        for g_ in gens:
                            try:
                                next(g_)
                                alive = True
                            except StopIteration:
                                pass
                if sample:
                    P.add("sp", lambda e, st=st, c=c: e.dma_start(
                        out=dv(O["C_s"], c * 4 * 65536, (256, 128), (65536, 4), (128 * 256, 2), (1, 256)),
                        in_=Cst[st].v(0, (514, 4), (257, 2), (1, 256))), reads=rCst[st], dma=True)
                    P.add("sp", lambda e, st=st, c=c: e.dma_start(
                        out=dv(O["n_s"], c * 1024, (1, 128), (256, 4), (128, 2), (1, 1)),
                        in_=Cst[st].v(256, (514, 4), (257, 2), (1, 1))), reads=rCst[st], dma=True)
            if (not sample) and t0 + n == NP_TOK:
                P.add("sp", lambda e: e.dma_start(
                    out=dv(O["C_p"], 0, (256, 128), (65536, 4), (128 * 256, 2), (1, 256)),
                    in_=Cst[0].v(0, (514, 4), (257, 2), (1, 256))), reads=rCst[0], dma=True)
                P.add("sp", lambda e: e.dma_start(
                    out=dv(O["n_p"], 0, (1, 128), (256, 4), (128, 2), (1, 1)),
                    in_=Cst[0].v(256, (514, 4), (257, 2), (1, 1))), reads=rCst[0], dma=True)
            for h in range(4):
                for ec in range(2):
                    P.add("act", lambda e, h=h, ec=ec: e.activation(
                        out=sq[ec].v(0, (1, n)), in_=hbuf.v((2 * h + ec) * NG, (1, n)), func=AF.Square),
                        reads=[rhb], writes=[rsq[ec]])
                    P.add("pe", lambda e, ec=ec: e.matmul(
                        C.ps[0].v(0, (1, n)), C.ones_b.v(0, (1, 128)), sq[ec].v(0, (1, n)),
                        start=(ec == 0), stop=(ec == 1)), reads=[rsq[ec], C.rconst], writes=[C.rps[0]])
                P.add("act", lambda e: e.activation(out=rs_.v(0, (1, n)), in_=C.ps[0].v(0, (1, n)), func=AF.Sqrt,
                                                    bias=C.epsb.v(0, (1, 1)), scale=1.0 / 256),
                      reads=[C.rps[0], C.rconst], writes=[rrs])
                P.add("dve", lambda e: e.reciprocal(rs_.v(0, (1, n)), rs_.v(0, (1, n))), reads=[rrs], writes=[rrs])
                for ec in range(2):
                    ch = 2 * h + ec
                    P.add("act", lambda e, ch=ch, ec=ec: e.activation(
                        out=sg[ec].v(0, (1, n)), in_=omt.v(ch * NG, (1, n)), func=AF.Sigmoid),
                        reads=[romt], writes=[rsg[ec]])
                    P.add("dve", lambda e, ch=ch, ec=ec: e.scalar_tensor_tensor(
                        out=tmpd[ec].v(0, (1, n)), in0=hbuf.v(ch * NG, (1, n)), scalar=outg.v(ch, (1, 1)),
                        in1=rs_.v(0, (1, n)), op0=ALU.mult, op1=ALU.mult), reads=[rhb, rrs, rcw], writes=[rtmpd[ec]])
                    P.add("dve", lambda e, ch=ch, ec=ec: e.tensor_tensor(
                        out=mot.v(ch * NG, (1, n)), in0=tmpd[ec].v(0, (1, n)), in1=sg[ec].v(0, (1, n)), op=ALU.mult),
                        reads=[rtmpd[ec], rsg[ec]], writes=[rmot])
            P.add("sp", lambda e: e.dma_start(out=dv(S["MOAO"], t0, (NKC * NT, 128), (NT, 8), (1, n)),
                                              in_=mot.v(0, (NG, 8), (1, n))), reads=[rmot], dma=True)

        for g in range(4):
            run_group(g * 512, 512, 64, False)
        run_group(NP_TOK, 64, 4, True)
        P.barrier()
        C.sb = C.sb_save


NEGB = -1.0e30
ATT_SCALE = 128 ** -0.5


def t5_bucket_np(d):
    d = np.maximum(np.asarray(d), 0)
    ratio = np.log(np.maximum(d, 1).astype(np.float32) / np.float32(16)) / np.float32(np.log(8.0))
    large = 16 + (ratio * np.float32(16)).astype(np.int32)
    large = np.minimum(large, 31)
    return np.where(d < 16, d, large)


def host_bias_tables(t5):
    s = np.arange(128)[:, None]
    t = np.arange(128)[None, :]
    kinds = [t5_bucket_np(t - s), t5_bucket_np(t - s + 128), np.full((128, 128), 31)]
    bdp = np.zeros((2, 3, 128, 4, 128), np.float32)
    for g in range(2):
        for k in range(3):
            for h in range(4):
                bdp[g, k, :, h, :] = t5[kinds[k], 4 * g + h]
    sk = np.arange(128)[:, None, None]
    kb = np.arange(16)[None, :, None]
    tt = np.arange(4)[None, None, :]
    bk = t5_bucket_np(2048 + tt - (kb * 128 + sk))
    bsp = np.zeros((2, 128, 16, 4, 4), np.float32)
    bsn = np.zeros((2, 4, 4, 4), np.float32)
    bn = t5_bucket_np(np.arange(4)[None, :] - np.arange(4)[:, None])
    for g in range(2):
        for h in range(4):
            bsp[g, :, :, h, :] = t5[bk, 4 * g + h]
            bsn[g, :, h, :] = t5[bn, 4 * g + h]
    return bdp.reshape(6, 128, 512), bsp.reshape(2, 128, 256), bsn.reshape(2, 4, 16)


def phase_dsa(C):
    P, I, S, O, nc = C.P, C.I, C.S, C.O, C.nc
    with contextlib.ExitStack() as es:
        def sb(name, width, dt):
            return SB(es.enter_context(nc.sbuf_tensor("ds_" + name, [128, padw(width, dt)], dt)), padw(width, dt), dt)
        psb = [SB(p.t.bitcast(BF16), 1024, BF16) for p in C.ps]
        ki2 = sb("ki2", NT, BF16)
        ka = sb("ka", 2 * NT, BF16)
        va = sb("va", 16 * 256, BF16)
        qi = sb("qi", 4 * NT, BF16)
        qa = sb("qa", 8 * NT, BF16)
        wi = sb("wi", 17 * 8, F32)
        bdp = sb("bdp", 6 * 512, F32)
        negtri = sb("negtri", 128, F32)
        rin = Res()
        P.add("sp", lambda e: e.dma_start(out=ki2.v(0, (1, NT)), in_=S["KI2"]), writes=[rin], dma=True)
        P.add("sp", lambda e: e.dma_start(out=ka.v(0, (1, 2 * NT)), in_=dv(S["KA"], 0, (2 * NT, 128), (1, 2 * NT))),
              writes=[rin], dma=True)
        P.add("sp", lambda e: e.dma_start(out=va.v(0, (256, 16), (1, 256)),
                                          in_=dv(S["VA"], 0, (256, 128), (128 * 256, 16), (1, 256))),
              writes=[rin], dma=True)
        P.add("sp", lambda e: e.dma_start(out=qi.v(0, (1, 4 * NT)), in_=dv(S["QI"], 0, (4 * NT, 128), (1, 4 * NT))),
              writes=[rin], dma=True)
        P.add("sp", lambda e: e.dma_start(out=qa.v(0, (1, 8 * NT)), in_=dv(S["QA"], 0, (8 * NT, 128), (1, 8 * NT))),
              writes=[rin], dma=True)
        P.add("sp", lambda e: e.dma_start(out=wi.v(0, (8, 16), (1, 8)),
                                          in_=dv(S["WI"], 0, (8, 128), (128 * 8, 16), (1, 8))), writes=[rin], dma=True)
        P.add("sp", lambda e: e.dma_start(out=wi.v(128, (1, 8), np_=64), in_=dv(S["WI"], NP_TOK * 8, (8, 64), (1, 8))),
              writes=[rin], dma=True)
        P.add("sp", lambda e: e.dma_start(out=bdp.v(0, (512, 6), (1, 512)),
                                          in_=dv(I["bdp"], 0, (512, 128), (128 * 512, 6), (1, 512))),
              writes=[rin], dma=True)
        P.add("sp", lambda e: e.dma_start(out=negtri.v(0, (1, 128)), in_=I["negtri"]), writes=[rin], dma=True)
        acc = sb("acc", 2048 + 64, F32)
        racc = Res()
        work = sb("work", 2048 + 64, F32)
        rwork = Res()
        rl = [sb("rl%d" % i, 512, F32) for i in range(2)]
        rrl = [Res(), Res()]
        mx = sb("mx", 8, F32)
        rmx = Res()
        msk = sb("msk", 2048 + 128, BF16)
        rmsk = Res()
        mskT = sb("mskT", 17 * 128, BF16)
        rmskT = Res()
        lg = [sb("lg%d" % i, 512, F32) for i in range(2)]
        rlg = [Res(), Res()]
        ex = [sb("ex%d" % i, 512, BF16) for i in range(2)]
        rex = [Res(), Res()]
        pT = [sb("pT%d" % i, 512, BF16) for i in range(2)]
        rpT = [Res(), Res()]
        rden = sb("rden", 512, F32)
        rrden = Res()
        aot = [sb("aot%d" % i, 8 * 128, BF16) for i in range(2)]
        raot = [Res(), Res()]
        cnt = {"e": 0, "l": 0}
        def scores(i):
            q0 = i * 128
            S_i = (i + 1) * 128
            npieces = (S_i + 511) // 512
            for pc in range(npieces):
                k0 = pc * 512
                kn = min(512, S_i - k0)
                for j in range(8):
                    pbase = 64 * (j % 2)
                    bank = cnt["e"] % 2
                    s = cnt["e"] % 2
                    cnt["e"] += 1
                    P.add("pe", lambda e, j=j, pbase=pbase, bank=bank, k0=k0, kn=kn, q0=q0: e.matmul(
                        C.ps[bank].v(0, (1, kn)), qi.v((j // 2) * NT + q0, (1, 128), p0=pbase, np_=64),
                        ki2.v(k0, (1, kn), p0=pbase, np_=64), start=True, stop=True),
                        reads=[rin], writes=[C.rps[bank]])
                    P.add("act", lambda e, s=s, bank=bank, kn=kn: e.activation(
                        out=rl[s].v(0, (1, kn)), in_=C.ps[bank].v(0, (1, kn)), func=AF.Relu),
                        reads=[C.rps[bank]], writes=[rrl[s]])
                    if j == 0:
                        P.add("dve", lambda e, s=s, k0=k0, kn=kn, i=i, j=j: e.tensor_scalar(
                            acc.v(k0, (1, kn)), rl[s].v(0, (1, kn)), wi.v(i * 8 + j, (1, 1)), None, op0=ALU.mult),
                            reads=[rrl[s], rin], writes=[racc])
                    else:
                        P.add("dve", lambda e, s=s, k0=k0, kn=kn, i=i, j=j: e.scalar_tensor_tensor(
                            out=acc.v(k0, (1, kn)), in0=rl[s].v(0, (1, kn)), scalar=wi.v(i * 8 + j, (1, 1)),
                            in1=acc.v(k0, (1, kn)), op0=ALU.mult, op1=ALU.add),
                            reads=[rrl[s], rin, racc], writes=[racc])
            P.add("dve", lambda e, q0=q0: e.tensor_tensor(
                out=acc.v(q0, (1, 128)), in0=acc.v(q0, (1, 128)), in1=negtri.v(0, (1, 128)), op=ALU.add),
                reads=[racc, rin], writes=[racc])

        def topk_gen(i):
            S_i = (i + 1) * 128
            if i >= 2:
                for r_ in range(32):
                    src = acc if r_ == 0 else work
                    rsrc = racc if r_ == 0 else rwork
                    P.add("dve", lambda e, src=src, S_i=S_i: e.max(out=mx.v(0, (1, 8)), in_=src.v(0, (1, S_i))),
                          reads=[rsrc], writes=[rmx])
                    if r_ < 31:
                        P.add("dve", lambda e, src=src, S_i=S_i: e.match_replace(
                            out=work.v(0, (1, S_i)), in_to_replace=mx.v(0, (1, 8)), in_values=src.v(0, (1, S_i)),
                            imm_value=NEGB), reads=[rsrc, rmx], writes=[rwork])
                    yield
                P.add("dve", lambda e, S_i=S_i: e.tensor_scalar(
                    msk.v(0, (1, S_i)), acc.v(0, (1, S_i)), mx.v(7, (1, 1)), None, op0=ALU.is_ge),
                    reads=[racc, rmx], writes=[rmsk])
            else:
                P.add("dve", lambda e, S_i=S_i: e.tensor_scalar(
                    msk.v(0, (1, S_i)), acc.v(0, (1, S_i)), -1.0e29, None, op0=ALU.is_ge),
                    reads=[racc], writes=[rmsk])
            yield

        def masks(i):
            for kb in range(i + 1):
                P.add("pe", lambda e, kb=kb: e.transpose(
                    psb[kb // 8].v((kb % 8) * 128, (1, 128)), msk.v(kb * 128, (1, 128)), C.identb.v(0, (1, 128))),
                    reads=[rmsk, C.rconst], writes=[C.rps[kb // 8]])
            for half in range((i + 8) // 8):
                nb_ = min(8, i + 1 - half * 8)
                P.add("act", lambda e, half=half, nb_=nb_: e.activation(
                    out=mskT.v(half * 1024, (1, nb_ * 128)), in_=psb[half].v(0, (1, nb_ * 128)), func=AF.Copy),
                    reads=[C.rps[half]], writes=[rmskT])

        def attn_gen(i):
            q0 = i * 128
            ao_s = i % 2
            for g in range(2):
                ob, db = (4, 5) if g == 0 else (6, 7)
                for kb in range(i + 1):
                    kind = 0 if kb == i else (1 if kb == i - 1 else 2)
                    l = cnt["l"] % 2
                    cnt["l"] += 1
                    lb = 2 + l
                    P.add("pe", lambda e, g=g, kb=kb, lb=lb, q0=q0: e.matmul(
                        C.ps[lb].v(0, (128, 4), (1, 128)), ka.v(g * NT + kb * 128, (1, 128)),
                        qa.v(4 * g * NT + q0, (NT, 4), (1, 128)), start=True, stop=True),
                        reads=[rin], writes=[C.rps[lb]])
                    P.add("dve", lambda e, l=l, lb=lb, g=g, kind=kind: e.scalar_tensor_tensor(
                        out=lg[l].v(0, (1, 512)), in0=C.ps[lb].v(0, (1, 512)), scalar=ATT_SCALE,
                        in1=bdp.v((g * 3 + kind) * 512, (1, 512)), op0=ALU.mult, op1=ALU.add),
                        reads=[C.rps[lb], rin], writes=[rlg[l]])
                    P.add("act", lambda e, l=l: e.activation(out=ex[l].v(0, (1, 512)), in_=lg[l].v(0, (1, 512)),
                                                             func=AF.Exp), reads=[rlg[l]], writes=[rex[l]])
                    P.add("dve", lambda e, l=l, kb=kb: e.tensor_tensor(
                        out=pT[l].v(0, (128, 4), (1, 128)), in0=ex[l].v(0, (128, 4), (1, 128)),
                        in1=mskT.v(kb * 128, (0, 4), (1, 128)), op=ALU.mult),
                        reads=[rex[l], rmskT], writes=[rpT[l]])
                    P.add("pe", lambda e, l=l, g=g, kb=kb, ob=ob, i=i: e.matmul(
                        C.ps[ob].v(0, (1, 512)), va.v(kb * 256 + g * 128, (1, 128)), pT[l].v(0, (1, 512)),
                        start=(kb == 0), stop=(kb == i)), reads=[rin, rpT[l]], writes=[C.rps[ob]])
                    P.add("pe", lambda e, l=l, kb=kb, db=db, i=i: e.matmul(
                        C.ps[db].v(0, (1, 512)), C.ones_b.v(0, (1, 128)), pT[l].v(0, (1, 512)),
                        start=(kb == 0), stop=(kb == i)), reads=[C.rconst, rpT[l]], writes=[C.rps[db]])
                    yield
                P.add("dve", lambda e, db=db: e.reciprocal(rden.v(0, (1, 512)), C.ps[db].v(0, (1, 512))),
                      reads=[C.rps[db]], writes=[rrden])
                P.add("dve", lambda e, ob=ob, g=g, ao_s=ao_s: e.tensor_tensor(
                    out=aot[ao_s].v(4 * g * 128, (1, 512)), in0=C.ps[ob].v(0, (1, 512)), in1=rden.v(0, (1, 512)),
                    op=ALU.mult), reads=[C.rps[ob], rrden], writes=[raot[ao_s]])
                yield
            P.add("sp", lambda e, ao_s=ao_s, q0=q0: e.dma_start(
                out=dv(S["MOAO"], 8 * NT + q0, (NKC * NT, 128), (NT, 8), (1, 128)),
                in_=aot[ao_s].v(0, (128, 8), (1, 128))), reads=[raot[ao_s]], dma=True)
            yield

        if "dsa_noprompt" not in C.dbg:
            scores(0)
            for _ in topk_gen(0):
                pass
            for i in range(16):
                masks(i)
                ga = attn_gen(i)
                if i + 1 < 16:
                    scores(i + 1)
                    gt = topk_gen(i + 1)
                    nt = 33 if i + 1 >= 2 else 1
                else:
                    gt = iter(())
                    nt = 0
                na = 2 * (i + 1) + 3
                a_done = t_done = 0
                while a_done < na or t_done < nt:
                    if t_done >= nt or (a_done < na and a_done * max(nt, 1) <= t_done * na):
                        next(ga, None)
                        a_done += 1
                    else:
                        next(gt, None)
                        t_done += 1
                for _ in ga:
                    pass
                for _ in gt:
                    pass
        P.barrier()


def phase_dsa_sample(C):
    P, I, S, O, nc = C.P, C.I, C.S, C.O, C.nc
    NK = 2048
    with contextlib.ExitStack() as es:
        def sb(name, width, dt):
            return SB(es.enter_context(nc.sbuf_tensor("dq_" + name, [128, padw(width, dt)], dt)), padw(width, dt), dt)
        psb = [SB(p.t.bitcast(BF16), 1024, BF16) for p in C.ps]
        ki2s = sb("ki2s", 64, BF16)
        kas = sb("kas", 2 * 64, BF16)
        qis = sb("qis", 4 * 64, BF16)
        qas = sb("qas", 8 * 64, BF16)
        wis = sb("wis", 16 * 8, F32)
        vns = sb("vns", 16 * 256, BF16)
        bsp = sb("bsp", 2 * 256, F32)
        bsn = sb("bsn", 2 * 16, F32)
        negtri = sb("negtri", 128, F32)
        ptr = sb("ptr", 256, I32)
        pidx = sb("pidx", 1, F32)
        idx = sb("idx", 256, U32)
        rin = Res()
        t0 = NP_TOK
        P.add("sp", lambda e: e.dma_start(out=ki2s.v(0, (1, 64)), in_=dv(S["KI2"], t0, (NT, 128), (1, 64))),
              writes=[rin], dma=True)
        P.add("sp", lambda e: e.dma_start(out=kas.v(0, (64, 2), (1, 64)),
                                          in_=dv(S["KA"], t0, (2 * NT, 128), (NT, 2), (1, 64))), writes=[rin], dma=True)
        P.add("sp", lambda e: e.dma_start(out=qis.v(0, (64, 4), (1, 64)),
                                          in_=dv(S["QI"], t0, (4 * NT, 128), (NT, 4), (1, 64))), writes=[rin], dma=True)
        P.add("sp", lambda e: e.dma_start(out=qas.v(0, (64, 8), (1, 64)),
                                          in_=dv(S["QA"], t0, (8 * NT, 128), (NT, 8), (1, 64))), writes=[rin], dma=True)
        P.add("sp", lambda e: e.dma_start(out=wis.v(0, (8, 16), (1, 8), np_=4),
                                          in_=dv(S["WI"], t0 * 8, (8, 4), (32, 16), (1, 8))), writes=[rin], dma=True)
        P.add("sp", lambda e: e.dma_start(out=vns.v(0, (256, 16), (1, 256), np_=4),
                                          in_=dv(S["VA"], t0 * 256, (256, 4), (1024, 16), (1, 256))),
              writes=[rin], dma=True)
        P.add("sp", lambda e: e.dma_start(out=bsp.v(0, (256, 2), (1, 256)),
                                          in_=dv(I["bsp"], 0, (256, 128), (128 * 256, 2), (1, 256))),
              writes=[rin], dma=True)
        P.add("sp", lambda e: e.dma_start(out=bsn.v(0, (16, 2), (1, 16), np_=4),
                                          in_=dv(I["bsn"], 0, (16, 4), (64, 2), (1, 16))), writes=[rin], dma=True)
        P.add("sp", lambda e: e.dma_start(out=negtri.v(0, (1, 128)), in_=I["negtri"]), writes=[rin], dma=True)
        P.add("sp", lambda e: e.dma_start(out=ptr.v(0, (1, 256)), in_=dv(I["ptab"], 0, (0, 128), (1, 256))),
              writes=[rin], dma=True)
        P.add("sp", lambda e: e.dma_start(out=pidx.v(0, (1, 1)), in_=I["pidx"]), writes=[rin], dma=True)
        ridx = Res()
        P.add("dve", lambda e: e.scalar_tensor_tensor(out=idx.v(0, (1, 256)), in0=ptr.v(0, (1, 256)), scalar=128.0,
                                                      in1=pidx.v(0, (0, 256)), op0=ALU.mult, op1=ALU.add),
              reads=[rin], writes=[ridx])
        kis = [sb("kis%d" % i, 16 * 128, F32) for i in range(2)]
        rkis = [Res(), Res()]
        kiT2 = [sb("kiT%d" % i, NK + 64, BF16) for i in range(2)]
        rkiT2 = [Res(), Res()]
        accb = [sb("accb%d" % i, NK + 64, F32) for i in range(2)]
        raccb = [Res(), Res()]
        rl = [sb("rl%d" % i, 512, F32) for i in range(2)]
        rrl = [Res(), Res()]
        accS = sb("accS", NK + 64, F32)
        raccS = Res()
        work = sb("work", NK + 64, F32)
        rwork = Res()
        mx = sb("mx", 8, F32)
        rmx = Res()
        msk = sb("msk", NK + 64, BF16)
        rmsk = Res()
        mskT = sb("mskT", 17 * 64, BF16)
        rmskT = Res()
        kcs = [sb("kcs%d" % i, 16 * 256, BF16) for i in range(2)]
        rkcs = [Res(), Res()]
        vcs = [sb("vcs%d" % i, 16 * 256, BF16) for i in range(2)]
        rvcs = [Res(), Res()]
        kTs2 = [sb("kTs%d" % i, 2 * NK, BF16) for i in range(2)]
        rkTs2 = [Res(), Res()]
        lgS = sb("lgS", 256 + 16, F32)
        rlgS = Res()
        lgn = sb("lgn", 16, F32)
        rlgn = Res()
        exS = sb("exS", 256, BF16)
        rexS = Res()
        exn = sb("exn", 16, BF16)
        rexn = Res()
        pTs = sb("pTs", 256, BF16)
        rpTs = Res()
        pTn = sb("pTn", 16, BF16)
        rpTn = Res()
        rden = sb("rden", 16, F32)
        rrden = Res()
        aoS = sb("aoS", 8 * 64, BF16)
        raoS = Res()
        cnt = {"e": 0}
        IO = bass.IndirectOffsetOnAxis
        for b in range(NSB):
            ks = b % 2
            kiT, rkiT = kiT2[ks], rkiT2[ks]
            for pg in range(16):
                P.add("pool", lambda e, ks=ks, pg=pg, b=b: e.indirect_dma_start(
                    out=kis[ks].v(pg * 128, (1, 64)), out_offset=None, in_=I["cidx"],
                    in_offset=IO(ap=idx.v(b * 16 + pg, (1, 1)), axis=0)), reads=[ridx], writes=[rkis[ks]], dma=True)
            P.add("act", lambda e, ks=ks: e.activation(out=kis[ks].v(64, (128, 16), (1, 64)),
                                                       in_=kis[ks].v(0, (128, 16), (1, 64)), func=AF.Copy),
                  reads=[rkis[ks]], writes=[rkis[ks]])
            for q4 in range(4):
                bank = 2 + q4 % 2
                for j in range(4):
                    pg = q4 * 4 + j
                    P.add("pe", lambda e, ks=ks, pg=pg, j=j, bank=bank: e.transpose(
                        C.ps[bank].v(j * 128, (1, 128)), kis[ks].v(pg * 128, (1, 128)), C.ident.v(0, (1, 128))),
                        reads=[rkis[ks], C.rconst], writes=[C.rps[bank]])
                P.add("act", lambda e, q4=q4, bank=bank: e.activation(
                    out=kiT.v(q4 * 512, (1, 512)), in_=C.ps[bank].v(0, (1, 512)), func=AF.Copy),
                    reads=[C.rps[bank]], writes=[rkiT])
            P.add("dve", lambda e, b=b: e.tensor_copy(kiT.v(NK, (1, 4)), ki2s.v(4 * b, (1, 4))),
                  reads=[rin, rkiT], writes=[rkiT])
            ab = b % 2
            for (k0, kn) in [(0, 512), (512, 512), (1024, 512), (1536, 512), (NK, 4)]:
                for j in range(8):
                    pbase = 64 * (j % 2)
                    bank = cnt["e"] % 2
                    s = cnt["e"] % 2
                    cnt["e"] += 1
                    P.add("pe", lambda e, j=j, pbase=pbase, bank=bank, k0=k0, kn=kn, b=b: e.matmul(
                        C.ps[bank].v(0, (1, kn), np_=4), qis.v((j // 2) * 64 + 4 * b, (1, 4), p0=pbase, np_=64),
                        kiT.v(k0, (1, kn), p0=pbase, np_=64), start=True, stop=True),
                        reads=[rin, rkiT], writes=[C.rps[bank]])
                    P.add("act", lambda e, s=s, bank=bank, kn=kn: e.activation(
                        out=rl[s].v(0, (1, kn), np_=4), in_=C.ps[bank].v(0, (1, kn), np_=4), func=AF.Relu),
                        reads=[C.rps[bank]], writes=[rrl[s]])
                    if j == 0:
                        P.add("dve", lambda e, s=s, k0=k0, kn=kn, b=b, j=j, ab=ab: e.tensor_scalar(
                            accb[ab].v(k0, (1, kn), np_=4), rl[s].v(0, (1, kn), np_=4),
                            wis.v(b * 8 + j, (1, 1), np_=4), None, op0=ALU.mult),
                            reads=[rrl[s], rin], writes=[raccb[ab]])
                    else:
                        P.add("dve", lambda e, s=s, k0=k0, kn=kn, b=b, j=j, ab=ab: e.scalar_tensor_tensor(
                            out=accb[ab].v(k0, (1, kn), np_=4), in0=rl[s].v(0, (1, kn), np_=4),
                            scalar=wis.v(b * 8 + j, (1, 1), np_=4), in1=accb[ab].v(k0, (1, kn), np_=4),
                            op0=ALU.mult, op1=ALU.add), reads=[rrl[s], rin, raccb[ab]], writes=[raccb[ab]])
            P.add("dve", lambda e, ab=ab: e.tensor_tensor(
                out=accb[ab].v(NK, (1, 4), np_=4), in0=accb[ab].v(NK, (1, 4), np_=4), in1=negtri.v(0, (1, 4), np_=4),
                op=ALU.add), reads=[raccb[ab], rin], writes=[raccb[ab]])
            P.add("sp", lambda e, ab=ab, b=b: e.dma_start(out=accS.v(0, (1, NK + 4), p0=4 * b, np_=4),
                                                          in_=accb[ab].v(0, (1, NK + 4), np_=4)),
                  reads=[raccb[ab]], writes=[raccS], dma=True)
        SS = NK + 4
        for r_ in range(32):
            src = accS if r_ == 0 else work
            rsrc = raccS if r_ == 0 else rwork
            P.add("dve", lambda e, src=src: e.max(out=mx.v(0, (1, 8), np_=64), in_=src.v(0, (1, SS), np_=64)),
                  reads=[rsrc], writes=[rmx])
            if r_ < 31:
                P.add("dve", lambda e, src=src: e.match_replace(
                    out=work.v(0, (1, SS), np_=64), in_to_replace=mx.v(0, (1, 8), np_=64),
                    in_values=src.v(0, (1, SS), np_=64), imm_value=NEGB), reads=[rsrc, rmx], writes=[rwork])
        P.add("dve", lambda e: e.tensor_scalar(msk.v(0, (1, SS), np_=64), accS.v(0, (1, SS), np_=64),
                                               mx.v(7, (1, 1), np_=64), None, op0=ALU.is_ge),
              reads=[raccS, rmx], writes=[rmsk])
        for kb in range(16):
            P.add("pe", lambda e, kb=kb: e.transpose(
                psb[kb // 8].v((kb % 8) * 64, (1, 64)), msk.v(kb * 128, (1, 128), np_=64),
                C.identb.v(0, (1, 64), np_=64)), reads=[rmsk, C.rconst], writes=[C.rps[kb // 8]])
        for half in range(2):
            P.add("act", lambda e, half=half: e.activation(
                out=mskT.v(half * 512, (1, 512)), in_=psb[half].v(0, (1, 512)), func=AF.Copy),
                reads=[C.rps[half]], writes=[rmskT])
        P.add("pe", lambda e: e.transpose(psb[2].v(0, (1, 64), np_=4), msk.v(NK, (1, 4), np_=64),
                                          C.identb.v(0, (1, 64), np_=64)), reads=[rmsk, C.rconst], writes=[C.rps[2]])
        P.add("act", lambda e: e.activation(out=mskT.v(16 * 64, (1, 64), np_=4), in_=psb[2].v(0, (1, 64), np_=4),
                                            func=AF.Copy), reads=[C.rps[2]], writes=[rmskT])
        for b in range(NSB):
            cs = b % 2
            kTs, rkTs = kTs2[cs], rkTs2[cs]
            for pg in range(16):
                P.add("pool", lambda e, cs=cs, pg=pg, b=b: e.indirect_dma_start(
                    out=kcs[cs].v(pg * 256, (1, 256)), out_offset=None, in_=I["ck"],
                    in_offset=IO(ap=idx.v(b * 16 + pg, (1, 1)), axis=0)), reads=[ridx], writes=[rkcs[cs]], dma=True)
                P.add("pool", lambda e, cs=cs, pg=pg, b=b: e.indirect_dma_start(
                    out=vcs[cs].v(pg * 256, (1, 256)), out_offset=None, in_=I["cv"],
                    in_offset=IO(ap=idx.v(b * 16 + pg, (1, 1)), axis=0)), reads=[ridx], writes=[rvcs[cs]], dma=True)
            for g in range(2):
                for half in range(2):
                    bank = 2 + half
                    for j in range(8):
                        pg = half * 8 + j
                        P.add("pe", lambda e, cs=cs, pg=pg, j=j, g=g, bank=bank: e.transpose(
                            psb[bank].v(j * 128, (1, 128)), kcs[cs].v(pg * 256 + g * 128, (1, 128)),
                            C.identb.v(0, (1, 128))), reads=[rkcs[cs], C.rconst], writes=[C.rps[bank]])
                    P.add("act" if half == 0 else "dve",
                          (lambda e, g=g, half=half, bank=bank: e.activation(
                              out=kTs.v(g * NK + half * 1024, (1, 1024)), in_=psb[bank].v(0, (1, 1024)), func=AF.Copy))
                          if half == 0 else
                          (lambda e, g=g, half=half, bank=bank: e.tensor_copy(
                              kTs.v(g * NK + half * 1024, (1, 1024)), psb[bank].v(0, (1, 1024)))),
                          reads=[C.rps[bank]], writes=[rkTs])
                lb = 4 + g
                for kb in range(16):
                    P.add("pe", lambda e, g=g, kb=kb, lb=lb, b=b: e.matmul(
                        C.ps[lb].v(kb * 16, (4, 4), (1, 4)), kTs.v(g * NK + kb * 128, (1, 128)),
                        qas.v(4 * g * 64 + 4 * b, (64, 4), (1, 4)), start=True, stop=True),
                        reads=[rkTs, rin], writes=[C.rps[lb]])
                P.add("pe", lambda e, g=g, lb=lb, b=b: e.matmul(
                    C.ps[lb].v(256, (4, 4), (1, 4), np_=4), kas.v(g * 64 + 4 * b, (1, 4)),
                    qas.v(4 * g * 64 + 4 * b, (64, 4), (1, 4)), start=True, stop=True),
                    reads=[rin], writes=[C.rps[lb]])
                P.add("dve", lambda e, g=g, lb=lb: e.scalar_tensor_tensor(
                    out=lgS.v(0, (1, 256)), in0=C.ps[lb].v(0, (1, 256)), scalar=ATT_SCALE,
                    in1=bsp.v(g * 256, (1, 256)), op0=ALU.mult, op1=ALU.add),
                    reads=[C.rps[lb], rin], writes=[rlgS])
                P.add("dve", lambda e, g=g, lb=lb: e.scalar_tensor_tensor(
                    out=lgn.v(0, (1, 16), np_=4), in0=C.ps[lb].v(256, (1, 16), np_=4), scalar=ATT_SCALE,
                    in1=bsn.v(g * 16, (1, 16), np_=4), op0=ALU.mult, op1=ALU.add),
                    reads=[C.rps[lb], rin], writes=[rlgn])
                P.add("act", lambda e: e.activation(out=exS.v(0, (1, 256)), in_=lgS.v(0, (1, 256)), func=AF.Exp),
                      reads=[rlgS], writes=[rexS])
                P.add("act", lambda e: e.activation(out=exn.v(0, (1, 16), np_=4), in_=lgn.v(0, (1, 16), np_=4),
                                                    func=AF.Exp), reads=[rlgn], writes=[rexn])
                P.add("dve", lambda e, b=b: e.tensor_tensor(
                    out=pTs.v(0, (16, 16), (4, 4), (1, 4)), in0=exS.v(0, (16, 16), (4, 4), (1, 4)),
                    in1=mskT.v(4 * b, (64, 16), (0, 4), (1, 4)), op=ALU.mult), reads=[rexS, rmskT], writes=[rpTs])
                P.add("dve", lambda e, b=b: e.tensor_tensor(
                    out=pTn.v(0, (4, 4), (1, 4), np_=4), in0=exn.v(0, (4, 4), (1, 4), np_=4),
                    in1=mskT.v(16 * 64 + 4 * b, (0, 4), (1, 4), np_=4), op=ALU.mult),
                    reads=[rexn, rmskT], writes=[rpTn])
                for kb in range(16):
                    P.add("pe", lambda e, cs=cs, g=g, kb=kb: e.matmul(
                        C.ps[6].v(0, (1, 16)), vcs[cs].v(kb * 256 + g * 128, (1, 128)), pTs.v(kb * 16, (1, 16)),
                        start=(kb == 0), stop=False), reads=[rvcs[cs], rpTs], writes=[C.rps[6]])
                P.add("pe", lambda e, g=g, b=b: e.matmul(
                    C.ps[6].v(0, (1, 16)), vns.v(b * 256 + g * 128, (1, 128), np_=4), pTn.v(0, (1, 16), np_=4),
                    start=False, stop=True), reads=[rin, rpTn], writes=[C.rps[6]])
                for kb in range(16):
                    P.add("pe", lambda e, kb=kb: e.matmul(
                        C.ps[7].v(0, (1, 16)), C.ones_b.v(0, (1, 128)), pTs.v(kb * 16, (1, 16)),
                        start=(kb == 0), stop=False), reads=[C.rconst, rpTs], writes=[C.rps[7]])
                P.add("pe", lambda e: e.matmul(
                    C.ps[7].v(0, (1, 16)), C.ones_b.v(0, (1, 128), np_=4), pTn.v(0, (1, 16), np_=4),
                    start=False, stop=True), reads=[C.rconst, rpTn], writes=[C.rps[7]])
                P.add("dve", lambda e: e.reciprocal(rden.v(0, (1, 16)), C.ps[7].v(0, (1, 16))),
                      reads=[C.rps[7]], writes=[rrden])
                P.add("dve", lambda e, g=g, b=b: e.tensor_tensor(
                    out=aoS.v(4 * g * 64 + 4 * b, (64, 4), (1, 4)), in0=C.ps[6].v(0, (4, 4), (1, 4)),
                    in1=rden.v(0, (4, 4), (1, 4)), op=ALU.mult), reads=[C.rps[6], rrden], writes=[raoS])
        P.add("sp", lambda e: e.dma_start(out=dv(S["MOAO"], 8 * NT + NP_TOK, (NKC * NT, 128), (NT, 8), (1, 64)),
                                          in_=aoS.v(0, (64, 8), (1, 64))), reads=[raoS], dma=True)
        P.barrier()


OUT_ORDER = ["y_p", "y_s", "k_p", "v_p", "idxk_p", "C_p", "n_p", "m_p", "conv_p",
             "k_s", "v_s", "idxk_s", "C_s", "n_s", "m_s", "conv_s"]


def kernel(**inputs):
    inp = {k: np.asarray(v) for k, v in inputs.items()}
    nc = build_program()
    sh = prep_shared(inp)
    in_maps = []
    for core in range(8):
        m = prep_core(inp, core, sh)
        in_maps.append({k: np.ascontiguousarray(v) for k, v in m.items()})
    res = run_bass_kernel_spmd(nc, in_maps, core_ids=list(range(8)))
    R = res.results

    def cat(name, shape_p):
        return np.stack([np.asarray(R[c][name], dtype=np.float32).reshape(shape_p) for c in range(8)], 0)

    y_p = cat("y_p", (2048, 2048))
    y_s = cat("y_s", (16, 4, 2048)).reshape(128, 4, 2048)
    k_p = cat("k_p", (2048, 2, 128))[None]
    v_p = cat("v_p", (2048, 2, 128))[None]
    i_p = cat("idxk_p", (2048, 64))[None]
    C_p = cat("C_p", (4, 256, 256))[None]
    n_p = cat("n_p", (4, 256))[None]
    m_p = cat("m_p", (4,))[None]
    cv_p = cat("conv_p", (3, 2048))[None]
    k_s = cat("k_s", (16, 4, 2, 128)).reshape(128, 4, 2, 128)[None]
    v_s = cat("v_s", (16, 4, 2, 128)).reshape(128, 4, 2, 128)[None]
    i_s = cat("idxk_s", (16, 4, 64)).reshape(128, 4, 64)[None]
    C_s = cat("C_s", (16, 4, 256, 256)).reshape(128, 4, 256, 256)[None]
    n_s = cat("n_s", (16, 4, 256)).reshape(128, 4, 256)[None]
    m_s = cat("m_s", (16, 4)).reshape(128, 4)[None]
    cv_s = cat("conv_s", (16, 3, 2048)).reshape(128, 3, 2048)[None]
    return (y_p, y_s, k_p, v_p, i_p, C_p, n_p, m_p, cv_p, k_s, v_s, i_s, C_s, n_s, m_s, cv_s)
```
